# Optimizing a Trainium2 kernel written in Bass

```python
import math
import jax, jax.numpy as jnp
from jax import lax
import numpy as np

D_MODEL = 1024
BATCH = 4
SEQ = 8192
DEPTH = 2

DIFF_HEADS = 4
DIFF_QK_DIM = 32
DIFF_V_DIM = 2 * DIFF_QK_DIM
ATTN_BLOCK = 128
DIL_HEADS = 6
DIL_HEAD_DIM = 64
DIL_PATTERNS = ((128, 1), (512, 4), (2048, 16))
DIL_BLOCK = 128
SSD_HEADS = 6
SSD_HEAD_DIM = 64
SSD_GROUPS = 2
SSD_HEADS_PER_GROUP = SSD_HEADS // SSD_GROUPS
SSD_STATE = 128
SSD_CONV = 4
SSD_CHUNK = 128
SSD_INNER = SSD_HEADS * SSD_HEAD_DIM
SSD_XBC = SSD_INNER + 2 * SSD_GROUPS * SSD_STATE
DIFF_WIDTH = DIFF_HEADS * DIFF_V_DIM
DIL_WIDTH = DIL_HEADS * DIL_HEAD_DIM
MIX_WIDTH = DIFF_WIDTH + DIL_WIDTH + SSD_INNER
IN_SIZES = (DIFF_HEADS * 2 * DIFF_QK_DIM, DIFF_HEADS * 2 * DIFF_QK_DIM, DIFF_WIDTH,
            DIL_WIDTH, DIL_WIDTH, DIL_WIDTH,
            SSD_INNER, SSD_XBC, SSD_HEADS)
IN_WIDTH = sum(IN_SIZES)
IN_OFFSETS = tuple(sum(IN_SIZES[:i + 1]) for i in range(len(IN_SIZES) - 1))
D_FF = 2816
FFN_CONV = 3
ROPE_THETA = 10000.0
NORM_EPS = 1e-6

kernel_name = "hymba_style_diff_dilated_ssd_hybrid"


def rms_norm(x, gain, eps=NORM_EPS):
    xf = x.astype(jnp.float32)
    y = xf * lax.rsqrt(jnp.mean(xf * xf, axis=-1, keepdims=True) + eps)
    return (y * gain.astype(jnp.float32)).astype(x.dtype)


def rope(x, pos):
    dim = x.shape[-1]
    half = dim // 2
    inv_freq = jnp.exp(-math.log(ROPE_THETA) * jnp.arange(half, dtype=jnp.float32) / half)
    ang = pos.astype(jnp.float32)[:, None] * inv_freq[None, :]
    shape = (1, pos.shape[0]) + (1,) * (x.ndim - 3) + (half,)
    cos = jnp.cos(ang).reshape(shape)
    sin = jnp.sin(ang).reshape(shape)
    xf = x.astype(jnp.float32)
    x1, x2 = xf[..., :half], xf[..., half:]
    out = jnp.concatenate([x1 * cos - x2 * sin, x2 * cos + x1 * sin], axis=-1)
    return out.astype(x.dtype)


def causal_dwconv(x, w, b):
    K, C = w.shape
    y = lax.conv_general_dilated(
        x, w[:, None, :].astype(x.dtype), window_strides=(1,), padding=[(K - 1, 0)],
        dimension_numbers=('NWC', 'WIO', 'NWC'), feature_group_count=C)
    return y + b.astype(x.dtype)


def diff_attention(q, k, v, lam, head_gain, lambda_init):
    Bn, S, H, _, dq = q.shape
    dv = v.shape[-1]
    nblk = S // ATTN_BLOCK
    scale = dq ** -0.5
    qb = q.reshape(Bn, nblk, ATTN_BLOCK, H, 2, dq).transpose(1, 0, 2, 3, 4, 5)
    kpos = jnp.arange(S)

    def block(args):
        qblk, start = args
        s = jnp.einsum('bqhmd,bkhmd->bhmqk', qblk, k,
                       preferred_element_type=jnp.float32) * scale
        qpos = start + jnp.arange(ATTN_BLOCK)
        mask = kpos[None, :] <= qpos[:, None]
        p = jax.nn.softmax(jnp.where(mask, s, -jnp.inf), axis=-1)
        a = (p[:, :, 0] - lam * p[:, :, 1]).astype(v.dtype)
        return jnp.einsum('bhqk,bkhe->bqhe', a, v)

    starts = jnp.arange(nblk) * ATTN_BLOCK
    o = lax.map(block, (qb, starts))
    o = o.transpose(1, 0, 2, 3, 4).reshape(Bn, S, H, dv)
    o = rms_norm(o, head_gain) * (1.0 - lambda_init)
    return o.reshape(Bn, S, H * dv)


def dilated_pattern(q, k, v, window, dilation):
    Bn, S, H, Dh = q.shape
    span = window // dilation
    unit = dilation * DIL_BLOCK
    L = -(-S // unit) * unit
    M = L // dilation
    nb = M // DIL_BLOCK

    def to_blocks(t):
        t = jnp.pad(t, ((0, 0), (0, L - S), (0, 0), (0, 0)))
        t = t.reshape(Bn, M, dilation, H, Dh).transpose(0, 2, 3, 1, 4)
        return t.reshape(Bn, dilation, H, nb, DIL_BLOCK, Dh)

    def with_prev(t):
        prev = jnp.pad(t, ((0, 0), (0, 0), (0, 0), (1, 0), (0, 0), (0, 0)))[:, :, :, :-1]
        return jnp.concatenate([prev, t], axis=4)

    qb = to_blocks(q)
    kk = with_prev(to_blocks(k))
    vv = with_prev(to_blocks(v))
    s = jnp.einsum('brhnqd,brhnkd->brhnqk', qb, kk,
                   preferred_element_type=jnp.float32) * (Dh ** -0.5)
    qi = jnp.arange(nb)[:, None, None] * DIL_BLOCK + jnp.arange(DIL_BLOCK)[None, :, None]
    kj = (jnp.arange(nb)[:, None, None] - 1) * DIL_BLOCK + jnp.arange(2 * DIL_BLOCK)[None, None, :]
    dist = qi - kj
    mask = (kj >= 0) & (dist >= 0) & (dist <= span)
    s = jnp.where(mask, s, -jnp.inf)
    m = jnp.max(s, axis=-1, keepdims=True)
    p = jnp.exp(s - m)
    den = jnp.sum(p, axis=-1, keepdims=True)
    o = jnp.einsum('brhnqk,brhnkd->brhnqd', (p / den).astype(v.dtype), vv)
    lse = (m + jnp.log(den))[..., 0]
    o = o.reshape(Bn, dilation, H, M, Dh).transpose(0, 3, 1, 2, 4).reshape(Bn, L, H, Dh)[:, :S]
    lse = lse.reshape(Bn, dilation, H, M).transpose(0, 3, 1, 2).reshape(Bn, L, H)[:, :S]
    return o, lse


def dilated_attention(q, k, v):
    Bn, S, H, Dh = q.shape
    res = [dilated_pattern(q, k, v, w, d) for (w, d) in DIL_PATTERNS]
    outs = jnp.stack([r[0] for r in res]).astype(jnp.float32)
    wts = jax.nn.softmax(jnp.stack([r[1] for r in res]), axis=0)
    o = jnp.sum(wts[..., None] * outs, axis=0)
    return o.astype(q.dtype).reshape(Bn, S, H * Dh)


def segsum(a):
    T = a.shape[-1]
    cs = jnp.cumsum(a, axis=-1)
    diff = cs[..., :, None] - cs[..., None, :]
    mask = jnp.tril(jnp.ones((T, T), dtype=bool))
    return jnp.where(mask, diff, -jnp.inf)


def ssd_mixer(z, xbc, dt, conv_w, conv_b, dt_bias, A_log, D, norm_gain):
    Bn, S, _ = z.shape
    G, HG, P, N = SSD_GROUPS, SSD_HEADS_PER_GROUP, SSD_HEAD_DIM, SSD_STATE
    nc, l = S // SSD_CHUNK, SSD_CHUNK
    xbc = jax.nn.silu(causal_dwconv(xbc, conv_w, conv_b))
    xs, Bs, Cs = jnp.split(xbc, [SSD_INNER, SSD_INNER + G * N], axis=-1)
    x = xs.reshape(Bn, nc, l, G, HG, P).astype(jnp.float32)
    Bm = Bs.reshape(Bn, nc, l, G, N).astype(jnp.float32)
    Cm = Cs.reshape(Bn, nc, l, G, N).astype(jnp.float32)
    dt = jax.nn.softplus(dt.astype(jnp.float32) + dt_bias.astype(jnp.float32))
    A = -jnp.exp(A_log.astype(jnp.float32))
    dt_c = dt.reshape(Bn, nc, l, G, HG)
    a = (dt * A).reshape(Bn, nc, l, G, HG).transpose(0, 3, 4, 1, 2)
    a_cs = jnp.cumsum(a, axis=-1)
    Lmat = jnp.exp(segsum(a))
    xdt = x * dt_c[..., None]
    cb = jnp.einsum('bclgn,bcsgn->bcgls', Cm, Bm)
    y_diag = jnp.einsum('bcgls,bghcls,bcsghp->bclghp', cb, Lmat, xdt)
    decay_states = jnp.exp(a_cs[..., -1:] - a_cs)
    states = jnp.einsum('bcsgn,bghcs,bcsghp->cbghpn', Bm, decay_states, xdt)
    chunk_decay = jnp.exp(a_cs[..., -1]).transpose(3, 0, 1, 2)

    def step(h, inp):
        s_c, dec = inp
        return dec[..., None, None] * h + s_c, h

    _, prev = lax.scan(step, jnp.zeros_like(states[0]), (states, chunk_decay))
    y_off = jnp.einsum('bclgn,cbghpn,bghcl->bclghp', Cm, prev, jnp.exp(a_cs))
    y = y_diag + y_off + x * D.astype(jnp.float32).reshape(G, HG, 1)
    y = y.reshape(Bn, S, SSD_INNER) * jax.nn.silu(z.astype(jnp.float32))
    yg = y.reshape(Bn, S, G, SSD_INNER // G)
    yg = yg * lax.rsqrt(jnp.mean(yg * yg, axis=-1, keepdims=True) + NORM_EPS)
    out = yg.reshape(Bn, S, SSD_INNER) * norm_gain.astype(jnp.float32)
    return out.astype(z.dtype)


def setup_inputs(seed: int = 0) -> dict:
    key = jax.random.key(seed)
    ks = jax.random.split(key, 24)
    f32 = jnp.float32

    def gain(k, n):
        return 1.0 + 0.01 * jax.random.normal(k, (DEPTH, n), f32)

    dt0 = jnp.exp(jax.random.uniform(ks[8], (DEPTH, SSD_HEADS), f32,
                                     math.log(0.001), math.log(0.1)))
    return {
        "x": jax.random.normal(ks[0], (BATCH, SEQ, D_MODEL), f32),
        "pre_mix_norm": gain(ks[1], D_MODEL),
        "w_in": jax.random.normal(ks[2], (DEPTH, D_MODEL, IN_WIDTH), f32) * D_MODEL ** -0.5,
        "diff_lambda": 0.1 * jax.random.normal(ks[3], (DEPTH, 4, DIFF_QK_DIM), f32),
        "diff_head_norm": gain(ks[4], DIFF_V_DIM),
        "ssd_conv_w": jax.random.normal(ks[5], (DEPTH, SSD_CONV, SSD_XBC), f32) * SSD_CONV ** -0.5,
        "ssd_conv_b": 0.01 * jax.random.normal(ks[6], (DEPTH, SSD_XBC), f32),
        "ssd_dt_bias": dt0 + jnp.log(-jnp.expm1(-dt0)),
        "ssd_A_log": jnp.log(jax.random.uniform(ks[9], (DEPTH, SSD_HEADS), f32, 1.0, 16.0)),
        "ssd_D": 1.0 + 0.1 * jax.random.normal(ks[10], (DEPTH, SSD_HEADS), f32),
        "ssd_norm": gain(ks[11], SSD_INNER),
        "w_out": jax.random.normal(ks[12], (DEPTH, MIX_WIDTH, D_MODEL), f32) * MIX_WIDTH ** -0.5,
        "post_mix_norm": gain(ks[13], D_MODEL),
        "pre_ffn_norm": gain(ks[14], D_MODEL),
        "ffn_up": jax.random.normal(ks[15], (DEPTH, D_MODEL, 2 * D_FF), f32) * D_MODEL ** -0.5,
        "ffn_conv_w": jax.random.normal(ks[16], (DEPTH, FFN_CONV, 2 * D_FF), f32) * FFN_CONV ** -0.5,
        "ffn_conv_b": 0.01 * jax.random.normal(ks[17], (DEPTH, 2 * D_FF), f32),
        "ffn_down": jax.random.normal(ks[18], (DEPTH, D_FF, D_MODEL), f32) * D_FF ** -0.5,
        "post_ffn_norm": gain(ks[19], D_MODEL),
    }


def reference(x, pre_mix_norm, w_in, diff_lambda, diff_head_norm, ssd_conv_w, ssd_conv_b,
              ssd_dt_bias, ssd_A_log, ssd_D, ssd_norm, w_out, post_mix_norm, pre_ffn_norm,
              ffn_up, ffn_conv_w, ffn_conv_b, ffn_down, post_ffn_norm):
    Bn, S, _ = x.shape
    pos = jnp.arange(S, dtype=jnp.int32)
    for layer in range(DEPTH):
        lambda_init = 0.8 - 0.6 * math.exp(-0.3 * layer)
        h = rms_norm(x, pre_mix_norm[layer])
        proj = h @ w_in[layer]
        dq, dk, dv, lq, lk, lv, z, xbc, dt = jnp.split(proj, IN_OFFSETS, axis=-1)
        dq = rope(dq.reshape(Bn, S, DIFF_HEADS, 2, DIFF_QK_DIM), pos)
        dk = rope(dk.reshape(Bn, S, DIFF_HEADS, 2, DIFF_QK_DIM), pos)
        dv = dv.reshape(Bn, S, DIFF_HEADS, DIFF_V_DIM)
        lam_p = diff_lambda[layer].astype(jnp.float32)
        lam = (jnp.exp(jnp.sum(lam_p[0] * lam_p[1])) - jnp.exp(jnp.sum(lam_p[2] * lam_p[3]))
               + lambda_init)
        o_diff = diff_attention(dq, dk, dv, lam, diff_head_norm[layer], lambda_init)
        lq = rope(lq.reshape(Bn, S, DIL_HEADS, DIL_HEAD_DIM), pos)
        lk = rope(lk.reshape(Bn, S, DIL_HEADS, DIL_HEAD_DIM), pos)
        lv = lv.reshape(Bn, S, DIL_HEADS, DIL_HEAD_DIM)
        o_dil = dilated_attention(lq, lk, lv)
        o_ssd = ssd_mixer(z, xbc, dt, ssd_conv_w[layer], ssd_conv_b[layer], ssd_dt_bias[layer],
                          ssd_A_log[layer], ssd_D[layer], ssd_norm[layer])
        mix = jnp.concatenate([o_diff.astype(x.dtype), o_dil.astype(x.dtype),
                               o_ssd.astype(x.dtype)], axis=-1) @ w_out[layer]
        x = x + rms_norm(mix, post_mix_norm[layer])
        h = rms_norm(x, pre_ffn_norm[layer])
        u = causal_dwconv(h @ ffn_up[layer], ffn_conv_w[layer], ffn_conv_b[layer])
        g, val = jnp.split(u, [D_FF], axis=-1)
        f = (jax.nn.silu(g) * val) @ ffn_down[layer]
        x = x + rms_norm(f, post_ffn_norm[layer])
    return x
```

```python
import math
import os
from contextlib import ExitStack

import numpy as np
import ml_dtypes

import concourse.bass as bass
import concourse.mybir as mybir
from concourse.bass_utils import run_bass_kernel_spmd

F32 = mybir.dt.float32
BF16 = mybir.dt.bfloat16
AF = mybir.ActivationFunctionType
ALU = mybir.AluOpType
AX = mybir.AxisListType

D = 1024
EPS = 1e-6
TB = 512
SAME_ENGINE_SYNC = True


class Buf:
    __slots__ = ("name", "w", "r", "excl")

    def __init__(self, name="", excl=False):
        self.name = name
        self.w = None
        self.r = {}
        self.excl = excl


def PBuf(name=""):
    return Buf(name, True)


class _Eng:
    def __init__(self, name, h, sem):
        self.name, self.h, self.sem, self.cnt, self.seen = name, h, sem, 0, {}


class _DSem:
    def __init__(self, sem):
        self.sem, self.cnt = sem, 0


class Sched:
    def __init__(self, nc, es):
        self.nc = nc
        self.es = es
        self.E = {}
        for name, h in (("pe", nc.tensor), ("act", nc.scalar), ("dve", nc.vector),
                        ("pool", nc.gpsimd), ("sp", nc.sync)):
            self.E[name] = _Eng(name, h, es.enter_context(nc.semaphore("s_" + name)))
        self.bar = es.enter_context(nc.semaphore("s_bar"))
        self.barcnt = 0
        self.dsems = []
        self.nsem = 0

    def dsem(self):
        self.nsem += 1
        d = _DSem(self.es.enter_context(self.nc.semaphore("d%d" % self.nsem)))
        self.dsems.append(d)
        return d

    def _wait(self, e, reads, writes):
        deps = {}

        def add(t):
            if t is not None and deps.get(t[0], (None, 0))[1] < t[1]:
                deps[t[0]] = t

        for b in reads:
            add(b.w)
        for b in writes:
            add(b.w)
            for s, v in b.r.items():
                add((s, v))
        for s, (sem, val) in deps.items():
            if sem is e.sem and (e.name == "pe" or not SAME_ENGINE_SYNC):
                continue
            if e.seen.get(sem, 0) >= val:
                continue
            e.h.wait_ge(sem, val)
            e.seen[sem] = val

    @staticmethod
    def _mark(tok, reads, writes):
        for b in reads:
            if b.r.get(tok[0], 0) < tok[1]:
                b.r[tok[0]] = tok[1]
        for b in writes:
            b.w = tok
            b.r = {}

    def op(self, eng, fn, reads=(), writes=(), inc=True):
        e = self.E[eng]
        if any(b.excl for b in reads):
            writes = list(writes) + [b for b in reads if b.excl]
            reads = [b for b in reads if not b.excl]
        self._wait(e, reads, writes)
        ins = fn(e.h)
        if inc:
            e.cnt += 1
            ins.then_inc(e.sem, 1)
            tok = (e.sem, e.cnt)
        else:
            tok = (e.sem, e.cnt + 1)
        self._mark(tok, reads, writes)
        return tok

    def dma(self, ds, pairs, reads=(), writes=(), eng="sp"):
        e = self.E[eng]
        self._wait(e, reads, writes)
        for out, in_ in pairs:
            ds.cnt += 16
            e.h.dma_start(out=out, in_=in_).then_inc(ds.sem, 16)
        tok = (ds.sem, ds.cnt)
        self._mark(tok, reads, writes)
        return tok

    def barrier(self):
        sp = self.E["sp"]
        for o in self.E.values():
            if o is not sp and o.cnt > 0 and sp.seen.get(o.sem, 0) < o.cnt:
                sp.h.wait_ge(o.sem, o.cnt)
                sp.seen[o.sem] = o.cnt
        for d in self.dsems:
            if d.cnt > 0 and sp.seen.get(d.sem, 0) < d.cnt:
                sp.h.wait_ge(d.sem, d.cnt)
                sp.seen[d.sem] = d.cnt
        self.barcnt += 1
        sp.h.sem_inc(self.bar, 1)
        for o in self.E.values():
            if o is not sp:
                o.h.wait_ge(self.bar, self.barcnt)
            for o2 in self.E.values():
                o.seen[o2.sem] = o2.cnt
            for d in self.dsems:
                o.seen[d.sem] = d.cnt


class Cfg:
    def __init__(self, S=8192, L=2, HD=4, HL=6, G=2, DFF=2816, TBF=256):
        self.S, self.L, self.HD, self.HL, self.G, self.DFF, self.TBF = S, L, HD, HL, G, DFF, TBF
        self.HS = 3 * G
        self.NB = S // TB
        off = 0
        self.fm = []
        for nm, cnt, m in (("dq", HD, 64), ("dk", HD, 64), ("lq", HL // 2, 128), ("lk", HL // 2, 128),
                           ("lv", HL // 2, 128), ("xs", self.HS // 2, 128), ("B", G, 128), ("C", G, 128)):
            for i in range(cnt):
                self.fm.append((nm, i, off, m))
                off += m
        self.off_dv = off
        off += HD * 64
        self.off_z = off
        off += self.HS * 64 + self.HS
        self.NCOL = off
        self.NXBC = self.HS // 2 + 2 * G
        self.MIXC = (HD * 64 + HL * 64 + self.HS * 64) // 128
        self.FC = DFF // 128


def _rope_tables(S, head_dim, rows):
    half = head_dim // 2
    inv = np.exp(-math.log(10000.0) * np.arange(half, dtype=np.float32) / half).astype(np.float32)
    ang = np.arange(S, dtype=np.float32)[None, :] * inv[:, None]
    cos = np.cos(ang).astype(np.float32)
    sin = np.sin(ang).astype(np.float32)
    p = np.arange(rows)
    j = p % half
    first = (p % head_dim) < half
    cosT = cos[j]
    sinT = np.where(first[:, None], -sin[j], sin[j])
    perm = np.zeros((rows, rows), np.float32)
    partner = np.where(first, p + half, p - half)
    perm[p, partner] = 1.0
    return cosT.astype(np.float32), sinT.astype(np.float32), perm


def host_consts(cfg):
    S = cfg.S
    c = {}
    cd, sd, pd = _rope_tables(S, 32, 64)
    cl, sl, pl = _rope_tables(S, 64, 128)
    c["cosd"], c["sind"], c["cosl"], c["sinl"] = [a.astype(ml_dtypes.bfloat16) for a in (cd, sd, cl, sl)]
    c["permd"] = pd.astype(ml_dtypes.bfloat16)
    c["perml"] = pl.astype(ml_dtypes.bfloat16)
    c["ident_bf"] = np.eye(128, dtype=np.float32).astype(ml_dtypes.bfloat16)
    c["ident_f"] = np.eye(128, dtype=np.float32)
    k = np.arange(128)[:, None]
    q = np.arange(512)[None, :]
    c["dmask"] = np.stack([(q >= 128 * di + k) for di in range(4)]).astype(np.float32).astype(ml_dtypes.bfloat16)
    qq = np.arange(128)[None, :]
    c["lmask"] = np.concatenate([(qq <= k), (qq >= k)], axis=1).astype(np.float32).astype(ml_dtypes.bfloat16)
    c["triu"] = (k <= qq).astype(np.float32)
    c["negm"] = np.where(qq >= k, 0.0, -30000.0).astype(np.float32)
    sel = np.zeros((65, 64), np.float32)
    sel[64, :] = 1.0
    c["sel"] = sel
    c["ones64"] = np.ones((64, 64), np.float32)
    return c


CONST_SHAPES = None


def host_params(cfg, inp, b_heads=None):
    HD, HL, G, HS = cfg.HD, cfg.HL, cfg.G, cfg.HS
    w = np.asarray(inp["w_in"])
    L = w.shape[0]
    o = 0
    segs = {}
    for nm, n in (("dq", 256), ("dk", 256), ("dv", 256), ("lq", 384), ("lk", 384), ("lv", 384),
                  ("z", 384), ("xs", 384), ("B", 256), ("C", 256), ("dt", 6)):
        segs[nm] = (o, o + n)
        o += n
    cols = []
    for nm in ("dq", "dk", "lq", "lk", "lv", "xs", "B", "C", "dv", "z", "dt"):
        a, b = segs[nm]
        cols.append(np.arange(a, b))
    cols = np.concatenate(cols)
    p = {}
    p["w_in"] = np.ascontiguousarray(w[:, :, cols])
    p["g_premix"] = np.ascontiguousarray(np.asarray(inp["pre_mix_norm"]).reshape(L, 8, 128).transpose(0, 2, 1))
    p["g_preffn"] = np.ascontiguousarray(np.asarray(inp["pre_ffn_norm"]).reshape(L, 8, 128).transpose(0, 2, 1))
    p["g_postmix"] = np.ascontiguousarray(np.asarray(inp["post_mix_norm"]))
    p["g_postffn"] = np.ascontiguousarray(np.asarray(inp["post_ffn_norm"]))
    xo = segs["xs"][0]
    cw = np.asarray(inp["ssd_conv_w"])
    cb = np.asarray(inp["ssd_conv_b"])
    p["cw"] = np.ascontiguousarray(cw.reshape(L, 4, 7, 128).transpose(0, 3, 2, 1))
    p["cb"] = np.ascontiguousarray(cb.reshape(L, 7, 128).transpose(0, 2, 1))
    p["dt_bias"] = np.ascontiguousarray(np.asarray(inp["ssd_dt_bias"]))
    p["A_log"] = np.ascontiguousarray(np.asarray(inp["ssd_A_log"]))
    p["ssd_D"] = np.ascontiguousarray(np.asarray(inp["ssd_D"]))
    p["ssd_norm"] = np.ascontiguousarray(np.asarray(inp["ssd_norm"]))
    p["lam"] = np.ascontiguousarray(np.asarray(inp["diff_lambda"]).reshape(L, 128))
    p["hgain"] = np.ascontiguousarray(np.asarray(inp["diff_head_norm"]).reshape(L, 64, 1))
    p["w_out"] = np.ascontiguousarray(np.asarray(inp["w_out"]))
    p["ffn_up"] = np.ascontiguousarray(np.asarray(inp["ffn_up"]))
    fw = np.asarray(inp["ffn_conv_w"])
    fb = np.asarray(inp["ffn_conv_b"])
    p["fcw"] = np.ascontiguousarray(fw.reshape(L, 3, 44, 128).transpose(0, 3, 2, 1))
    p["fcb"] = np.ascontiguousarray(fb.reshape(L, 44, 128).transpose(0, 2, 1))
    p["ffn_down"] = np.ascontiguousarray(np.asarray(inp["ffn_down"]))
    return p


class K:
    pass


def build(cfg, debug=False):
    nc = bass.Bass("TRN2", target_bir_lowering=False)
    S, L, HD, HL, G, HS = cfg.S, cfg.L, cfg.HD, cfg.HL, cfg.G, cfg.HS
    NB = cfg.NB
    es = ExitStack()
    sc = Sched(nc, es)

    def din(name, shape, dt=F32):
        return nc.dram_tensor(name, list(shape), dt, kind="ExternalInput").ap()

    def dscr(name, shape, dt):
        return nc.dram_tensor(name, list(shape), dt, kind=("ExternalOutput" if debug else "Internal")).ap()

    x_in = din("x", [S, D])
    W = {}
    W["w_in"] = din("w_in", [L, D, cfg.NCOL])
    W["g_premix"] = din("g_premix", [L, 128, 8])
    W["g_preffn"] = din("g_preffn", [L, 128, 8])
    W["g_postmix"] = din("g_postmix", [L, D])
    W["g_postffn"] = din("g_postffn", [L, D])
    W["cw"] = din("cw", [L, 128, 7, 4])
    W["cb"] = din("cb", [L, 128, 7])
    W["dt_bias"] = din("dt_bias", [L, 6])
    W["A_log"] = din("A_log", [L, 6])
    W["ssd_D"] = din("ssd_D", [L, 6])
    W["ssd_norm"] = din("ssd_norm", [L, 384])
    W["lam"] = din("lam", [L, 128])
    W["hgain"] = din("hgain", [L, 64, 1])
    W["w_out"] = din("w_out", [L, D, D])
    W["ffn_up"] = din("ffn_up", [L, D, 2 * cfg.DFF])
    W["fcw"] = din("fcw", [L, 128, 44, 3])
    W["fcb"] = din("fcb", [L, 128, 44])
    W["ffn_down"] = din("ffn_down", [L, cfg.DFF, D])
    C = {}
    C["cosd"] = din("cosd", [64, S], BF16); C["sind"] = din("sind", [64, S], BF16)
    C["cosl"] = din("cosl", [128, S], BF16); C["sinl"] = din("sinl", [128, S], BF16)
    C["permd"] = din("permd", [64, 64], BF16); C["perml"] = din("perml", [128, 128], BF16)
    C["ident_bf"] = din("ident_bf", [128, 128], BF16); C["ident_f"] = din("ident_f", [128, 128])
    C["dmask"] = din("dmask", [4, 128, 512], BF16); C["lmask"] = din("lmask", [128, 256], BF16)
    C["triu"] = din("triu", [128, 128]); C["negm"] = din("negm", [128, 128])
    C["sel"] = din("sel", [65, 64]); C["ones64"] = din("ones64", [64, 64])
    out = nc.dram_tensor("out", [S, D], F32, kind="ExternalOutput").ap()

    qTd = dscr("qTd", [HD, 64, S], BF16); kTd = dscr("kTd", [HD, 64, S], BF16)
    vd = dscr("vd", [S, HD * 65], BF16)
    qTl = dscr("qTl", [HL // 2, 128, S], BF16); kTl = dscr("kTl", [HL // 2, 128, S], BF16)
    vTl = dscr("vTl", [HL // 2, 128, S], BF16)
    xbcT = dscr("xbcT", [cfg.NXBC, 128, S], BF16)
    zs = dscr("zs", [S, HS * 64], BF16)
    dts = dscr("dts", [S, HS], F32)
    mixT = dscr("mixT", [cfg.MIXC, 128, S], BF16)
    xa = dscr("xa", [S, D], F32)
    xb = dscr("xb", [S, D], F32)

    def sb(name, shape, dt=F32):
        return es.enter_context(nc.sbuf_tensor(name, list(shape), dt))

    ident_bf = sb("sb_ident_bf", [128, 128], BF16)
    ident_f = sb("sb_ident_f", [128, 128])
    cB = Buf("consts")
    cds = sc.dsem()
    sc.dma(cds, [(ident_bf[:], C["ident_bf"]), (ident_f[:], C["ident_f"])], writes=[cB])

    k = K()
    k.nc, k.sc, k.cfg, k.W, k.C, k.cB = nc, sc, cfg, W, C, cB
    k.ident_bf, k.ident_f = ident_bf, ident_f
    k.scr = dict(qTd=qTd, kTd=kTd, vd=vd, qTl=qTl, kTl=kTl, vTl=vTl, xbcT=xbcT, zs=zs, dts=dts,
                 mixT=mixT, xa=xa, xb=xb)

    stages = cfg.stages if hasattr(cfg, "stages") else "12345"
    for l in range(L):
        x_src = x_in if l == 0 else xb
        x_dst = out if l == L - 1 else xb
        if "1" in stages:
            phase1(k, l, x_src)
            sc.barrier()
        if "2" in stages:
            phase2(k, l)
            sc.barrier()
        if "3" in stages:
            phase3(k, l)
            sc.barrier()
        if "4" in stages:
            phase4(k, l)
            sc.barrier()
        if "5" in stages:
            phase5a(k, l, x_src)
            sc.barrier()
            phase5b(k, l, x_dst)
            sc.barrier()
    sc.barrier()
    es.close()
    return nc


def load_weights_bf16(k, ps, dst, src_rows, ncols, gain, stg, stgB, ds, kchunks, col_chunk=512):
    sc = k.sc
    i = 0
    for kc in range(kchunks):
        for c0 in range(0, ncols, col_chunk):
            cw = min(col_chunk, ncols - c0)
            sl = i % len(stg)
            sc.dma(ds[sl], [(stg[sl][:, :cw], src_rows[kc * 128:(kc + 1) * 128, c0:c0 + cw])], writes=[stgB[sl]])
            eng = "dve" if i % 2 == 0 else "pool"
            if gain is not None:
                if i % 2 == 0:
                    sc.op("dve", lambda h, sl=sl, cw=cw, kc=kc, c0=c0: h.tensor_scalar(
                        out=dst[:, kc, c0:c0 + cw], in0=stg[sl][:, :cw], scalar1=gain[:, kc:kc + 1], scalar2=None,
                        op0=ALU.mult), reads=[stgB[sl], k.gB], writes=[k.wB])
                else:
                    sc.op("act", lambda h, sl=sl, cw=cw, kc=kc, c0=c0: h.activation(
                        out=dst[:, kc, c0:c0 + cw], in_=stg[sl][:, :cw], func=AF.Copy, scale=gain[:, kc:kc + 1]),
                        reads=[stgB[sl], k.gB], writes=[k.wB])
            else:
                sc.op(eng, lambda h, sl=sl, cw=cw, kc=kc, c0=c0: h.tensor_copy(
                    out=dst[:, kc, c0:c0 + cw], in_=stg[sl][:, :cw]), reads=[stgB[sl]], writes=[k.wB])
            i += 1


def rms_rstd(k, ps, ss, rstd, n, width, ssB, rsB):
    sc = k.sc
    sc.op("act", lambda h: h.activation(out=rstd[:, :n], in_=ss[:, :n], func=AF.Ln, scale=1.0 / width, bias=EPS),
          reads=[ssB], writes=[rsB])
    sc.op("act", lambda h: h.activation(out=rstd[:, :n], in_=rstd[:, :n], func=AF.Exp, scale=-0.5), reads=[rsB], writes=[rsB])


def phase1(k, l, x_src):
    nc, sc, cfg, W, C = k.nc, k.sc, k.cfg, k.W, k.C
    S, HD, HL, G, HS, NB = cfg.S, cfg.HD, cfg.HL, cfg.G, cfg.HS, cfg.NB
    NX = cfg.NXBC
    with ExitStack() as ps:
        def sb(name, shape, dt=F32):
            return ps.enter_context(nc.sbuf_tensor("L%d_%s" % (l, name), list(shape), dt))

        def pt(name, shape, dt=F32):
            return ps.enter_context(nc.psum_tensor("L%d_%s" % (l, name), list(shape), dt))

        wbf = sb("p1_w", [128, 8, cfg.NCOL], BF16)
        gain = sb("p1_g", [128, 8])
        stg = [sb("p1_stg%d" % i, [128, 512]) for i in range(3)]
        stgB = [Buf() for _ in range(3)]
        ds = [sc.dsem() for _ in range(3)]
        k.gB, k.wB = Buf("gain"), Buf("w")
        dsm = sc.dsem()
        permd = sb("p1_permd", [64, 64], BF16)
        perml = sb("p1_perml", [128, 128], BF16)
        cw = sb("p1_cw", [128, 7, 4])
        cb = sb("p1_cb", [128, 7])
        sc.dma(dsm, [(gain[:], W["g_premix"][l]), (permd[:], C["permd"]), (perml[:], C["perml"]),
                     (cw[:], W["cw"][l]), (cb[:], W["cb"][l])], writes=[k.gB])
        CUT = int(os.environ.get("KCUT", "99"))
        if CUT >= 1:
            load_weights_bf16(k, ps, wbf, W["w_in"][l], cfg.NCOL, gain, stg, stgB, ds, 8)

        xt = sb("p1_x", [128, 4, D])
        xtB = [Buf() for _ in range(4)]
        xds = [sc.dsem() for _ in range(4)]
        junk = sb("p1_junk", [128, D], BF16)
        junkB = Buf()
        ss = sb("p1_ss", [128, 4]); ssB = Buf()
        rstd = sb("p1_rstd", [128, 4]); rsB = Buf()
        xs = sb("p1_xs", [128, 4, D], BF16); xsB = Buf()
        hT = sb("p1_hT", [128, 8, TB], BF16); hTB = Buf()
        psT = [pt("p1_psT%d" % i, [128, 2, TB], BF16) for i in range(2)]
        psTB = [PBuf() for _ in range(2)]
        psA = [pt("p1_psA%d" % i, [128, TB]) for i in range(3)]
        psAB = [PBuf() for _ in range(3)]
        psR = [pt("p1_psR%d" % i, [128, TB]) for i in range(2)]
        psRB = [PBuf() for _ in range(2)]
        psB = [pt("p1_psB%d" % i, [128, 512]) for i in range(1)]
        psBB = [PBuf() for _ in range(1)]
        tabs = {}
        for nm, rows in (("cosd", 64), ("sind", 64), ("cosl", 128), ("sinl", 128)):
            tabs[nm] = [sb("p1_%s%d" % (nm, i), [rows, TB], BF16) for i in range(2)]
        tabB = [Buf() for _ in range(2)]
        tds = [sc.dsem() for _ in range(2)]
        xbf = [sb("p1_xbf%d" % i, [128, TB], BF16) for i in range(4)]
        xbfB = [Buf() for _ in range(4)]
        t1 = [sb("p1_t1%d" % i, [128, TB]) for i in range(2)]
        t1B = [Buf() for _ in range(2)]
        t2 = [sb("p1_t2%d" % i, [128, TB]) for i in range(2)]
        t2B = [Buf() for _ in range(2)]
        qk_st = [sb("p1_qk%d" % i, [64, 2 * HD, TB], BF16) for i in range(2)]
        l_st = [sb("p1_l%d" % i, [128, 3 * (HL // 2), TB], BF16) for i in range(2)]
        xbc_st = [sb("p1_xbc%d" % i, [128, NX, TB], BF16) for i in range(2)]
        v_st = [sb("p1_v%d" % i, [128, 4, HD, 65], BF16) for i in range(2)]
        z_st = [sb("p1_z%d" % i, [128, 4, HS * 64], BF16) for i in range(2)]
        dt_st = [sb("p1_dt%d" % i, [128, 4, HS]) for i in range(2)]
        from collections import defaultdict
        stD = [defaultdict(Buf) for _ in range(2)]
        sds = [sc.dsem() for _ in range(2)]
        u = sb("p1_u", [128, NX, 3 + TB]); uB = [Buf() for _ in range(NX)]
        yc = [sb("p1_yc%d" % i, [128, TB]) for i in range(3)]
        ycB = [Buf() for _ in range(3)]
        sc.op("pool", lambda h: h.memset(u[:], 0.0), writes=uB)
        for i in range(2):
            sc.op("pool", lambda h, i=i: h.memset(v_st[i][:], 1.0), writes=[stD[i][("v", n)] for n in range(4)])

        ci = 0
        for t in range(NB):
            if CUT < 2:
                break
            pb = t % 2
            tok0 = t * TB

            def prefetch(tt):
                tk = tt * TB
                for n in range(4):
                    sc.dma(xds[n], [(xt[:, n, :], x_src[tk + n * 128:tk + (n + 1) * 128, :])], writes=[xtB[n]])
                sc.dma(tds[tt % 2], [(tabs[nm][tt % 2][:], C[nm][:, tk:tk + TB]) for nm in ("cosd", "sind", "cosl", "sinl")],
                       writes=[tabB[tt % 2]])

            if t == 0:
                prefetch(0)
            for n in range(4):
                sc.op("act", lambda h, n=n: h.activation(out=junk[:], in_=xt[:, n, :], func=AF.Square,
                                                         accum_out=ss[:, n:n + 1]),
                      reads=[xtB[n]], writes=[junkB, ssB])
            if CUT < 3:
                continue
            rms_rstd(k, ps, ss, rstd, 4, D, ssB, rsB)
            for n in range(4):
                if n % 2 == 0:
                    sc.op("dve", lambda h, n=n: h.tensor_scalar(
                        out=xs[:, n, :], in0=xt[:, n, :], scalar1=rstd[:, n:n + 1], scalar2=None, op0=ALU.mult),
                        reads=[xtB[n], rsB], writes=[xsB])
                else:
                    sc.op("act", lambda h, n=n: h.activation(out=xs[:, n, :], in_=xt[:, n, :], func=AF.Copy, scale=rstd[:, n:n + 1]),
                          reads=[xtB[n], rsB], writes=[xsB])
            if CUT < 4:
                continue
            for cp in range(4):
                pi = cp % 2
                for c2 in range(2):
                    c = cp * 2 + c2
                    for n in range(4):
                        last = (c2 == 1 and n == 3)
                        sc.op("pe", lambda h, c=c, c2=c2, n=n, pi=pi: h.transpose(
                            out=psT[pi][:, c2, n * 128:(n + 1) * 128], in_=xs[:, n, c * 128:(c + 1) * 128],
                            identity=k.ident_bf[:]), reads=[xsB, k.cB], writes=[psTB[pi]], inc=last)
                if cp % 2 == 0:
                    sc.op("act", lambda h, cp=cp, pi=pi: h.copy(out=hT[:, 2 * cp:2 * cp + 2, :], in_=psT[pi][:]),
                          reads=[psTB[pi]], writes=[hTB])
                else:
                    sc.op("dve", lambda h, cp=cp, pi=pi: h.tensor_copy(out=hT[:, 2 * cp:2 * cp + 2, :], in_=psT[pi][:]),
                          reads=[psTB[pi]], writes=[hTB])
            if CUT < 5:
                continue
            if t + 1 < NB:
                prefetch(t + 1)
            if t > 0:
                sc.op("pool", lambda h: h.tensor_copy(out=u[:, :, 0:3], in_=u[:, :, TB:TB + 3]), reads=uB, writes=uB)
            fm = cfg.fm

            def s_pe(i):
                nm, idx, off, M = fm[i]
                a = i % 3
                for kc in range(8):
                    sc.op("pe", lambda h: h.matmul(psA[a][0:M, :], lhsT=wbf[:, kc, off:off + M], rhs=hT[:, kc, :],
                                                   start=(kc == 0), stop=(kc == 7)), reads=[k.wB, hTB], writes=[psAB[a]], inc=(kc == 7))

            def s_copy(i):
                nm, idx, off, M = fm[i]
                a = i % 3
                if nm in ("dq", "dk", "lq", "lk"):
                    x4 = i % 4
                    sc.op("act", lambda h: h.copy(out=xbf[x4][0:M, :], in_=psA[a][0:M, :]), reads=[psAB[a]], writes=[xbfB[x4]])
                elif nm == "lv":
                    sc.op("act", lambda h: h.copy(out=l_st[pb][:, 2 * (HL // 2) + idx, :], in_=psA[a][:]),
                          reads=[psAB[a]], writes=[stD[pb][("lv", idx)]])
                else:
                    xi = {"xs": 0, "B": HS // 2, "C": HS // 2 + G}[nm] + idx
                    sc.op("act", lambda h: h.copy(out=u[:, xi, 3:3 + TB], in_=psA[a][:]), reads=[psAB[a]], writes=[uB[xi]])

            def s_mid(i):
                nm, idx, off, M = fm[i]
                if nm in ("dq", "dk", "lq", "lk"):
                    perm = permd if M == 64 else perml
                    x4 = i % 4
                    r2 = i % 2
                    sc.op("pe", lambda h: h.matmul(psR[r2][0:M, :], lhsT=perm[0:M, 0:M], rhs=xbf[x4][0:M, :], start=True, stop=True),
                          reads=[xbfB[x4], k.gB], writes=[psRB[r2]])
                elif nm != "lv":
                    xi = {"xs": 0, "B": HS // 2, "C": HS // 2 + G}[nm] + idx
                    gi = {"xs": 0, "B": 3, "C": 5}[nm] + idx
                    y3 = i % 3
                    sc.op("dve", lambda h: h.tensor_scalar(out=yc[y3][:], in0=u[:, xi, 3:3 + TB], scalar1=cw[:, gi, 3:4], scalar2=cb[:, gi:gi + 1],
                                                           op0=ALU.mult, op1=ALU.add), reads=[uB[xi], k.gB], writes=[ycB[y3]])

            def s_ew(i):
                nm, idx, off, M = fm[i]
                if nm in ("dq", "dk", "lq", "lk"):
                    cosn, sinn = ("cosd", "sind") if M == 64 else ("cosl", "sinl")
                    x4 = i % 4
                    r2 = i % 2
                    sc.op("dve", lambda h: h.tensor_tensor(out=t1[r2][0:M, :], in0=xbf[x4][0:M, :], in1=tabs[cosn][pb][0:M, :], op=ALU.mult),
                          reads=[xbfB[x4], tabB[pb]], writes=[t1B[r2]])
                    sc.op("dve", lambda h: h.tensor_tensor(out=t2[r2][0:M, :], in0=psR[r2][0:M, :], in1=tabs[sinn][pb][0:M, :], op=ALU.mult),
                          reads=[psRB[r2], tabB[pb]], writes=[t2B[r2]])
                elif nm != "lv":
                    xi = {"xs": 0, "B": HS // 2, "C": HS // 2 + G}[nm] + idx
                    gi = {"xs": 0, "B": 3, "C": 5}[nm] + idx
                    y3 = i % 3
                    for kk in range(3):
                        sc.op("dve", lambda h: h.scalar_tensor_tensor(out=yc[y3][:], in0=u[:, xi, kk:kk + TB], scalar=cw[:, gi, kk:kk + 1],
                                                                      in1=yc[y3][:], op0=ALU.mult, op1=ALU.add),
                              reads=[uB[xi], ycB[y3], k.gB], writes=[ycB[y3]])

            def s_fin(i):
                nm, idx, off, M = fm[i]
                if nm in ("dq", "dk", "lq", "lk"):
                    r2 = i % 2
                    if nm in ("dq", "dk"):
                        dst = qk_st[pb][:, (0 if nm == "dq" else HD) + idx, :]
                    else:
                        dst = l_st[pb][:, (0 if nm == "lq" else HL // 2) + idx, :]
                    sc.op("pool", lambda h: h.tensor_tensor(out=dst, in0=t1[r2][0:M, :], in1=t2[r2][0:M, :], op=ALU.add),
                          reads=[t1B[r2], t2B[r2]], writes=[stD[pb][(nm, idx)]])
                elif nm != "lv":
                    xi = {"xs": 0, "B": HS // 2, "C": HS // 2 + G}[nm] + idx
                    y3 = i % 3
                    sc.op("act", lambda h: h.activation(out=xbc_st[pb][:, xi, :], in_=yc[y3][:], func=AF.Silu),
                          reads=[ycB[y3]], writes=[stD[pb][("xbc", xi)]])

            skewed(len(fm), [s_pe, s_copy, s_mid, s_ew, s_fin])
            if CUT < 6:
                continue
            tmb = [psB[0], psR[0], psR[1]]
            tmB = [psBB[0], psRB[0], psRB[1]]
            for n in range(4):
                a = (2 * n) % 3
                for kc in range(8):
                    sc.op("pe", lambda h, kc=kc, n=n, a=a: h.matmul(
                        tmb[a][:, 0:HD * 64], lhsT=hT[:, kc, n * 128:(n + 1) * 128],
                        rhs=wbf[:, kc, cfg.off_dv:cfg.off_dv + HD * 64], start=(kc == 0), stop=(kc == 7)),
                        reads=[k.wB, hTB], writes=[tmB[a]], inc=(kc == 7))
                sc.op("act", lambda h, n=n, a=a: h.copy(
                    out=v_st[pb][:, n, :, 0:64], in_=tmb[a][:, 0:HD * 64].rearrange("p (h e) -> p h e", e=64)),
                    reads=[tmB[a]], writes=[stD[pb][("v", n)]])
                nz = HS * 64 + HS
                a = (2 * n + 1) % 3
                for kc in range(8):
                    sc.op("pe", lambda h, kc=kc, n=n, a=a: h.matmul(
                        tmb[a][:, 0:nz], lhsT=hT[:, kc, n * 128:(n + 1) * 128],
                        rhs=wbf[:, kc, cfg.off_z:cfg.off_z + nz], start=(kc == 0), stop=(kc == 7)),
                        reads=[k.wB, hTB], writes=[tmB[a]], inc=(kc == 7))
                sc.op("dve", lambda h, n=n, a=a: h.tensor_copy(out=z_st[pb][:, n, :], in_=tmb[a][:, 0:HS * 64]),
                      reads=[tmB[a]], writes=[stD[pb][("z", n)]])
                sc.op("dve", lambda h, n=n, a=a: h.tensor_copy(out=dt_st[pb][:, n, :], in_=tmb[a][:, HS * 64:nz]),
                      reads=[tmB[a]], writes=[stD[pb][("dt", n)]])
            if CUT < 7:
                continue
            scr = k.scr
            pairs = [
                (scr["qTd"][:, :, tok0:tok0 + TB].rearrange("h p s -> p h s"), qk_st[pb][:, 0:HD, :]),
                (scr["kTd"][:, :, tok0:tok0 + TB].rearrange("h p s -> p h s"), qk_st[pb][:, HD:2 * HD, :]),
                (scr["qTl"][:, :, tok0:tok0 + TB].rearrange("h p s -> p h s"), l_st[pb][:, 0:HL // 2, :]),
                (scr["kTl"][:, :, tok0:tok0 + TB].rearrange("h p s -> p h s"), l_st[pb][:, HL // 2:HL, :]),
                (scr["vTl"][:, :, tok0:tok0 + TB].rearrange("h p s -> p h s"), l_st[pb][:, HL:3 * (HL // 2), :]),
                (scr["xbcT"][:, :, tok0:tok0 + TB].rearrange("h p s -> p h s"), xbc_st[pb][:]),
                (scr["vd"][tok0:tok0 + TB, :].rearrange("(n p) f -> p n f", p=128),
                 v_st[pb][:].rearrange("p n h e -> p n (h e)")),
                (scr["zs"][tok0:tok0 + TB, :].rearrange("(n p) f -> p n f", p=128), z_st[pb][:]),
                (scr["dts"][tok0:tok0 + TB, :].rearrange("(n p) f -> p n f", p=128), dt_st[pb][:]),
            ]
            sc.dma(sds[pb], pairs, reads=list(stD[pb].values()))


def skewed(n, stages):
    ns = len(stages)
    for kk in range(n + ns - 1):
        for si, fn in enumerate(stages):
            i = kk - si
            if 0 <= i < n:
                fn(i)


def _mk(k, l, ps):
    nc = k.nc

    def sb(name, shape, dt=F32):
        return ps.enter_context(nc.sbuf_tensor("L%d_%s" % (l, name), list(shape), dt))

    def pt(name, shape, dt=F32):
        return ps.enter_context(nc.psum_tensor("L%d_%s" % (l, name), list(shape), dt))

    return sb, pt


def phase2(k, l):
    nc, sc, cfg, W, C = k.nc, k.sc, k.cfg, k.W, k.C
    S, HD, NB = cfg.S, cfg.HD, cfg.NB
    lam_init = 0.8 - 0.6 * math.exp(-0.3 * l)
    scr = k.scr
    with ExitStack() as ps:
        sb, pt = _mk(k, l, ps)
        kT = sb("p2_kT", [128, HD // 2, S], BF16)
        vfull = sb("p2_v", [128, (S // 128) * HD * 65 + 128], BF16)
        v = vfull[:, 0:(S // 128) * HD * 65].rearrange("p (n f) -> p n f", f=HD * 65)
        dmask = sb("p2_dmask", [128, 4, 512], BF16)
        sel = sb("p2_sel", [65, 64]); ones64 = sb("p2_ones", [64, 64])
        hg = sb("p2_hg", [64, 1]); lam = sb("p2_lam", [64, 128])
        lt = sb("p2_lt", [64, 64]); lsum = sb("p2_ls", [64, 2]); neg_lam = sb("p2_nl", [64, 1]); gsc = sb("p2_gsc", [64, 1])
        cB = Buf(); d0 = sc.dsem()
        sc.dma(d0, [(kT[:, a2, :], scr["kTd"][2 * a2:2 * a2 + 2].rearrange("b p s -> (b p) s")) for a2 in range(HD // 2)] +
               [(v, scr["vd"].rearrange("(n p) f -> p n f", p=128)),
                (dmask[:], C["dmask"].rearrange("d p q -> p d q")), (sel[:], C["sel"]), (ones64[:], C["ones64"]),
                (hg[:], W["hgain"][l]), (lam[:], W["lam"][l:l + 1, :].broadcast_to([64, 128]))], writes=[cB])
        pB = Buf()
        sc.op("pool", lambda h: h.memset(vfull[:, (S // 128) * HD * 65:], 0.0), writes=[cB])
        sc.op("dve", lambda h: h.tensor_tensor(out=lt[:, 0:32], in0=lam[:, 0:32], in1=lam[:, 32:64], op=ALU.mult), reads=[cB], writes=[pB])
        sc.op("dve", lambda h: h.tensor_tensor(out=lt[:, 32:64], in0=lam[:, 64:96], in1=lam[:, 96:128], op=ALU.mult), reads=[cB], writes=[pB])
        sc.op("dve", lambda h: h.tensor_reduce(out=lsum[:], in_=lt[:].rearrange("p (a b) -> p a b", b=32), axis=AX.X, op=ALU.add),
              reads=[pB], writes=[pB])
        sc.op("act", lambda h: h.activation(out=lsum[:], in_=lsum[:], func=AF.Exp), reads=[pB], writes=[pB])
        sc.op("dve", lambda h: h.tensor_tensor(out=neg_lam[:], in0=lsum[:, 1:2], in1=lsum[:, 0:1], op=ALU.subtract), reads=[pB], writes=[pB])
        sc.op("dve", lambda h: h.tensor_scalar(out=neg_lam[:], in0=neg_lam[:], scalar1=-lam_init, scalar2=None, op0=ALU.add), reads=[pB], writes=[pB])
        sc.op("dve", lambda h: h.tensor_scalar(out=gsc[:], in0=hg[:], scalar1=(1.0 - lam_init), scalar2=None, op0=ALU.mult), reads=[cB, pB], writes=[pB])

        qT = [sb("p2_q%d" % i, [128, HD * 2, TB], BF16) for i in range(2)]
        qB = [Buf() for _ in range(2)]; qds = [sc.dsem() for _ in range(2)]
        for i in range(2):
            sc.op("pool", lambda h, i=i: h.memset(qT[i][:], 0.0), writes=[qB[i]])
        psS = [pt("p2_psS%d" % i, [128, 2, 512]) for i in range(2)]; psSB = [PBuf() for _ in range(2)]
        psO = [pt("p2_psO%d" % m, [128, 512]) for m in range(2)]
        psOB = [PBuf() for m in range(2)]
        psE = [pt("p2_psE%d" % i, [128, 512]) for i in range(2)]; psEB = [PBuf() for _ in range(2)]
        pT = [sb("p2_pT%d" % i, [128, 2, 512], BF16) for i in range(3)]; pTB = [Buf() for _ in range(3)]
        X = [sb("p2_X%d" % m, [65, 512]) for m in range(2)]; XB = [Buf() for _ in range(2)]
        r = [sb("p2_r%d" % m, [64, 512]) for m in range(2)]; rB = [Buf() for _ in range(2)]
        o = sb("p2_o", [64, 512]); oB = Buf()
        sq = sb("p2_sq", [64, 512]); sqB = Buf()
        rs = sb("p2_rs", [64, 512]); rsB = Buf()
        ost = [sb("p2_ost%d" % i, [64, HD, 512], BF16) for i in range(2)]
        ostB = [Buf() for _ in range(2)]; ods = [sc.dsem() for _ in range(2)]
        scale = 32 ** -0.5
        cnt = 0
        pending = []

        def flush(n=None):
            c = len(pending) if n is None else min(n, len(pending))
            for _ in range(c):
                pending.pop(0)()

        for t in range(NB):
            tok0 = t * TB
            qb = t % 2
            sc.dma(qds[qb], [(qT[qb][32 * ((h2 % 2) * 2 + m2):32 * ((h2 % 2) * 2 + m2) + 32, h2 * 2 + m2, :],
                              scr["qTd"][h2, 32 * m2:32 * m2 + 32, tok0:tok0 + TB]) for h2 in range(HD) for m2 in range(2)],
                   writes=[qB[qb]])
            for hh in range(HD):
                nk = 4 * t + 4

                def qk(i):
                    sl = (cnt + i) % 2
                    for m in range(2):
                        sc.op("pe", lambda h: h.matmul(psS[sl][:, m, :], lhsT=kT[:, hh // 2, i * 128:(i + 1) * 128],
                                                       rhs=qT[qb][:, hh * 2 + m, :], start=True, stop=True),
                              reads=[cB, qB[qb]], writes=[psSB[sl]], inc=(m == 1))

                qk(0)
                for i in range(nk):
                    sl = (cnt + i) % 2
                    p3 = (cnt + i) % 3
                    if i + 1 < nk:
                        qk(i + 1)
                    sc.op("act", lambda h: h.activation(out=pT[p3][:], in_=psS[sl][:], func=AF.Exp, scale=scale),
                          reads=[psSB[sl]], writes=[pTB[p3]])
                    if i >= 4 * t:
                        di = i - 4 * t
                        sc.op("dve" if i % 2 == 0 else "pool", lambda h: h.tensor_tensor(
                            out=pT[p3][:], in0=pT[p3][:], in1=dmask[:, di:di + 1, :].broadcast_to([128, 2, 512]), op=ALU.mult),
                            reads=[cB, pTB[p3]], writes=[pTB[p3]])
                    for m in range(2):
                        sc.op("pe", lambda h: h.matmul(psO[m][:, :], lhsT=vfull[:, (i * HD + hh) * 65:(i * HD + hh) * 65 + 128], rhs=pT[p3][:, m, :],
                                                       start=(i == 0), stop=(i == nk - 1)),
                              reads=[cB, pTB[p3]], writes=[psOB[m]], inc=(i == nk - 1))
                    if i >= 1:
                        flush(-(-len(pending) // max(1, nk - 1 - i)) if i < nk - 1 else None)
                cnt += nk
                flush()
                sc.op("act", lambda h: h.copy(out=X[0][:], in_=psO[0][0:65, :]), reads=[psOB[0]], writes=[XB[0]])
                sc.op("dve", lambda h: h.tensor_copy(out=X[1][:], in_=psO[1][0:65, :]), reads=[psOB[1]], writes=[XB[1]])

                def E(eng, fn, reads, writes):
                    pending.append(lambda: sc.op(eng, fn, reads=reads, writes=writes))

                for m in range(2):
                    E("pe", lambda h, m=m: h.matmul(psE[m][0:64, :], lhsT=sel[:], rhs=X[m][:], start=True, stop=True),
                      [cB, XB[m]], [psEB[m]])
                    E("dve", lambda h, m=m: h.reciprocal(out=r[m][:], in_=psE[m][0:64, :]), [psEB[m]], [rB[m]])
                    E("dve" if m == 0 else "pool", lambda h, m=m: h.tensor_tensor(
                        out=r[m][:], in0=X[m][0:64, :], in1=r[m][:], op=ALU.mult), [XB[m], rB[m]], [rB[m]])
                E("dve", lambda h: h.scalar_tensor_tensor(out=o[:], in0=r[1][:], scalar=neg_lam[:, 0:1], in1=r[0][:],
                                                          op0=ALU.mult, op1=ALU.add), [rB[0], rB[1], pB], [oB])
                E("act", lambda h: h.activation(out=sq[:], in_=o[:], func=AF.Square), [oB], [sqB])
                E("pe", lambda h: h.matmul(psE[0][0:64, :], lhsT=ones64[:], rhs=sq[:], start=True, stop=True), [cB, sqB], [psEB[0]])
                E("act", lambda h: h.activation(out=rs[:], in_=psE[0][0:64, :], func=AF.Ln, scale=1.0 / 64, bias=EPS), [psEB[0]], [rsB])
                E("act", lambda h: h.activation(out=rs[:], in_=rs[:], func=AF.Exp, scale=-0.5), [rsB], [rsB])
                E("dve", lambda h, qb=qb, hh=hh: h.scalar_tensor_tensor(out=ost[qb][:, hh, :], in0=o[:], scalar=gsc[:, 0:1], in1=rs[:],
                                                                       op0=ALU.mult, op1=ALU.mult), [oB, rsB, pB], [ostB[qb]])
            pending.append(lambda qb=qb, tok0=tok0: sc.dma(
                ods[qb], [(scr["mixT"][hh2 // 2, (hh2 % 2) * 64:(hh2 % 2) * 64 + 64, tok0:tok0 + TB], ost[qb][:, hh2, :])
                          for hh2 in range(HD)], reads=[ostB[qb]]))
        flush()


def phase3(k, l):
    nc, sc, cfg, W, C = k.nc, k.sc, k.cfg, k.W, k.C
    S, HD, HL = cfg.S, cfg.HD, cfg.HL
    scr = k.scr
    SBK = 2048
    NSB = S // SBK
    NC3 = HL // 2
    PATS = (1, 4, 16)
    scale = 64 ** -0.5
    with ExitStack() as ps:
        sb, pt = _mk(k, l, ps)
        qP = [sb("p3_q%d" % i, [128, NC3, SBK], BF16) for i in range(2)]; qB = Buf()
        for i in range(2):
            sc.op("pool", lambda h, i=i: h.memset(qP[i][:], 0.0), writes=[qB])
        kL = [sb("p3_k%d" % i, [128, NC3, SBK], BF16) for i in range(2)]; kB = [Buf() for _ in range(2)]
        vT = sb("p3_vT", [128, NC3, SBK], BF16); vTB = Buf()
        lds = [sc.dsem() for _ in range(3)]
        vt = [[sb("p3_vt%d_%d" % (pi, i), [128, 16, HL * 65], BF16) for i in range(2)] for pi in range(3)]
        vtB = [[Buf() for i in range(2)] for pi in range(3)]
        lmask = sb("p3_lmask", [128, 256], BF16); sel = sb("p3_sel", [65, 64])
        cB = Buf(); d0 = sc.dsem()
        sc.dma(d0, [(lmask[:], C["lmask"]), (sel[:], C["sel"])], writes=[cB])
        for pi in range(3):
            for i in range(2):
                sc.op("pool", lambda h: h.memset(vt[pi][i][:], 1.0), writes=[vtB[pi][i]])
        acc = [sb("p3_acc%d" % i, [65, SBK]) for i in range(2)]; accB = [Buf() for _ in range(2)]
        pT = [sb("p3_pT%d" % i, [128, 4, 256], BF16) for i in range(3)]; pTB = [Buf() for _ in range(3)]
        rr = sb("p3_rr", [64, 512]); rrB = Buf()
        ost = [sb("p3_ost%d" % i, [64, SBK], BF16) for i in range(2)]; ostB = [Buf() for _ in range(2)]
        ods = [sc.dsem() for _ in range(2)]
        psV = pt("p3_psV", [128, NC3, 128], BF16); psVB = PBuf()
        psS = [pt("p3_psS%d" % i, [128, 4, 256]) for i in range(2)]; psSB = [PBuf() for _ in range(2)]
        psO = [pt("p3_psO%d" % i, [128, 4, 128]) for i in range(2)]; psOB = [PBuf() for _ in range(2)]
        psE = pt("p3_psE", [128, 512]); psEB = PBuf()
        gi = 0
        hi = 0
        for u in range(NSB):
            ub = u % 2
            t0 = u * SBK
            sc.dma(lds[0], [(qP[par][64 * par:64 * par + 64, :, :],
                             scr["qTl"][:, 64 * par:64 * par + 64, t0:t0 + SBK].rearrange("c p s -> p c s")) for par in range(2)],
                   writes=[qB])
            sc.dma(lds[1], [(kL[ub][:], scr["kTl"][:, :, t0:t0 + SBK].rearrange("c p s -> p c s"))], writes=[kB[ub]])
            sc.dma(lds[2], [(vT[:], scr["vTl"][:, :, t0:t0 + SBK].rearrange("c p s -> p c s"))], writes=[vTB])
            vi = 0
            for pi, d in enumerate(PATS):
                for ti in range(16):
                    r_, nbl = ti % d, ti // d
                    off = 128 * d * nbl + r_
                    for c in range(NC3):
                        sc.op("pe", lambda h: h.transpose(out=psV[:, c, :], in_=vT[:, c, off:off + 127 * d + 1:d],
                                                          identity=k.ident_bf[:]),
                              reads=[vTB, k.cB], writes=[psVB], inc=(c == NC3 - 1))
                    dst = vt[pi][ub][:, ti, :].rearrange("p (h e) -> p h e", e=65)[:, :, 0:64]
                    src = psV[:].rearrange("p c (h e) -> p (c h) e", e=64)
                    if vi % 2 == 0:
                        sc.op("act", lambda h: h.copy(out=dst, in_=src), reads=[psVB], writes=[vtB[pi][ub]])
                    else:
                        sc.op("dve", lambda h: h.tensor_copy(out=dst, in_=src), reads=[psVB], writes=[vtB[pi][ub]])
                    vi += 1
            for hh in range(HL):
                c = hh // 2
                p0 = 64 * (hh % 2)
                ab = hi % 2
                hi += 1
                groups = [(pi, d, tg) for pi, d in enumerate(PATS) for tg in range(4)]
                gbase = gi
                gi += len(groups)

                def tiles_of(g):
                    pi, d, tg = groups[g]
                    tiles = []
                    for q4 in range(4):
                        ti = tg * 4 + q4
                        r_, nbl = ti % d, ti // d
                        off = 128 * d * nbl + r_
                        if nbl > 0:
                            prev = (ub, off - 128 * d, ti - d)
                        elif u > 0:
                            nbp = 16 // d - 1
                            prev = (1 - ub, 128 * d * nbp + r_, r_ + d * nbp)
                        else:
                            prev = None
                        tiles.append((ti, off, prev))
                    return tiles

                def g_qk(g):
                    pi, d, tg = groups[g]
                    sl = (gbase + g) % 2
                    for q4, (ti, off, prev) in enumerate(tiles_of(g)):
                        Q = qP[hh % 2][:, c, off:off + 127 * d + 1:d]
                        Kc = kL[ub][:, c, off:off + 127 * d + 1:d]
                        if prev is not None:
                            Kp = kL[prev[0]][:, c, prev[1]:prev[1] + 127 * d + 1:d]
                            rd = [qB, kB[ub], kB[prev[0]]]
                        else:
                            Kp = Kc
                            rd = [qB, kB[ub]]
                        sc.op("pe", lambda h: h.matmul(psS[sl][:, q4, 0:128], lhsT=Kp, rhs=Q, start=True, stop=True),
                              reads=rd, writes=[psSB[sl]], inc=False)
                        sc.op("pe", lambda h: h.matmul(psS[sl][:, q4, 128:256], lhsT=Kc, rhs=Q, start=True, stop=True),
                              reads=rd, writes=[psSB[sl]], inc=(q4 == 3))

                def g_exp(g):
                    sl = (gbase + g) % 2
                    p3 = (gbase + g) % 3
                    sc.op("act", lambda h: h.activation(out=pT[p3][:], in_=psS[sl][:], func=AF.Exp, scale=scale),
                          reads=[psSB[sl]], writes=[pTB[p3]])
                    sc.op("dve" if g % 2 == 0 else "pool", lambda h: h.tensor_tensor(
                        out=pT[p3][:], in0=pT[p3][:], in1=lmask[:, None, :].broadcast_to([128, 4, 256]), op=ALU.mult),
                        reads=[pTB[p3], cB], writes=[pTB[p3]])

                def g_pv(g):
                    pi, d, tg = groups[g]
                    sl = (gbase + g) % 2
                    p3 = (gbase + g) % 3
                    for q4, (ti, off, prev) in enumerate(tiles_of(g)):
                        if prev is not None:
                            sc.op("pe", lambda h: h.matmul(psO[sl][0:65, q4, :], lhsT=vt[pi][prev[0]][:, prev[2], hh * 65:(hh + 1) * 65],
                                                           rhs=pT[p3][:, q4, 0:128], start=True, stop=False),
                                  reads=[pTB[p3], vtB[pi][prev[0]]], writes=[psOB[sl]], inc=False)
                        sc.op("pe", lambda h: h.matmul(psO[sl][0:65, q4, :], lhsT=vt[pi][ub][:, ti, hh * 65:(hh + 1) * 65],
                                                       rhs=pT[p3][:, q4, 128:256], start=(prev is None), stop=True),
                              reads=[pTB[p3], vtB[pi][ub]], writes=[psOB[sl]], inc=(q4 == 3))

                def g_acc(g):
                    pi, d, tg = groups[g]
                    sl = (gbase + g) % 2
                    if d == 1:
                        dstv = acc[ab][:, tg * 512:(tg + 1) * 512].rearrange("p (a j) -> p a j", j=128)
                    elif d == 4:
                        dstv = acc[ab][:, tg * 512:(tg + 1) * 512].rearrange("p (j r) -> p r j", r=4)
                    else:
                        dstv = acc[ab][:].rearrange("p (j r) -> p r j", r=16)[:, tg * 4:tg * 4 + 4, :]
                    if pi == 0:
                        sc.op("dve", lambda h: h.tensor_copy(out=dstv, in_=psO[sl][0:65, :, :]), reads=[psOB[sl]], writes=[accB[ab]])
                    else:
                        sc.op("dve", lambda h: h.tensor_tensor(out=dstv, in0=dstv, in1=psO[sl][0:65, :, :], op=ALU.add),
                              reads=[psOB[sl], accB[ab]], writes=[accB[ab]])

                skewed(len(groups), [g_qk, g_exp, g_pv, g_acc])
                for sbk in range(4):
                    cs_ = slice(sbk * 512, (sbk + 1) * 512)
                    sc.op("pe", lambda h: h.matmul(psE[0:64, :], lhsT=sel[:], rhs=acc[ab][:, cs_], start=True, stop=True),
                          reads=[cB, accB[ab]], writes=[psEB])
                    sc.op("dve", lambda h: h.reciprocal(out=rr[:], in_=psE[0:64, :]), reads=[psEB], writes=[rrB])
                    sc.op("pool", lambda h: h.tensor_tensor(out=ost[ab][:, cs_], in0=acc[ab][0:64, cs_], in1=rr[:], op=ALU.mult),
                          reads=[rrB, accB[ab]], writes=[ostB[ab]])
                row = HD * 64 + hh * 64
                sc.dma(ods[ab], [(scr["mixT"][row // 128, (row % 128):(row % 128) + 64, t0:t0 + SBK], ost[ab][:])],
                       reads=[ostB[ab]])


def phase4(k, l):
    nc, sc, cfg, W, C = k.nc, k.sc, k.cfg, k.W, k.C
    S, HD, HL, G, HS, NB, NX = cfg.S, cfg.HD, cfg.HL, cfg.G, cfg.HS, cfg.NB, cfg.NXBC
    scr = k.scr
    XC = HS // 2
    HW = HS * 64
    with ExitStack() as ps:
        sb, pt = _mk(k, l, ps)
        triu = sb("p4_triu", [128, 128]); negm = sb("p4_negm", [128, 128])
        dtb = sb("p4_dtb", [128, HS]); alog = sb("p4_alog", [128, HS]); Dd = sb("p4_D", [128, HS]); gn = sb("p4_gn", [128, HW])
        negA = sb("p4_negA", [128, HS])
        cB = Buf(); d0 = sc.dsem()
        sc.dma(d0, [(triu[:], C["triu"]), (negm[:], C["negm"]),
                    (dtb[:], W["dt_bias"][l:l + 1, :].broadcast_to([128, HS])),
                    (alog[:], W["A_log"][l:l + 1, :].broadcast_to([128, HS])),
                    (Dd[:], W["ssd_D"][l:l + 1, :].broadcast_to([128, HS])),
                    (gn[:], W["ssd_norm"][l:l + 1, :].broadcast_to([128, HW]))], writes=[cB])
        pB = Buf()
        sc.op("act", lambda h: h.activation(out=negA[:], in_=alog[:], func=AF.Exp), reads=[cB], writes=[pB])
        sc.op("dve", lambda h: h.tensor_scalar(out=negA[:], in0=negA[:], scalar1=-1.0, scalar2=None, op0=ALU.mult), reads=[pB], writes=[pB])
        xb_ = [sb("p4_xb%d" % i, [128, NX, TB], BF16) for i in range(2)]
        z_ = [sb("p4_z%d" % i, [128, 4, HW], BF16) for i in range(2)]
        dt_ = [sb("p4_dt%d" % i, [128, 4, HS]) for i in range(2)]
        inB = [Buf() for _ in range(2)]; lds = [sc.dsem() for _ in range(2)]
        dtp = sb("p4_dtp", [128, 4, HS]); aa = sb("p4_a", [128, 4, HS]); dB = Buf()
        a_bc = sb("p4_abc", [128, HS, 128]); abB = Buf()
        cs_sb2 = [sb("p4_cs%d" % i, [128, HS]) for i in range(2)]; csl2 = [sb("p4_csl%d" % i, [128, HS]) for i in range(2)]; csB2 = [Buf() for _ in range(2)]
        arg = sb("p4_arg", [128, HS, 128]); argB = Buf()
        E = sb("p4_E", [128, HS, 128]); EB = Buf()
        MT2 = [sb("p4_MT%d" % i, [128, HS, 128], BF16) for i in range(2)]; MTB2 = [Buf() for _ in range(2)]
        x_sb2 = [sb("p4_x%d" % i, [128, HW]) for i in range(2)]; B_sb2 = [sb("p4_B%d" % i, [128, G * 128], BF16) for i in range(2)]; xB2 = [Buf() for _ in range(2)]
        xdt2 = [sb("p4_xdt%d" % i, [128, HS, 64], BF16) for i in range(2)]; xdtB2 = [Buf() for _ in range(2)]
        xdd2 = [sb("p4_xdd%d" % i, [128, HS, 64], BF16) for i in range(2)]; xddB2 = [Buf() for _ in range(2)]
        ecs2 = [sb("p4_ecs%d" % i, [128, HS]) for i in range(2)]; dst_ = sb("p4_dst", [128, HS]); edec2 = [sb("p4_edec%d" % i, [128, HS]) for i in range(2)]; eB2 = [Buf() for _ in range(2)]; dstB = Buf()
        t13 = [sb("p4_t1%d" % i, [128, HW]) for i in range(3)]; t1B3 = [Buf() for _ in range(3)]
        t33 = [sb("p4_t3%d" % i, [128, HW]) for i in range(3)]; t3B3 = [Buf() for _ in range(3)]
        yv = sb("p4_yv", [128, HW]); yvB = Buf()
        szb = [sb("p4_szb%d" % i, [128, 4, HW]) for i in range(2)]; szbB = [Buf() for _ in range(2)]
        junk = sb("p4_junk", [128, HW], BF16); junkB = Buf()
        ss2 = sb("p4_ss2", [128, G]); rs2 = sb("p4_rs2", [128, G]); ssB = Buf(); rsB = Buf()
        yn = sb("p4_yn", [128, HW], BF16); ynB = Buf()
        yst = [sb("p4_yst%d" % i, [128, XC, TB], BF16) for i in range(2)]; ystB = [Buf() for _ in range(2)]
        sds = [sc.dsem() for _ in range(2)]
        st = sb("p4_st", [128, HW]); st_bf = sb("p4_stbf", [128, HW], BF16); stB = Buf(); stbB = Buf()
        ps1 = pt("p4_ps1", [128, 512]); ps1B = PBuf()
        psR = pt("p4_psR", [128, 2, 512]); psRB = PBuf()
        psXT = pt("p4_psXT", [128, 1024], BF16); psXTB = PBuf(); psXTb = pt("p4_psXTb", [128, 1024], BF16); psXTbB = PBuf()
        psY = pt("p4_psY", [128, 512]); psYB = PBuf()
        psYO = pt("p4_psYO", [128, 512]); psYOB = PBuf()
        psS = pt("p4_psS", [128, 512]); psSB = PBuf()
        sc.op("pool", lambda h: h.memset(st[:], 0.0), writes=[stB])
        sc.op("pool", lambda h: h.memset(st_bf[:], 0.0), writes=[stbB])
        XTO = XC * 128 + G * 128
        for t in range(NB):
            tok0 = t * TB
            b = t % 2
            sc.dma(lds[b], [(xb_[b][:], scr["xbcT"][:, :, tok0:tok0 + TB].rearrange("c p s -> p c s")),
                            (z_[b][:], scr["zs"][tok0:tok0 + TB, :].rearrange("(n p) f -> p n f", p=128)),
                            (dt_[b][:], scr["dts"][tok0:tok0 + TB, :].rearrange("(n p) f -> p n f", p=128))],
                   writes=[inB[b]])
            sc.op("dve", lambda h: h.tensor_tensor(out=dtp[:], in0=dt_[b][:], in1=dtb[:, None, :].broadcast_to([128, 4, HS]), op=ALU.add),
                  reads=[inB[b], cB], writes=[dB])
            sc.op("act", lambda h: h.activation(out=dtp[:], in_=dtp[:], func=AF.Exp), reads=[dB], writes=[dB])
            sc.op("act", lambda h: h.activation(out=dtp[:], in_=dtp[:], func=AF.Ln, bias=1.0), reads=[dB], writes=[dB])
            sc.op("dve", lambda h: h.tensor_tensor(out=aa[:], in0=dtp[:], in1=negA[:, None, :].broadcast_to([128, 4, HS]), op=ALU.mult),
                  reads=[dB, pB], writes=[dB])
            sc.op("act", lambda h: h.activation(out=szb[b][:], in_=z_[b][:], func=AF.Silu), reads=[inB[b]], writes=[szbB[b]])
            def front(n):
                q = n % 2
                csl_ = slice(n * 128, (n + 1) * 128)
                sc.op("dve", lambda h: h.tensor_copy(out=a_bc[:], in_=aa[:, n, :, None].broadcast_to([128, HS, 128])),
                      reads=[dB], writes=[abB])
                sc.op("pe", lambda h: h.matmul(ps1[:, 256:256 + HS], lhsT=triu[:], rhs=aa[:, n, :], start=True, stop=True),
                      reads=[cB, dB], writes=[ps1B])
                for hh in range(HS):
                    sc.op("pe", lambda h: h.matmul(psR[:, hh // 4, (hh % 4) * 128:(hh % 4 + 1) * 128], lhsT=a_bc[:, hh, :], rhs=triu[:],
                                                   start=True, stop=True), reads=[cB, abB], writes=[psRB], inc=(hh == HS - 1))
                psRv = psR[:].rearrange("p a (b c) -> p (a b) c", c=128)[:, 0:HS, :]
                sc.op("act", lambda h: h.copy(out=cs_sb2[q][:], in_=ps1[:, 256:256 + HS]), reads=[ps1B], writes=[csB2[q]])
                sc.op("act", lambda h: h.copy(out=csl2[q][:], in_=psRv[:, :, 127]), reads=[psRB], writes=[csB2[q]])
                sc.op("dve", lambda h: h.tensor_tensor(out=arg[:], in0=psRv, in1=cs_sb2[q][:, :, None].broadcast_to([128, HS, 128]), op=ALU.subtract),
                      reads=[psRB, csB2[q]], writes=[argB])
                sc.op("pool", lambda h: h.tensor_tensor(out=arg[:], in0=arg[:], in1=negm[:, None, :].broadcast_to([128, HS, 128]), op=ALU.add),
                      reads=[argB, cB], writes=[argB])
                sc.op("act", lambda h: h.activation(out=E[:], in_=arg[:], func=AF.Exp), reads=[argB], writes=[EB])
                for g in range(G):
                    sc.op("pe", lambda h: h.matmul(ps1[:, g * 128:(g + 1) * 128], lhsT=xb_[b][:, XC + g, csl_], rhs=xb_[b][:, XC + G + g, csl_],
                                                   start=True, stop=True), reads=[inB[b]], writes=[ps1B], inc=(g == G - 1))
                for g in range(G):
                    sc.op("dve", lambda h: h.tensor_tensor(out=MT2[q][:, 3 * g:3 * g + 3, :], in0=E[:, 3 * g:3 * g + 3, :],
                                                           in1=ps1[:, None, g * 128:(g + 1) * 128].broadcast_to([128, 3, 128]), op=ALU.mult),
                          reads=[EB, ps1B], writes=[MTB2[q]])
                for j in range(XC + G):
                    sc.op("pe", lambda h: h.transpose(out=psXT[:, j * 128:(j + 1) * 128], in_=xb_[b][:, j, csl_], identity=k.ident_bf[:]),
                          reads=[inB[b], k.cB], writes=[psXTB], inc=(j == XC + G - 1))
                sc.op("act", lambda h: h.copy(out=x_sb2[q][:], in_=psXT[:, 0:HW]), reads=[psXTB], writes=[xB2[q]])
                sc.op("act", lambda h: h.copy(out=B_sb2[q][:], in_=psXT[:, HW:HW + G * 128]), reads=[psXTB], writes=[xB2[q]])
                sc.op("dve", lambda h: h.tensor_tensor(out=xdt2[q][:], in0=x_sb2[q][:].rearrange("p (h e) -> p h e", e=64),
                                                       in1=dtp[:, n, :, None].broadcast_to([128, HS, 64]), op=ALU.mult),
                      reads=[xB2[q], dB], writes=[xdtB2[q]])
                sc.op("act", lambda h: h.activation(out=ecs2[q][:], in_=cs_sb2[q][:], func=AF.Exp), reads=[csB2[q]], writes=[eB2[q]])
                sc.op("pool", lambda h: h.tensor_tensor(out=t33[n % 3][:].rearrange("p (h e) -> p h e", e=64), in0=x_sb2[q][:].rearrange("p (h e) -> p h e", e=64),
                                                        in1=Dd[:, :, None].broadcast_to([128, HS, 64]), op=ALU.mult),
                      reads=[xB2[q], cB], writes=[t3B3[n % 3]])
                sc.op("dve", lambda h: h.tensor_tensor(out=dst_[:], in0=csl2[q][:], in1=cs_sb2[q][:], op=ALU.subtract), reads=[csB2[q]], writes=[dstB])
                sc.op("act", lambda h: h.activation(out=dst_[:], in_=dst_[:], func=AF.Exp), reads=[dstB], writes=[dstB])
                sc.op("act", lambda h: h.activation(out=edec2[q][:], in_=csl2[q][:], func=AF.Exp), reads=[csB2[q]], writes=[eB2[q]])
                sc.op("dve", lambda h: h.tensor_tensor(out=xdd2[q][:], in0=xdt2[q][:], in1=dst_[:, :, None].broadcast_to([128, HS, 64]), op=ALU.mult),
                      reads=[xdtB2[q], dstB], writes=[xddB2[q]])

            def back(n):
                q = n % 2
                csl_ = slice(n * 128, (n + 1) * 128)
                for hh in range(HS):
                    sc.op("pe", lambda h: h.matmul(psY[:, hh * 64:(hh + 1) * 64], lhsT=MT2[q][:, hh, :], rhs=xdt2[q][:, hh, :], start=True, stop=True),
                          reads=[MTB2[q], xdtB2[q]], writes=[psYB], inc=(hh == HS - 1))
                for g in range(G):
                    sc.op("pe", lambda h: h.matmul(psYO[:, g * 192:(g + 1) * 192], lhsT=xb_[b][:, XC + G + g, csl_], rhs=st_bf[:, g * 192:(g + 1) * 192],
                                                   start=True, stop=True), reads=[inB[b], stbB], writes=[psYOB], inc=(g == G - 1))
                sc.op("dve", lambda h: h.tensor_tensor(out=t13[n % 3][:].rearrange("p (h e) -> p h e", e=64), in0=psYO[:, 0:HW].rearrange("p (h e) -> p h e", e=64),
                                                       in1=ecs2[q][:, :, None].broadcast_to([128, HS, 64]), op=ALU.mult),
                      reads=[psYOB, eB2[q]], writes=[t1B3[n % 3]])
                sc.op("dve", lambda h: h.tensor_tensor(out=t13[n % 3][:], in0=psY[:, 0:HW], in1=t13[n % 3][:], op=ALU.add), reads=[psYB, t1B3[n % 3]], writes=[t1B3[n % 3]])
                for g in range(G):
                    sc.op("pe", lambda h: h.matmul(psS[:, g * 192:(g + 1) * 192], lhsT=B_sb2[q][:, g * 128:(g + 1) * 128],
                                                   rhs=xdd2[q][:, 3 * g:3 * g + 3, :].rearrange("p h e -> p (h e)"), start=True, stop=True),
                          reads=[xB2[q], xddB2[q]], writes=[psSB], inc=(g == G - 1))
                sc.op("dve", lambda h: h.tensor_tensor(out=st[:].rearrange("p (h e) -> p h e", e=64), in0=st[:].rearrange("p (h e) -> p h e", e=64),
                                                       in1=edec2[q][:, :, None].broadcast_to([128, HS, 64]), op=ALU.mult),
                      reads=[stB, eB2[q]], writes=[stB])
                sc.op("dve", lambda h: h.tensor_tensor(out=st[:], in0=st[:], in1=psS[:, 0:HW], op=ALU.add), reads=[stB, psSB], writes=[stB])
                sc.op("pool", lambda h: h.tensor_copy(out=st_bf[:], in_=st[:]), reads=[stB], writes=[stbB])

            def post(n):
                q = n % 2
                csl_ = slice(n * 128, (n + 1) * 128)
                sc.op("pool", lambda h: h.tensor_tensor(out=yv[:], in0=t13[n % 3][:], in1=t33[n % 3][:], op=ALU.add), reads=[t1B3[n % 3], t3B3[n % 3]], writes=[yvB])
                sc.op("dve", lambda h: h.tensor_tensor(out=yv[:], in0=yv[:], in1=szb[b][:, n, :], op=ALU.mult), reads=[yvB, szbB[b]], writes=[yvB])
                for g in range(G):
                    sc.op("act", lambda h: h.activation(out=junk[:, 0:192], in_=yv[:, g * 192:(g + 1) * 192], func=AF.Square,
                                                        accum_out=ss2[:, g:g + 1]), reads=[yvB], writes=[junkB, ssB])
                rms_rstd(k, ps, ss2, rs2, G, 192, ssB, rsB)
                for g in range(G):
                    sc.op("dve", lambda h: h.scalar_tensor_tensor(out=yn[:, g * 192:(g + 1) * 192], in0=yv[:, g * 192:(g + 1) * 192],
                                                                  scalar=rs2[:, g:g + 1], in1=gn[:, g * 192:(g + 1) * 192],
                                                                  op0=ALU.mult, op1=ALU.mult), reads=[yvB, rsB, cB], writes=[ynB])
                for j in range(XC):
                    sc.op("pe", lambda h: h.transpose(out=psXTb[:, j * 128:(j + 1) * 128], in_=yn[:, j * 128:(j + 1) * 128],
                                                      identity=k.ident_bf[:]), reads=[ynB, k.cB], writes=[psXTbB], inc=(j == XC - 1))
                sc.op("act", lambda h: h.copy(out=yst[b][:, :, csl_], in_=psXTb[:, 0:XC * 128].rearrange("p (j s) -> p j s", s=128)),
                      reads=[psXTbB], writes=[ystB[b]])

            front(0)
            for n in range(4):
                if n + 1 < 4:
                    front(n + 1)
                back(n)
                if n > 0:
                    post(n - 1)
            post(3)
            c0 = (HD * 64 + HL * 64) // 128
            sc.dma(sds[b], [(scr["mixT"][c0:c0 + XC, :, tok0:tok0 + TB].rearrange("c p s -> p c s"), yst[b][:])], reads=[ystB[b]])


def post_norm_add(k, sc, psF, psFB, n, ss, ssB, rstd, rsB, gb, gB, yt, ytB, xres, xresB, xo, xoB, junk, junkB, add_eng="pool"):
    sc.op("act", lambda h: h.activation(out=junk[:], in_=psF[:], func=AF.Square, accum_out=ss[:, n:n + 1]),
          reads=[psFB], writes=[junkB, ssB])
    rms_rstd(k, None, ss[:, n:n + 1], rstd[:, n:n + 1], 1, D, ssB, rsB)
    sc.op("dve", lambda h: h.scalar_tensor_tensor(out=yt[:], in0=psF[:], scalar=rstd[:, n:n + 1], in1=gb[:],
                                                  op0=ALU.mult, op1=ALU.mult), reads=[psFB, rsB, gB], writes=[ytB])
    sc.op(add_eng, lambda h: h.tensor_tensor(out=xo[:, n, :], in0=xres[:, n, :], in1=yt[:], op=ALU.add),
          reads=[ytB, xresB], writes=[xoB])


def phase5a(k, l, x_src):
    nc, sc, cfg, W, C = k.nc, k.sc, k.cfg, k.W, k.C
    S, NB, MC = cfg.S, cfg.NB, cfg.MIXC
    scr = k.scr
    with ExitStack() as ps:
        sb, pt = _mk(k, l, ps)
        wout = sb("p5a_w", [128, MC, D], BF16)
        stg = [sb("p5a_stg%d" % i, [128, 512]) for i in range(3)]
        stgB = [Buf() for _ in range(3)]; ds = [sc.dsem() for _ in range(3)]
        k.gB, k.wB = Buf(), Buf()
        gb = sb("p5a_g", [128, D]); gB = Buf(); d0 = sc.dsem()
        sc.dma(d0, [(gb[:], W["g_postmix"][l:l + 1, :].broadcast_to([128, D]))], writes=[gB])
        load_weights_bf16(k, ps, wout, W["w_out"][l], D, None, stg, stgB, ds, MC)
        mT = [sb("p5a_m%d" % i, [128, MC, TB], BF16) for i in range(2)]; mB = [Buf() for _ in range(2)]
        xr = [sb("p5a_x%d" % i, [128, 4, D]) for i in range(2)]; xB = [Buf() for _ in range(2)]
        lds = [sc.dsem() for _ in range(2)]
        xo = [sb("p5a_xo%d" % i, [128, 4, D]) for i in range(2)]; xoB = [Buf() for _ in range(2)]
        sds = [sc.dsem() for _ in range(2)]
        psF = [pt("p5a_psF%d" % i, [128, D]) for i in range(2)]; psFB = [PBuf() for _ in range(2)]
        ss = sb("p5a_ss", [128, 4]); ssB = Buf(); rstd = sb("p5a_rstd", [128, 4]); rsB = Buf()
        yt = sb("p5a_yt", [128, D]); ytB = Buf()
        junk = sb("p5a_junk", [128, D], BF16); junkB = Buf()
        ci = 0
        for t in range(NB):
            tok0 = t * TB
            b = t % 2
            sc.dma(lds[b], [(mT[b][:], scr["mixT"][:, :, tok0:tok0 + TB].rearrange("c p s -> p c s")),
                            (xr[b][:], x_src[tok0:tok0 + TB, :].rearrange("(n p) d -> p n d", p=128))],
                   writes=[mB[b], xB[b]])
            for n in range(4):
                a = ci % 2
                ci += 1
                for half in range(2):
                    for kc in range(MC):
                        sc.op("pe", lambda h, kc=kc, half=half: h.matmul(
                            psF[a][:, half * 512:(half + 1) * 512], lhsT=mT[b][:, kc, n * 128:(n + 1) * 128],
                            rhs=wout[:, kc, half * 512:(half + 1) * 512], start=(kc == 0), stop=(kc == MC - 1)),
                            reads=[k.wB, mB[b]], writes=[psFB[a]], inc=(kc == MC - 1 and half == 1))
                post_norm_add(k, sc, psF[a], psFB[a], n, ss, ssB, rstd, rsB, gb, gB, yt, ytB, xr[b], xB[b], xo[b], xoB[b], junk, junkB,
                              add_eng=("dve" if n % 2 == 0 else "pool"))
            sc.dma(sds[b], [(scr["xa"][tok0:tok0 + TB, :].rearrange("(n p) d -> p n d", p=128), xo[b][:])], reads=[xoB[b]])


def phase5b(k, l, x_dst):
    nc, sc, cfg, W, C = k.nc, k.sc, k.cfg, k.W, k.C
    S, FC, TBF = cfg.S, cfg.FC, cfg.TBF
    NT = TBF // 128
    scr = k.scr
    with ExitStack() as ps:
        sb, pt = _mk(k, l, ps)
        wup = sb("p5b_wup", [128, 8, 2 * cfg.DFF], BF16)
        wdn = sb("p5b_wdn", [128, FC, D], BF16)
        stg = [sb("p5b_stg%d" % i, [128, 512]) for i in range(3)]
        stgB = [Buf() for _ in range(3)]; ds = [sc.dsem() for _ in range(3)]
        k.gB, k.wB = Buf(), Buf()
        gain = sb("p5b_gain", [128, 8]); gb = sb("p5b_g", [128, D]); gB = Buf(); d0 = sc.dsem()
        fcw = sb("p5b_fcw", [128, 2 * FC, 3]); fcb = sb("p5b_fcb", [128, 2 * FC])
        sc.dma(d0, [(gb[:], W["g_postffn"][l:l + 1, :].broadcast_to([128, D])), (gain[:], W["g_preffn"][l]),
                    (fcw[:], W["fcw"][l]), (fcb[:], W["fcb"][l])], writes=[gB, k.gB])
        load_weights_bf16(k, ps, wup, W["ffn_up"][l], 2 * cfg.DFF, gain, stg, stgB, ds, 8)
        load_weights_bf16(k, ps, wdn, W["ffn_down"][l], D, None, stg, stgB, ds, FC)
        xr2 = [sb("p5b_x%d" % i, [128, NT, D]) for i in range(2)]; xB2 = [[Buf() for _ in range(NT)] for i in range(2)]
        lds2 = [[sc.dsem() for _ in range(NT)] for i in range(2)]; sds2 = [sc.dsem() for _ in range(2)]
        xs2 = [sb("p5b_xs%d" % i, [128, NT, D], BF16) for i in range(2)]; xsB2 = [Buf() for _ in range(2)]
        hT2 = [sb("p5b_hT%d" % i, [128, 8, 2 + TBF], BF16) for i in range(2)]; hTB2 = [Buf() for _ in range(2)]
        aT = sb("p5b_aT", [128, FC, TBF], BF16); aTB = Buf()
        NU = 4
        u = [sb("p5b_u%d" % i, [128, 2 + TBF]) for i in range(NU)]; uB = [Buf() for _ in range(NU)]; uhB = [Buf() for _ in range(NU)]
        y = [sb("p5b_y%d" % i, [128, TBF]) for i in range(NU)]; yB = [Buf() for _ in range(NU)]
        sg = [sb("p5b_sg%d" % i, [128, TBF]) for i in range(2)]; sgB = [Buf() for _ in range(2)]
        ss = sb("p5b_ss", [128, 4]); ssB = Buf(); rstd = sb("p5b_rstd", [128, 4]); rsB = Buf()
        yt = sb("p5b_yt", [128, D]); ytB = Buf()
        junk = sb("p5b_junk", [128, D], BF16); junkB = Buf()
        psT = [pt("p5b_psT%d" % i, [128, 2, 512], BF16) for i in range(2)]; psTB = [PBuf() for _ in range(2)]
        psU = [pt("p5b_psU%d" % i, [128, 512]) for i in range(NU)]; psUB = [PBuf() for _ in range(NU)]
        psF = [pt("p5b_psF%d" % i, [128, D]) for i in range(1)]; psFB = [PBuf() for _ in range(1)]
        for i in range(2):
            sc.op("pool", lambda h, i=i: h.memset(hT2[i][:], 0.0), writes=[hTB2[i]])
        ci = 0
        NBLK = S // TBF

        def do_norm(t):
            b = t % 2
            tok0 = t * TBF
            for n in range(NT):
                sc.dma(lds2[b][n], [(xr2[b][:, n, :], scr["xa"][tok0 + n * 128:tok0 + (n + 1) * 128, :])], writes=[xB2[b][n]])
            for n in range(NT):
                sc.op("act", lambda h, n=n: h.activation(out=junk[:], in_=xr2[b][:, n, :], func=AF.Square, accum_out=ss[:, n:n + 1]),
                      reads=[xB2[b][n]], writes=[junkB, ssB])
            rms_rstd(k, ps, ss, rstd, NT, D, ssB, rsB)
            for n in range(NT):
                if n % 2 == 0:
                    sc.op("dve", lambda h, n=n: h.tensor_scalar(
                        out=xs2[b][:, n, :], in0=xr2[b][:, n, :], scalar1=rstd[:, n:n + 1], scalar2=None, op0=ALU.mult),
                        reads=[xB2[b][n], rsB], writes=[xsB2[b]])
                else:
                    sc.op("act", lambda h, n=n: h.activation(out=xs2[b][:, n, :], in_=xr2[b][:, n, :], func=AF.Copy, scale=rstd[:, n:n + 1]),
                          reads=[xB2[b][n], rsB], writes=[xsB2[b]])
            if t > 0:
                sc.op("pool", lambda h: h.tensor_copy(out=hT2[b][:, :, 0:2], in_=hT2[1 - b][:, :, TBF:TBF + 2]), reads=[hTB2[1 - b]], writes=[hTB2[b]])
            for cp in range(4):
                pi = cp % 2
                for c2 in range(2):
                    c = cp * 2 + c2
                    for n in range(NT):
                        sc.op("pe", lambda h, c=c, c2=c2, n=n: h.transpose(
                            out=psT[pi][:, c2, n * 128:(n + 1) * 128], in_=xs2[b][:, n, c * 128:(c + 1) * 128],
                            identity=k.ident_bf[:]), reads=[xsB2[b], k.cB], writes=[psTB[pi]], inc=(c2 == 1 and n == NT - 1))
                sc.op("act", lambda h, cp=cp: h.copy(out=hT2[b][:, 2 * cp:2 * cp + 2, 2:2 + TBF], in_=psT[pi][:, :, 0:TBF]),
                      reads=[psTB[pi]], writes=[hTB2[b]])

        def do_chunks(t):
            b = t % 2
            chunks = [(j, wi, c) for j in range(FC) for wi, c in enumerate((j, FC + j))]

            def st_pe(i):
                j, wi, c = chunks[i]
                a = i % NU
                for kc in range(8):
                    sc.op("pe", lambda h: h.matmul(psU[a][:, 0:TBF + 2], lhsT=wup[:, kc, c * 128:(c + 1) * 128], rhs=hT2[b][:, kc, :],
                                                   start=(kc == 0), stop=(kc == 7)), reads=[k.wB, hTB2[b]], writes=[psUB[a]], inc=(kc == 7))

            def st_copy(i):
                j, wi, c = chunks[i]
                a = i % NU
                sc.op("act", lambda h: h.copy(out=u[a][:, 0:2 + TBF], in_=psU[a][:, 0:2 + TBF]), reads=[psUB[a]], writes=[uB[a]])
                sc.op("act", lambda h: h.activation(out=y[a][:], in_=psU[a][:, 2:2 + TBF], func=AF.Identity, scale=fcw[:, c, 2:3], bias=fcb[:, c:c + 1]),
                      reads=[psUB[a], k.gB], writes=[yB[a]])

            def st_conv(i):
                j, wi, c = chunks[i]
                a = i % NU
                for kk in range(2):
                    sc.op("dve", lambda h: h.scalar_tensor_tensor(out=y[a][:], in0=u[a][:, kk:kk + TBF], scalar=fcw[:, c, kk:kk + 1], in1=y[a][:],
                                                                  op0=ALU.mult, op1=ALU.add), reads=[uB[a], yB[a], k.gB], writes=[yB[a]])

            def st_gate(i):
                j, wi, c = chunks[i]
                a = i % NU
                if wi == 0:
                    sc.op("act", lambda h: h.activation(out=sg[j % 2][:], in_=y[a][:], func=AF.Silu), reads=[yB[a]], writes=[sgB[j % 2]])
                else:
                    sc.op("pool", lambda h: h.tensor_tensor(out=aT[:, j, :], in0=sg[j % 2][:], in1=y[a][:], op=ALU.mult),
                          reads=[sgB[j % 2], yB[a]], writes=[aTB])

            skewed(len(chunks), [st_pe, st_copy, st_conv, st_gate])

        def do_down(t):
            b = t % 2
            tok0 = t * TBF
            for n in range(NT):
                a = 0
                for half in range(2):
                    for j in range(FC):
                        sc.op("pe", lambda h, j=j, half=half: h.matmul(
                            psF[a][:, half * 512:(half + 1) * 512], lhsT=aT[:, j, n * 128:(n + 1) * 128],
                            rhs=wdn[:, j, half * 512:(half + 1) * 512], start=(j == 0), stop=(j == FC - 1)),
                            reads=[k.wB, aTB], writes=[psFB[a]], inc=(j == FC - 1 and half == 1))
                post_norm_add(k, sc, psF[a], psFB[a], n, ss, ssB, rstd, rsB, gb, gB, yt, ytB, xr2[b], xB2[b][n], xr2[b], xB2[b][n], junk, junkB)
            sc.dma(sds2[b], [(x_dst[tok0:tok0 + TBF, :].rearrange("(n p) d -> p n d", p=128), xr2[b][:])], reads=xB2[b])

        do_norm(0)
        for t in range(NBLK):
            do_chunks(t)
            if t + 1 < NBLK:
                do_norm(t + 1)
            do_down(t)


def kernel(**inputs):
    cfg = Cfg()
    nc = build(cfg)
    hp = host_params(cfg, inputs)
    consts = host_consts(cfg)
    x = np.asarray(inputs["x"], np.float32)
    nb = x.shape[0]
    in_maps = []
    for b in range(nb):
        m = {"x": np.ascontiguousarray(x[b])}
        m.update(hp)
        m.update(consts)
        in_maps.append(m)
    res = run_bass_kernel_spmd(nc, in_maps, core_ids=list(range(nb)))
    return np.stack([np.asarray(res.results[b]["out"], np.float32) for b in range(nb)])
```

```python
import math
import os
from contextlib import ExitStack

import numpy as np
import ml_dtypes

import concourse.bass as bass
import concourse.mybir as mybir
from concourse.bass_utils import run_bass_kernel_spmd

F32 = mybir.dt.float32
BF16 = mybir.dt.bfloat16
AF = mybir.ActivationFunctionType
ALU = mybir.AluOpType
AX = mybir.AxisListType

D = 1024
EPS = 1e-6
TB = 512
SAME_ENGINE_SYNC = True


class Buf:
    __slots__ = ("name", "w", "r", "excl")

    def __init__(self, name="", excl=False):
        self.name = name
        self.w = None
        self.r = {}
        self.excl = excl


def PBuf(name=""):
    return Buf(name, True)


class _Eng:
    def __init__(self, name, h, sem):
        self.name, self.h, self.sem, self.cnt, self.seen = name, h, sem, 0, {}


class _DSem:
    def __init__(self, sem):
        self.sem, self.cnt = sem, 0


class Sched:
    def __init__(self, nc, es):
        self.nc = nc
        self.es = es
        self.E = {}
        for name, h in (("pe", nc.tensor), ("act", nc.scalar), ("dve", nc.vector),
                        ("pool", nc.gpsimd), ("sp", nc.sync)):
            self.E[name] = _Eng(name, h, es.enter_context(nc.semaphore("s_" + name)))
        self.bar = es.enter_context(nc.semaphore("s_bar"))
        self.barcnt = 0
        self.dsems = []
        self.nsem = 0

    def dsem(self):
        self.nsem += 1
        d = _DSem(self.es.enter_context(self.nc.semaphore("d%d" % self.nsem)))
        self.dsems.append(d)
        return d

    def _wait(self, e, reads, writes):
        deps = {}

        def add(t):
            if t is not None and deps.get(t[0], (None, 0))[1] < t[1]:
                deps[t[0]] = t

        for b in reads:
            add(b.w)
        for b in writes:
            add(b.w)
            for s, v in b.r.items():
                add((s, v))
        for s, (sem, val) in deps.items():
            if sem is e.sem and (e.name == "pe" or not SAME_ENGINE_SYNC):
                continue
            if e.seen.get(sem, 0) >= val:
                continue
            e.h.wait_ge(sem, val)
            e.seen[sem] = val

    @staticmethod
    def _mark(tok, reads, writes):
        for b in reads:
            if b.r.get(tok[0], 0) < tok[1]:
                b.r[tok[0]] = tok[1]
        for b in writes:
            b.w = tok
            b.r = {}

    def op(self, eng, fn, reads=(), writes=(), inc=True):
        e = self.E[eng]
        if any(b.excl for b in reads):
            writes = list(writes) + [b for b in reads if b.excl]
            reads = [b for b in reads if not b.excl]
        self._wait(e, reads, writes)
        ins = fn(e.h)
        if inc:
            e.cnt += 1
            ins.then_inc(e.sem, 1)
            tok = (e.sem, e.cnt)
        else:
            tok = (e.sem, e.cnt + 1)
        self._mark(tok, reads, writes)
        return tok

    def dma(self, ds, pairs, reads=(), writes=(), eng="sp"):
        e = self.E[eng]
        self._wait(e, reads, writes)
        for out, in_ in pairs:
            ds.cnt += 16
            e.h.dma_start(out=out, in_=in_).then_inc(ds.sem, 16)
        tok = (ds.sem, ds.cnt)
        self._mark(tok, reads, writes)
        return tok

    def barrier(self):
        sp = self.E["sp"]
        for o in self.E.values():
            if o is not sp and o.cnt > 0 and sp.seen.get(o.sem, 0) < o.cnt:
                sp.h.wait_ge(o.sem, o.cnt)
                sp.seen[o.sem] = o.cnt
        for d in self.dsems:
            if d.cnt > 0 and sp.seen.get(d.sem, 0) < d.cnt:
                sp.h.wait_ge(d.sem, d.cnt)
                sp.seen[d.sem] = d.cnt
        self.barcnt += 1
        sp.h.sem_inc(self.bar, 1)
        for o in self.E.values():
            if o is not sp:
                o.h.wait_ge(self.bar, self.barcnt)
            for o2 in self.E.values():
                o.seen[o2.sem] = o2.cnt
            for d in self.dsems:
                o.seen[d.sem] = d.cnt


class Cfg:
    def __init__(self, S=8192, L=2, HD=4, HL=6, G=2, DFF=2816, TBF=256):
        self.S, self.L, self.HD, self.HL, self.G, self.DFF, self.TBF = S, L, HD, HL, G, DFF, TBF
        self.HS = 3 * G
        self.NB = S // TB
        off = 0
        self.fm = []
        for nm, cnt, m in (("dq", HD, 64), ("dk", HD, 64), ("lq", HL // 2, 128), ("lk", HL // 2, 128),
                           ("lv", HL // 2, 128), ("xs", self.HS // 2, 128), ("B", G, 128), ("C", G, 128)):
            for i in range(cnt):
                self.fm.append((nm, i, off, m))
                off += m
        self.off_dv = off
        off += HD * 64
        self.off_z = off
        off += self.HS * 64 + self.HS
        self.NCOL = off
        self.NXBC = self.HS // 2 + 2 * G
        self.MIXC = (HD * 64 + HL * 64 + self.HS * 64) // 128
        self.FC = DFF // 128


def _rope_tables(S, head_dim, rows):
    half = head_dim // 2
    inv = np.exp(-math.log(10000.0) * np.arange(half, dtype=np.float32) / half).astype(np.float32)
    ang = np.arange(S, dtype=np.float32)[None, :] * inv[:, None]
    cos = np.cos(ang).astype(np.float32)
    sin = np.sin(ang).astype(np.float32)
    p = np.arange(rows)
    j = p % half
    first = (p % head_dim) < half
    cosT = cos[j]
    sinT = np.where(first[:, None], -sin[j], sin[j])
    perm = np.zeros((rows, rows), np.float32)
    partner = np.where(first, p + half, p - half)
    perm[p, partner] = 1.0
    return cosT.astype(np.float32), sinT.astype(np.float32), perm


def host_consts(cfg):
    S = cfg.S
    c = {}
    cd, sd, pd = _rope_tables(S, 32, 64)
    cl, sl, pl = _rope_tables(S, 64, 128)
    c["cosd"], c["sind"], c["cosl"], c["sinl"] = [a.astype(ml_dtypes.bfloat16) for a in (cd, sd, cl, sl)]
    c["permd"] = pd.astype(ml_dtypes.bfloat16)
    c["perml"] = pl.astype(ml_dtypes.bfloat16)
    c["ident_bf"] = np.eye(128, dtype=np.float32).astype(ml_dtypes.bfloat16)
    c["ident_f"] = np.eye(128, dtype=np.float32)
    k = np.arange(128)[:, None]
    q = np.arange(512)[None, :]
    c["dmask"] = np.stack([(q >= 128 * di + k) for di in range(4)]).astype(np.float32).astype(ml_dtypes.bfloat16)
    qq = np.arange(128)[None, :]
    c["lmask"] = np.concatenate([(qq <= k), (qq >= k)], axis=1).astype(np.float32).astype(ml_dtypes.bfloat16)
    c["triu"] = (k <= qq).astype(np.float32)
    c["negm"] = np.where(qq >= k, 0.0, -30000.0).astype(np.float32)
    sel = np.zeros((65, 64), np.float32)
    sel[64, :] = 1.0
    c["sel"] = sel
    c["ones64"] = np.ones((64, 64), np.float32)
    return c


CONST_SHAPES = None


def host_params(cfg, inp, b_heads=None):
    HD, HL, G, HS = cfg.HD, cfg.HL, cfg.G, cfg.HS
    w = np.asarray(inp["w_in"])
    L = w.shape[0]
    o = 0
    segs = {}
    for nm, n in (("dq", 256), ("dk", 256), ("dv", 256), ("lq", 384), ("lk", 384), ("lv", 384),
                  ("z", 384), ("xs", 384), ("B", 256), ("C", 256), ("dt", 6)):
        segs[nm] = (o, o + n)
        o += n
    cols = []
    for nm in ("dq", "dk", "lq", "lk", "lv", "xs", "B", "C", "dv", "z", "dt"):
        a, b = segs[nm]
        cols.append(np.arange(a, b))
    cols = np.concatenate(cols)
    p = {}
    p["w_in"] = np.ascontiguousarray(w[:, :, cols])
    p["g_premix"] = np.ascontiguousarray(np.asarray(inp["pre_mix_norm"]).reshape(L, 8, 128).transpose(0, 2, 1))
    p["g_preffn"] = np.ascontiguousarray(np.asarray(inp["pre_ffn_norm"]).reshape(L, 8, 128).transpose(0, 2, 1))
    p["g_postmix"] = np.ascontiguousarray(np.asarray(inp["post_mix_norm"]))
    p["g_postffn"] = np.ascontiguousarray(np.asarray(inp["post_ffn_norm"]))
    xo = segs["xs"][0]
    cw = np.asarray(inp["ssd_conv_w"])
    cb = np.asarray(inp["ssd_conv_b"])
    p["cw"] = np.ascontiguousarray(cw.reshape(L, 4, 7, 128).transpose(0, 3, 2, 1))
    p["cb"] = np.ascontiguousarray(cb.reshape(L, 7, 128).transpose(0, 2, 1))
    p["dt_bias"] = np.ascontiguousarray(np.asarray(inp["ssd_dt_bias"]))
    p["A_log"] = np.ascontiguousarray(np.asarray(inp["ssd_A_log"]))
    p["ssd_D"] = np.ascontiguousarray(np.asarray(inp["ssd_D"]))
    p["ssd_norm"] = np.ascontiguousarray(np.asarray(inp["ssd_norm"]))
    p["lam"] = np.ascontiguousarray(np.asarray(inp["diff_lambda"]).reshape(L, 128))
    p["hgain"] = np.ascontiguousarray(np.asarray(inp["diff_head_norm"]).reshape(L, 64, 1))
    p["w_out"] = np.ascontiguousarray(np.asarray(inp["w_out"]))
    p["ffn_up"] = np.ascontiguousarray(np.asarray(inp["ffn_up"]))
    fw = np.asarray(inp["ffn_conv_w"])
    fb = np.asarray(inp["ffn_conv_b"])
    p["fcw"] = np.ascontiguousarray(fw.reshape(L, 3, 44, 128).transpose(0, 3, 2, 1))
    p["fcb"] = np.ascontiguousarray(fb.reshape(L, 44, 128).transpose(0, 2, 1))
    p["ffn_down"] = np.ascontiguousarray(np.asarray(inp["ffn_down"]))
    return p


class K:
    pass


def build(cfg, debug=False):
    nc = bass.Bass("TRN2", target_bir_lowering=False)
    S, L, HD, HL, G, HS = cfg.S, cfg.L, cfg.HD, cfg.HL, cfg.G, cfg.HS
    NB = cfg.NB
    es = ExitStack()
    sc = Sched(nc, es)

    def din(name, shape, dt=F32):
        return nc.dram_tensor(name, list(shape), dt, kind="ExternalInput").ap()

    def dscr(name, shape, dt):
        return nc.dram_tensor(name, list(shape), dt, kind=("ExternalOutput" if debug else "Internal")).ap()

    x_in = din("x", [S, D])
    W = {}
    W["w_in"] = din("w_in", [L, D, cfg.NCOL])
    W["g_premix"] = din("g_premix", [L, 128, 8])
    W["g_preffn"] = din("g_preffn", [L, 128, 8])
    W["g_postmix"] = din("g_postmix", [L, D])
    W["g_postffn"] = din("g_postffn", [L, D])
    W["cw"] = din("cw", [L, 128, 7, 4])
    W["cb"] = din("cb", [L, 128, 7])
    W["dt_bias"] = din("dt_bias", [L, 6])
    W["A_log"] = din("A_log", [L, 6])
    W["ssd_D"] = din("ssd_D", [L, 6])
    W["ssd_norm"] = din("ssd_norm", [L, 384])
    W["lam"] = din("lam", [L, 128])
    W["hgain"] = din("hgain", [L, 64, 1])
    W["w_out"] = din("w_out", [L, D, D])
    W["ffn_up"] = din("ffn_up", [L, D, 2 * cfg.DFF])
    W["fcw"] = din("fcw", [L, 128, 44, 3])
    W["fcb"] = din("fcb", [L, 128, 44])
    W["ffn_down"] = din("ffn_down", [L, cfg.DFF, D])
    C = {}
    C["cosd"] = din("cosd", [64, S], BF16); C["sind"] = din("sind", [64, S], BF16)
    C["cosl"] = din("cosl", [128, S], BF16); C["sinl"] = din("sinl", [128, S], BF16)
    C["permd"] = din("permd", [64, 64], BF16); C["perml"] = din("perml", [128, 128], BF16)
    C["ident_bf"] = din("ident_bf", [128, 128], BF16); C["ident_f"] = din("ident_f", [128, 128])
    C["dmask"] = din("dmask", [4, 128, 512], BF16); C["lmask"] = din("lmask", [128, 256], BF16)
    C["triu"] = din("triu", [128, 128]); C["negm"] = din("negm", [128, 128])
    C["sel"] = din("sel", [65, 64]); C["ones64"] = din("ones64", [64, 64])
    out = nc.dram_tensor("out", [S, D], F32, kind="ExternalOutput").ap()

    qTd = dscr("qTd", [HD, 64, S], BF16); kTd = dscr("kTd", [HD, 64, S], BF16)
    vd = dscr("vd", [S, HD * 65], BF16)
    qTl = dscr("qTl", [HL // 2, 128, S], BF16); kTl = dscr("kTl", [HL // 2, 128, S], BF16)
    vTl = dscr("vTl", [HL // 2, 128, S], BF16)
    xbcT = dscr("xbcT", [cfg.NXBC, 128, S], BF16)
    zs = dscr("zs", [S, HS * 64], BF16)
    dts = dscr("dts", [S, HS], F32)
    mixT = dscr("mixT", [cfg.MIXC, 128, S], BF16)
    xa = dscr("xa", [S, D], F32)
    xb = dscr("xb", [S, D], F32)

    def sb(name, shape, dt=F32):
        return es.enter_context(nc.sbuf_tensor(name, list(shape), dt))

    ident_bf = sb("sb_ident_bf", [128, 128], BF16)
    ident_f = sb("sb_ident_f", [128, 128])
    cB = Buf("consts")
    cds = sc.dsem()
    sc.dma(cds, [(ident_bf[:], C["ident_bf"]), (ident_f[:], C["ident_f"])], writes=[cB])

    k = K()
    k.nc, k.sc, k.cfg, k.W, k.C, k.cB = nc, sc, cfg, W, C, cB
    k.ident_bf, k.ident_f = ident_bf, ident_f
    k.scr = dict(qTd=qTd, kTd=kTd, vd=vd, qTl=qTl, kTl=kTl, vTl=vTl, xbcT=xbcT, zs=zs, dts=dts,
                 mixT=mixT, xa=xa, xb=xb)

    stages = cfg.stages if hasattr(cfg, "stages") else "12345"
    for l in range(L):
        x_src = x_in if l == 0 else xb
        x_dst = out if l == L - 1 else xb
        if "1" in stages:
            phase1(k, l, x_src)
            sc.barrier()
        if "2" in stages:
            phase2(k, l)
            sc.barrier()
        if "3" in stages:
            phase3(k, l)
            sc.barrier()
        if "4" in stages:
            phase4(k, l)
            sc.barrier()
        if "5" in stages:
            phase5a(k, l, x_src)
            sc.barrier()
            phase5b(k, l, x_dst)
            sc.barrier()
    sc.barrier()
    es.close()
    return nc


def load_weights_bf16(k, ps, dst, src_rows, ncols, gain, stg, stgB, ds, kchunks, col_chunk=512):
    sc = k.sc
    i = 0
    for kc in range(kchunks):
        for c0 in range(0, ncols, col_chunk):
            cw = min(col_chunk, ncols - c0)
            sl = i % len(stg)
            sc.dma(ds[sl], [(stg[sl][:, :cw], src_rows[kc * 128:(kc + 1) * 128, c0:c0 + cw])], writes=[stgB[sl]])
            eng = "dve" if i % 2 == 0 else "pool"
            if gain is not None:
                if i % 2 == 0:
                    sc.op("dve", lambda h, sl=sl, cw=cw, kc=kc, c0=c0: h.tensor_scalar(
                        out=dst[:, kc, c0:c0 + cw], in0=stg[sl][:, :cw], scalar1=gain[:, kc:kc + 1], scalar2=None,
                        op0=ALU.mult), reads=[stgB[sl], k.gB], writes=[k.wB])
                else:
                    sc.op("act", lambda h, sl=sl, cw=cw, kc=kc, c0=c0: h.activation(
                        out=dst[:, kc, c0:c0 + cw], in_=stg[sl][:, :cw], func=AF.Copy, scale=gain[:, kc:kc + 1]),
                        reads=[stgB[sl], k.gB], writes=[k.wB])
            else:
                sc.op(eng, lambda h, sl=sl, cw=cw, kc=kc, c0=c0: h.tensor_copy(
                    out=dst[:, kc, c0:c0 + cw], in_=stg[sl][:, :cw]), reads=[stgB[sl]], writes=[k.wB])
            i += 1


def rms_rstd(k, ps, ss, rstd, n, width, ssB, rsB):
    sc = k.sc
    sc.op("act", lambda h: h.activation(out=rstd[:, :n], in_=ss[:, :n], func=AF.Ln, scale=1.0 / width, bias=EPS),
          reads=[ssB], writes=[rsB])
    sc.op("act", lambda h: h.activation(out=rstd[:, :n], in_=rstd[:, :n], func=AF.Exp, scale=-0.5), reads=[rsB], writes=[rsB])


def phase1(k, l, x_src):
    nc, sc, cfg, W, C = k.nc, k.sc, k.cfg, k.W, k.C
    S, HD, HL, G, HS, NB = cfg.S, cfg.HD, cfg.HL, cfg.G, cfg.HS, cfg.NB
    NX = cfg.NXBC
    with ExitStack() as ps:
        def sb(name, shape, dt=F32):
            return ps.enter_context(nc.sbuf_tensor("L%d_%s" % (l, name), list(shape), dt))

        def pt(name, shape, dt=F32):
            return ps.enter_context(nc.psum_tensor("L%d_%s" % (l, name), list(shape), dt))

        wbf = sb("p1_w", [128, 8, cfg.NCOL], BF16)
        gain = sb("p1_g", [128, 8])
        stg = [sb("p1_stg%d" % i, [128, 512]) for i in range(3)]
        stgB = [Buf() for _ in range(3)]
        ds = [sc.dsem() for _ in range(3)]
        k.gB, k.wB = Buf("gain"), Buf("w")
        dsm = sc.dsem()
        permd = sb("p1_permd", [64, 64], BF16)
        perml = sb("p1_perml", [128, 128], BF16)
        cw = sb("p1_cw", [128, 7, 4])
        cb = sb("p1_cb", [128, 7])
        sc.dma(dsm, [(gain[:], W["g_premix"][l]), (permd[:], C["permd"]), (perml[:], C["perml"]),
                     (cw[:], W["cw"][l]), (cb[:], W["cb"][l])], writes=[k.gB])
        CUT = int(os.environ.get("KCUT", "99"))
        if CUT >= 1:
            load_weights_bf16(k, ps, wbf, W["w_in"][l], cfg.NCOL, gain, stg, stgB, ds, 8)

        xt = sb("p1_x", [128, 4, D])
        xtB = [Buf() for _ in range(4)]
        xds = [sc.dsem() for _ in range(4)]
        junk = sb("p1_junk", [128, D], BF16)
        junkB = Buf()
        ss = sb("p1_ss", [128, 4]); ssB = Buf()
        rstd = sb("p1_rstd", [128, 4]); rsB = Buf()
        xs = sb("p1_xs", [128, 4, D], BF16); xsB = Buf()
        hT = sb("p1_hT", [128, 8, TB], BF16); hTB = Buf()
        psT = [pt("p1_psT%d" % i, [128, 2, TB], BF16) for i in range(2)]
        psTB = [PBuf() for _ in range(2)]
        psA = [pt("p1_psA%d" % i, [128, TB]) for i in range(3)]
        psAB = [PBuf() for _ in range(3)]
        psR = [pt("p1_psR%d" % i, [128, TB]) for i in range(2)]
        psRB = [PBuf() for _ in range(2)]
        psB = [pt("p1_psB%d" % i, [128, 512]) for i in range(1)]
        psBB = [PBuf() for _ in range(1)]
        tabs = {}
        for nm, rows in (("cosd", 64), ("sind", 64), ("cosl", 128), ("sinl", 128)):
            tabs[nm] = [sb("p1_%s%d" % (nm, i), [rows, TB], BF16) for i in range(2)]
        tabB = [Buf() for _ in range(2)]
        tds = [sc.dsem() for _ in range(2)]
        xbf = [sb("p1_xbf%d" % i, [128, TB], BF16) for i in range(4)]
        xbfB = [Buf() for _ in range(4)]
        t1 = [sb("p1_t1%d" % i, [128, TB]) for i in range(2)]
        t1B = [Buf() for _ in range(2)]
        t2 = [sb("p1_t2%d" % i, [128, TB]) for i in range(2)]
        t2B = [Buf() for _ in range(2)]
        qk_st = [sb("p1_qk%d" % i, [64, 2 * HD, TB], BF16) for i in range(2)]
        l_st = [sb("p1_l%d" % i, [128, 3 * (HL // 2), TB], BF16) for i in range(2)]
        xbc_st = [sb("p1_xbc%d" % i, [128, NX, TB], BF16) for i in range(2)]
        v_st = [sb("p1_v%d" % i, [128, 4, HD, 65], BF16) for i in range(2)]
        z_st = [sb("p1_z%d" % i, [128, 4, HS * 64], BF16) for i in range(2)]
        dt_st = [sb("p1_dt%d" % i, [128, 4, HS]) for i in range(2)]
        from collections import defaultdict
        stD = [defaultdict(Buf) for _ in range(2)]
        sds = [sc.dsem() for _ in range(2)]
        u = sb("p1_u", [128, NX, 3 + TB]); uB = [Buf() for _ in range(NX)]
        yc = [sb("p1_yc%d" % i, [128, TB]) for i in range(3)]
        ycB = [Buf() for _ in range(3)]
        sc.op("pool", lambda h: h.memset(u[:], 0.0), writes=uB)
        for i in range(2):
            sc.op("pool", lambda h, i=i: h.memset(v_st[i][:], 1.0), writes=[stD[i][("v", n)] for n in range(4)])

        ci = 0
        for t in range(NB):
            if CUT < 2:
                break
            pb = t % 2
            tok0 = t * TB

            def prefetch(tt):
                tk = tt * TB
                for n in range(4):
                    sc.dma(xds[n], [(xt[:, n, :], x_src[tk + n * 128:tk + (n + 1) * 128, :])], writes=[xtB[n]])
                sc.dma(tds[tt % 2], [(tabs[nm][tt % 2][:], C[nm][:, tk:tk + TB]) for nm in ("cosd", "sind", "cosl", "sinl")],
                       writes=[tabB[tt % 2]])

            if t == 0:
                prefetch(0)
            for n in range(4):
                sc.op("act", lambda h, n=n: h.activation(out=junk[:], in_=xt[:, n, :], func=AF.Square,
                                                         accum_out=ss[:, n:n + 1]),
                      reads=[xtB[n]], writes=[junkB, ssB])
            if CUT < 3:
                continue
            rms_rstd(k, ps, ss, rstd, 4, D, ssB, rsB)
            for n in range(4):
                if n % 2 == 0:
                    sc.op("dve", lambda h, n=n: h.tensor_scalar(
                        out=xs[:, n, :], in0=xt[:, n, :], scalar1=rstd[:, n:n + 1], scalar2=None, op0=ALU.mult),
                        reads=[xtB[n], rsB], writes=[xsB])
                else:
                    sc.op("act", lambda h, n=n: h.activation(out=xs[:, n, :], in_=xt[:, n, :], func=AF.Copy, scale=rstd[:, n:n + 1]),
                          reads=[xtB[n], rsB], writes=[xsB])
            if CUT < 4:
                continue
            for cp in range(4):
                pi = cp % 2
                for c2 in range(2):
                    c = cp * 2 + c2
                    for n in range(4):
                        last = (c2 == 1 and n == 3)
                        sc.op("pe", lambda h, c=c, c2=c2, n=n, pi=pi: h.transpose(
                            out=psT[pi][:, c2, n * 128:(n + 1) * 128], in_=xs[:, n, c * 128:(c + 1) * 128],
                            identity=k.ident_bf[:]), reads=[xsB, k.cB], writes=[psTB[pi]], inc=last)
                if cp % 2 == 0:
                    sc.op("act", lambda h, cp=cp, pi=pi: h.copy(out=hT[:, 2 * cp:2 * cp + 2, :], in_=psT[pi][:]),
                          reads=[psTB[pi]], writes=[hTB])
                else:
                    sc.op("dve", lambda h, cp=cp, pi=pi: h.tensor_copy(out=hT[:, 2 * cp:2 * cp + 2, :], in_=psT[pi][:]),
                          reads=[psTB[pi]], writes=[hTB])
            if CUT < 5:
                continue
            if t + 1 < NB:
                prefetch(t + 1)
            if t > 0:
                sc.op("pool", lambda h: h.tensor_copy(out=u[:, :, 0:3], in_=u[:, :, TB:TB + 3]), reads=uB, writes=uB)
            fm = cfg.fm

            def s_pe(i):
                nm, idx, off, M = fm[i]
                a = i % 3
                for kc in range(8):
                    sc.op("pe", lambda h: h.matmul(psA[a][0:M, :], lhsT=wbf[:, kc, off:off + M], rhs=hT[:, kc, :],
                                                   start=(kc == 0), stop=(kc == 7)), reads=[k.wB, hTB], writes=[psAB[a]], inc=(kc == 7))

            def s_copy(i):
                nm, idx, off, M = fm[i]
                a = i % 3
                if nm in ("dq", "dk", "lq", "lk"):
                    x4 = i % 4
                    sc.op("act", lambda h: h.copy(out=xbf[x4][0:M, :], in_=psA[a][0:M, :]), reads=[psAB[a]], writes=[xbfB[x4]])
                elif nm == "lv":
                    sc.op("act", lambda h: h.copy(out=l_st[pb][:, 2 * (HL // 2) + idx, :], in_=psA[a][:]),
                          reads=[psAB[a]], writes=[stD[pb][("lv", idx)]])
                else:
                    xi = {"xs": 0, "B": HS // 2, "C": HS // 2 + G}[nm] + idx
                    sc.op("act", lambda h: h.copy(out=u[:, xi, 3:3 + TB], in_=psA[a][:]), reads=[psAB[a]], writes=[uB[xi]])

            def s_mid(i):
                nm, idx, off, M = fm[i]
                if nm in ("dq", "dk", "lq", "lk"):
                    perm = permd if M == 64 else perml
                    x4 = i % 4
                    r2 = i % 2
                    sc.op("pe", lambda h: h.matmul(psR[r2][0:M, :], lhsT=perm[0:M, 0:M], rhs=xbf[x4][0:M, :], start=True, stop=True),
                          reads=[xbfB[x4], k.gB], writes=[psRB[r2]])
                elif nm != "lv":
                    xi = {"xs": 0, "B": HS // 2, "C": HS // 2 + G}[nm] + idx
                    gi = {"xs": 0, "B": 3, "C": 5}[nm] + idx
                    y3 = i % 3
                    sc.op("dve", lambda h: h.tensor_scalar(out=yc[y3][:], in0=u[:, xi, 3:3 + TB], scalar1=cw[:, gi, 3:4], scalar2=cb[:, gi:gi + 1],
                                                           op0=ALU.mult, op1=ALU.add), reads=[uB[xi], k.gB], writes=[ycB[y3]])

            def s_ew(i):
                nm, idx, off, M = fm[i]
                if nm in ("dq", "dk", "lq", "lk"):
                    cosn, sinn = ("cosd", "sind") if M == 64 else ("cosl", "sinl")
                    x4 = i % 4
                    r2 = i % 2
                    sc.op("dve", lambda h: h.tensor_tensor(out=t1[r2][0:M, :], in0=xbf[x4][0:M, :], in1=tabs[cosn][pb][0:M, :], op=ALU.mult),
                          reads=[xbfB[x4], tabB[pb]], writes=[t1B[r2]])
                    sc.op("dve", lambda h: h.tensor_tensor(out=t2[r2][0:M, :], in0=psR[r2][0:M, :], in1=tabs[sinn][pb][0:M, :], op=ALU.mult),
                          reads=[psRB[r2], tabB[pb]], writes=[t2B[r2]])
                elif nm != "lv":
                    xi = {"xs": 0, "B": HS // 2, "C": HS // 2 + G}[nm] + idx
                    gi = {"xs": 0, "B": 3, "C": 5}[nm] + idx
                    y3 = i % 3
                    for kk in range(3):
                        sc.op("dve", lambda h: h.scalar_tensor_tensor(out=yc[y3][:], in0=u[:, xi, kk:kk + TB], scalar=cw[:, gi, kk:kk + 1],
                                                                      in1=yc[y3][:], op0=ALU.mult, op1=ALU.add),
                              reads=[uB[xi], ycB[y3], k.gB], writes=[ycB[y3]])

            def s_fin(i):
                nm, idx, off, M = fm[i]
                if nm in ("dq", "dk", "lq", "lk"):
                    r2 = i % 2
                    if nm in ("dq", "dk"):
                        dst = qk_st[pb][:, (0 if nm == "dq" else HD) + idx, :]
                    else:
                        dst = l_st[pb][:, (0 if nm == "lq" else HL // 2) + idx, :]
                    sc.op("pool", lambda h: h.tensor_tensor(out=dst, in0=t1[r2][0:M, :], in1=t2[r2][0:M, :], op=ALU.add),
                          reads=[t1B[r2], t2B[r2]], writes=[stD[pb][(nm, idx)]])
                elif nm != "lv":
                    xi = {"xs": 0, "B": HS // 2, "C": HS // 2 + G}[nm] + idx
                    y3 = i % 3
                    sc.op("act", lambda h: h.activation(out=xbc_st[pb][:, xi, :], in_=yc[y3][:], func=AF.Silu),
                          reads=[ycB[y3]], writes=[stD[pb][("xbc", xi)]])

            skewed(len(fm), [s_pe, s_copy, s_mid, s_ew, s_fin])
            if CUT < 6:
                continue
            tmb = [psB[0], psR[0], psR[1]]
            tmB = [psBB[0], psRB[0], psRB[1]]
            for n in range(4):
                a = (2 * n) % 3
                for kc in range(8):
                    sc.op("pe", lambda h, kc=kc, n=n, a=a: h.matmul(
                        tmb[a][:, 0:HD * 64], lhsT=hT[:, kc, n * 128:(n + 1) * 128],
                        rhs=wbf[:, kc, cfg.off_dv:cfg.off_dv + HD * 64], start=(kc == 0), stop=(kc == 7)),
                        reads=[k.wB, hTB], writes=[tmB[a]], inc=(kc == 7))
                sc.op("act", lambda h, n=n, a=a: h.copy(
                    out=v_st[pb][:, n, :, 0:64], in_=tmb[a][:, 0:HD * 64].rearrange("p (h e) -> p h e", e=64)),
                    reads=[tmB[a]], writes=[stD[pb][("v", n)]])
                nz = HS * 64 + HS
                a = (2 * n + 1) % 3
                for kc in range(8):
                    sc.op("pe", lambda h, kc=kc, n=n, a=a: h.matmul(
                        tmb[a][:, 0:nz], lhsT=hT[:, kc, n * 128:(n + 1) * 128],
                        rhs=wbf[:, kc, cfg.off_z:cfg.off_z + nz], start=(kc == 0), stop=(kc == 7)),
                        reads=[k.wB, hTB], writes=[tmB[a]], inc=(kc == 7))
                sc.op("dve", lambda h, n=n, a=a: h.tensor_copy(out=z_st[pb][:, n, :], in_=tmb[a][:, 0:HS * 64]),
                      reads=[tmB[a]], writes=[stD[pb][("z", n)]])
                sc.op("dve", lambda h, n=n, a=a: h.tensor_copy(out=dt_st[pb][:, n, :], in_=tmb[a][:, HS * 64:nz]),
                      reads=[tmB[a]], writes=[stD[pb][("dt", n)]])
            if CUT < 7:
                continue
            scr = k.scr
            pairs = [
                (scr["qTd"][:, :, tok0:tok0 + TB].rearrange("h p s -> p h s"), qk_st[pb][:, 0:HD, :]),
                (scr["kTd"][:, :, tok0:tok0 + TB].rearrange("h p s -> p h s"), qk_st[pb][:, HD:2 * HD, :]),
                (scr["qTl"][:, :, tok0:tok0 + TB].rearrange("h p s -> p h s"), l_st[pb][:, 0:HL // 2, :]),
                (scr["kTl"][:, :, tok0:tok0 + TB].rearrange("h p s -> p h s"), l_st[pb][:, HL // 2:HL, :]),
                (scr["vTl"][:, :, tok0:tok0 + TB].rearrange("h p s -> p h s"), l_st[pb][:, HL:3 * (HL // 2), :]),
                (scr["xbcT"][:, :, tok0:tok0 + TB].rearrange("h p s -> p h s"), xbc_st[pb][:]),
                (scr["vd"][tok0:tok0 + TB, :].rearrange("(n p) f -> p n f", p=128),
                 v_st[pb][:].rearrange("p n h e -> p n (h e)")),
                (scr["zs"][tok0:tok0 + TB, :].rearrange("(n p) f -> p n f", p=128), z_st[pb][:]),
                (scr["dts"][tok0:tok0 + TB, :].rearrange("(n p) f -> p n f", p=128), dt_st[pb][:]),
            ]
            sc.dma(sds[pb], pairs, reads=list(stD[pb].values()))


def skewed(n, stages):
    ns = len(stages)
    for kk in range(n + ns - 1):
        for si, fn in enumerate(stages):
            i = kk - si
            if 0 <= i < n:
                fn(i)


def _mk(k, l, ps):
    nc = k.nc

    def sb(name, shape, dt=F32):
        return ps.enter_context(nc.sbuf_tensor("L%d_%s" % (l, name), list(shape), dt))

    def pt(name, shape, dt=F32):
        return ps.enter_context(nc.psum_tensor("L%d_%s" % (l, name), list(shape), dt))

    return sb, pt


def phase2(k, l):
    nc, sc, cfg, W, C = k.nc, k.sc, k.cfg, k.W, k.C
    S, HD, NB = cfg.S, cfg.HD, cfg.NB
    lam_init = 0.8 - 0.6 * math.exp(-0.3 * l)
    scr = k.scr
    with ExitStack() as ps:
        sb, pt = _mk(k, l, ps)
        kT = sb("p2_kT", [128, HD // 2, S], BF16)
        vfull = sb("p2_v", [128, (S // 128) * HD * 65 + 128], BF16)
        v = vfull[:, 0:(S // 128) * HD * 65].rearrange("p (n f) -> p n f", f=HD * 65)
        dmask = sb("p2_dmask", [128, 4, 512], BF16)
        sel = sb("p2_sel", [65, 64]); ones64 = sb("p2_ones", [64, 64])
        hg = sb("p2_hg", [64, 1]); lam = sb("p2_lam", [64, 128])
        lt = sb("p2_lt", [64, 64]); lsum = sb("p2_ls", [64, 2]); neg_lam = sb("p2_nl", [64, 1]); gsc = sb("p2_gsc", [64, 1])
        cB = Buf(); d0 = sc.dsem()
        sc.dma(d0, [(kT[:, a2, :], scr["kTd"][2 * a2:2 * a2 + 2].rearrange("b p s -> (b p) s")) for a2 in range(HD // 2)] +
               [(v, scr["vd"].rearrange("(n p) f -> p n f", p=128)),
                (dmask[:], C["dmask"].rearrange("d p q -> p d q")), (sel[:], C["sel"]), (ones64[:], C["ones64"]),
                (hg[:], W["hgain"][l]), (lam[:], W["lam"][l:l + 1, :].broadcast_to([64, 128]))], writes=[cB])
        pB = Buf()
        sc.op("pool", lambda h: h.memset(vfull[:, (S // 128) * HD * 65:], 0.0), writes=[cB])
        sc.op("dve", lambda h: h.tensor_tensor(out=lt[:, 0:32], in0=lam[:, 0:32], in1=lam[:, 32:64], op=ALU.mult), reads=[cB], writes=[pB])
        sc.op("dve", lambda h: h.tensor_tensor(out=lt[:, 32:64], in0=lam[:, 64:96], in1=lam[:, 96:128], op=ALU.mult), reads=[cB], writes=[pB])
        sc.op("dve", lambda h: h.tensor_reduce(out=lsum[:], in_=lt[:].rearrange("p (a b) -> p a b", b=32), axis=AX.X, op=ALU.add),
              reads=[pB], writes=[pB])
        sc.op("act", lambda h: h.activation(out=lsum[:], in_=lsum[:], func=AF.Exp), reads=[pB], writes=[pB])
        sc.op("dve", lambda h: h.tensor_tensor(out=neg_lam[:], in0=lsum[:, 1:2], in1=lsum[:, 0:1], op=ALU.subtract), reads=[pB], writes=[pB])
        sc.op("dve", lambda h: h.tensor_scalar(out=neg_lam[:], in0=neg_lam[:], scalar1=-lam_init, scalar2=None, op0=ALU.add), reads=[pB], writes=[pB])
        sc.op("dve", lambda h: h.tensor_scalar(out=gsc[:], in0=hg[:], scalar1=(1.0 - lam_init), scalar2=None, op0=ALU.mult), reads=[cB, pB], writes=[pB])

        qT = [sb("p2_q%d" % i, [128, HD * 2, TB], BF16) for i in range(2)]
        qB = [Buf() for _ in range(2)]; qds = [sc.dsem() for _ in range(2)]
        for i in range(2):
            sc.op("pool", lambda h, i=i: h.memset(qT[i][:], 0.0), writes=[qB[i]])
        psS = [pt("p2_psS%d" % i, [128, 2, 512]) for i in range(2)]; psSB = [PBuf() for _ in range(2)]
        psO = [pt("p2_psO%d" % m, [128, 512]) for m in range(2)]
        psOB = [PBuf() for m in range(2)]
        psE = [pt("p2_psE%d" % i, [128, 512]) for i in range(2)]; psEB = [PBuf() for _ in range(2)]
        pT = [sb("p2_pT%d" % i, [128, 2, 512], BF16) for i in range(3)]; pTB = [Buf() for _ in range(3)]
        X = [sb("p2_X%d" % m, [65, 512]) for m in range(2)]; XB = [Buf() for _ in range(2)]
        r = [sb("p2_r%d" % m, [64, 512]) for m in range(2)]; rB = [Buf() for _ in range(2)]
        o = sb("p2_o", [64, 512]); oB = Buf()
        sq = sb("p2_sq", [64, 512]); sqB = Buf()
        rs = sb("p2_rs", [64, 512]); rsB = Buf()
        ost = [sb("p2_ost%d" % i, [64, HD, 512], BF16) for i in range(2)]
        ostB = [Buf() for _ in range(2)]; ods = [sc.dsem() for _ in range(2)]
        scale = 32 ** -0.5
        cnt = 0
        pending = []

        def flush(n=None):
            c = len(pending) if n is None else min(n, len(pending))
            for _ in range(c):
                pending.pop(0)()

        for t in range(NB):
            tok0 = t * TB
            qb = t % 2
            sc.dma(qds[qb], [(qT[qb][32 * ((h2 % 2) * 2 + m2):32 * ((h2 % 2) * 2 + m2) + 32, h2 * 2 + m2, :],
                              scr["qTd"][h2, 32 * m2:32 * m2 + 32, tok0:tok0 + TB]) for h2 in range(HD) for m2 in range(2)],
                   writes=[qB[qb]])
            for hh in range(HD):
                nk = 4 * t + 4

                def qk(i):
                    sl = (cnt + i) % 2
                    for m in range(2):
                        sc.op("pe", lambda h: h.matmul(psS[sl][:, m, :], lhsT=kT[:, hh // 2, i * 128:(i + 1) * 128],
                                                       rhs=qT[qb][:, hh * 2 + m, :], start=True, stop=True),
                              reads=[cB, qB[qb]], writes=[psSB[sl]], inc=(m == 1))

                qk(0)
                for i in range(nk):
                    sl = (cnt + i) % 2
                    p3 = (cnt + i) % 3
                    if i + 1 < nk:
                        qk(i + 1)
                    sc.op("act", lambda h: h.activation(out=pT[p3][:], in_=psS[sl][:], func=AF.Exp, scale=scale),
                          reads=[psSB[sl]], writes=[pTB[p3]])
                    if i >= 4 * t:
                        di = i - 4 * t
                        sc.op("dve" if i % 2 == 0 else "pool", lambda h: h.tensor_tensor(
                            out=pT[p3][:], in0=pT[p3][:], in1=dmask[:, di:di + 1, :].broadcast_to([128, 2, 512]), op=ALU.mult),
                            reads=[cB, pTB[p3]], writes=[pTB[p3]])
                    for m in range(2):
                        sc.op("pe", lambda h: h.matmul(psO[m][:, :], lhsT=vfull[:, (i * HD + hh) * 65:(i * HD + hh) * 65 + 128], rhs=pT[p3][:, m, :],
                                                       start=(i == 0), stop=(i == nk - 1)),
                              reads=[cB, pTB[p3]], writes=[psOB[m]], inc=(i == nk - 1))
                    if i == 0:
                        npend0 = len(pending)
                    if i >= 1:
                        if i < nk - 1:
                            target_left = npend0 - (npend0 * i) // (nk - 1)
                            flush(max(0, len(pending) - target_left))
                        else:
                            flush()
                cnt += nk
                flush()
                sc.op("dve", lambda h: h.tensor_copy(out=X[0][:], in_=psO[0][0:65, :]), reads=[psOB[0]], writes=[XB[0]])
                sc.op("dve", lambda h: h.tensor_copy(out=X[1][:], in_=psO[1][0:65, :]), reads=[psOB[1]], writes=[XB[1]])

                def E(eng, fn, reads, writes):
                    pending.append(lambda: sc.op(eng, fn, reads=reads, writes=writes))

                for m in range(2):
                    E("pe", lambda h, m=m: h.matmul(psE[m][0:64, :], lhsT=sel[:], rhs=X[m][:], start=True, stop=True),
                      [cB, XB[m]], [psEB[m]])
                    E("dve", lambda h, m=m: h.reciprocal(out=r[m][:], in_=psE[m][0:64, :]), [psEB[m]], [rB[m]])
                    E("dve" if m == 0 else "pool", lambda h, m=m: h.tensor_tensor(
                        out=r[m][:], in0=X[m][0:64, :], in1=r[m][:], op=ALU.mult), [XB[m], rB[m]], [rB[m]])
                E("dve", lambda h: h.scalar_tensor_tensor(out=o[:], in0=r[1][:], scalar=neg_lam[:, 0:1], in1=r[0][:],
                                                          op0=ALU.mult, op1=ALU.add), [rB[0], rB[1], pB], [oB])
                E("pool", lambda h: h.tensor_tensor(out=sq[:], in0=o[:], in1=o[:], op=ALU.mult), [oB], [sqB])
                E("pe", lambda h: h.matmul(psE[0][0:64, :], lhsT=ones64[:], rhs=sq[:], start=True, stop=True), [cB, sqB], [psEB[0]])
                E("act", lambda h: h.activation(out=rs[:], in_=psE[0][0:64, :], func=AF.Ln, scale=1.0 / 64, bias=EPS), [psEB[0]], [rsB])
                E("act", lambda h: h.activation(out=rs[:], in_=rs[:], func=AF.Exp, scale=-0.5), [rsB], [rsB])
                E("dve", lambda h, qb=qb, hh=hh: h.scalar_tensor_tensor(out=ost[qb][:, hh, :], in0=o[:], scalar=gsc[:, 0:1], in1=rs[:],
                                                                       op0=ALU.mult, op1=ALU.mult), [oB, rsB, pB], [ostB[qb]])
            pending.append(lambda qb=qb, tok0=tok0: sc.dma(
                ods[qb], [(scr["mixT"][hh2 // 2, (hh2 % 2) * 64:(hh2 % 2) * 64 + 64, tok0:tok0 + TB], ost[qb][:, hh2, :])
                          for hh2 in range(HD)], reads=[ostB[qb]]))
        flush()


def phase3(k, l):
    nc, sc, cfg, W, C = k.nc, k.sc, k.cfg, k.W, k.C
    S, HD, HL = cfg.S, cfg.HD, cfg.HL
    scr = k.scr
    SBK = 2048
    NSB = S // SBK
    NC3 = HL // 2
    PATS = (1, 4, 16)
    scale = 64 ** -0.5
    with ExitStack() as ps:
        sb, pt = _mk(k, l, ps)
        qP = [sb("p3_q%d" % i, [128, NC3, SBK], BF16) for i in range(2)]; qB = Buf()
        for i in range(2):
            sc.op("pool", lambda h, i=i: h.memset(qP[i][:], 0.0), writes=[qB])
        kL = [sb("p3_k%d" % i, [128, NC3, SBK], BF16) for i in range(2)]; kB = [Buf() for _ in range(2)]
        vT = sb("p3_vT", [128, NC3, SBK], BF16); vTB = Buf()
        lds = [sc.dsem() for _ in range(3)]
        vt = [[sb("p3_vt%d_%d" % (pi, i), [128, 16, HL * 65], BF16) for i in range(2)] for pi in range(3)]
        vtB = [[Buf() for i in range(2)] for pi in range(3)]
        lmask = sb("p3_lmask", [128, 256], BF16); sel = sb("p3_sel", [65, 64])
        cB = Buf(); d0 = sc.dsem()
        sc.dma(d0, [(lmask[:], C["lmask"]), (sel[:], C["sel"])], writes=[cB])
        for pi in range(3):
            for i in range(2):
                sc.op("pool", lambda h: h.memset(vt[pi][i][:], 1.0), writes=[vtB[pi][i]])
        acc = [sb("p3_acc%d" % i, [65, SBK]) for i in range(2)]; accB = [Buf() for _ in range(2)]
        pT = [sb("p3_pT%d" % i, [128, 4, 256], BF16) for i in range(3)]; pTB = [Buf() for _ in range(3)]
        rr = sb("p3_rr", [64, 512]); rrB = Buf()
        ost = [sb("p3_ost%d" % i, [64, SBK], BF16) for i in range(2)]; ostB = [Buf() for _ in range(2)]
        ods = [sc.dsem() for _ in range(2)]
        psV = pt("p3_psV", [128, NC3, 128], BF16); psVB = PBuf()
        psS = [pt("p3_psS%d" % i, [128, 4, 256]) for i in range(2)]; psSB = [PBuf() for _ in range(2)]
        psO = [pt("p3_psO%d" % i, [128, 4, 128]) for i in range(2)]; psOB = [PBuf() for _ in range(2)]
        psE = pt("p3_psE", [128, 512]); psEB = PBuf()
        gi = 0
        hi = 0
        for u in range(NSB):
            ub = u % 2
            t0 = u * SBK
            sc.dma(lds[0], [(qP[par][64 * par:64 * par + 64, :, :],
                             scr["qTl"][:, 64 * par:64 * par + 64, t0:t0 + SBK].rearrange("c p s -> p c s")) for par in range(2)],
                   writes=[qB])
            sc.dma(lds[1], [(kL[ub][:], scr["kTl"][:, :, t0:t0 + SBK].rearrange("c p s -> p c s"))], writes=[kB[ub]])
            sc.dma(lds[2], [(vT[:], scr["vTl"][:, :, t0:t0 + SBK].rearrange("c p s -> p c s"))], writes=[vTB])
            vi = 0
            for pi, d in enumerate(PATS):
                for ti in range(16):
                    r_, nbl = ti % d, ti // d
                    off = 128 * d * nbl + r_
                    for c in range(NC3):
                        sc.op("pe", lambda h: h.transpose(out=psV[:, c, :], in_=vT[:, c, off:off + 127 * d + 1:d],
                                                          identity=k.ident_bf[:]),
                              reads=[vTB, k.cB], writes=[psVB], inc=(c == NC3 - 1))
                    dst = vt[pi][ub][:, ti, :].rearrange("p (h e) -> p h e", e=65)[:, :, 0:64]
                    src = psV[:].rearrange("p c (h e) -> p (c h) e", e=64)
                    if vi % 2 == 0:
                        sc.op("act", lambda h: h.copy(out=dst, in_=src), reads=[psVB], writes=[vtB[pi][ub]])
                    else:
                        sc.op("dve", lambda h: h.tensor_copy(out=dst, in_=src), reads=[psVB], writes=[vtB[pi][ub]])
                    vi += 1
            for hh in range(HL):
                c = hh // 2
                p0 = 64 * (hh % 2)
                ab = hi % 2
                hi += 1
                groups = [(pi, d, tg) for pi, d in enumerate(PATS) for tg in range(4)]
                gbase = gi
                gi += len(groups)

                def tiles_of(g):
                    pi, d, tg = groups[g]
                    tiles = []
                    for q4 in range(4):
                        ti = tg * 4 + q4
                        r_, nbl = ti % d, ti // d
                        off = 128 * d * nbl + r_
                        if nbl > 0:
                            prev = (ub, off - 128 * d, ti - d)
                        elif u > 0:
                            nbp = 16 // d - 1
                            prev = (1 - ub, 128 * d * nbp + r_, r_ + d * nbp)
                        else:
                            prev = None
                        tiles.append((ti, off, prev))
                    return tiles

                def g_qk(g):
                    pi, d, tg = groups[g]
                    sl = (gbase + g) % 2
                    for q4, (ti, off, prev) in enumerate(tiles_of(g)):
                        Q = qP[hh % 2][:, c, off:off + 127 * d + 1:d]
                        Kc = kL[ub][:, c, off:off + 127 * d + 1:d]
                        if prev is not None:
                            Kp = kL[prev[0]][:, c, prev[1]:prev[1] + 127 * d + 1:d]
                            rd = [qB, kB[ub], kB[prev[0]]]
                        else:
                            Kp = Kc
                            rd = [qB, kB[ub]]
                        sc.op("pe", lambda h: h.matmul(psS[sl][:, q4, 0:128], lhsT=Kp, rhs=Q, start=True, stop=True),
                              reads=rd, writes=[psSB[sl]], inc=False)
                        sc.op("pe", lambda h: h.matmul(psS[sl][:, q4, 128:256], lhsT=Kc, rhs=Q, start=True, stop=True),
                              reads=rd, writes=[psSB[sl]], inc=(q4 == 3))

                def g_exp(g):
                    sl = (gbase + g) % 2
                    p3 = (gbase + g) % 3
                    sc.op("act", lambda h: h.activation(out=pT[p3][:], in_=psS[sl][:], func=AF.Exp, scale=scale),
                          reads=[psSB[sl]], writes=[pTB[p3]])
                    sc.op("dve" if g % 2 == 0 else "pool", lambda h: h.tensor_tensor(
                        out=pT[p3][:], in0=pT[p3][:], in1=lmask[:, None, :].broadcast_to([128, 4, 256]), op=ALU.mult),
                        reads=[pTB[p3], cB], writes=[pTB[p3]])

                def g_pv(g):
                    pi, d, tg = groups[g]
                    sl = (gbase + g) % 2
                    p3 = (gbase + g) % 3
                    for q4, (ti, off, prev) in enumerate(tiles_of(g)):
                        if prev is not None:
                            sc.op("pe", lambda h: h.matmul(psO[sl][0:65, q4, :], lhsT=vt[pi][prev[0]][:, prev[2], hh * 65:(hh + 1) * 65],
                                                           rhs=pT[p3][:, q4, 0:128], start=True, stop=False),
                                  reads=[pTB[p3], vtB[pi][prev[0]]], writes=[psOB[sl]], inc=False)
                        sc.op("pe", lambda h: h.matmul(psO[sl][0:65, q4, :], lhsT=vt[pi][ub][:, ti, hh * 65:(hh + 1) * 65],
                                                       rhs=pT[p3][:, q4, 128:256], start=(prev is None), stop=True),
                              reads=[pTB[p3], vtB[pi][ub]], writes=[psOB[sl]], inc=(q4 == 3))

                def g_acc(g):
                    pi, d, tg = groups[g]
                    sl = (gbase + g) % 2
                    if d == 1:
                        dstv = acc[ab][:, tg * 512:(tg + 1) * 512].rearrange("p (a j) -> p a j", j=128)
                    elif d == 4:
                        dstv = acc[ab][:, tg * 512:(tg + 1) * 512].rearrange("p (j r) -> p r j", r=4)
                    else:
                        dstv = acc[ab][:].rearrange("p (j r) -> p r j", r=16)[:, tg * 4:tg * 4 + 4, :]
                    if pi == 0:
                        sc.op("dve", lambda h: h.tensor_copy(out=dstv, in_=psO[sl][0:65, :, :]), reads=[psOB[sl]], writes=[accB[ab]])
                    else:
                        sc.op("dve", lambda h: h.tensor_tensor(out=dstv, in0=dstv, in1=psO[sl][0:65, :, :], op=ALU.add),
                              reads=[psOB[sl], accB[ab]], writes=[accB[ab]])

                skewed(len(groups), [g_qk, g_exp, g_pv, g_acc])
                for sbk in range(4):
                    cs_ = slice(sbk * 512, (sbk + 1) * 512)
                    sc.op("pe", lambda h: h.matmul(psE[0:64, :], lhsT=sel[:], rhs=acc[ab][:, cs_], start=True, stop=True),
                          reads=[cB, accB[ab]], writes=[psEB])
                    sc.op("dve", lambda h: h.reciprocal(out=rr[:], in_=psE[0:64, :]), reads=[psEB], writes=[rrB])
                    sc.op("pool", lambda h: h.tensor_tensor(out=ost[ab][:, cs_], in0=acc[ab][0:64, cs_], in1=rr[:], op=ALU.mult),
                          reads=[rrB, accB[ab]], writes=[ostB[ab]])
                row = HD * 64 + hh * 64
                sc.dma(ods[ab], [(scr["mixT"][row // 128, (row % 128):(row % 128) + 64, t0:t0 + SBK], ost[ab][:])],
                       reads=[ostB[ab]])


def phase4(k, l):
    nc, sc, cfg, W, C = k.nc, k.sc, k.cfg, k.W, k.C
    S, HD, HL, G, HS, NB, NX = cfg.S, cfg.HD, cfg.HL, cfg.G, cfg.HS, cfg.NB, cfg.NXBC
    scr = k.scr
    XC = HS // 2
    HW = HS * 64
    with ExitStack() as ps:
        sb, pt = _mk(k, l, ps)
        triu = sb("p4_triu", [128, 128]); negm = sb("p4_negm", [128, 128])
        dtb = sb("p4_dtb", [128, HS]); alog = sb("p4_alog", [128, HS]); Dd = sb("p4_D", [128, HS]); gn = sb("p4_gn", [128, HW])
        negA = sb("p4_negA", [128, HS])
        cB = Buf(); d0 = sc.dsem()
        sc.dma(d0, [(triu[:], C["triu"]), (negm[:], C["negm"]),
                    (dtb[:], W["dt_bias"][l:l + 1, :].broadcast_to([128, HS])),
                    (alog[:], W["A_log"][l:l + 1, :].broadcast_to([128, HS])),
                    (Dd[:], W["ssd_D"][l:l + 1, :].broadcast_to([128, HS])),
                    (gn[:], W["ssd_norm"][l:l + 1, :].broadcast_to([128, HW]))], writes=[cB])
        pB = Buf()
        sc.op("act", lambda h: h.activation(out=negA[:], in_=alog[:], func=AF.Exp), reads=[cB], writes=[pB])
        sc.op("dve", lambda h: h.tensor_scalar(out=negA[:], in0=negA[:], scalar1=-1.0, scalar2=None, op0=ALU.mult), reads=[pB], writes=[pB])
        xb_ = [sb("p4_xb%d" % i, [128, NX, TB], BF16) for i in range(2)]
        z_ = [sb("p4_z%d" % i, [128, 4, HW], BF16) for i in range(2)]
        dt_ = [sb("p4_dt%d" % i, [128, 4, HS]) for i in range(2)]
        inB = [Buf() for _ in range(2)]; lds = [sc.dsem() for _ in range(2)]
        dtp = sb("p4_dtp", [128, 4, HS]); aa = sb("p4_a", [128, 4, HS]); dB = Buf()
        a_bc = sb("p4_abc", [128, HS, 128]); abB = Buf()
        cs_sb2 = [sb("p4_cs%d" % i, [128, HS]) for i in range(2)]; csl2 = [sb("p4_csl%d" % i, [128, HS]) for i in range(2)]; csB2 = [Buf() for _ in range(2)]
        arg = sb("p4_arg", [128, HS, 128]); argB = Buf()
        E = sb("p4_E", [128, HS, 128]); EB = Buf()
        MT2 = [sb("p4_MT%d" % i, [128, HS, 128], BF16) for i in range(2)]; MTB2 = [Buf() for _ in range(2)]
        x_sb2 = [sb("p4_x%d" % i, [128, HW]) for i in range(2)]; B_sb2 = [sb("p4_B%d" % i, [128, G * 128], BF16) for i in range(2)]; xB2 = [Buf() for _ in range(2)]
        xdt2 = [sb("p4_xdt%d" % i, [128, HS, 64], BF16) for i in range(2)]; xdtB2 = [Buf() for _ in range(2)]
        xdd2 = [sb("p4_xdd%d" % i, [128, HS, 64], BF16) for i in range(2)]; xddB2 = [Buf() for _ in range(2)]
        ecs2 = [sb("p4_ecs%d" % i, [128, HS]) for i in range(2)]; dst_ = sb("p4_dst", [128, HS]); edec2 = [sb("p4_edec%d" % i, [128, HS]) for i in range(2)]; eB2 = [Buf() for _ in range(2)]; dstB = Buf()
        t13 = [sb("p4_t1%d" % i, [128, HW]) for i in range(3)]; t1B3 = [Buf() for _ in range(3)]
        t33 = [sb("p4_t3%d" % i, [128, HW]) for i in range(3)]; t3B3 = [Buf() for _ in range(3)]
        yv = sb("p4_yv", [128, HW]); yvB = Buf()
        szb = [sb("p4_szb%d" % i, [128, 4, HW]) for i in range(2)]; szbB = [Buf() for _ in range(2)]
        junk = sb("p4_junk", [128, HW], BF16); junkB = Buf()
        ss2 = sb("p4_ss2", [128, G]); rs2 = sb("p4_rs2", [128, G]); ssB = Buf(); rsB = Buf()
        yn = sb("p4_yn", [128, HW], BF16); ynB = Buf()
        yst = [sb("p4_yst%d" % i, [128, XC, TB], BF16) for i in range(2)]; ystB = [Buf() for _ in range(2)]
        sds = [sc.dsem() for _ in range(2)]
        st = sb("p4_st", [128, HW]); st_bf = sb("p4_stbf", [128, HW], BF16); stB = Buf(); stbB = Buf()
        ps1 = pt("p4_ps1", [128, 512]); ps1B = PBuf()
        psR = pt("p4_psR", [128, 2, 512]); psRB = PBuf()
        psXT = pt("p4_psXT", [128, 1024], BF16); psXTB = PBuf(); psXTb = pt("p4_psXTb", [128, 1024], BF16); psXTbB = PBuf()
        psY = pt("p4_psY", [128, 512]); psYB = PBuf()
        psYO = pt("p4_psYO", [128, 512]); psYOB = PBuf()
        psS = pt("p4_psS", [128, 512]); psSB = PBuf()
        sc.op("pool", lambda h: h.memset(st[:], 0.0), writes=[stB])
        sc.op("pool", lambda h: h.memset(st_bf[:], 0.0), writes=[stbB])
        XTO = XC * 128 + G * 128
        def load4(tt):
            bb = tt % 2
            tk = tt * TB
            sc.dma(lds[bb], [(xb_[bb][:], scr["xbcT"][:, :, tk:tk + TB].rearrange("c p s -> p c s")),
                             (z_[bb][:], scr["zs"][tk:tk + TB, :].rearrange("(n p) f -> p n f", p=128)),
                             (dt_[bb][:], scr["dts"][tk:tk + TB, :].rearrange("(n p) f -> p n f", p=128))],
                   writes=[inB[bb]])

        load4(0)
        for t in range(NB):
            tok0 = t * TB
            b = t % 2
            if t + 1 < NB:
                load4(t + 1)
            sc.op("dve", lambda h: h.tensor_tensor(out=dtp[:], in0=dt_[b][:], in1=dtb[:, None, :].broadcast_to([128, 4, HS]), op=ALU.add),
                  reads=[inB[b], cB], writes=[dB])
            sc.op("act", lambda h: h.activation(out=dtp[:], in_=dtp[:], func=AF.Exp), reads=[dB], writes=[dB])
            sc.op("act", lambda h: h.activation(out=dtp[:], in_=dtp[:], func=AF.Ln, bias=1.0), reads=[dB], writes=[dB])
            sc.op("dve", lambda h: h.tensor_tensor(out=aa[:], in0=dtp[:], in1=negA[:, None, :].broadcast_to([128, 4, HS]), op=ALU.mult),
                  reads=[dB, pB], writes=[dB])
            sc.op("act", lambda h: h.activation(out=szb[b][:], in_=z_[b][:], func=AF.Silu), reads=[inB[b]], writes=[szbB[b]])
            def front(n):
                q = n % 2
                csl_ = slice(n * 128, (n + 1) * 128)
                sc.op("dve", lambda h: h.tensor_copy(out=a_bc[:], in_=aa[:, n, :, None].broadcast_to([128, HS, 128])),
                      reads=[dB], writes=[abB])
                sc.op("pe", lambda h: h.matmul(ps1[:, 256:256 + HS], lhsT=triu[:], rhs=aa[:, n, :], start=True, stop=True),
                      reads=[cB, dB], writes=[ps1B])
                for hh in range(HS):
                    sc.op("pe", lambda h: h.matmul(psR[:, hh // 4, (hh % 4) * 128:(hh % 4 + 1) * 128], lhsT=a_bc[:, hh, :], rhs=triu[:],
                                                   start=True, stop=True), reads=[cB, abB], writes=[psRB], inc=(hh == HS - 1))
                psRv = psR[:].rearrange("p a (b c) -> p (a b) c", c=128)[:, 0:HS, :]
                sc.op("act", lambda h: h.copy(out=cs_sb2[q][:], in_=ps1[:, 256:256 + HS]), reads=[ps1B], writes=[csB2[q]])
                sc.op("act", lambda h: h.copy(out=csl2[q][:], in_=psRv[:, :, 127]), reads=[psRB], writes=[csB2[q]])
                sc.op("dve", lambda h: h.tensor_tensor(out=arg[:], in0=psRv, in1=cs_sb2[q][:, :, None].broadcast_to([128, HS, 128]), op=ALU.subtract),
                      reads=[psRB, csB2[q]], writes=[argB])
                sc.op("dve", lambda h: h.tensor_tensor(out=arg[:], in0=arg[:], in1=negm[:, None, :].broadcast_to([128, HS, 128]), op=ALU.add),
                      reads=[argB, cB], writes=[argB])
                sc.op("act", lambda h: h.activation(out=E[:], in_=arg[:], func=AF.Exp), reads=[argB], writes=[EB])
                for g in range(G):
                    sc.op("pe", lambda h: h.matmul(ps1[:, g * 128:(g + 1) * 128], lhsT=xb_[b][:, XC + g, csl_], rhs=xb_[b][:, XC + G + g, csl_],
                                                   start=True, stop=True), reads=[inB[b]], writes=[ps1B], inc=(g == G - 1))
                for g in range(G):
                    sc.op("dve", lambda h: h.tensor_tensor(out=MT2[q][:, 3 * g:3 * g + 3, :], in0=E[:, 3 * g:3 * g + 3, :],
                                                           in1=ps1[:, None, g * 128:(g + 1) * 128].broadcast_to([128, 3, 128]), op=ALU.mult),
                          reads=[EB, ps1B], writes=[MTB2[q]])
                for j in range(XC + G):
                    sc.op("pe", lambda h: h.transpose(out=psXT[:, j * 128:(j + 1) * 128], in_=xb_[b][:, j, csl_], identity=k.ident_bf[:]),
                          reads=[inB[b], k.cB], writes=[psXTB], inc=(j == XC + G - 1))
                sc.op("act", lambda h: h.copy(out=x_sb2[q][:], in_=psXT[:, 0:HW]), reads=[psXTB], writes=[xB2[q]])
                sc.op("act", lambda h: h.copy(out=B_sb2[q][:], in_=psXT[:, HW:HW + G * 128]), reads=[psXTB], writes=[xB2[q]])
                sc.op("dve", lambda h: h.tensor_tensor(out=xdt2[q][:], in0=x_sb2[q][:].rearrange("p (h e) -> p h e", e=64),
                                                       in1=dtp[:, n, :, None].broadcast_to([128, HS, 64]), op=ALU.mult),
                      reads=[xB2[q], dB], writes=[xdtB2[q]])
                sc.op("act", lambda h: h.activation(out=ecs2[q][:], in_=cs_sb2[q][:], func=AF.Exp), reads=[csB2[q]], writes=[eB2[q]])
                sc.op("pool", lambda h: h.tensor_tensor(out=t33[n % 3][:].rearrange("p (h e) -> p h e", e=64), in0=x_sb2[q][:].rearrange("p (h e) -> p h e", e=64),
                                                        in1=Dd[:, :, None].broadcast_to([128, HS, 64]), op=ALU.mult),
                      reads=[xB2[q], cB], writes=[t3B3[n % 3]])
                sc.op("dve", lambda h: h.tensor_tensor(out=dst_[:], in0=csl2[q][:], in1=cs_sb2[q][:], op=ALU.subtract), reads=[csB2[q]], writes=[dstB])
                sc.op("act", lambda h: h.activation(out=dst_[:], in_=dst_[:], func=AF.Exp), reads=[dstB], writes=[dstB])
                sc.op("act", lambda h: h.activation(out=edec2[q][:], in_=csl2[q][:], func=AF.Exp), reads=[csB2[q]], writes=[eB2[q]])
                sc.op("dve", lambda h: h.tensor_tensor(out=xdd2[q][:], in0=xdt2[q][:], in1=dst_[:, :, None].broadcast_to([128, HS, 64]), op=ALU.mult),
                      reads=[xdtB2[q], dstB], writes=[xddB2[q]])

            def back(n):
                q = n % 2
                csl_ = slice(n * 128, (n + 1) * 128)
                for hh in range(HS):
                    sc.op("pe", lambda h: h.matmul(psY[:, hh * 64:(hh + 1) * 64], lhsT=MT2[q][:, hh, :], rhs=xdt2[q][:, hh, :], start=True, stop=True),
                          reads=[MTB2[q], xdtB2[q]], writes=[psYB], inc=(hh == HS - 1))
                for g in range(G):
                    sc.op("pe", lambda h: h.matmul(psYO[:, g * 192:(g + 1) * 192], lhsT=xb_[b][:, XC + G + g, csl_], rhs=st_bf[:, g * 192:(g + 1) * 192],
                                                   start=True, stop=True), reads=[inB[b], stbB], writes=[psYOB], inc=(g == G - 1))
                sc.op("dve", lambda h: h.tensor_tensor(out=t13[n % 3][:].rearrange("p (h e) -> p h e", e=64), in0=psYO[:, 0:HW].rearrange("p (h e) -> p h e", e=64),
                                                       in1=ecs2[q][:, :, None].broadcast_to([128, HS, 64]), op=ALU.mult),
                      reads=[psYOB, eB2[q]], writes=[t1B3[n % 3]])
                sc.op("dve", lambda h: h.tensor_tensor(out=t13[n % 3][:], in0=psY[:, 0:HW], in1=t13[n % 3][:], op=ALU.add), reads=[psYB, t1B3[n % 3]], writes=[t1B3[n % 3]])
                for g in range(G):
                    sc.op("pe", lambda h: h.matmul(psS[:, g * 192:(g + 1) * 192], lhsT=B_sb2[q][:, g * 128:(g + 1) * 128],
                                                   rhs=xdd2[q][:, 3 * g:3 * g + 3, :].rearrange("p h e -> p (h e)"), start=True, stop=True),
                          reads=[xB2[q], xddB2[q]], writes=[psSB], inc=(g == G - 1))
                sc.op("dve", lambda h: h.tensor_tensor(out=st[:].rearrange("p (h e) -> p h e", e=64), in0=st[:].rearrange("p (h e) -> p h e", e=64),
                                                       in1=edec2[q][:, :, None].broadcast_to([128, HS, 64]), op=ALU.mult),
                      reads=[stB, eB2[q]], writes=[stB])
                sc.op("dve", lambda h: h.tensor_tensor(out=st[:], in0=st[:], in1=psS[:, 0:HW], op=ALU.add), reads=[stB, psSB], writes=[stB])
                sc.op("pool", lambda h: h.tensor_copy(out=st_bf[:], in_=st[:]), reads=[stB], writes=[stbB])

            def post(n):
                q = n % 2
                csl_ = slice(n * 128, (n + 1) * 128)
                sc.op("pool", lambda h: h.tensor_tensor(out=yv[:], in0=t13[n % 3][:], in1=t33[n % 3][:], op=ALU.add), reads=[t1B3[n % 3], t3B3[n % 3]], writes=[yvB])
                sc.op("dve", lambda h: h.tensor_tensor(out=yv[:], in0=yv[:], in1=szb[b][:, n, :], op=ALU.mult), reads=[yvB, szbB[b]], writes=[yvB])
                for g in range(G):
                    sc.op("act", lambda h: h.activation(out=junk[:, 0:192], in_=yv[:, g * 192:(g + 1) * 192], func=AF.Square,
                                                        accum_out=ss2[:, g:g + 1]), reads=[yvB], writes=[junkB, ssB])
                rms_rstd(k, ps, ss2, rs2, G, 192, ssB, rsB)
                for g in range(G):
                    sc.op("dve", lambda h: h.scalar_tensor_tensor(out=yn[:, g * 192:(g + 1) * 192], in0=yv[:, g * 192:(g + 1) * 192],
                                                                  scalar=rs2[:, g:g + 1], in1=gn[:, g * 192:(g + 1) * 192],
                                                                  op0=ALU.mult, op1=ALU.mult), reads=[yvB, rsB, cB], writes=[ynB])
                for j in range(XC):
                    sc.op("pe", lambda h: h.transpose(out=psXTb[:, j * 128:(j + 1) * 128], in_=yn[:, j * 128:(j + 1) * 128],
                                                      identity=k.ident_bf[:]), reads=[ynB, k.cB], writes=[psXTbB], inc=(j == XC - 1))
                sc.op("act", lambda h: h.copy(out=yst[b][:, :, csl_], in_=psXTb[:, 0:XC * 128].rearrange("p (j s) -> p j s", s=128)),
                      reads=[psXTbB], writes=[ystB[b]])

            front(0)
            for n in range(4):
                if n + 1 < 4:
                    front(n + 1)
                back(n)
                if n > 0:
                    post(n - 1)
            post(3)
            c0 = (HD * 64 + HL * 64) // 128
            sc.dma(sds[b], [(scr["mixT"][c0:c0 + XC, :, tok0:tok0 + TB].rearrange("c p s -> p c s"), yst[b][:])], reads=[ystB[b]])


def post_norm_add(k, sc, psF, psFB, n, ss, ssB, rstd, rsB, gb, gB, yt, ytB, xres, xresB, xo, xoB, junk, junkB, add_eng="pool"):
    sc.op("act", lambda h: h.activation(out=junk[:], in_=psF[:], func=AF.Square, accum_out=ss[:, n:n + 1]),
          reads=[psFB], writes=[junkB, ssB])
    rms_rstd(k, None, ss[:, n:n + 1], rstd[:, n:n + 1], 1, D, ssB, rsB)
    sc.op("dve", lambda h: h.scalar_tensor_tensor(out=yt[:], in0=psF[:], scalar=rstd[:, n:n + 1], in1=gb[:],
                                                  op0=ALU.mult, op1=ALU.mult), reads=[psFB, rsB, gB], writes=[ytB])
    sc.op(add_eng, lambda h: h.tensor_tensor(out=xo[:, n, :], in0=xres[:, n, :], in1=yt[:], op=ALU.add),
          reads=[ytB, xresB], writes=[xoB])


def phase5a(k, l, x_src):
    nc, sc, cfg, W, C = k.nc, k.sc, k.cfg, k.W, k.C
    S, NB, MC = cfg.S, cfg.NB, cfg.MIXC
    scr = k.scr
    with ExitStack() as ps:
        sb, pt = _mk(k, l, ps)
        wout = sb("p5a_w", [128, MC, D], BF16)
        stg = [sb("p5a_stg%d" % i, [128, 512]) for i in range(3)]
        stgB = [Buf() for _ in range(3)]; ds = [sc.dsem() for _ in range(3)]
        k.gB, k.wB = Buf(), Buf()
        gb = sb("p5a_g", [128, D]); gB = Buf(); d0 = sc.dsem()
        sc.dma(d0, [(gb[:], W["g_postmix"][l:l + 1, :].broadcast_to([128, D]))], writes=[gB])
        load_weights_bf16(k, ps, wout, W["w_out"][l], D, None, stg, stgB, ds, MC)
        mT = [sb("p5a_m%d" % i, [128, MC, TB], BF16) for i in range(2)]; mB = [Buf() for _ in range(2)]
        xr = [sb("p5a_x%d" % i, [128, 4, D]) for i in range(2)]; xB = [Buf() for _ in range(2)]
        lds = [sc.dsem() for _ in range(2)]
        xo = [sb("p5a_xo%d" % i, [128, 4, D]) for i in range(2)]; xoB = [Buf() for _ in range(2)]
        sds = [sc.dsem() for _ in range(2)]
        psF = [pt("p5a_psF%d" % i, [128, D]) for i in range(2)]; psFB = [PBuf() for _ in range(2)]
        ss = sb("p5a_ss", [128, 4]); ssB = Buf(); rstd = sb("p5a_rstd", [128, 4]); rsB = Buf()
        yt = sb("p5a_yt", [128, D]); ytB = Buf()
        junk = sb("p5a_junk", [128, D], BF16); junkB = Buf()
        ci = 0
        def load5a(tt):
            bb = tt % 2
            tk = tt * TB
            sc.dma(lds[bb], [(mT[bb][:], scr["mixT"][:, :, tk:tk + TB].rearrange("c p s -> p c s")),
                             (xr[bb][:], x_src[tk:tk + TB, :].rearrange("(n p) d -> p n d", p=128))],
                   writes=[mB[bb], xB[bb]])

        load5a(0)
        for t in range(NB):
            tok0 = t * TB
            b = t % 2
            if t + 1 < NB:
                load5a(t + 1)
            for n in range(4):
                a = ci % 2
                ci += 1
                for half in range(2):
                    for kc in range(MC):
                        sc.op("pe", lambda h, kc=kc, half=half: h.matmul(
                            psF[a][:, half * 512:(half + 1) * 512], lhsT=mT[b][:, kc, n * 128:(n + 1) * 128],
                            rhs=wout[:, kc, half * 512:(half + 1) * 512], start=(kc == 0), stop=(kc == MC - 1)),
                            reads=[k.wB, mB[b]], writes=[psFB[a]], inc=(kc == MC - 1 and half == 1))
                post_norm_add(k, sc, psF[a], psFB[a], n, ss, ssB, rstd, rsB, gb, gB, yt, ytB, xr[b], xB[b], xo[b], xoB[b], junk, junkB,
                              add_eng=("dve" if n % 2 == 0 else "pool"))
            sc.dma(sds[b], [(scr["xa"][tok0:tok0 + TB, :].rearrange("(n p) d -> p n d", p=128), xo[b][:])], reads=[xoB[b]])


def phase5b(k, l, x_dst):
    nc, sc, cfg, W, C = k.nc, k.sc, k.cfg, k.W, k.C
    S, FC, TBF = cfg.S, cfg.FC, cfg.TBF
    NT = TBF // 128
    scr = k.scr
    with ExitStack() as ps:
        sb, pt = _mk(k, l, ps)
        wup = sb("p5b_wup", [128, 8, 2 * cfg.DFF], BF16)
        wdn = sb("p5b_wdn", [128, FC, D], BF16)
        stg = [sb("p5b_stg%d" % i, [128, 512]) for i in range(3)]
        stgB = [Buf() for _ in range(3)]; ds = [sc.dsem() for _ in range(3)]
        k.gB, k.wB = Buf(), Buf()
        gain = sb("p5b_gain", [128, 8]); gb = sb("p5b_g", [128, D]); gB = Buf(); d0 = sc.dsem()
        fcw = sb("p5b_fcw", [128, 2 * FC, 3]); fcb = sb("p5b_fcb", [128, 2 * FC])
        sc.dma(d0, [(gb[:], W["g_postffn"][l:l + 1, :].broadcast_to([128, D])), (gain[:], W["g_preffn"][l]),
                    (fcw[:], W["fcw"][l]), (fcb[:], W["fcb"][l])], writes=[gB, k.gB])
        load_weights_bf16(k, ps, wup, W["ffn_up"][l], 2 * cfg.DFF, gain, stg, stgB, ds, 8)
        load_weights_bf16(k, ps, wdn, W["ffn_down"][l], D, None, stg, stgB, ds, FC)
        xr2 = [sb("p5b_x%d" % i, [128, NT, D]) for i in range(2)]; xB2 = [[Buf() for _ in range(NT)] for i in range(2)]
        lds2 = [[sc.dsem() for _ in range(NT)] for i in range(2)]; sds2 = [sc.dsem() for _ in range(2)]
        xs2 = [sb("p5b_xs%d" % i, [128, NT, D], BF16) for i in range(2)]; xsB2 = [Buf() for _ in range(2)]
        hT2 = [sb("p5b_hT%d" % i, [128, 8, 2 + TBF], BF16) for i in range(2)]; hTB2 = [Buf() for _ in range(2)]
        aT = sb("p5b_aT", [128, FC, TBF], BF16); aTB = Buf()
        NU = 4
        u = [sb("p5b_u%d" % i, [128, 2 + TBF]) for i in range(NU)]; uB = [Buf() for _ in range(NU)]; uhB = [Buf() for _ in range(NU)]
        y = [sb("p5b_y%d" % i, [128, TBF]) for i in range(NU)]; yB = [Buf() for _ in range(NU)]
        sg = [sb("p5b_sg%d" % i, [128, TBF]) for i in range(2)]; sgB = [Buf() for _ in range(2)]
        ss = sb("p5b_ss", [128, 4]); ssB = Buf(); rstd = sb("p5b_rstd", [128, 4]); rsB = Buf()
        yt = sb("p5b_yt", [128, D]); ytB = Buf()
        junk = sb("p5b_junk", [128, D], BF16); junkB = Buf()
        psT = [pt("p5b_psT%d" % i, [128, 2, 512], BF16) for i in range(2)]; psTB = [PBuf() for _ in range(2)]
        psU = [pt("p5b_psU%d" % i, [128, 512]) for i in range(NU)]; psUB = [PBuf() for _ in range(NU)]
        psF = [pt("p5b_psF%d" % i, [128, D]) for i in range(1)]; psFB = [PBuf() for _ in range(1)]
        for i in range(2):
            sc.op("pool", lambda h, i=i: h.memset(hT2[i][:], 0.0), writes=[hTB2[i]])
        ci = 0
        NBLK = S // TBF

        def do_norm(t):
            b = t % 2
            tok0 = t * TBF
            for n in range(NT):
                sc.dma(lds2[b][n], [(xr2[b][:, n, :], scr["xa"][tok0 + n * 128:tok0 + (n + 1) * 128, :])], writes=[xB2[b][n]])
            for n in range(NT):
                sc.op("act", lambda h, n=n: h.activation(out=junk[:], in_=xr2[b][:, n, :], func=AF.Square, accum_out=ss[:, n:n + 1]),
                      reads=[xB2[b][n]], writes=[junkB, ssB])
            rms_rstd(k, ps, ss, rstd, NT, D, ssB, rsB)
            for n in range(NT):
                if n % 2 == 0:
                    sc.op("dve", lambda h, n=n: h.tensor_scalar(
                        out=xs2[b][:, n, :], in0=xr2[b][:, n, :], scalar1=rstd[:, n:n + 1], scalar2=None, op0=ALU.mult),
                        reads=[xB2[b][n], rsB], writes=[xsB2[b]])
                else:
                    sc.op("act", lambda h, n=n: h.activation(out=xs2[b][:, n, :], in_=xr2[b][:, n, :], func=AF.Copy, scale=rstd[:, n:n + 1]),
                          reads=[xB2[b][n], rsB], writes=[xsB2[b]])
            if t > 0:
                sc.op("pool", lambda h: h.tensor_copy(out=hT2[b][:, :, 0:2], in_=hT2[1 - b][:, :, TBF:TBF + 2]), reads=[hTB2[1 - b]], writes=[hTB2[b]])
            for cp in range(4):
                pi = cp % 2
                for c2 in range(2):
                    c = cp * 2 + c2
                    for n in range(NT):
                        sc.op("pe", lambda h, c=c, c2=c2, n=n: h.transpose(
                            out=psT[pi][:, c2, n * 128:(n + 1) * 128], in_=xs2[b][:, n, c * 128:(c + 1) * 128],
                            identity=k.ident_bf[:]), reads=[xsB2[b], k.cB], writes=[psTB[pi]], inc=(c2 == 1 and n == NT - 1))
                sc.op("act", lambda h, cp=cp: h.copy(out=hT2[b][:, 2 * cp:2 * cp + 2, 2:2 + TBF], in_=psT[pi][:, :, 0:TBF]),
                      reads=[psTB[pi]], writes=[hTB2[b]])

        def do_chunks(t):
            b = t % 2
            chunks = [(j, wi, c) for j in range(FC) for wi, c in enumerate((j, FC + j))]

            def st_pe(i):
                j, wi, c = chunks[i]
                a = i % NU
                for kc in range(8):
                    sc.op("pe", lambda h: h.matmul(psU[a][:, 0:TBF + 2], lhsT=wup[:, kc, c * 128:(c + 1) * 128], rhs=hT2[b][:, kc, :],
                                                   start=(kc == 0), stop=(kc == 7)), reads=[k.wB, hTB2[b]], writes=[psUB[a]], inc=(kc == 7))

            def st_copy(i):
                j, wi, c = chunks[i]
                a = i % NU
                sc.op("act", lambda h: h.copy(out=u[a][:, 0:2 + TBF], in_=psU[a][:, 0:2 + TBF]), reads=[psUB[a]], writes=[uB[a]])
                sc.op("act", lambda h: h.activation(out=y[a][:], in_=psU[a][:, 2:2 + TBF], func=AF.Identity, scale=fcw[:, c, 2:3], bias=fcb[:, c:c + 1]),
                      reads=[psUB[a], k.gB], writes=[yB[a]])

            def st_conv(i):
                j, wi, c = chunks[i]
                a = i % NU
                for kk in range(2):
                    sc.op("dve", lambda h: h.scalar_tensor_tensor(out=y[a][:], in0=u[a][:, kk:kk + TBF], scalar=fcw[:, c, kk:kk + 1], in1=y[a][:],
                                                                  op0=ALU.mult, op1=ALU.add), reads=[uB[a], yB[a], k.gB], writes=[yB[a]])

            def st_gate(i):
                j, wi, c = chunks[i]
                a = i % NU
                if wi == 0:
                    sc.op("act", lambda h: h.activation(out=sg[j % 2][:], in_=y[a][:], func=AF.Silu), reads=[yB[a]], writes=[sgB[j % 2]])
                else:
                    sc.op("pool", lambda h: h.tensor_tensor(out=aT[:, j, :], in0=sg[j % 2][:], in1=y[a][:], op=ALU.mult),
                          reads=[sgB[j % 2], yB[a]], writes=[aTB])

            skewed(len(chunks), [st_pe, st_copy, st_conv, st_gate])

        def do_down(t):
            b = t % 2
            tok0 = t * TBF
            for n in range(NT):
                a = 0
                for half in range(2):
                    for j in range(FC):
                        sc.op("pe", lambda h, j=j, half=half: h.matmul(
                            psF[a][:, half * 512:(half + 1) * 512], lhsT=aT[:, j, n * 128:(n + 1) * 128],
                            rhs=wdn[:, j, half * 512:(half + 1) * 512], start=(j == 0), stop=(j == FC - 1)),
                            reads=[k.wB, aTB], writes=[psFB[a]], inc=(j == FC - 1 and half == 1))
                post_norm_add(k, sc, psF[a], psFB[a], n, ss, ssB, rstd, rsB, gb, gB, yt, ytB, xr2[b], xB2[b][n], xr2[b], xB2[b][n], junk, junkB)
            sc.dma(sds2[b], [(x_dst[tok0:tok0 + TBF, :].rearrange("(n p) d -> p n d", p=128), xr2[b][:])], reads=xB2[b])

        do_norm(0)
        for t in range(NBLK):
            do_chunks(t)
            if t + 1 < NBLK:
                do_norm(t + 1)
            do_down(t)


def kernel(**inputs):
    cfg = Cfg()
    nc = build(cfg)
    hp = host_params(cfg, inputs)
    consts = host_consts(cfg)
    x = np.asarray(inputs["x"], np.float32)
    nb = x.shape[0]
    in_maps = []
    for b in range(nb):
        m = {"x": np.ascontiguousarray(x[b])}
        m.update(hp)
        m.update(consts)
        in_maps.append(m)
    res = run_bass_kernel_spmd(nc, in_maps, core_ids=list(range(nb)))
    return np.stack([np.asarray(res.results[b]["out"], np.float32) for b in range(nb)])
```

```python
import math
import os
from contextlib import ExitStack

import numpy as np
import ml_dtypes

import concourse.bass as bass
import concourse.mybir as mybir
from concourse.bass_utils import run_bass_kernel_spmd

F32 = mybir.dt.float32
BF16 = mybir.dt.bfloat16
AF = mybir.ActivationFunctionType
ALU = mybir.AluOpType
AX = mybir.AxisListType

D = 1024
EPS = 1e-6
TB = 512
SAME_ENGINE_SYNC = True


class Buf:
    __slots__ = ("name", "w", "r", "excl")

    def __init__(self, name="", excl=False):
        self.name = name
        self.w = None
        self.r = {}
        self.excl = excl


def PBuf(name=""):
    return Buf(name, True)


class _Eng:
    def __init__(self, name, h, sem):
        self.name, self.h, self.sem, self.cnt, self.seen = name, h, sem, 0, {}


class _DSem:
    def __init__(self, sem):
        self.sem, self.cnt = sem, 0


class Sched:
    def __init__(self, nc, es):
        self.nc = nc
        self.es = es
        self.E = {}
        for name, h in (("pe", nc.tensor), ("act", nc.scalar), ("dve", nc.vector),
                        ("pool", nc.gpsimd), ("sp", nc.sync)):
            self.E[name] = _Eng(name, h, es.enter_context(nc.semaphore("s_" + name)))
        self.bar = es.enter_context(nc.semaphore("s_bar"))
        self.barcnt = 0
        self.dsems = []
        self.nsem = 0

    def dsem(self):
        self.nsem += 1
        d = _DSem(self.es.enter_context(self.nc.semaphore("d%d" % self.nsem)))
        self.dsems.append(d)
        return d

    def _wait(self, e, reads, writes):
        deps = {}

        def add(t):
            if t is not None and deps.get(t[0], (None, 0))[1] < t[1]:
                deps[t[0]] = t

        for b in reads:
            add(b.w)
        for b in writes:
            add(b.w)
            for s, v in b.r.items():
                add((s, v))
        for s, (sem, val) in deps.items():
            if sem is e.sem and (e.name == "pe" or not SAME_ENGINE_SYNC):
                continue
            if e.seen.get(sem, 0) >= val:
                continue
            e.h.wait_ge(sem, val)
            e.seen[sem] = val

    @staticmethod
    def _mark(tok, reads, writes):
        for b in reads:
            if b.r.get(tok[0], 0) < tok[1]:
                b.r[tok[0]] = tok[1]
        for b in writes:
            b.w = tok
            b.r = {}

    def op(self, eng, fn, reads=(), writes=(), inc=True):
        e = self.E[eng]
        if any(b.excl for b in reads):
            writes = list(writes) + [b for b in reads if b.excl]
            reads = [b for b in reads if not b.excl]
        self._wait(e, reads, writes)
        ins = fn(e.h)
        if inc:
            e.cnt += 1
            ins.then_inc(e.sem, 1)
            tok = (e.sem, e.cnt)
        else:
            tok = (e.sem, e.cnt + 1)
        self._mark(tok, reads, writes)
        return tok

    def dma(self, ds, pairs, reads=(), writes=(), eng="sp"):
        e = self.E[eng]
        self._wait(e, reads, writes)
        for out, in_ in pairs:
            ds.cnt += 16
            e.h.dma_start(out=out, in_=in_).then_inc(ds.sem, 16)
        tok = (ds.sem, ds.cnt)
        self._mark(tok, reads, writes)
        return tok

    def barrier(self):
        sp = self.E["sp"]
        for o in self.E.values():
            if o is not sp and o.cnt > 0 and sp.seen.get(o.sem, 0) < o.cnt:
                sp.h.wait_ge(o.sem, o.cnt)
                sp.seen[o.sem] = o.cnt
        for d in self.dsems:
            if d.cnt > 0 and sp.seen.get(d.sem, 0) < d.cnt:
                sp.h.wait_ge(d.sem, d.cnt)
                sp.seen[d.sem] = d.cnt
        self.barcnt += 1
        sp.h.sem_inc(self.bar, 1)
        for o in self.E.values():
            if o is not sp:
                o.h.wait_ge(self.bar, self.barcnt)
            for o2 in self.E.values():
                o.seen[o2.sem] = o2.cnt
            for d in self.dsems:
                o.seen[d.sem] = d.cnt


class Cfg:
    def __init__(self, S=8192, L=2, HD=4, HL=6, G=2, DFF=2816, TBF=256):
        self.S, self.L, self.HD, self.HL, self.G, self.DFF, self.TBF = S, L, HD, HL, G, DFF, TBF
        self.HS = 3 * G
        self.NB = S // TB
        off = 0
        self.fm = []
        for nm, cnt, m in (("dq", HD, 64), ("dk", HD, 64), ("lq", HL // 2, 128), ("lk", HL // 2, 128),
                           ("lv", HL // 2, 128), ("xs", self.HS // 2, 128), ("B", G, 128), ("C", G, 128)):
            for i in range(cnt):
                self.fm.append((nm, i, off, m))
                off += m
        self.off_dv = off
        off += HD * 64
        self.off_z = off
        off += self.HS * 64 + self.HS
        self.NCOL = off
        self.NXBC = self.HS // 2 + 2 * G
        self.MIXC = (HD * 64 + HL * 64 + self.HS * 64) // 128
        self.FC = DFF // 128


def _rope_tables(S, head_dim, rows):
    half = head_dim // 2
    inv = np.exp(-math.log(10000.0) * np.arange(half, dtype=np.float32) / half).astype(np.float32)
    ang = np.arange(S, dtype=np.float32)[None, :] * inv[:, None]
    cos = np.cos(ang).astype(np.float32)
    sin = np.sin(ang).astype(np.float32)
    p = np.arange(rows)
    j = p % half
    first = (p % head_dim) < half
    cosT = cos[j]
    sinT = np.where(first[:, None], -sin[j], sin[j])
    perm = np.zeros((rows, rows), np.float32)
    partner = np.where(first, p + half, p - half)
    perm[p, partner] = 1.0
    return cosT.astype(np.float32), sinT.astype(np.float32), perm


def host_consts(cfg):
    S = cfg.S
    c = {}
    cd, sd, pd = _rope_tables(S, 32, 64)
    cl, sl, pl = _rope_tables(S, 64, 128)
    c["cosd"], c["sind"], c["cosl"], c["sinl"] = [a.astype(ml_dtypes.bfloat16) for a in (cd, sd, cl, sl)]
    c["permd"] = pd.astype(ml_dtypes.bfloat16)
    c["perml"] = pl.astype(ml_dtypes.bfloat16)
    c["ident_bf"] = np.eye(128, dtype=np.float32).astype(ml_dtypes.bfloat16)
    c["ident_f"] = np.eye(128, dtype=np.float32)
    k = np.arange(128)[:, None]
    q = np.arange(512)[None, :]
    c["dmask"] = np.stack([(q >= 128 * di + k) for di in range(4)]).astype(np.float32).astype(ml_dtypes.bfloat16)
    qq = np.arange(128)[None, :]
    c["lmask"] = np.concatenate([(qq <= k), (qq >= k)], axis=1).astype(np.float32).astype(ml_dtypes.bfloat16)
    c["triu"] = (k <= qq).astype(np.float32)
    c["negm"] = np.where(qq >= k, 0.0, -30000.0).astype(np.float32)
    sel = np.zeros((65, 64), np.float32)
    sel[64, :] = 1.0
    c["sel"] = sel
    c["ones64"] = np.ones((64, 64), np.float32)
    return c


CONST_SHAPES = None


def host_params(cfg, inp, b_heads=None):
    HD, HL, G, HS = cfg.HD, cfg.HL, cfg.G, cfg.HS
    w = np.asarray(inp["w_in"])
    L = w.shape[0]
    o = 0
    segs = {}
    for nm, n in (("dq", 256), ("dk", 256), ("dv", 256), ("lq", 384), ("lk", 384), ("lv", 384),
                  ("z", 384), ("xs", 384), ("B", 256), ("C", 256), ("dt", 6)):
        segs[nm] = (o, o + n)
        o += n
    cols = []
    for nm in ("dq", "dk", "lq", "lk", "lv", "xs", "B", "C", "dv", "z", "dt"):
        a, b = segs[nm]
        cols.append(np.arange(a, b))
    cols = np.concatenate(cols)
    p = {}
    p["w_in"] = np.ascontiguousarray(w[:, :, cols])
    p["g_premix"] = np.ascontiguousarray(np.asarray(inp["pre_mix_norm"]).reshape(L, 8, 128).transpose(0, 2, 1))
    p["g_preffn"] = np.ascontiguousarray(np.asarray(inp["pre_ffn_norm"]).reshape(L, 8, 128).transpose(0, 2, 1))
    p["g_postmix"] = np.ascontiguousarray(np.asarray(inp["post_mix_norm"]))
    p["g_postffn"] = np.ascontiguousarray(np.asarray(inp["post_ffn_norm"]))
    xo = segs["xs"][0]
    cw = np.asarray(inp["ssd_conv_w"])
    cb = np.asarray(inp["ssd_conv_b"])
    p["cw"] = np.ascontiguousarray(cw.reshape(L, 4, 7, 128).transpose(0, 3, 2, 1))
    p["cb"] = np.ascontiguousarray(cb.reshape(L, 7, 128).transpose(0, 2, 1))
    p["dt_bias"] = np.ascontiguousarray(np.asarray(inp["ssd_dt_bias"]))
    p["A_log"] = np.ascontiguousarray(np.asarray(inp["ssd_A_log"]))
    p["ssd_D"] = np.ascontiguousarray(np.asarray(inp["ssd_D"]))
    p["ssd_norm"] = np.ascontiguousarray(np.asarray(inp["ssd_norm"]))
    p["lam"] = np.ascontiguousarray(np.asarray(inp["diff_lambda"]).reshape(L, 128))
    p["hgain"] = np.ascontiguousarray(np.asarray(inp["diff_head_norm"]).reshape(L, 64, 1))
    p["w_out"] = np.ascontiguousarray(np.asarray(inp["w_out"]))
    p["ffn_up"] = np.ascontiguousarray(np.asarray(inp["ffn_up"]))
    fw = np.asarray(inp["ffn_conv_w"])
    fb = np.asarray(inp["ffn_conv_b"])
    p["fcw"] = np.ascontiguousarray(fw.reshape(L, 3, 44, 128).transpose(0, 3, 2, 1))
    p["fcb"] = np.ascontiguousarray(fb.reshape(L, 44, 128).transpose(0, 2, 1))
    p["ffn_down"] = np.ascontiguousarray(np.asarray(inp["ffn_down"]))
    return p


class K:
    pass


def build(cfg, debug=False):
    nc = bass.Bass("TRN2", target_bir_lowering=False)
    S, L, HD, HL, G, HS = cfg.S, cfg.L, cfg.HD, cfg.HL, cfg.G, cfg.HS
    NB = cfg.NB
    es = ExitStack()
    sc = Sched(nc, es)

    def din(name, shape, dt=F32):
        return nc.dram_tensor(name, list(shape), dt, kind="ExternalInput").ap()

    def dscr(name, shape, dt):
        return nc.dram_tensor(name, list(shape), dt, kind=("ExternalOutput" if debug else "Internal")).ap()

    x_in = din("x", [S, D])
    W = {}
    W["w_in"] = din("w_in", [L, D, cfg.NCOL])
    W["g_premix"] = din("g_premix", [L, 128, 8])
    W["g_preffn"] = din("g_preffn", [L, 128, 8])
    W["g_postmix"] = din("g_postmix", [L, D])
    W["g_postffn"] = din("g_postffn", [L, D])
    W["cw"] = din("cw", [L, 128, 7, 4])
    W["cb"] = din("cb", [L, 128, 7])
    W["dt_bias"] = din("dt_bias", [L, 6])
    W["A_log"] = din("A_log", [L, 6])
    W["ssd_D"] = din("ssd_D", [L, 6])
    W["ssd_norm"] = din("ssd_norm", [L, 384])
    W["lam"] = din("lam", [L, 128])
    W["hgain"] = din("hgain", [L, 64, 1])
    W["w_out"] = din("w_out", [L, D, D])
    W["ffn_up"] = din("ffn_up", [L, D, 2 * cfg.DFF])
    W["fcw"] = din("fcw", [L, 128, 44, 3])
    W["fcb"] = din("fcb", [L, 128, 44])
    W["ffn_down"] = din("ffn_down", [L, cfg.DFF, D])
    C = {}
    C["cosd"] = din("cosd", [64, S], BF16); C["sind"] = din("sind", [64, S], BF16)
    C["cosl"] = din("cosl", [128, S], BF16); C["sinl"] = din("sinl", [128, S], BF16)
    C["permd"] = din("permd", [64, 64], BF16); C["perml"] = din("perml", [128, 128], BF16)
    C["ident_bf"] = din("ident_bf", [128, 128], BF16); C["ident_f"] = din("ident_f", [128, 128])
    C["dmask"] = din("dmask", [4, 128, 512], BF16); C["lmask"] = din("lmask", [128, 256], BF16)
    C["triu"] = din("triu", [128, 128]); C["negm"] = din("negm", [128, 128])
    C["sel"] = din("sel", [65, 64]); C["ones64"] = din("ones64", [64, 64])
    out = nc.dram_tensor("out", [S, D], F32, kind="ExternalOutput").ap()

    qTd = dscr("qTd", [HD, 64, S], BF16); kTd = dscr("kTd", [HD, 64, S], BF16)
    vd = dscr("vd", [S, HD * 65], BF16)
    qTl = dscr("qTl", [HL // 2, 128, S], BF16); kTl = dscr("kTl", [HL // 2, 128, S], BF16)
    vTl = dscr("vTl", [HL // 2, 128, S], BF16)
    xbcT = dscr("xbcT", [cfg.NXBC, 128, S], BF16)
    zs = dscr("zs", [S, HS * 64], BF16)
    dts = dscr("dts", [S, HS], F32)
    mixT = dscr("mixT", [cfg.MIXC, 128, S], BF16)
    xa = dscr("xa", [S, D], F32)
    xb = dscr("xb", [S, D], F32)

    def sb(name, shape, dt=F32):
        return es.enter_context(nc.sbuf_tensor(name, list(shape), dt))

    ident_bf = sb("sb_ident_bf", [128, 128], BF16)
    ident_f = sb("sb_ident_f", [128, 128])
    cB = Buf("consts")
    cds = sc.dsem()
    sc.dma(cds, [(ident_bf[:], C["ident_bf"]), (ident_f[:], C["ident_f"])], writes=[cB])

    k = K()
    k.nc, k.sc, k.cfg, k.W, k.C, k.cB = nc, sc, cfg, W, C, cB
    k.ident_bf, k.ident_f = ident_bf, ident_f
    k.scr = dict(qTd=qTd, kTd=kTd, vd=vd, qTl=qTl, kTl=kTl, vTl=vTl, xbcT=xbcT, zs=zs, dts=dts,
                 mixT=mixT, xa=xa, xb=xb)

    stages = cfg.stages if hasattr(cfg, "stages") else "12345"
    for l in range(L):
        x_src = x_in if l == 0 else xb
        x_dst = out if l == L - 1 else xb
        if "1" in stages:
            phase1(k, l, x_src)
            sc.barrier()
        if "2" in stages:
            phase2(k, l)
            sc.barrier()
        if "3" in stages:
            phase3(k, l)
            sc.barrier()
        if "4" in stages:
            phase4(k, l)
            sc.barrier()
        if "5" in stages:
            phase5a(k, l, x_src)
            sc.barrier()
            phase5b(k, l, x_dst)
            sc.barrier()
    sc.barrier()
    es.close()
    return nc


def load_weights_bf16(k, ps, dst, src_rows, ncols, gain, stg, stgB, ds, kchunks, col_chunk=512):
    sc = k.sc
    i = 0
    for kc in range(kchunks):
        for c0 in range(0, ncols, col_chunk):
            cw = min(col_chunk, ncols - c0)
            sl = i % len(stg)
            sc.dma(ds[sl], [(stg[sl][:, :cw], src_rows[kc * 128:(kc + 1) * 128, c0:c0 + cw])], writes=[stgB[sl]])
            eng = "dve" if i % 2 == 0 else "pool"
            if gain is not None:
                if i % 2 == 0:
                    sc.op("dve", lambda h, sl=sl, cw=cw, kc=kc, c0=c0: h.tensor_scalar(
                        out=dst[:, kc, c0:c0 + cw], in0=stg[sl][:, :cw], scalar1=gain[:, kc:kc + 1], scalar2=None,
                        op0=ALU.mult), reads=[stgB[sl], k.gB], writes=[k.wB])
                else:
                    sc.op("act", lambda h, sl=sl, cw=cw, kc=kc, c0=c0: h.activation(
                        out=dst[:, kc, c0:c0 + cw], in_=stg[sl][:, :cw], func=AF.Copy, scale=gain[:, kc:kc + 1]),
                        reads=[stgB[sl], k.gB], writes=[k.wB])
            else:
                sc.op(eng, lambda h, sl=sl, cw=cw, kc=kc, c0=c0: h.tensor_copy(
                    out=dst[:, kc, c0:c0 + cw], in_=stg[sl][:, :cw]), reads=[stgB[sl]], writes=[k.wB])
            i += 1


def rms_rstd(k, ps, ss, rstd, n, width, ssB, rsB):
    sc = k.sc
    sc.op("act", lambda h: h.activation(out=rstd[:, :n], in_=ss[:, :n], func=AF.Ln, scale=1.0 / width, bias=EPS),
          reads=[ssB], writes=[rsB])
    sc.op("act", lambda h: h.activation(out=rstd[:, :n], in_=rstd[:, :n], func=AF.Exp, scale=-0.5), reads=[rsB], writes=[rsB])


def phase1(k, l, x_src):
    nc, sc, cfg, W, C = k.nc, k.sc, k.cfg, k.W, k.C
    S, HD, HL, G, HS, NB = cfg.S, cfg.HD, cfg.HL, cfg.G, cfg.HS, cfg.NB
    NX = cfg.NXBC
    with ExitStack() as ps:
        def sb(name, shape, dt=F32):
            return ps.enter_context(nc.sbuf_tensor("L%d_%s" % (l, name), list(shape), dt))

        def pt(name, shape, dt=F32):
            return ps.enter_context(nc.psum_tensor("L%d_%s" % (l, name), list(shape), dt))

        wbf = sb("p1_w", [128, 8, cfg.NCOL], BF16)
        gain = sb("p1_g", [128, 8])
        stg = [sb("p1_stg%d" % i, [128, 512]) for i in range(3)]
        stgB = [Buf() for _ in range(3)]
        ds = [sc.dsem() for _ in range(3)]
        k.gB, k.wB = Buf("gain"), Buf("w")
        dsm = sc.dsem()
        permd = sb("p1_permd", [64, 64], BF16)
        perml = sb("p1_perml", [128, 128], BF16)
        cw = sb("p1_cw", [128, 7, 4])
        cb = sb("p1_cb", [128, 7])
        sc.dma(dsm, [(gain[:], W["g_premix"][l]), (permd[:], C["permd"]), (perml[:], C["perml"]),
                     (cw[:], W["cw"][l]), (cb[:], W["cb"][l])], writes=[k.gB])
        CUT = int(os.environ.get("KCUT", "99"))
        if CUT >= 1:
            load_weights_bf16(k, ps, wbf, W["w_in"][l], cfg.NCOL, gain, stg, stgB, ds, 8)

        xt = sb("p1_x", [128, 4, D])
        xtB = [Buf() for _ in range(4)]
        xds = [sc.dsem() for _ in range(4)]
        junk = sb("p1_junk", [128, D], BF16)
        junkB = Buf()
        ss = sb("p1_ss", [128, 4]); ssB = Buf()
        rstd = sb("p1_rstd", [128, 4]); rsB = Buf()
        xs = sb("p1_xs", [128, 4, D], BF16); xsB = Buf()
        hT = sb("p1_hT", [128, 8, TB], BF16); hTB = Buf()
        psT = [pt("p1_psT%d" % i, [128, 2, TB], BF16) for i in range(2)]
        psTB = [PBuf() for _ in range(2)]
        psA = [pt("p1_psA%d" % i, [128, TB]) for i in range(3)]
        psAB = [PBuf() for _ in range(3)]
        psR = [pt("p1_psR%d" % i, [128, TB]) for i in range(2)]
        psRB = [PBuf() for _ in range(2)]
        psB = [pt("p1_psB%d" % i, [128, 512]) for i in range(1)]
        psBB = [PBuf() for _ in range(1)]
        tabs = {}
        for nm, rows in (("cosd", 64), ("sind", 64), ("cosl", 128), ("sinl", 128)):
            tabs[nm] = [sb("p1_%s%d" % (nm, i), [rows, TB], BF16) for i in range(2)]
        tabB = [Buf() for _ in range(2)]
        tds = [sc.dsem() for _ in range(2)]
        xbf = [sb("p1_xbf%d" % i, [128, TB], BF16) for i in range(4)]
        xbfB = [Buf() for _ in range(4)]
        t1 = [sb("p1_t1%d" % i, [128, TB]) for i in range(2)]
        t1B = [Buf() for _ in range(2)]
        t2 = [sb("p1_t2%d" % i, [128, TB]) for i in range(2)]
        t2B = [Buf() for _ in range(2)]
        qk_st = [sb("p1_qk%d" % i, [64, 2 * HD, TB], BF16) for i in range(2)]
        l_st = [sb("p1_l%d" % i, [128, 3 * (HL // 2), TB], BF16) for i in range(2)]
        xbc_st = [sb("p1_xbc%d" % i, [128, NX, TB], BF16) for i in range(2)]
        v_st = [sb("p1_v%d" % i, [128, 4, HD, 65], BF16) for i in range(2)]
        z_st = [sb("p1_z%d" % i, [128, 4, HS * 64], BF16) for i in range(2)]
        dt_st = [sb("p1_dt%d" % i, [128, 4, HS]) for i in range(2)]
        from collections import defaultdict
        stD = [defaultdict(Buf) for _ in range(2)]
        sds = [sc.dsem() for _ in range(2)]
        u = sb("p1_u", [128, NX, 3 + TB]); uB = [Buf() for _ in range(NX)]
        yc = [sb("p1_yc%d" % i, [128, TB]) for i in range(3)]
        ycB = [Buf() for _ in range(3)]
        sc.op("pool", lambda h: h.memset(u[:], 0.0), writes=uB)
        for i in range(2):
            sc.op("pool", lambda h, i=i: h.memset(v_st[i][:], 1.0), writes=[stD[i][("v", n)] for n in range(4)])

        ci = 0
        for t in range(NB):
            if CUT < 2:
                break
            pb = t % 2
            tok0 = t * TB

            def prefetch(tt):
                tk = tt * TB
                for n in range(4):
                    sc.dma(xds[n], [(xt[:, n, :], x_src[tk + n * 128:tk + (n + 1) * 128, :])], writes=[xtB[n]])
                sc.dma(tds[tt % 2], [(tabs[nm][tt % 2][:], C[nm][:, tk:tk + TB]) for nm in ("cosd", "sind", "cosl", "sinl")],
                       writes=[tabB[tt % 2]])

            if t == 0:
                prefetch(0)
            for n in range(4):
                sc.op("act", lambda h, n=n: h.activation(out=junk[:], in_=xt[:, n, :], func=AF.Square,
                                                         accum_out=ss[:, n:n + 1]),
                      reads=[xtB[n]], writes=[junkB, ssB])
            if CUT < 3:
                continue
            rms_rstd(k, ps, ss, rstd, 4, D, ssB, rsB)
            for n in range(4):
                if n % 2 == 0:
                    sc.op("dve", lambda h, n=n: h.tensor_scalar(
                        out=xs[:, n, :], in0=xt[:, n, :], scalar1=rstd[:, n:n + 1], scalar2=None, op0=ALU.mult),
                        reads=[xtB[n], rsB], writes=[xsB])
                else:
                    sc.op("act", lambda h, n=n: h.activation(out=xs[:, n, :], in_=xt[:, n, :], func=AF.Copy, scale=rstd[:, n:n + 1]),
                          reads=[xtB[n], rsB], writes=[xsB])
            if CUT < 4:
                continue
            for cp in range(4):
                pi = cp % 2
                for c2 in range(2):
                    c = cp * 2 + c2
                    for n in range(4):
                        last = (c2 == 1 and n == 3)
                        sc.op("pe", lambda h, c=c, c2=c2, n=n, pi=pi: h.transpose(
                            out=psT[pi][:, c2, n * 128:(n + 1) * 128], in_=xs[:, n, c * 128:(c + 1) * 128],
                            identity=k.ident_bf[:]), reads=[xsB, k.cB], writes=[psTB[pi]], inc=last)
                if cp % 2 == 0:
                    sc.op("act", lambda h, cp=cp, pi=pi: h.copy(out=hT[:, 2 * cp:2 * cp + 2, :], in_=psT[pi][:]),
                          reads=[psTB[pi]], writes=[hTB])
                else:
                    sc.op("dve", lambda h, cp=cp, pi=pi: h.tensor_copy(out=hT[:, 2 * cp:2 * cp + 2, :], in_=psT[pi][:]),
                          reads=[psTB[pi]], writes=[hTB])
            if CUT < 5:
                continue
            if t + 1 < NB:
                prefetch(t + 1)
            if t > 0:
                sc.op("pool", lambda h: h.tensor_copy(out=u[:, :, 0:3], in_=u[:, :, TB:TB + 3]), reads=uB, writes=uB)
            fm = cfg.fm

            def s_pe(i):
                nm, idx, off, M = fm[i]
                a = i % 3
                for kc in range(8):
                    sc.op("pe", lambda h: h.matmul(psA[a][0:M, :], lhsT=wbf[:, kc, off:off + M], rhs=hT[:, kc, :],
                                                   start=(kc == 0), stop=(kc == 7)), reads=[k.wB, hTB], writes=[psAB[a]], inc=(kc == 7))

            def s_copy(i):
                nm, idx, off, M = fm[i]
                a = i % 3
                if nm in ("dq", "dk", "lq", "lk"):
                    x4 = i % 4
                    sc.op("act", lambda h: h.copy(out=xbf[x4][0:M, :], in_=psA[a][0:M, :]), reads=[psAB[a]], writes=[xbfB[x4]])
                elif nm == "lv":
                    sc.op("act", lambda h: h.copy(out=l_st[pb][:, 2 * (HL // 2) + idx, :], in_=psA[a][:]),
                          reads=[psAB[a]], writes=[stD[pb][("lv", idx)]])
                else:
                    xi = {"xs": 0, "B": HS // 2, "C": HS // 2 + G}[nm] + idx
                    sc.op("act", lambda h: h.copy(out=u[:, xi, 3:3 + TB], in_=psA[a][:]), reads=[psAB[a]], writes=[uB[xi]])

            def s_mid(i):
                nm, idx, off, M = fm[i]
                if nm in ("dq", "dk", "lq", "lk"):
                    perm = permd if M == 64 else perml
                    x4 = i % 4
                    r2 = i % 2
                    sc.op("pe", lambda h: h.matmul(psR[r2][0:M, :], lhsT=perm[0:M, 0:M], rhs=xbf[x4][0:M, :], start=True, stop=True),
                          reads=[xbfB[x4], k.gB], writes=[psRB[r2]])
                elif nm != "lv":
                    xi = {"xs": 0, "B": HS // 2, "C": HS // 2 + G}[nm] + idx
                    gi = {"xs": 0, "B": 3, "C": 5}[nm] + idx
                    y3 = i % 3
                    sc.op("dve", lambda h: h.tensor_scalar(out=yc[y3][:], in0=u[:, xi, 3:3 + TB], scalar1=cw[:, gi, 3:4], scalar2=cb[:, gi:gi + 1],
                                                           op0=ALU.mult, op1=ALU.add), reads=[uB[xi], k.gB], writes=[ycB[y3]])

            def s_ew(i):
                nm, idx, off, M = fm[i]
                if nm in ("dq", "dk", "lq", "lk"):
                    cosn, sinn = ("cosd", "sind") if M == 64 else ("cosl", "sinl")
                    x4 = i % 4
                    r2 = i % 2
                    sc.op("dve", lambda h: h.tensor_tensor(out=t1[r2][0:M, :], in0=xbf[x4][0:M, :], in1=tabs[cosn][pb][0:M, :], op=ALU.mult),
                          reads=[xbfB[x4], tabB[pb]], writes=[t1B[r2]])
                    sc.op("dve", lambda h: h.tensor_tensor(out=t2[r2][0:M, :], in0=psR[r2][0:M, :], in1=tabs[sinn][pb][0:M, :], op=ALU.mult),
                          reads=[psRB[r2], tabB[pb]], writes=[t2B[r2]])
                elif nm != "lv":
                    xi = {"xs": 0, "B": HS // 2, "C": HS // 2 + G}[nm] + idx
                    gi = {"xs": 0, "B": 3, "C": 5}[nm] + idx
                    y3 = i % 3
                    for kk in range(3):
                        sc.op("dve", lambda h: h.scalar_tensor_tensor(out=yc[y3][:], in0=u[:, xi, kk:kk + TB], scalar=cw[:, gi, kk:kk + 1],
                                                                      in1=yc[y3][:], op0=ALU.mult, op1=ALU.add),
                              reads=[uB[xi], ycB[y3], k.gB], writes=[ycB[y3]])

            def s_fin(i):
                nm, idx, off, M = fm[i]
                if nm in ("dq", "dk", "lq", "lk"):
                    r2 = i % 2
                    if nm in ("dq", "dk"):
                        dst = qk_st[pb][:, (0 if nm == "dq" else HD) + idx, :]
                    else:
                        dst = l_st[pb][:, (0 if nm == "lq" else HL // 2) + idx, :]
                    sc.op("pool", lambda h: h.tensor_tensor(out=dst, in0=t1[r2][0:M, :], in1=t2[r2][0:M, :], op=ALU.add),
                          reads=[t1B[r2], t2B[r2]], writes=[stD[pb][(nm, idx)]])
                elif nm != "lv":
                    xi = {"xs": 0, "B": HS // 2, "C": HS // 2 + G}[nm] + idx
                    y3 = i % 3
                    sc.op("act", lambda h: h.activation(out=xbc_st[pb][:, xi, :], in_=yc[y3][:], func=AF.Silu),
                          reads=[ycB[y3]], writes=[stD[pb][("xbc", xi)]])

            skewed(len(fm), [s_pe, s_copy, s_mid, s_ew, s_fin])
            if CUT < 6:
                continue
            tmb = [psB[0], psR[0], psR[1]]
            tmB = [psBB[0], psRB[0], psRB[1]]
            for n in range(4):
                a = (2 * n) % 3
                for kc in range(8):
                    sc.op("pe", lambda h, kc=kc, n=n, a=a: h.matmul(
                        tmb[a][:, 0:HD * 64], lhsT=hT[:, kc, n * 128:(n + 1) * 128],
                        rhs=wbf[:, kc, cfg.off_dv:cfg.off_dv + HD * 64], start=(kc == 0), stop=(kc == 7)),
                        reads=[k.wB, hTB], writes=[tmB[a]], inc=(kc == 7))
                sc.op("act", lambda h, n=n, a=a: h.copy(
                    out=v_st[pb][:, n, :, 0:64], in_=tmb[a][:, 0:HD * 64].rearrange("p (h e) -> p h e", e=64)),
                    reads=[tmB[a]], writes=[stD[pb][("v", n)]])
                nz = HS * 64 + HS
                a = (2 * n + 1) % 3
                for kc in range(8):
                    sc.op("pe", lambda h, kc=kc, n=n, a=a: h.matmul(
                        tmb[a][:, 0:nz], lhsT=hT[:, kc, n * 128:(n + 1) * 128],
                        rhs=wbf[:, kc, cfg.off_z:cfg.off_z + nz], start=(kc == 0), stop=(kc == 7)),
                        reads=[k.wB, hTB], writes=[tmB[a]], inc=(kc == 7))
                sc.op("dve", lambda h, n=n, a=a: h.tensor_copy(out=z_st[pb][:, n, :], in_=tmb[a][:, 0:HS * 64]),
                      reads=[tmB[a]], writes=[stD[pb][("z", n)]])
                sc.op("dve", lambda h, n=n, a=a: h.tensor_copy(out=dt_st[pb][:, n, :], in_=tmb[a][:, HS * 64:nz]),
                      reads=[tmB[a]], writes=[stD[pb][("dt", n)]])
            if CUT < 7:
                continue
            scr = k.scr
            pairs = [
                (scr["qTd"][:, :, tok0:tok0 + TB].rearrange("h p s -> p h s"), qk_st[pb][:, 0:HD, :]),
                (scr["kTd"][:, :, tok0:tok0 + TB].rearrange("h p s -> p h s"), qk_st[pb][:, HD:2 * HD, :]),
                (scr["qTl"][:, :, tok0:tok0 + TB].rearrange("h p s -> p h s"), l_st[pb][:, 0:HL // 2, :]),
                (scr["kTl"][:, :, tok0:tok0 + TB].rearrange("h p s -> p h s"), l_st[pb][:, HL // 2:HL, :]),
                (scr["vTl"][:, :, tok0:tok0 + TB].rearrange("h p s -> p h s"), l_st[pb][:, HL:3 * (HL // 2), :]),
                (scr["xbcT"][:, :, tok0:tok0 + TB].rearrange("h p s -> p h s"), xbc_st[pb][:]),
                (scr["vd"][tok0:tok0 + TB, :].rearrange("(n p) f -> p n f", p=128),
                 v_st[pb][:].rearrange("p n h e -> p n (h e)")),
                (scr["zs"][tok0:tok0 + TB, :].rearrange("(n p) f -> p n f", p=128), z_st[pb][:]),
                (scr["dts"][tok0:tok0 + TB, :].rearrange("(n p) f -> p n f", p=128), dt_st[pb][:]),
            ]
            sc.dma(sds[pb], pairs, reads=list(stD[pb].values()))


def skewed(n, stages):
    ns = len(stages)
    for kk in range(n + ns - 1):
        for si, fn in enumerate(stages):
            i = kk - si
            if 0 <= i < n:
                fn(i)


def _mk(k, l, ps):
    nc = k.nc

    def sb(name, shape, dt=F32):
        return ps.enter_context(nc.sbuf_tensor("L%d_%s" % (l, name), list(shape), dt))

    def pt(name, shape, dt=F32):
        return ps.enter_context(nc.psum_tensor("L%d_%s" % (l, name), list(shape), dt))

    return sb, pt


def phase2(k, l):
    nc, sc, cfg, W, C = k.nc, k.sc, k.cfg, k.W, k.C
    S, HD, NB = cfg.S, cfg.HD, cfg.NB
    lam_init = 0.8 - 0.6 * math.exp(-0.3 * l)
    scr = k.scr
    with ExitStack() as ps:
        sb, pt = _mk(k, l, ps)
        kT = sb("p2_kT", [128, HD // 2, S], BF16)
        vfull = sb("p2_v", [128, (S // 128) * HD * 65 + 128], BF16)
        v = vfull[:, 0:(S // 128) * HD * 65].rearrange("p (n f) -> p n f", f=HD * 65)
        dmask = sb("p2_dmask", [128, 4, 512], BF16)
        sel = sb("p2_sel", [65, 64]); ones64 = sb("p2_ones", [64, 64])
        hg = sb("p2_hg", [64, 1]); lam = sb("p2_lam", [64, 128])
        lt = sb("p2_lt", [64, 64]); lsum = sb("p2_ls", [64, 2]); neg_lam = sb("p2_nl", [64, 1]); gsc = sb("p2_gsc", [64, 1])
        cB = Buf(); d0 = sc.dsem()
        sc.dma(d0, [(kT[:, a2, :], scr["kTd"][2 * a2:2 * a2 + 2].rearrange("b p s -> (b p) s")) for a2 in range(HD // 2)] +
               [(v, scr["vd"].rearrange("(n p) f -> p n f", p=128)),
                (dmask[:], C["dmask"].rearrange("d p q -> p d q")), (sel[:], C["sel"]), (ones64[:], C["ones64"]),
                (hg[:], W["hgain"][l]), (lam[:], W["lam"][l:l + 1, :].broadcast_to([64, 128]))], writes=[cB])
        pB = Buf()
        sc.op("pool", lambda h: h.memset(vfull[:, (S // 128) * HD * 65:], 0.0), writes=[cB])
        sc.op("dve", lambda h: h.tensor_tensor(out=lt[:, 0:32], in0=lam[:, 0:32], in1=lam[:, 32:64], op=ALU.mult), reads=[cB], writes=[pB])
        sc.op("dve", lambda h: h.tensor_tensor(out=lt[:, 32:64], in0=lam[:, 64:96], in1=lam[:, 96:128], op=ALU.mult), reads=[cB], writes=[pB])
        sc.op("dve", lambda h: h.tensor_reduce(out=lsum[:], in_=lt[:].rearrange("p (a b) -> p a b", b=32), axis=AX.X, op=ALU.add),
              reads=[pB], writes=[pB])
        sc.op("act", lambda h: h.activation(out=lsum[:], in_=lsum[:], func=AF.Exp), reads=[pB], writes=[pB])
        sc.op("dve", lambda h: h.tensor_tensor(out=neg_lam[:], in0=lsum[:, 1:2], in1=lsum[:, 0:1], op=ALU.subtract), reads=[pB], writes=[pB])
        sc.op("dve", lambda h: h.tensor_scalar(out=neg_lam[:], in0=neg_lam[:], scalar1=-lam_init, scalar2=None, op0=ALU.add), reads=[pB], writes=[pB])
        sc.op("dve", lambda h: h.tensor_scalar(out=gsc[:], in0=hg[:], scalar1=(1.0 - lam_init), scalar2=None, op0=ALU.mult), reads=[cB, pB], writes=[pB])

        qT = [sb("p2_q%d" % i, [128, HD * 2, TB], BF16) for i in range(2)]
        qB = [Buf() for _ in range(2)]; qds = [sc.dsem() for _ in range(2)]
        for i in range(2):
            sc.op("pool", lambda h, i=i: h.memset(qT[i][:], 0.0), writes=[qB[i]])
        psS = [pt("p2_psS%d" % i, [128, 2, 512]) for i in range(2)]; psSB = [PBuf() for _ in range(2)]
        psO = [pt("p2_psO%d" % m, [128, 512]) for m in range(2)]
        psOB = [PBuf() for m in range(2)]
        psE = [pt("p2_psE%d" % i, [128, 512]) for i in range(2)]; psEB = [PBuf() for _ in range(2)]
        pT = [sb("p2_pT%d" % i, [128, 2, 512], BF16) for i in range(3)]; pTB = [Buf() for _ in range(3)]
        X = [sb("p2_X%d" % m, [65, 512]) for m in range(2)]; XB = [Buf() for _ in range(2)]
        r = [sb("p2_r%d" % m, [64, 512]) for m in range(2)]; rB = [Buf() for _ in range(2)]
        o = sb("p2_o", [64, 512]); oB = Buf()
        sq = sb("p2_sq", [64, 512]); sqB = Buf()
        rs = sb("p2_rs", [64, 512]); rsB = Buf()
        ost = [sb("p2_ost%d" % i, [64, HD, 512], BF16) for i in range(2)]
        ostB = [Buf() for _ in range(2)]; ods = [sc.dsem() for _ in range(2)]
        scale = 32 ** -0.5
        cnt = 0
        pending = []

        def flush(n=None):
            c = len(pending) if n is None else min(n, len(pending))
            for _ in range(c):
                pending.pop(0)()

        for t in range(NB):
            tok0 = t * TB
            qb = t % 2
            sc.dma(qds[qb], [(qT[qb][32 * ((h2 % 2) * 2 + m2):32 * ((h2 % 2) * 2 + m2) + 32, h2 * 2 + m2, :],
                              scr["qTd"][h2, 32 * m2:32 * m2 + 32, tok0:tok0 + TB]) for h2 in range(HD) for m2 in range(2)],
                   writes=[qB[qb]])
            for hh in range(HD):
                nk = 4 * t + 4

                def qk(i):
                    sl = (cnt + i) % 2
                    for m in range(2):
                        sc.op("pe", lambda h: h.matmul(psS[sl][:, m, :], lhsT=kT[:, hh // 2, i * 128:(i + 1) * 128],
                                                       rhs=qT[qb][:, hh * 2 + m, :], start=True, stop=True),
                              reads=[cB, qB[qb]], writes=[psSB[sl]], inc=(m == 1))

                qk(0)
                for i in range(nk):
                    sl = (cnt + i) % 2
                    p3 = (cnt + i) % 3
                    if i + 1 < nk:
                        qk(i + 1)
                    sc.op("act", lambda h: h.activation(out=pT[p3][:], in_=psS[sl][:], func=AF.Exp, scale=scale),
                          reads=[psSB[sl]], writes=[pTB[p3]])
                    if i >= 4 * t:
                        di = i - 4 * t
                        sc.op("dve" if i % 2 == 0 else "pool", lambda h: h.tensor_tensor(
                            out=pT[p3][:], in0=pT[p3][:], in1=dmask[:, di:di + 1, :].broadcast_to([128, 2, 512]), op=ALU.mult),
                            reads=[cB, pTB[p3]], writes=[pTB[p3]])
                    for m in range(2):
                        sc.op("pe", lambda h: h.matmul(psO[m][:, :], lhsT=vfull[:, (i * HD + hh) * 65:(i * HD + hh) * 65 + 128], rhs=pT[p3][:, m, :],
                                                       start=(i == 0), stop=(i == nk - 1)),
                              reads=[cB, pTB[p3]], writes=[psOB[m]], inc=(i == nk - 1))
                    if i == 0:
                        npend0 = len(pending)
                    if i >= 1:
                        if i < nk - 1:
                            target_left = npend0 - (npend0 * i) // (nk - 1)
                            flush(max(0, len(pending) - target_left))
                        else:
                            flush()
                cnt += nk
                flush()
                sc.op("dve", lambda h: h.tensor_copy(out=X[0][:], in_=psO[0][0:65, :]), reads=[psOB[0]], writes=[XB[0]])
                sc.op("dve", lambda h: h.tensor_copy(out=X[1][:], in_=psO[1][0:65, :]), reads=[psOB[1]], writes=[XB[1]])

                def E(eng, fn, reads, writes):
                    pending.append(lambda: sc.op(eng, fn, reads=reads, writes=writes))

                for m in range(2):
                    E("pe", lambda h, m=m: h.matmul(psE[m][0:64, :], lhsT=sel[:], rhs=X[m][:], start=True, stop=True),
                      [cB, XB[m]], [psEB[m]])
                    E("dve", lambda h, m=m: h.reciprocal(out=r[m][:], in_=psE[m][0:64, :]), [psEB[m]], [rB[m]])
                    E("dve" if m == 0 else "pool", lambda h, m=m: h.tensor_tensor(
                        out=r[m][:], in0=X[m][0:64, :], in1=r[m][:], op=ALU.mult), [XB[m], rB[m]], [rB[m]])
                E("dve", lambda h: h.scalar_tensor_tensor(out=o[:], in0=r[1][:], scalar=neg_lam[:, 0:1], in1=r[0][:],
                                                          op0=ALU.mult, op1=ALU.add), [rB[0], rB[1], pB], [oB])
                E("pool", lambda h: h.tensor_tensor(out=sq[:], in0=o[:], in1=o[:], op=ALU.mult), [oB], [sqB])
                E("pe", lambda h: h.matmul(psE[0][0:64, :], lhsT=ones64[:], rhs=sq[:], start=True, stop=True), [cB, sqB], [psEB[0]])
                E("act", lambda h: h.activation(out=rs[:], in_=psE[0][0:64, :], func=AF.Ln, scale=1.0 / 64, bias=EPS), [psEB[0]], [rsB])
                E("act", lambda h: h.activation(out=rs[:], in_=rs[:], func=AF.Exp, scale=-0.5), [rsB], [rsB])
                E("dve", lambda h, qb=qb, hh=hh: h.scalar_tensor_tensor(out=ost[qb][:, hh, :], in0=o[:], scalar=gsc[:, 0:1], in1=rs[:],
                                                                       op0=ALU.mult, op1=ALU.mult), [oB, rsB, pB], [ostB[qb]])
            pending.append(lambda qb=qb, tok0=tok0: sc.dma(
                ods[qb], [(scr["mixT"][hh2 // 2, (hh2 % 2) * 64:(hh2 % 2) * 64 + 64, tok0:tok0 + TB], ost[qb][:, hh2, :])
                          for hh2 in range(HD)], reads=[ostB[qb]]))
        flush()


def phase3(k, l):
    nc, sc, cfg, W, C = k.nc, k.sc, k.cfg, k.W, k.C
    S, HD, HL = cfg.S, cfg.HD, cfg.HL
    scr = k.scr
    SBK = 2048
    NSB = S // SBK
    NC3 = HL // 2
    PATS = (1, 4, 16)
    scale = 64 ** -0.5
    with ExitStack() as ps:
        sb, pt = _mk(k, l, ps)
        qP = [sb("p3_q%d" % i, [128, NC3, SBK], BF16) for i in range(2)]; qB = Buf()
        for i in range(2):
            sc.op("pool", lambda h, i=i: h.memset(qP[i][:], 0.0), writes=[qB])
        kL = [sb("p3_k%d" % i, [128, NC3, SBK], BF16) for i in range(2)]; kB = [Buf() for _ in range(2)]
        vT = sb("p3_vT", [128, NC3, SBK], BF16); vTB = Buf()
        lds = [sc.dsem() for _ in range(3)]
        vt = [[sb("p3_vt%d_%d" % (pi, i), [128, 16, HL * 65], BF16) for i in range(2)] for pi in range(3)]
        vtB = [[Buf() for i in range(2)] for pi in range(3)]
        lmask = sb("p3_lmask", [128, 256], BF16); sel = sb("p3_sel", [65, 64])
        cB = Buf(); d0 = sc.dsem()
        sc.dma(d0, [(lmask[:], C["lmask"]), (sel[:], C["sel"])], writes=[cB])
        for pi in range(3):
            for i in range(2):
                sc.op("pool", lambda h: h.memset(vt[pi][i][:], 1.0), writes=[vtB[pi][i]])
        acc = [sb("p3_acc%d" % i, [65, SBK]) for i in range(2)]; accB = [Buf() for _ in range(2)]
        pT = [sb("p3_pT%d" % i, [128, 4, 256], BF16) for i in range(3)]; pTB = [Buf() for _ in range(3)]
        rr = sb("p3_rr", [64, 512]); rrB = Buf()
        ost = [sb("p3_ost%d" % i, [64, SBK], BF16) for i in range(2)]; ostB = [Buf() for _ in range(2)]
        ods = [sc.dsem() for _ in range(2)]
        psV = pt("p3_psV", [128, NC3, 128], BF16); psVB = PBuf()
        psS = [pt("p3_psS%d" % i, [128, 4, 256]) for i in range(2)]; psSB = [PBuf() for _ in range(2)]
        psO = [pt("p3_psO%d" % i, [128, 4, 128]) for i in range(2)]; psOB = [PBuf() for _ in range(2)]
        psE = pt("p3_psE", [128, 512]); psEB = PBuf()
        gi = 0
        hi = 0
        for u in range(NSB):
            ub = u % 2
            t0 = u * SBK
            sc.dma(lds[0], [(qP[par][64 * par:64 * par + 64, :, :],
                             scr["qTl"][:, 64 * par:64 * par + 64, t0:t0 + SBK].rearrange("c p s -> p c s")) for par in range(2)],
                   writes=[qB])
            sc.dma(lds[1], [(kL[ub][:], scr["kTl"][:, :, t0:t0 + SBK].rearrange("c p s -> p c s"))], writes=[kB[ub]])
            sc.dma(lds[2], [(vT[:], scr["vTl"][:, :, t0:t0 + SBK].rearrange("c p s -> p c s"))], writes=[vTB])
            vi = 0
            for pi, d in enumerate(PATS):
                for ti in range(16):
                    r_, nbl = ti % d, ti // d
                    off = 128 * d * nbl + r_
                    for c in range(NC3):
                        sc.op("pe", lambda h: h.transpose(out=psV[:, c, :], in_=vT[:, c, off:off + 127 * d + 1:d],
                                                          identity=k.ident_bf[:]),
                              reads=[vTB, k.cB], writes=[psVB], inc=(c == NC3 - 1))
                    dst = vt[pi][ub][:, ti, :].rearrange("p (h e) -> p h e", e=65)[:, :, 0:64]
                    src = psV[:].rearrange("p c (h e) -> p (c h) e", e=64)
                    if vi % 2 == 0:
                        sc.op("act", lambda h: h.copy(out=dst, in_=src), reads=[psVB], writes=[vtB[pi][ub]])
                    else:
                        sc.op("dve", lambda h: h.tensor_copy(out=dst, in_=src), reads=[psVB], writes=[vtB[pi][ub]])
                    vi += 1
            for hh in range(HL):
                c = hh // 2
                p0 = 64 * (hh % 2)
                ab = hi % 2
                hi += 1
                groups = [(pi, d, tg) for pi, d in enumerate(PATS) for tg in range(4)]
                gbase = gi
                gi += len(groups)

                def tiles_of(g):
                    pi, d, tg = groups[g]
                    tiles = []
                    for q4 in range(4):
                        ti = tg * 4 + q4
                        r_, nbl = ti % d, ti // d
                        off = 128 * d * nbl + r_
                        if nbl > 0:
                            prev = (ub, off - 128 * d, ti - d)
                        elif u > 0:
                            nbp = 16 // d - 1
                            prev = (1 - ub, 128 * d * nbp + r_, r_ + d * nbp)
                        else:
                            prev = None
                        tiles.append((ti, off, prev))
                    return tiles

                def g_qk(g):
                    pi, d, tg = groups[g]
                    sl = (gbase + g) % 2
                    for q4, (ti, off, prev) in enumerate(tiles_of(g)):
                        Q = qP[hh % 2][:, c, off:off + 127 * d + 1:d]
                        Kc = kL[ub][:, c, off:off + 127 * d + 1:d]
                        if prev is not None:
                            Kp = kL[prev[0]][:, c, prev[1]:prev[1] + 127 * d + 1:d]
                            rd = [qB, kB[ub], kB[prev[0]]]
                        else:
                            Kp = Kc
                            rd = [qB, kB[ub]]
                        sc.op("pe", lambda h: h.matmul(psS[sl][:, q4, 0:128], lhsT=Kp, rhs=Q, start=True, stop=True),
                              reads=rd, writes=[psSB[sl]], inc=False)
                        sc.op("pe", lambda h: h.matmul(psS[sl][:, q4, 128:256], lhsT=Kc, rhs=Q, start=True, stop=True),
                              reads=rd, writes=[psSB[sl]], inc=(q4 == 3))

                def g_exp(g):
                    sl = (gbase + g) % 2
                    p3 = (gbase + g) % 3
                    sc.op("act", lambda h: h.activation(out=pT[p3][:], in_=psS[sl][:], func=AF.Exp, scale=scale),
                          reads=[psSB[sl]], writes=[pTB[p3]])
                    sc.op("dve" if g % 2 == 0 else "pool", lambda h: h.tensor_tensor(
                        out=pT[p3][:], in0=pT[p3][:], in1=lmask[:, None, :].broadcast_to([128, 4, 256]), op=ALU.mult),
                        reads=[pTB[p3], cB], writes=[pTB[p3]])

                def g_pv(g):
                    pi, d, tg = groups[g]
                    sl = (gbase + g) % 2
                    p3 = (gbase + g) % 3
                    for q4, (ti, off, prev) in enumerate(tiles_of(g)):
                        if prev is not None:
                            sc.op("pe", lambda h: h.matmul(psO[sl][0:65, q4, :], lhsT=vt[pi][prev[0]][:, prev[2], hh * 65:(hh + 1) * 65],
                                                           rhs=pT[p3][:, q4, 0:128], start=True, stop=False),
                                  reads=[pTB[p3], vtB[pi][prev[0]]], writes=[psOB[sl]], inc=False)
                        sc.op("pe", lambda h: h.matmul(psO[sl][0:65, q4, :], lhsT=vt[pi][ub][:, ti, hh * 65:(hh + 1) * 65],
                                                       rhs=pT[p3][:, q4, 128:256], start=(prev is None), stop=True),
                              reads=[pTB[p3], vtB[pi][ub]], writes=[psOB[sl]], inc=(q4 == 3))

                def g_acc(g):
                    pi, d, tg = groups[g]
                    sl = (gbase + g) % 2
                    if d == 1:
                        dstv = acc[ab][:, tg * 512:(tg + 1) * 512].rearrange("p (a j) -> p a j", j=128)
                    elif d == 4:
                        dstv = acc[ab][:, tg * 512:(tg + 1) * 512].rearrange("p (j r) -> p r j", r=4)
                    else:
                        dstv = acc[ab][:].rearrange("p (j r) -> p r j", r=16)[:, tg * 4:tg * 4 + 4, :]
                    if pi == 0:
                        sc.op("dve", lambda h: h.tensor_copy(out=dstv, in_=psO[sl][0:65, :, :]), reads=[psOB[sl]], writes=[accB[ab]])
                    else:
                        sc.op("dve", lambda h: h.tensor_tensor(out=dstv, in0=dstv, in1=psO[sl][0:65, :, :], op=ALU.add),
                              reads=[psOB[sl], accB[ab]], writes=[accB[ab]])

                skewed(len(groups), [g_qk, g_exp, g_pv, g_acc])
                for sbk in range(4):
                    cs_ = slice(sbk * 512, (sbk + 1) * 512)
                    sc.op("pe", lambda h: h.matmul(psE[0:64, :], lhsT=sel[:], rhs=acc[ab][:, cs_], start=True, stop=True),
                          reads=[cB, accB[ab]], writes=[psEB])
                    sc.op("dve", lambda h: h.reciprocal(out=rr[:], in_=psE[0:64, :]), reads=[psEB], writes=[rrB])
                    sc.op("pool", lambda h: h.tensor_tensor(out=ost[ab][:, cs_], in0=acc[ab][0:64, cs_], in1=rr[:], op=ALU.mult),
                          reads=[rrB, accB[ab]], writes=[ostB[ab]])
                row = HD * 64 + hh * 64
                sc.dma(ods[ab], [(scr["mixT"][row // 128, (row % 128):(row % 128) + 64, t0:t0 + SBK], ost[ab][:])],
                       reads=[ostB[ab]])


def phase4(k, l):
    nc, sc, cfg, W, C = k.nc, k.sc, k.cfg, k.W, k.C
    S, HD, HL, G, HS, NB, NX = cfg.S, cfg.HD, cfg.HL, cfg.G, cfg.HS, cfg.NB, cfg.NXBC
    scr = k.scr
    XC = HS // 2
    HW = HS * 64
    with ExitStack() as ps:
        sb, pt = _mk(k, l, ps)
        triu = sb("p4_triu", [128, 128]); negm = sb("p4_negm", [128, 128])
        dtb = sb("p4_dtb", [128, HS]); alog = sb("p4_alog", [128, HS]); Dd = sb("p4_D", [128, HS]); gn = sb("p4_gn", [128, HW])
        negA = sb("p4_negA", [128, HS])
        cB = Buf(); d0 = sc.dsem()
        sc.dma(d0, [(triu[:], C["triu"]), (negm[:], C["negm"]),
                    (dtb[:], W["dt_bias"][l:l + 1, :].broadcast_to([128, HS])),
                    (alog[:], W["A_log"][l:l + 1, :].broadcast_to([128, HS])),
                    (Dd[:], W["ssd_D"][l:l + 1, :].broadcast_to([128, HS])),
                    (gn[:], W["ssd_norm"][l:l + 1, :].broadcast_to([128, HW]))], writes=[cB])
        pB = Buf()
        sc.op("act", lambda h: h.activation(out=negA[:], in_=alog[:], func=AF.Exp), reads=[cB], writes=[pB])
        sc.op("dve", lambda h: h.tensor_scalar(out=negA[:], in0=negA[:], scalar1=-1.0, scalar2=None, op0=ALU.mult), reads=[pB], writes=[pB])
        xb_ = [sb("p4_xb%d" % i, [128, NX, TB], BF16) for i in range(2)]
        z_ = [sb("p4_z%d" % i, [128, 4, HW], BF16) for i in range(2)]
        dt_ = [sb("p4_dt%d" % i, [128, 4, HS]) for i in range(2)]
        inB = [Buf() for _ in range(2)]; lds = [sc.dsem() for _ in range(2)]
        dtp = sb("p4_dtp", [128, 4, HS]); aa = sb("p4_a", [128, 4, HS]); dB = Buf()
        a_bc = sb("p4_abc", [128, HS, 128]); abB = Buf()
        cs_sb2 = [sb("p4_cs%d" % i, [128, HS]) for i in range(2)]; csl2 = [sb("p4_csl%d" % i, [128, HS]) for i in range(2)]; csB2 = [Buf() for _ in range(2)]
        arg = sb("p4_arg", [128, HS, 128]); argB = Buf()
        E = sb("p4_E", [128, HS, 128]); EB = Buf()
        MT2 = [sb("p4_MT%d" % i, [128, HS, 128], BF16) for i in range(2)]; MTB2 = [Buf() for _ in range(2)]
        x_sb2 = [sb("p4_x%d" % i, [128, HW]) for i in range(2)]; B_sb2 = [sb("p4_B%d" % i, [128, G * 128], BF16) for i in range(2)]; xB2 = [Buf() for _ in range(2)]
        xdt2 = [sb("p4_xdt%d" % i, [128, HS, 64], BF16) for i in range(2)]; xdtB2 = [Buf() for _ in range(2)]
        xdd2 = [sb("p4_xdd%d" % i, [128, HS, 64], BF16) for i in range(2)]; xddB2 = [Buf() for _ in range(2)]
        ecs2 = [sb("p4_ecs%d" % i, [128, HS]) for i in range(2)]; dst_ = sb("p4_dst", [128, HS]); edec2 = [sb("p4_edec%d" % i, [128, HS]) for i in range(2)]; eB2 = [Buf() for _ in range(2)]; dstB = Buf()
        t13 = [sb("p4_t1%d" % i, [128, HW]) for i in range(3)]; t1B3 = [Buf() for _ in range(3)]
        t33 = [sb("p4_t3%d" % i, [128, HW]) for i in range(3)]; t3B3 = [Buf() for _ in range(3)]
        yv = sb("p4_yv", [128, HW]); yvB = Buf()
        szb = [sb("p4_szb%d" % i, [128, 4, HW]) for i in range(2)]; szbB = [Buf() for _ in range(2)]
        junk = sb("p4_junk", [128, HW], BF16); junkB = Buf()
        ss2 = sb("p4_ss2", [128, G]); rs2 = sb("p4_rs2", [128, G]); ssB = Buf(); rsB = Buf()
        yn = sb("p4_yn", [128, HW], BF16); ynB = Buf()
        yst = [sb("p4_yst%d" % i, [128, XC, TB], BF16) for i in range(2)]; ystB = [Buf() for _ in range(2)]
        sds = [sc.dsem() for _ in range(2)]
        st = sb("p4_st", [128, HW]); st_bf = sb("p4_stbf", [128, HW], BF16); stB = Buf(); stbB = Buf()
        ps1 = pt("p4_ps1", [128, 512]); ps1B = PBuf()
        psR = pt("p4_psR", [128, 2, 512]); psRB = PBuf()
        psXT = pt("p4_psXT", [128, 1024], BF16); psXTB = PBuf(); psXTb = pt("p4_psXTb", [128, 1024], BF16); psXTbB = PBuf()
        psY = pt("p4_psY", [128, 512]); psYB = PBuf()
        psYO = pt("p4_psYO", [128, 512]); psYOB = PBuf()
        psS = pt("p4_psS", [128, 512]); psSB = PBuf()
        sc.op("pool", lambda h: h.memset(st[:], 0.0), writes=[stB])
        sc.op("pool", lambda h: h.memset(st_bf[:], 0.0), writes=[stbB])
        XTO = XC * 128 + G * 128
        def load4(tt):
            bb = tt % 2
            tk = tt * TB
            sc.dma(lds[bb], [(xb_[bb][:], scr["xbcT"][:, :, tk:tk + TB].rearrange("c p s -> p c s")),
                             (z_[bb][:], scr["zs"][tk:tk + TB, :].rearrange("(n p) f -> p n f", p=128)),
                             (dt_[bb][:], scr["dts"][tk:tk + TB, :].rearrange("(n p) f -> p n f", p=128))],
                   writes=[inB[bb]])

        load4(0)
        for t in range(NB):
            tok0 = t * TB
            b = t % 2
            if t + 1 < NB:
                load4(t + 1)
            sc.op("dve", lambda h: h.tensor_tensor(out=dtp[:], in0=dt_[b][:], in1=dtb[:, None, :].broadcast_to([128, 4, HS]), op=ALU.add),
                  reads=[inB[b], cB], writes=[dB])
            sc.op("act", lambda h: h.activation(out=dtp[:], in_=dtp[:], func=AF.Exp), reads=[dB], writes=[dB])
            sc.op("act", lambda h: h.activation(out=dtp[:], in_=dtp[:], func=AF.Ln, bias=1.0), reads=[dB], writes=[dB])
            sc.op("dve", lambda h: h.tensor_tensor(out=aa[:], in0=dtp[:], in1=negA[:, None, :].broadcast_to([128, 4, HS]), op=ALU.mult),
                  reads=[dB, pB], writes=[dB])
            sc.op("act", lambda h: h.activation(out=szb[b][:], in_=z_[b][:], func=AF.Silu), reads=[inB[b]], writes=[szbB[b]])
            def front(n):
                q = n % 2
                csl_ = slice(n * 128, (n + 1) * 128)
                sc.op("dve", lambda h: h.tensor_copy(out=a_bc[:], in_=aa[:, n, :, None].broadcast_to([128, HS, 128])),
                      reads=[dB], writes=[abB])
                sc.op("pe", lambda h: h.matmul(ps1[:, 256:256 + HS], lhsT=triu[:], rhs=aa[:, n, :], start=True, stop=True),
                      reads=[cB, dB], writes=[ps1B])
                for hh in range(HS):
                    sc.op("pe", lambda h: h.matmul(psR[:, hh // 4, (hh % 4) * 128:(hh % 4 + 1) * 128], lhsT=a_bc[:, hh, :], rhs=triu[:],
                                                   start=True, stop=True), reads=[cB, abB], writes=[psRB], inc=(hh == HS - 1))
                psRv = psR[:].rearrange("p a (b c) -> p (a b) c", c=128)[:, 0:HS, :]
                sc.op("act", lambda h: h.copy(out=cs_sb2[q][:], in_=ps1[:, 256:256 + HS]), reads=[ps1B], writes=[csB2[q]])
                sc.op("act", lambda h: h.copy(out=csl2[q][:], in_=psRv[:, :, 127]), reads=[psRB], writes=[csB2[q]])
                sc.op("dve", lambda h: h.tensor_tensor(out=arg[:], in0=psRv, in1=cs_sb2[q][:, :, None].broadcast_to([128, HS, 128]), op=ALU.subtract),
                      reads=[psRB, csB2[q]], writes=[argB])
                sc.op("dve", lambda h: h.tensor_tensor(out=arg[:], in0=arg[:], in1=negm[:, None, :].broadcast_to([128, HS, 128]), op=ALU.add),
                      reads=[argB, cB], writes=[argB])
                sc.op("act", lambda h: h.activation(out=E[:], in_=arg[:], func=AF.Exp), reads=[argB], writes=[EB])
                for g in range(G):
                    sc.op("pe", lambda h: h.matmul(ps1[:, g * 128:(g + 1) * 128], lhsT=xb_[b][:, XC + g, csl_], rhs=xb_[b][:, XC + G + g, csl_],
                                                   start=True, stop=True), reads=[inB[b]], writes=[ps1B], inc=(g == G - 1))
                for g in range(G):
                    sc.op("dve", lambda h: h.tensor_tensor(out=MT2[q][:, 3 * g:3 * g + 3, :], in0=E[:, 3 * g:3 * g + 3, :],
                                                           in1=ps1[:, None, g * 128:(g + 1) * 128].broadcast_to([128, 3, 128]), op=ALU.mult),
                          reads=[EB, ps1B], writes=[MTB2[q]])
                for j in range(XC + G):
                    sc.op("pe", lambda h: h.transpose(out=psXT[:, j * 128:(j + 1) * 128], in_=xb_[b][:, j, csl_], identity=k.ident_bf[:]),
                          reads=[inB[b], k.cB], writes=[psXTB], inc=(j == XC + G - 1))
                sc.op("act", lambda h: h.copy(out=x_sb2[q][:], in_=psXT[:, 0:HW]), reads=[psXTB], writes=[xB2[q]])
                sc.op("act", lambda h: h.copy(out=B_sb2[q][:], in_=psXT[:, HW:HW + G * 128]), reads=[psXTB], writes=[xB2[q]])
                sc.op("dve", lambda h: h.tensor_tensor(out=xdt2[q][:], in0=x_sb2[q][:].rearrange("p (h e) -> p h e", e=64),
                                                       in1=dtp[:, n, :, None].broadcast_to([128, HS, 64]), op=ALU.mult),
                      reads=[xB2[q], dB], writes=[xdtB2[q]])
                sc.op("act", lambda h: h.activation(out=ecs2[q][:], in_=cs_sb2[q][:], func=AF.Exp), reads=[csB2[q]], writes=[eB2[q]])
                sc.op("pool", lambda h: h.tensor_tensor(out=t33[n % 3][:].rearrange("p (h e) -> p h e", e=64), in0=x_sb2[q][:].rearrange("p (h e) -> p h e", e=64),
                                                        in1=Dd[:, :, None].broadcast_to([128, HS, 64]), op=ALU.mult),
                      reads=[xB2[q], cB], writes=[t3B3[n % 3]])
                sc.op("dve", lambda h: h.tensor_tensor(out=dst_[:], in0=csl2[q][:], in1=cs_sb2[q][:], op=ALU.subtract), reads=[csB2[q]], writes=[dstB])
                sc.op("act", lambda h: h.activation(out=dst_[:], in_=dst_[:], func=AF.Exp), reads=[dstB], writes=[dstB])
                sc.op("act", lambda h: h.activation(out=edec2[q][:], in_=csl2[q][:], func=AF.Exp), reads=[csB2[q]], writes=[eB2[q]])
                sc.op("dve", lambda h: h.tensor_tensor(out=xdd2[q][:], in0=xdt2[q][:], in1=dst_[:, :, None].broadcast_to([128, HS, 64]), op=ALU.mult),
                      reads=[xdtB2[q], dstB], writes=[xddB2[q]])

            def back(n):
                q = n % 2
                csl_ = slice(n * 128, (n + 1) * 128)
                for hh in range(HS):
                    sc.op("pe", lambda h: h.matmul(psY[:, hh * 64:(hh + 1) * 64], lhsT=MT2[q][:, hh, :], rhs=xdt2[q][:, hh, :], start=True, stop=True),
                          reads=[MTB2[q], xdtB2[q]], writes=[psYB], inc=(hh == HS - 1))
                for g in range(G):
                    sc.op("pe", lambda h: h.matmul(psYO[:, g * 192:(g + 1) * 192], lhsT=xb_[b][:, XC + G + g, csl_], rhs=st_bf[:, g * 192:(g + 1) * 192],
                                                   start=True, stop=True), reads=[inB[b], stbB], writes=[psYOB], inc=(g == G - 1))
                sc.op("dve", lambda h: h.tensor_tensor(out=t13[n % 3][:].rearrange("p (h e) -> p h e", e=64), in0=psYO[:, 0:HW].rearrange("p (h e) -> p h e", e=64),
                                                       in1=ecs2[q][:, :, None].broadcast_to([128, HS, 64]), op=ALU.mult),
                      reads=[psYOB, eB2[q]], writes=[t1B3[n % 3]])
                sc.op("dve", lambda h: h.tensor_tensor(out=t13[n % 3][:], in0=psY[:, 0:HW], in1=t13[n % 3][:], op=ALU.add), reads=[psYB, t1B3[n % 3]], writes=[t1B3[n % 3]])
                for g in range(G):
                    sc.op("pe", lambda h: h.matmul(psS[:, g * 192:(g + 1) * 192], lhsT=B_sb2[q][:, g * 128:(g + 1) * 128],
                                                   rhs=xdd2[q][:, 3 * g:3 * g + 3, :].rearrange("p h e -> p (h e)"), start=True, stop=True),
                          reads=[xB2[q], xddB2[q]], writes=[psSB], inc=(g == G - 1))
                sc.op("dve", lambda h: h.tensor_tensor(out=st[:].rearrange("p (h e) -> p h e", e=64), in0=st[:].rearrange("p (h e) -> p h e", e=64),
                                                       in1=edec2[q][:, :, None].broadcast_to([128, HS, 64]), op=ALU.mult),
                      reads=[stB, eB2[q]], writes=[stB])
                sc.op("dve", lambda h: h.tensor_tensor(out=st[:], in0=st[:], in1=psS[:, 0:HW], op=ALU.add), reads=[stB, psSB], writes=[stB])
                sc.op("pool", lambda h: h.tensor_copy(out=st_bf[:], in_=st[:]), reads=[stB], writes=[stbB])

            def post(n):
                q = n % 2
                csl_ = slice(n * 128, (n + 1) * 128)
                sc.op("pool", lambda h: h.tensor_tensor(out=yv[:], in0=t13[n % 3][:], in1=t33[n % 3][:], op=ALU.add), reads=[t1B3[n % 3], t3B3[n % 3]], writes=[yvB])
                sc.op("dve", lambda h: h.tensor_tensor(out=yv[:], in0=yv[:], in1=szb[b][:, n, :], op=ALU.mult), reads=[yvB, szbB[b]], writes=[yvB])
                for g in range(G):
                    sc.op("act", lambda h: h.activation(out=junk[:, 0:192], in_=yv[:, g * 192:(g + 1) * 192], func=AF.Square,
                                                        accum_out=ss2[:, g:g + 1]), reads=[yvB], writes=[junkB, ssB])
                rms_rstd(k, ps, ss2, rs2, G, 192, ssB, rsB)
                for g in range(G):
                    sc.op("dve", lambda h: h.scalar_tensor_tensor(out=yn[:, g * 192:(g + 1) * 192], in0=yv[:, g * 192:(g + 1) * 192],
                                                                  scalar=rs2[:, g:g + 1], in1=gn[:, g * 192:(g + 1) * 192],
                                                                  op0=ALU.mult, op1=ALU.mult), reads=[yvB, rsB, cB], writes=[ynB])
                for j in range(XC):
                    sc.op("pe", lambda h: h.transpose(out=psXTb[:, j * 128:(j + 1) * 128], in_=yn[:, j * 128:(j + 1) * 128],
                                                      identity=k.ident_bf[:]), reads=[ynB, k.cB], writes=[psXTbB], inc=(j == XC - 1))
                sc.op("act", lambda h: h.copy(out=yst[b][:, :, csl_], in_=psXTb[:, 0:XC * 128].rearrange("p (j s) -> p j s", s=128)),
                      reads=[psXTbB], writes=[ystB[b]])

            front(0)
            for n in range(4):
                if n + 1 < 4:
                    front(n + 1)
                back(n)
                if n > 0:
                    post(n - 1)
            post(3)
            c0 = (HD * 64 + HL * 64) // 128
            sc.dma(sds[b], [(scr["mixT"][c0:c0 + XC, :, tok0:tok0 + TB].rearrange("c p s -> p c s"), yst[b][:])], reads=[ystB[b]])


def post_norm_add(k, sc, psF, psFB, n, ss, ssB, rstd, rsB, gb, gB, yt, ytB, xres, xresB, xo, xoB, junk, junkB, add_eng="pool"):
    sc.op("act", lambda h: h.activation(out=junk[:], in_=psF[:], func=AF.Square, accum_out=ss[:, n:n + 1]),
          reads=[psFB], writes=[junkB, ssB])
    rms_rstd(k, None, ss[:, n:n + 1], rstd[:, n:n + 1], 1, D, ssB, rsB)
    sc.op("dve", lambda h: h.scalar_tensor_tensor(out=yt[:], in0=psF[:], scalar=rstd[:, n:n + 1], in1=gb[:],
                                                  op0=ALU.mult, op1=ALU.mult), reads=[psFB, rsB, gB], writes=[ytB])
    sc.op(add_eng, lambda h: h.tensor_tensor(out=xo[:, n, :], in0=xres[:, n, :], in1=yt[:], op=ALU.add),
          reads=[ytB, xresB], writes=[xoB])


def phase5a(k, l, x_src):
    nc, sc, cfg, W, C = k.nc, k.sc, k.cfg, k.W, k.C
    S, NB, MC = cfg.S, cfg.NB, cfg.MIXC
    scr = k.scr
    with ExitStack() as ps:
        sb, pt = _mk(k, l, ps)
        wout = sb("p5a_w", [128, MC, D], BF16)
        stg = [sb("p5a_stg%d" % i, [128, 512]) for i in range(3)]
        stgB = [Buf() for _ in range(3)]; ds = [sc.dsem() for _ in range(3)]
        k.gB, k.wB = Buf(), Buf()
        gb = sb("p5a_g", [128, D]); gB = Buf(); d0 = sc.dsem()
        sc.dma(d0, [(gb[:], W["g_postmix"][l:l + 1, :].broadcast_to([128, D]))], writes=[gB])
        load_weights_bf16(k, ps, wout, W["w_out"][l], D, None, stg, stgB, ds, MC)
        mT = [sb("p5a_m%d" % i, [128, MC, TB], BF16) for i in range(2)]; mB = [Buf() for _ in range(2)]
        xr = [sb("p5a_x%d" % i, [128, 4, D]) for i in range(2)]; xB = [Buf() for _ in range(2)]
        lds = [sc.dsem() for _ in range(2)]
        xo = [sb("p5a_xo%d" % i, [128, 4, D]) for i in range(2)]; xoB = [Buf() for _ in range(2)]
        sds = [sc.dsem() for _ in range(2)]
        psF = [pt("p5a_psF%d" % i, [128, D]) for i in range(2)]; psFB = [PBuf() for _ in range(2)]
        ss = sb("p5a_ss", [128, 4]); ssB = Buf(); rstd = sb("p5a_rstd", [128, 4]); rsB = Buf()
        yt = sb("p5a_yt", [128, D]); ytB = Buf()
        junk = sb("p5a_junk", [128, D], BF16); junkB = Buf()
        ci = 0
        def load5a(tt):
            bb = tt % 2
            tk = tt * TB
            sc.dma(lds[bb], [(mT[bb][:], scr["mixT"][:, :, tk:tk + TB].rearrange("c p s -> p c s")),
                             (xr[bb][:], x_src[tk:tk + TB, :].rearrange("(n p) d -> p n d", p=128))],
                   writes=[mB[bb], xB[bb]])

        load5a(0)
        for t in range(NB):
            tok0 = t * TB
            b = t % 2
            if t + 1 < NB:
                load5a(t + 1)
            for n in range(4):
                a = ci % 2
                ci += 1
                for half in range(2):
                    for kc in range(MC):
                        sc.op("pe", lambda h, kc=kc, half=half: h.matmul(
                            psF[a][:, half * 512:(half + 1) * 512], lhsT=mT[b][:, kc, n * 128:(n + 1) * 128],
                            rhs=wout[:, kc, half * 512:(half + 1) * 512], start=(kc == 0), stop=(kc == MC - 1)),
                            reads=[k.wB, mB[b]], writes=[psFB[a]], inc=(kc == MC - 1 and half == 1))
                post_norm_add(k, sc, psF[a], psFB[a], n, ss, ssB, rstd, rsB, gb, gB, yt, ytB, xr[b], xB[b], xo[b], xoB[b], junk, junkB,
                              add_eng=("dve" if n % 2 == 0 else "pool"))
            sc.dma(sds[b], [(scr["xa"][tok0:tok0 + TB, :].rearrange("(n p) d -> p n d", p=128), xo[b][:])], reads=[xoB[b]])


def phase5b(k, l, x_dst):
    nc, sc, cfg, W, C = k.nc, k.sc, k.cfg, k.W, k.C
    S, FC, TBF = cfg.S, cfg.FC, cfg.TBF
    NT = TBF // 128
    scr = k.scr
    with ExitStack() as ps:
        sb, pt = _mk(k, l, ps)
        wup = sb("p5b_wup", [128, 8, 2 * cfg.DFF], BF16)
        wdn = sb("p5b_wdn", [128, FC, D], BF16)
        stg = [sb("p5b_stg%d" % i, [128, 512]) for i in range(3)]
        stgB = [Buf() for _ in range(3)]; ds = [sc.dsem() for _ in range(3)]
        k.gB, k.wB = Buf(), Buf()
        gain = sb("p5b_gain", [128, 8]); gb = sb("p5b_g", [128, D]); gB = Buf(); d0 = sc.dsem()
        fcw = sb("p5b_fcw", [128, 2 * FC, 3]); fcb = sb("p5b_fcb", [128, 2 * FC])
        sc.dma(d0, [(gb[:], W["g_postffn"][l:l + 1, :].broadcast_to([128, D])), (gain[:], W["g_preffn"][l]),
                    (fcw[:], W["fcw"][l]), (fcb[:], W["fcb"][l])], writes=[gB, k.gB])
        load_weights_bf16(k, ps, wup, W["ffn_up"][l], 2 * cfg.DFF, gain, stg, stgB, ds, 8)
        load_weights_bf16(k, ps, wdn, W["ffn_down"][l], D, None, stg, stgB, ds, FC)
        xr2 = [sb("p5b_x%d" % i, [128, NT, D]) for i in range(2)]; xB2 = [[Buf() for _ in range(NT)] for i in range(2)]
        lds2 = [[sc.dsem() for _ in range(NT)] for i in range(2)]; sds2 = [sc.dsem() for _ in range(2)]
        xs2 = [sb("p5b_xs%d" % i, [128, NT, D], BF16) for i in range(2)]; xsB2 = [Buf() for _ in range(2)]
        hT2 = [sb("p5b_hT%d" % i, [128, 8, 2 + TBF], BF16) for i in range(2)]; hTB2 = [Buf() for _ in range(2)]
        aT = sb("p5b_aT", [128, FC, TBF], BF16); aTB = Buf()
        NU = 4
        u = [sb("p5b_u%d" % i, [128, 2 + TBF]) for i in range(NU)]; uB = [Buf() for _ in range(NU)]; uhB = [Buf() for _ in range(NU)]
        y = [sb("p5b_y%d" % i, [128, TBF]) for i in range(NU)]; yB = [Buf() for _ in range(NU)]
        sg = [sb("p5b_sg%d" % i, [128, TBF]) for i in range(2)]; sgB = [Buf() for _ in range(2)]
        ss = sb("p5b_ss", [128, 4]); ssB = Buf(); rstd = sb("p5b_rstd", [128, 4]); rsB = Buf()
        yt = sb("p5b_yt", [128, D]); ytB = Buf()
        junk = sb("p5b_junk", [128, D], BF16); junkB = Buf()
        psT = [pt("p5b_psT%d" % i, [128, 2, 512], BF16) for i in range(2)]; psTB = [PBuf() for _ in range(2)]
        psU = [pt("p5b_psU%d" % i, [128, 512]) for i in range(NU)]; psUB = [PBuf() for _ in range(NU)]
        psF = [pt("p5b_psF%d" % i, [128, D]) for i in range(1)]; psFB = [PBuf() for _ in range(1)]
        for i in range(2):
            sc.op("pool", lambda h, i=i: h.memset(hT2[i][:], 0.0), writes=[hTB2[i]])
        ci = 0
        NBLK = S // TBF

        def do_norm(t):
            b = t % 2
            tok0 = t * TBF
            for n in range(NT):
                sc.dma(lds2[b][n], [(xr2[b][:, n, :], scr["xa"][tok0 + n * 128:tok0 + (n + 1) * 128, :])], writes=[xB2[b][n]])
            for n in range(NT):
                sc.op("act", lambda h, n=n: h.activation(out=junk[:], in_=xr2[b][:, n, :], func=AF.Square, accum_out=ss[:, n:n + 1]),
                      reads=[xB2[b][n]], writes=[junkB, ssB])
            rms_rstd(k, ps, ss, rstd, NT, D, ssB, rsB)
            for n in range(NT):
                if n % 2 == 0:
                    sc.op("dve", lambda h, n=n: h.tensor_scalar(
                        out=xs2[b][:, n, :], in0=xr2[b][:, n, :], scalar1=rstd[:, n:n + 1], scalar2=None, op0=ALU.mult),
                        reads=[xB2[b][n], rsB], writes=[xsB2[b]])
                else:
                    sc.op("act", lambda h, n=n: h.activation(out=xs2[b][:, n, :], in_=xr2[b][:, n, :], func=AF.Copy, scale=rstd[:, n:n + 1]),
                          reads=[xB2[b][n], rsB], writes=[xsB2[b]])
            if t > 0:
                sc.op("pool", lambda h: h.tensor_copy(out=hT2[b][:, :, 0:2], in_=hT2[1 - b][:, :, TBF:TBF + 2]), reads=[hTB2[1 - b]], writes=[hTB2[b]])
            for cp in range(4):
                pi = cp % 2
                for c2 in range(2):
                    c = cp * 2 + c2
                    for n in range(NT):
                        sc.op("pe", lambda h, c=c, c2=c2, n=n: h.transpose(
                            out=psT[pi][:, c2, n * 128:(n + 1) * 128], in_=xs2[b][:, n, c * 128:(c + 1) * 128],
                            identity=k.ident_bf[:]), reads=[xsB2[b], k.cB], writes=[psTB[pi]], inc=(c2 == 1 and n == NT - 1))
                sc.op("act", lambda h, cp=cp: h.copy(out=hT2[b][:, 2 * cp:2 * cp + 2, 2:2 + TBF], in_=psT[pi][:, :, 0:TBF]),
                      reads=[psTB[pi]], writes=[hTB2[b]])

        def do_chunks(t):
            b = t % 2
            chunks = [(j, wi, c) for j in range(FC) for wi, c in enumerate((j, FC + j))]

            def st_pe(i):
                j, wi, c = chunks[i]
                a = i % NU
                for kc in range(8):
                    sc.op("pe", lambda h: h.matmul(psU[a][:, 0:TBF + 2], lhsT=wup[:, kc, c * 128:(c + 1) * 128], rhs=hT2[b][:, kc, :],
                                                   start=(kc == 0), stop=(kc == 7)), reads=[k.wB, hTB2[b]], writes=[psUB[a]], inc=(kc == 7))

            def st_copy(i):
                j, wi, c = chunks[i]
                a = i % NU
                sc.op("act", lambda h: h.copy(out=u[a][:, 0:2 + TBF], in_=psU[a][:, 0:2 + TBF]), reads=[psUB[a]], writes=[uB[a]])
                sc.op("act", lambda h: h.activation(out=y[a][:], in_=psU[a][:, 2:2 + TBF], func=AF.Identity, scale=fcw[:, c, 2:3], bias=fcb[:, c:c + 1]),
                      reads=[psUB[a], k.gB], writes=[yB[a]])

            def st_conv(i):
                j, wi, c = chunks[i]
                a = i % NU
                for kk in range(2):
                    sc.op("dve", lambda h: h.scalar_tensor_tensor(out=y[a][:], in0=u[a][:, kk:kk + TBF], scalar=fcw[:, c, kk:kk + 1], in1=y[a][:],
                                                                  op0=ALU.mult, op1=ALU.add), reads=[uB[a], yB[a], k.gB], writes=[yB[a]])

            def st_gate(i):
                j, wi, c = chunks[i]
                a = i % NU
                if wi == 0:
                    sc.op("act", lambda h: h.activation(out=sg[j % 2][:], in_=y[a][:], func=AF.Silu), reads=[yB[a]], writes=[sgB[j % 2]])
                else:
                    sc.op("pool", lambda h: h.tensor_tensor(out=aT[:, j, :], in0=sg[j % 2][:], in1=y[a][:], op=ALU.mult),
                          reads=[sgB[j % 2], yB[a]], writes=[aTB])

            skewed(len(chunks), [st_pe, st_copy, st_conv, st_gate])

        def do_down(t):
            b = t % 2
            tok0 = t * TBF
            for n in range(NT):
                a = 0
                for half in range(2):
                    for j in range(FC):
                        sc.op("pe", lambda h, j=j, half=half: h.matmul(
                            psF[a][:, half * 512:(half + 1) * 512], lhsT=aT[:, j, n * 128:(n + 1) * 128],
                            rhs=wdn[:, j, half * 512:(half + 1) * 512], start=(j == 0), stop=(j == FC - 1)),
                            reads=[k.wB, aTB], writes=[psFB[a]], inc=(j == FC - 1 and half == 1))
                post_norm_add(k, sc, psF[a], psFB[a], n, ss, ssB, rstd, rsB, gb, gB, yt, ytB, xr2[b], xB2[b][n], xr2[b], xB2[b][n], junk, junkB)
            sc.dma(sds2[b], [(x_dst[tok0:tok0 + TBF, :].rearrange("(n p) d -> p n d", p=128), xr2[b][:])], reads=xB2[b])

        do_norm(0)
        for t in range(NBLK):
            do_chunks(t)
            if t + 1 < NBLK:
                do_norm(t + 1)
            do_down(t)


def kernel(**inputs):
    cfg = Cfg()
    nc = build(cfg)
    hp = host_params(cfg, inputs)
    consts = host_consts(cfg)
    x = np.asarray(inputs["x"], np.float32)
    nb = x.shape[0]
    work = [0, 1, 4, 5][:nb]
    zeros = {"x": np.zeros_like(x[0])}
    for kk, v in hp.items():
        zeros[kk] = np.zeros_like(v)
    for kk, v in consts.items():
        zeros[kk] = np.zeros_like(v)
    in_maps = []
    for c in range(8):
        if c in work:
            m = {"x": np.ascontiguousarray(x[work.index(c)])}
            m.update(hp)
            m.update(consts)
        else:
            m = zeros
        in_maps.append(m)
    res = run_bass_kernel_spmd(nc, in_maps, core_ids=list(range(8)))
    return np.stack([np.asarray(res.results[c]["out"], np.float32) for c in work])
```

```python
import math
import os
from contextlib import ExitStack

import numpy as np
import ml_dtypes

import concourse.bass as bass
import concourse.mybir as mybir
from concourse.bass_utils import run_bass_kernel_spmd

F32 = mybir.dt.float32
BF16 = mybir.dt.bfloat16
AF = mybir.ActivationFunctionType
ALU = mybir.AluOpType
AX = mybir.AxisListType

D = 1024
EPS = 1e-6
TB = 512
SAME_ENGINE_SYNC = True


class Buf:
    __slots__ = ("name", "w", "r", "excl")

    def __init__(self, name="", excl=False):
        self.name = name
        self.w = None
        self.r = {}
        self.excl = excl


def PBuf(name=""):
    return Buf(name, True)


class _Eng:
    def __init__(self, name, h, sem):
        self.name, self.h, self.sem, self.cnt, self.seen = name, h, sem, 0, {}


class _DSem:
    def __init__(self, sem):
        self.sem, self.cnt = sem, 0


class Sched:
    def __init__(self, nc, es):
        self.nc = nc
        self.es = es
        self.E = {}
        for name, h in (("pe", nc.tensor), ("act", nc.scalar), ("dve", nc.vector),
                        ("pool", nc.gpsimd), ("sp", nc.sync)):
            self.E[name] = _Eng(name, h, es.enter_context(nc.semaphore("s_" + name)))
        self.bar = es.enter_context(nc.semaphore("s_bar"))
        self.barcnt = 0
        self.dsems = []
        self.nsem = 0

    def dsem(self):
        self.nsem += 1
        d = _DSem(self.es.enter_context(self.nc.semaphore("d%d" % self.nsem)))
        self.dsems.append(d)
        return d

    def _wait(self, e, reads, writes):
        deps = {}

        def add(t):
            if t is not None and deps.get(t[0], (None, 0))[1] < t[1]:
                deps[t[0]] = t

        for b in reads:
            add(b.w)
        for b in writes:
            add(b.w)
            for s, v in b.r.items():
                add((s, v))
        for s, (sem, val) in deps.items():
            if sem is e.sem and (e.name == "pe" or not SAME_ENGINE_SYNC):
                continue
            if e.seen.get(sem, 0) >= val:
                continue
            e.h.wait_ge(sem, val)
            e.seen[sem] = val

    @staticmethod
    def _mark(tok, reads, writes):
        for b in reads:
            if b.r.get(tok[0], 0) < tok[1]:
                b.r[tok[0]] = tok[1]
        for b in writes:
            b.w = tok
            b.r = {}

    def op(self, eng, fn, reads=(), writes=(), inc=True):
        e = self.E[eng]
        if any(b.excl for b in reads):
            writes = list(writes) + [b for b in reads if b.excl]
            reads = [b for b in reads if not b.excl]
        self._wait(e, reads, writes)
        ins = fn(e.h)
        if inc:
            e.cnt += 1
            ins.then_inc(e.sem, 1)
            tok = (e.sem, e.cnt)
        else:
            tok = (e.sem, e.cnt + 1)
        self._mark(tok, reads, writes)
        return tok

    def dma(self, ds, pairs, reads=(), writes=(), eng="sp"):
        e = self.E[eng]
        self._wait(e, reads, writes)
        for out, in_ in pairs:
            ds.cnt += 16
            e.h.dma_start(out=out, in_=in_).then_inc(ds.sem, 16)
        tok = (ds.sem, ds.cnt)
        self._mark(tok, reads, writes)
        return tok

    def barrier(self):
        sp = self.E["sp"]
        for o in self.E.values():
            if o is not sp and o.cnt > 0 and sp.seen.get(o.sem, 0) < o.cnt:
                sp.h.wait_ge(o.sem, o.cnt)
                sp.seen[o.sem] = o.cnt
        for d in self.dsems:
            if d.cnt > 0 and sp.seen.get(d.sem, 0) < d.cnt:
                sp.h.wait_ge(d.sem, d.cnt)
                sp.seen[d.sem] = d.cnt
        self.barcnt += 1
        sp.h.sem_inc(self.bar, 1)
        for o in self.E.values():
            if o is not sp:
                o.h.wait_ge(self.bar, self.barcnt)
            for o2 in self.E.values():
                o.seen[o2.sem] = o2.cnt
            for d in self.dsems:
                o.seen[d.sem] = d.cnt


class Cfg:
    def __init__(self, S=8192, L=2, HD=4, HL=6, G=2, DFF=2816, TBF=256):
        self.S, self.L, self.HD, self.HL, self.G, self.DFF, self.TBF = S, L, HD, HL, G, DFF, TBF
        self.HS = 3 * G
        self.NB = S // TB
        off = 0
        self.fm = []
        for nm, cnt, m in (("dq", HD, 64), ("dk", HD, 64), ("lq", HL // 2, 128), ("lk", HL // 2, 128),
                           ("lv", HL // 2, 128), ("xs", self.HS // 2, 128), ("B", G, 128), ("C", G, 128)):
            for i in range(cnt):
                self.fm.append((nm, i, off, m))
                off += m
        self.off_dv = off
        off += HD * 64
        self.off_z = off
        off += self.HS * 64 + self.HS
        self.NCOL = off
        self.NXBC = self.HS // 2 + 2 * G
        self.MIXC = (HD * 64 + HL * 64 + self.HS * 64) // 128
        self.FC = DFF // 128


def _rope_tables(S, head_dim, rows):
    half = head_dim // 2
    inv = np.exp(-math.log(10000.0) * np.arange(half, dtype=np.float32) / half).astype(np.float32)
    ang = np.arange(S, dtype=np.float32)[None, :] * inv[:, None]
    cos = np.cos(ang).astype(np.float32)
    sin = np.sin(ang).astype(np.float32)
    p = np.arange(rows)
    j = p % half
    first = (p % head_dim) < half
    cosT = cos[j]
    sinT = np.where(first[:, None], -sin[j], sin[j])
    perm = np.zeros((rows, rows), np.float32)
    partner = np.where(first, p + half, p - half)
    perm[p, partner] = 1.0
    return cosT.astype(np.float32), sinT.astype(np.float32), perm


def host_consts(cfg):
    S = cfg.S
    c = {}
    cd, sd, pd = _rope_tables(S, 32, 64)
    cl, sl, pl = _rope_tables(S, 64, 128)
    c["cosd"], c["sind"], c["cosl"], c["sinl"] = [a.astype(ml_dtypes.bfloat16) for a in (cd, sd, cl, sl)]
    c["permd"] = pd.astype(ml_dtypes.bfloat16)
    c["perml"] = pl.astype(ml_dtypes.bfloat16)
    c["ident_bf"] = np.eye(128, dtype=np.float32).astype(ml_dtypes.bfloat16)
    c["ident_f"] = np.eye(128, dtype=np.float32)
    k = np.arange(128)[:, None]
    q = np.arange(512)[None, :]
    c["dmask"] = np.stack([(q >= 128 * di + k) for di in range(4)]).astype(np.float32).astype(ml_dtypes.bfloat16)
    qq = np.arange(128)[None, :]
    c["lmask"] = np.concatenate([(qq <= k), (qq >= k)], axis=1).astype(np.float32).astype(ml_dtypes.bfloat16)
    c["triu"] = (k <= qq).astype(np.float32)
    c["negm"] = np.where(qq >= k, 0.0, -30000.0).astype(np.float32)
    sel = np.zeros((65, 64), np.float32)
    sel[64, :] = 1.0
    c["sel"] = sel
    c["ones64"] = np.ones((64, 64), np.float32)
    return c


CONST_SHAPES = None


def host_params(cfg, inp, b_heads=None):
    HD, HL, G, HS = cfg.HD, cfg.HL, cfg.G, cfg.HS
    w = np.asarray(inp["w_in"])
    L = w.shape[0]
    o = 0
    segs = {}
    for nm, n in (("dq", 256), ("dk", 256), ("dv", 256), ("lq", 384), ("lk", 384), ("lv", 384),
                  ("z", 384), ("xs", 384), ("B", 256), ("C", 256), ("dt", 6)):
        segs[nm] = (o, o + n)
        o += n
    cols = []
    for nm in ("dq", "dk", "lq", "lk", "lv", "xs", "B", "C", "dv", "z", "dt"):
        a, b = segs[nm]
        cols.append(np.arange(a, b))
    cols = np.concatenate(cols)
    p = {}
    p["w_in"] = np.ascontiguousarray(w[:, :, cols])
    p["g_premix"] = np.ascontiguousarray(np.asarray(inp["pre_mix_norm"]).reshape(L, 8, 128).transpose(0, 2, 1))
    p["g_preffn"] = np.ascontiguousarray(np.asarray(inp["pre_ffn_norm"]).reshape(L, 8, 128).transpose(0, 2, 1))
    p["g_postmix"] = np.ascontiguousarray(np.asarray(inp["post_mix_norm"]))
    p["g_postffn"] = np.ascontiguousarray(np.asarray(inp["post_ffn_norm"]))
    xo = segs["xs"][0]
    cw = np.asarray(inp["ssd_conv_w"])
    cb = np.asarray(inp["ssd_conv_b"])
    p["cw"] = np.ascontiguousarray(cw.reshape(L, 4, 7, 128).transpose(0, 3, 2, 1))
    p["cb"] = np.ascontiguousarray(cb.reshape(L, 7, 128).transpose(0, 2, 1))
    p["dt_bias"] = np.ascontiguousarray(np.asarray(inp["ssd_dt_bias"]))
    p["A_log"] = np.ascontiguousarray(np.asarray(inp["ssd_A_log"]))
    p["ssd_D"] = np.ascontiguousarray(np.asarray(inp["ssd_D"]))
    p["ssd_norm"] = np.ascontiguousarray(np.asarray(inp["ssd_norm"]))
    p["lam"] = np.ascontiguousarray(np.asarray(inp["diff_lambda"]).reshape(L, 128))
    p["hgain"] = np.ascontiguousarray(np.asarray(inp["diff_head_norm"]).reshape(L, 64, 1))
    p["w_out"] = np.ascontiguousarray(np.asarray(inp["w_out"]))
    p["ffn_up"] = np.ascontiguousarray(np.asarray(inp["ffn_up"]))
    fw = np.asarray(inp["ffn_conv_w"])
    fb = np.asarray(inp["ffn_conv_b"])
    p["fcw"] = np.ascontiguousarray(fw.reshape(L, 3, 44, 128).transpose(0, 3, 2, 1))
    p["fcb"] = np.ascontiguousarray(fb.reshape(L, 44, 128).transpose(0, 2, 1))
    p["ffn_down"] = np.ascontiguousarray(np.asarray(inp["ffn_down"]))
    return p


class K:
    pass


def build(cfg, debug=False):
    nc = bass.Bass("TRN2", target_bir_lowering=False)
    S, L, HD, HL, G, HS = cfg.S, cfg.L, cfg.HD, cfg.HL, cfg.G, cfg.HS
    NB = cfg.NB
    es = ExitStack()
    sc = Sched(nc, es)

    def din(name, shape, dt=F32):
        return nc.dram_tensor(name, list(shape), dt, kind="ExternalInput").ap()

    def dscr(name, shape, dt):
        return nc.dram_tensor(name, list(shape), dt, kind=("ExternalOutput" if debug else "Internal")).ap()

    x_in = din("x", [S, D])
    W = {}
    W["w_in"] = din("w_in", [L, D, cfg.NCOL])
    W["g_premix"] = din("g_premix", [L, 128, 8])
    W["g_preffn"] = din("g_preffn", [L, 128, 8])
    W["g_postmix"] = din("g_postmix", [L, D])
    W["g_postffn"] = din("g_postffn", [L, D])
    W["cw"] = din("cw", [L, 128, 7, 4])
    W["cb"] = din("cb", [L, 128, 7])
    W["dt_bias"] = din("dt_bias", [L, 6])
    W["A_log"] = din("A_log", [L, 6])
    W["ssd_D"] = din("ssd_D", [L, 6])
    W["ssd_norm"] = din("ssd_norm", [L, 384])
    W["lam"] = din("lam", [L, 128])
    W["hgain"] = din("hgain", [L, 64, 1])
    W["w_out"] = din("w_out", [L, D, D])
    W["ffn_up"] = din("ffn_up", [L, D, 2 * cfg.DFF])
    W["fcw"] = din("fcw", [L, 128, 44, 3])
    W["fcb"] = din("fcb", [L, 128, 44])
    W["ffn_down"] = din("ffn_down", [L, cfg.DFF, D])
    C = {}
    C["cosd"] = din("cosd", [64, S], BF16); C["sind"] = din("sind", [64, S], BF16)
    C["cosl"] = din("cosl", [128, S], BF16); C["sinl"] = din("sinl", [128, S], BF16)
    C["permd"] = din("permd", [64, 64], BF16); C["perml"] = din("perml", [128, 128], BF16)
    C["ident_bf"] = din("ident_bf", [128, 128], BF16); C["ident_f"] = din("ident_f", [128, 128])
    C["dmask"] = din("dmask", [4, 128, 512], BF16); C["lmask"] = din("lmask", [128, 256], BF16)
    C["triu"] = din("triu", [128, 128]); C["negm"] = din("negm", [128, 128])
    C["sel"] = din("sel", [65, 64]); C["ones64"] = din("ones64", [64, 64])
    out = nc.dram_tensor("out", [S, D], F32, kind="ExternalOutput").ap()

    qTd = dscr("qTd", [HD, 64, S], BF16); kTd = dscr("kTd", [HD, 64, S], BF16)
    vd = dscr("vd", [S, HD * 65], BF16)
    qTl = dscr("qTl", [HL // 2, 128, S], BF16); kTl = dscr("kTl", [HL // 2, 128, S], BF16)
    vTl = dscr("vTl", [HL // 2, 128, S], BF16)
    xbcT = dscr("xbcT", [cfg.NXBC, 128, S], BF16)
    zs = dscr("zs", [S, HS * 64], BF16)
    dts = dscr("dts", [S, HS], F32)
    mixT = dscr("mixT", [cfg.MIXC, 128, S], BF16)
    xa = dscr("xa", [S, D], F32)
    xb = dscr("xb", [S, D], F32)

    def sb(name, shape, dt=F32):
        return es.enter_context(nc.sbuf_tensor(name, list(shape), dt))

    ident_bf = sb("sb_ident_bf", [128, 128], BF16)
    ident_f = sb("sb_ident_f", [128, 128])
    cB = Buf("consts")
    cds = sc.dsem()
    sc.dma(cds, [(ident_bf[:], C["ident_bf"]), (ident_f[:], C["ident_f"])], writes=[cB])

    k = K()
    k.nc, k.sc, k.cfg, k.W, k.C, k.cB = nc, sc, cfg, W, C, cB
    k.ident_bf, k.ident_f = ident_bf, ident_f
    k.scr = dict(qTd=qTd, kTd=kTd, vd=vd, qTl=qTl, kTl=kTl, vTl=vTl, xbcT=xbcT, zs=zs, dts=dts,
                 mixT=mixT, xa=xa, xb=xb)

    stages = cfg.stages if hasattr(cfg, "stages") else "12345"
    for l in range(L):
        x_src = x_in if l == 0 else xb
        x_dst = out if l == L - 1 else xb
        if "1" in stages:
            phase1(k, l, x_src)
            sc.barrier()
        if "2" in stages:
            phase2(k, l)
            sc.barrier()
        if "3" in stages:
            phase3(k, l)
            sc.barrier()
        if "4" in stages:
            phase4(k, l)
            sc.barrier()
        if "5" in stages:
            phase5a(k, l, x_src)
            sc.barrier()
            phase5b(k, l, x_dst)
            sc.barrier()
    sc.barrier()
    es.close()
    return nc


def load_weights_bf16(k, ps, dst, src_rows, ncols, gain, stg, stgB, ds, kchunks, col_chunk=512):
    sc = k.sc
    i = 0
    for kc in range(kchunks):
        for c0 in range(0, ncols, col_chunk):
            cw = min(col_chunk, ncols - c0)
            sl = i % len(stg)
            sc.dma(ds[sl], [(stg[sl][:, :cw], src_rows[kc * 128:(kc + 1) * 128, c0:c0 + cw])], writes=[stgB[sl]])
            eng = "dve" if i % 2 == 0 else "pool"
            if gain is not None:
                if i % 2 == 0:
                    sc.op("dve", lambda h, sl=sl, cw=cw, kc=kc, c0=c0: h.tensor_scalar(
                        out=dst[:, kc, c0:c0 + cw], in0=stg[sl][:, :cw], scalar1=gain[:, kc:kc + 1], scalar2=None,
                        op0=ALU.mult), reads=[stgB[sl], k.gB], writes=[k.wB])
                else:
                    sc.op("act", lambda h, sl=sl, cw=cw, kc=kc, c0=c0: h.activation(
                        out=dst[:, kc, c0:c0 + cw], in_=stg[sl][:, :cw], func=AF.Copy, scale=gain[:, kc:kc + 1]),
                        reads=[stgB[sl], k.gB], writes=[k.wB])
            else:
                sc.op(eng, lambda h, sl=sl, cw=cw, kc=kc, c0=c0: h.tensor_copy(
                    out=dst[:, kc, c0:c0 + cw], in_=stg[sl][:, :cw]), reads=[stgB[sl]], writes=[k.wB])
            i += 1


def rms_rstd(k, ps, ss, rstd, n, width, ssB, rsB):
    sc = k.sc
    sc.op("act", lambda h: h.activation(out=rstd[:, :n], in_=ss[:, :n], func=AF.Ln, scale=1.0 / width, bias=EPS),
          reads=[ssB], writes=[rsB])
    sc.op("act", lambda h: h.activation(out=rstd[:, :n], in_=rstd[:, :n], func=AF.Exp, scale=-0.5), reads=[rsB], writes=[rsB])


def phase1(k, l, x_src):
    nc, sc, cfg, W, C = k.nc, k.sc, k.cfg, k.W, k.C
    S, HD, HL, G, HS, NB = cfg.S, cfg.HD, cfg.HL, cfg.G, cfg.HS, cfg.NB
    NX = cfg.NXBC
    with ExitStack() as ps:
        def sb(name, shape, dt=F32):
            return ps.enter_context(nc.sbuf_tensor("L%d_%s" % (l, name), list(shape), dt))

        def pt(name, shape, dt=F32):
            return ps.enter_context(nc.psum_tensor("L%d_%s" % (l, name), list(shape), dt))

        wbf = sb("p1_w", [128, 8, cfg.NCOL], BF16)
        gain = sb("p1_g", [128, 8])
        stg = [sb("p1_stg%d" % i, [128, 512]) for i in range(3)]
        stgB = [Buf() for _ in range(3)]
        ds = [sc.dsem() for _ in range(3)]
        k.gB, k.wB = Buf("gain"), Buf("w")
        dsm = sc.dsem()
        permd = sb("p1_permd", [64, 64], BF16)
        perml = sb("p1_perml", [128, 128], BF16)
        cw = sb("p1_cw", [128, 7, 4])
        cb = sb("p1_cb", [128, 7])
        sc.dma(dsm, [(gain[:], W["g_premix"][l]), (permd[:], C["permd"]), (perml[:], C["perml"]),
                     (cw[:], W["cw"][l]), (cb[:], W["cb"][l])], writes=[k.gB])
        CUT = int(os.environ.get("KCUT", "99"))
        if CUT >= 1:
            load_weights_bf16(k, ps, wbf, W["w_in"][l], cfg.NCOL, gain, stg, stgB, ds, 8)

        xt = sb("p1_x", [128, 4, D])
        xtB = [Buf() for _ in range(4)]
        xds = [sc.dsem() for _ in range(4)]
        junk = sb("p1_junk", [128, D], BF16)
        junkB = Buf()
        ss = sb("p1_ss", [128, 4]); ssB = Buf()
        rstd = sb("p1_rstd", [128, 4]); rsB = Buf()
        xs = sb("p1_xs", [128, 4, D], BF16); xsB = Buf()
        hT = sb("p1_hT", [128, 8, TB], BF16); hTB = Buf()
        psT = [pt("p1_psT%d" % i, [128, 2, TB], BF16) for i in range(2)]
        psTB = [PBuf() for _ in range(2)]
        psA = [pt("p1_psA%d" % i, [128, TB]) for i in range(3)]
        psAB = [PBuf() for _ in range(3)]
        psR = [pt("p1_psR%d" % i, [128, TB]) for i in range(2)]
        psRB = [PBuf() for _ in range(2)]
        psB = [pt("p1_psB%d" % i, [128, 512]) for i in range(1)]
        psBB = [PBuf() for _ in range(1)]
        tabs = {}
        for nm, rows in (("cosd", 64), ("sind", 64), ("cosl", 128), ("sinl", 128)):
            tabs[nm] = [sb("p1_%s%d" % (nm, i), [rows, TB], BF16) for i in range(2)]
        tabB = [Buf() for _ in range(2)]
        tds = [sc.dsem() for _ in range(2)]
        xbf = [sb("p1_xbf%d" % i, [128, TB], BF16) for i in range(4)]
        xbfB = [Buf() for _ in range(4)]
        t1 = [sb("p1_t1%d" % i, [128, TB]) for i in range(2)]
        t1B = [Buf() for _ in range(2)]
        t2 = [sb("p1_t2%d" % i, [128, TB]) for i in range(2)]
        t2B = [Buf() for _ in range(2)]
        qk_st = [sb("p1_qk%d" % i, [64, 2 * HD, TB], BF16) for i in range(2)]
        l_st = [sb("p1_l%d" % i, [128, 3 * (HL // 2), TB], BF16) for i in range(2)]
        xbc_st = [sb("p1_xbc%d" % i, [128, NX, TB], BF16) for i in range(2)]
        v_st = [sb("p1_v%d" % i, [128, 4, HD, 65], BF16) for i in range(2)]
        z_st = [sb("p1_z%d" % i, [128, 4, HS * 64], BF16) for i in range(2)]
        dt_st = [sb("p1_dt%d" % i, [128, 4, HS]) for i in range(2)]
        from collections import defaultdict
        stD = [defaultdict(Buf) for _ in range(2)]
        sds = [sc.dsem() for _ in range(2)]
        u = sb("p1_u", [128, NX, 3 + TB]); uB = [Buf() for _ in range(NX)]
        yc = [sb("p1_yc%d" % i, [128, TB]) for i in range(3)]
        ycB = [Buf() for _ in range(3)]
        sc.op("pool", lambda h: h.memset(u[:], 0.0), writes=uB)
        for i in range(2):
            sc.op("pool", lambda h, i=i: h.memset(v_st[i][:], 1.0), writes=[stD[i][("v", n)] for n in range(4)])

        ci = 0
        for t in range(NB):
            if CUT < 2:
                break
            pb = t % 2
            tok0 = t * TB

            def prefetch(tt):
                tk = tt * TB
                for n in range(4):
                    sc.dma(xds[n], [(xt[:, n, :], x_src[tk + n * 128:tk + (n + 1) * 128, :])], writes=[xtB[n]])
                sc.dma(tds[tt % 2], [(tabs[nm][tt % 2][:], C[nm][:, tk:tk + TB]) for nm in ("cosd", "sind", "cosl", "sinl")],
                       writes=[tabB[tt % 2]])

            if t == 0:
                prefetch(0)
            for n in range(4):
                sc.op("act", lambda h, n=n: h.activation(out=junk[:], in_=xt[:, n, :], func=AF.Square,
                                                         accum_out=ss[:, n:n + 1]),
                      reads=[xtB[n]], writes=[junkB, ssB])
            if CUT < 3:
                continue
            rms_rstd(k, ps, ss, rstd, 4, D, ssB, rsB)
            for n in range(4):
                if n % 2 == 0:
                    sc.op("dve", lambda h, n=n: h.tensor_scalar(
                        out=xs[:, n, :], in0=xt[:, n, :], scalar1=rstd[:, n:n + 1], scalar2=None, op0=ALU.mult),
                        reads=[xtB[n], rsB], writes=[xsB])
                else:
                    sc.op("act", lambda h, n=n: h.activation(out=xs[:, n, :], in_=xt[:, n, :], func=AF.Copy, scale=rstd[:, n:n + 1]),
                          reads=[xtB[n], rsB], writes=[xsB])
            if CUT < 4:
                continue
            for cp in range(4):
                pi = cp % 2
                for c2 in range(2):
                    c = cp * 2 + c2
                    for n in range(4):
                        last = (c2 == 1 and n == 3)
                        sc.op("pe", lambda h, c=c, c2=c2, n=n, pi=pi: h.transpose(
                            out=psT[pi][:, c2, n * 128:(n + 1) * 128], in_=xs[:, n, c * 128:(c + 1) * 128],
                            identity=k.ident_bf[:]), reads=[xsB, k.cB], writes=[psTB[pi]], inc=last)
                if cp % 2 == 0:
                    sc.op("act", lambda h, cp=cp, pi=pi: h.copy(out=hT[:, 2 * cp:2 * cp + 2, :], in_=psT[pi][:]),
                          reads=[psTB[pi]], writes=[hTB])
                else:
                    sc.op("dve", lambda h, cp=cp, pi=pi: h.tensor_copy(out=hT[:, 2 * cp:2 * cp + 2, :], in_=psT[pi][:]),
                          reads=[psTB[pi]], writes=[hTB])
            if CUT < 5:
                continue
            if t + 1 < NB:
                prefetch(t + 1)
            if t > 0:
                sc.op("pool", lambda h: h.tensor_copy(out=u[:, :, 0:3], in_=u[:, :, TB:TB + 3]), reads=uB, writes=uB)
            fm = cfg.fm

            def s_pe(i):
                nm, idx, off, M = fm[i]
                a = i % 3
                for kc in range(8):
                    sc.op("pe", lambda h: h.matmul(psA[a][0:M, :], lhsT=wbf[:, kc, off:off + M], rhs=hT[:, kc, :],
                                                   start=(kc == 0), stop=(kc == 7)), reads=[k.wB, hTB], writes=[psAB[a]], inc=(kc == 7))

            def s_copy(i):
                nm, idx, off, M = fm[i]
                a = i % 3
                if nm in ("dq", "dk", "lq", "lk"):
                    x4 = i % 4
                    sc.op("act", lambda h: h.copy(out=xbf[x4][0:M, :], in_=psA[a][0:M, :]), reads=[psAB[a]], writes=[xbfB[x4]])
                elif nm == "lv":
                    sc.op("act", lambda h: h.copy(out=l_st[pb][:, 2 * (HL // 2) + idx, :], in_=psA[a][:]),
                          reads=[psAB[a]], writes=[stD[pb][("lv", idx)]])
                else:
                    xi = {"xs": 0, "B": HS // 2, "C": HS // 2 + G}[nm] + idx
                    sc.op("act", lambda h: h.copy(out=u[:, xi, 3:3 + TB], in_=psA[a][:]), reads=[psAB[a]], writes=[uB[xi]])

            def s_mid(i):
                nm, idx, off, M = fm[i]
                if nm in ("dq", "dk", "lq", "lk"):
                    perm = permd if M == 64 else perml
                    x4 = i % 4
                    r2 = i % 2
                    sc.op("pe", lambda h: h.matmul(psR[r2][0:M, :], lhsT=perm[0:M, 0:M], rhs=xbf[x4][0:M, :], start=True, stop=True),
                          reads=[xbfB[x4], k.gB], writes=[psRB[r2]])
                elif nm != "lv":
                    xi = {"xs": 0, "B": HS // 2, "C": HS // 2 + G}[nm] + idx
                    gi = {"xs": 0, "B": 3, "C": 5}[nm] + idx
                    y3 = i % 3
                    sc.op("dve", lambda h: h.tensor_scalar(out=yc[y3][:], in0=u[:, xi, 3:3 + TB], scalar1=cw[:, gi, 3:4], scalar2=cb[:, gi:gi + 1],
                                                           op0=ALU.mult, op1=ALU.add), reads=[uB[xi], k.gB], writes=[ycB[y3]])

            def s_ew(i):
                nm, idx, off, M = fm[i]
                if nm in ("dq", "dk", "lq", "lk"):
                    cosn, sinn = ("cosd", "sind") if M == 64 else ("cosl", "sinl")
                    x4 = i % 4
                    r2 = i % 2
                    sc.op("dve", lambda h: h.tensor_tensor(out=t1[r2][0:M, :], in0=xbf[x4][0:M, :], in1=tabs[cosn][pb][0:M, :], op=ALU.mult),
                          reads=[xbfB[x4], tabB[pb]], writes=[t1B[r2]])
                    sc.op("dve", lambda h: h.tensor_tensor(out=t2[r2][0:M, :], in0=psR[r2][0:M, :], in1=tabs[sinn][pb][0:M, :], op=ALU.mult),
                          reads=[psRB[r2], tabB[pb]], writes=[t2B[r2]])
                elif nm != "lv":
                    xi = {"xs": 0, "B": HS // 2, "C": HS // 2 + G}[nm] + idx
                    gi = {"xs": 0, "B": 3, "C": 5}[nm] + idx
                    y3 = i % 3
                    for kk in range(3):
                        sc.op("dve", lambda h: h.scalar_tensor_tensor(out=yc[y3][:], in0=u[:, xi, kk:kk + TB], scalar=cw[:, gi, kk:kk + 1],
                                                                      in1=yc[y3][:], op0=ALU.mult, op1=ALU.add),
                              reads=[uB[xi], ycB[y3], k.gB], writes=[ycB[y3]])

            def s_fin(i):
                nm, idx, off, M = fm[i]
                if nm in ("dq", "dk", "lq", "lk"):
                    r2 = i % 2
                    if nm in ("dq", "dk"):
                        dst = qk_st[pb][:, (0 if nm == "dq" else HD) + idx, :]
                    else:
                        dst = l_st[pb][:, (0 if nm == "lq" else HL // 2) + idx, :]
                    sc.op("pool", lambda h: h.tensor_tensor(out=dst, in0=t1[r2][0:M, :], in1=t2[r2][0:M, :], op=ALU.add),
                          reads=[t1B[r2], t2B[r2]], writes=[stD[pb][(nm, idx)]])
                elif nm != "lv":
                    xi = {"xs": 0, "B": HS // 2, "C": HS // 2 + G}[nm] + idx
                    y3 = i % 3
                    sc.op("act", lambda h: h.activation(out=xbc_st[pb][:, xi, :], in_=yc[y3][:], func=AF.Silu),
                          reads=[ycB[y3]], writes=[stD[pb][("xbc", xi)]])

            skewed(len(fm), [s_pe, s_copy, s_mid, s_ew, s_fin])
            if CUT < 6:
                continue
            tmb = [psB[0], psR[0], psR[1]]
            tmB = [psBB[0], psRB[0], psRB[1]]
            for n in range(4):
                a = (2 * n) % 3
                for kc in range(8):
                    sc.op("pe", lambda h, kc=kc, n=n, a=a: h.matmul(
                        tmb[a][:, 0:HD * 64], lhsT=hT[:, kc, n * 128:(n + 1) * 128],
                        rhs=wbf[:, kc, cfg.off_dv:cfg.off_dv + HD * 64], start=(kc == 0), stop=(kc == 7)),
                        reads=[k.wB, hTB], writes=[tmB[a]], inc=(kc == 7))
                sc.op("act", lambda h, n=n, a=a: h.copy(
                    out=v_st[pb][:, n, :, 0:64], in_=tmb[a][:, 0:HD * 64].rearrange("p (h e) -> p h e", e=64)),
                    reads=[tmB[a]], writes=[stD[pb][("v", n)]])
                nz = HS * 64 + HS
                a = (2 * n + 1) % 3
                for kc in range(8):
                    sc.op("pe", lambda h, kc=kc, n=n, a=a: h.matmul(
                        tmb[a][:, 0:nz], lhsT=hT[:, kc, n * 128:(n + 1) * 128],
                        rhs=wbf[:, kc, cfg.off_z:cfg.off_z + nz], start=(kc == 0), stop=(kc == 7)),
                        reads=[k.wB, hTB], writes=[tmB[a]], inc=(kc == 7))
                sc.op("dve", lambda h, n=n, a=a: h.tensor_copy(out=z_st[pb][:, n, :], in_=tmb[a][:, 0:HS * 64]),
                      reads=[tmB[a]], writes=[stD[pb][("z", n)]])
                sc.op("dve", lambda h, n=n, a=a: h.tensor_copy(out=dt_st[pb][:, n, :], in_=tmb[a][:, HS * 64:nz]),
                      reads=[tmB[a]], writes=[stD[pb][("dt", n)]])
            if CUT < 7:
                continue
            scr = k.scr
            pairs = [
                (scr["qTd"][:, :, tok0:tok0 + TB].rearrange("h p s -> p h s"), qk_st[pb][:, 0:HD, :]),
                (scr["kTd"][:, :, tok0:tok0 + TB].rearrange("h p s -> p h s"), qk_st[pb][:, HD:2 * HD, :]),
                (scr["qTl"][:, :, tok0:tok0 + TB].rearrange("h p s -> p h s"), l_st[pb][:, 0:HL // 2, :]),
                (scr["kTl"][:, :, tok0:tok0 + TB].rearrange("h p s -> p h s"), l_st[pb][:, HL // 2:HL, :]),
                (scr["vTl"][:, :, tok0:tok0 + TB].rearrange("h p s -> p h s"), l_st[pb][:, HL:3 * (HL // 2), :]),
                (scr["xbcT"][:, :, tok0:tok0 + TB].rearrange("h p s -> p h s"), xbc_st[pb][:]),
                (scr["vd"][tok0:tok0 + TB, :].rearrange("(n p) f -> p n f", p=128),
                 v_st[pb][:].rearrange("p n h e -> p n (h e)")),
                (scr["zs"][tok0:tok0 + TB, :].rearrange("(n p) f -> p n f", p=128), z_st[pb][:]),
                (scr["dts"][tok0:tok0 + TB, :].rearrange("(n p) f -> p n f", p=128), dt_st[pb][:]),
            ]
            sc.dma(sds[pb], pairs, reads=list(stD[pb].values()))


def skewed(n, stages):
    ns = len(stages)
    for kk in range(n + ns - 1):
        for si, fn in enumerate(stages):
            i = kk - si
            if 0 <= i < n:
                fn(i)


def _mk(k, l, ps):
    nc = k.nc

    def sb(name, shape, dt=F32):
        return ps.enter_context(nc.sbuf_tensor("L%d_%s" % (l, name), list(shape), dt))

    def pt(name, shape, dt=F32):
        return ps.enter_context(nc.psum_tensor("L%d_%s" % (l, name), list(shape), dt))

    return sb, pt


def phase2(k, l):
    nc, sc, cfg, W, C = k.nc, k.sc, k.cfg, k.W, k.C
    S, HD, NB = cfg.S, cfg.HD, cfg.NB
    lam_init = 0.8 - 0.6 * math.exp(-0.3 * l)
    scr = k.scr
    with ExitStack() as ps:
        sb, pt = _mk(k, l, ps)
        kT = sb("p2_kT", [128, HD // 2, S], BF16)
        vfull = sb("p2_v", [128, (S // 128) * HD * 65 + 128], BF16)
        v = vfull[:, 0:(S // 128) * HD * 65].rearrange("p (n f) -> p n f", f=HD * 65)
        dmask = sb("p2_dmask", [128, 4, 512], BF16)
        sel = sb("p2_sel", [65, 64]); ones64 = sb("p2_ones", [64, 64])
        hg = sb("p2_hg", [64, 1]); lam = sb("p2_lam", [64, 128])
        lt = sb("p2_lt", [64, 64]); lsum = sb("p2_ls", [64, 2]); neg_lam = sb("p2_nl", [64, 1]); gsc = sb("p2_gsc", [64, 1])
        cB = Buf(); d0 = sc.dsem()
        sc.dma(d0, [(kT[:, a2, :], scr["kTd"][2 * a2:2 * a2 + 2].rearrange("b p s -> (b p) s")) for a2 in range(HD // 2)] +
               [(v, scr["vd"].rearrange("(n p) f -> p n f", p=128)),
                (dmask[:], C["dmask"].rearrange("d p q -> p d q")), (sel[:], C["sel"]), (ones64[:], C["ones64"]),
                (hg[:], W["hgain"][l]), (lam[:], W["lam"][l:l + 1, :].broadcast_to([64, 128]))], writes=[cB])
        pB = Buf()
        sc.op("pool", lambda h: h.memset(vfull[:, (S // 128) * HD * 65:], 0.0), writes=[cB])
        sc.op("dve", lambda h: h.tensor_tensor(out=lt[:, 0:32], in0=lam[:, 0:32], in1=lam[:, 32:64], op=ALU.mult), reads=[cB], writes=[pB])
        sc.op("dve", lambda h: h.tensor_tensor(out=lt[:, 32:64], in0=lam[:, 64:96], in1=lam[:, 96:128], op=ALU.mult), reads=[cB], writes=[pB])
        sc.op("dve", lambda h: h.tensor_reduce(out=lsum[:], in_=lt[:].rearrange("p (a b) -> p a b", b=32), axis=AX.X, op=ALU.add),
              reads=[pB], writes=[pB])
        sc.op("act", lambda h: h.activation(out=lsum[:], in_=lsum[:], func=AF.Exp), reads=[pB], writes=[pB])
        sc.op("dve", lambda h: h.tensor_tensor(out=neg_lam[:], in0=lsum[:, 1:2], in1=lsum[:, 0:1], op=ALU.subtract), reads=[pB], writes=[pB])
        sc.op("dve", lambda h: h.tensor_scalar(out=neg_lam[:], in0=neg_lam[:], scalar1=-lam_init, scalar2=None, op0=ALU.add), reads=[pB], writes=[pB])
        sc.op("dve", lambda h: h.tensor_scalar(out=gsc[:], in0=hg[:], scalar1=(1.0 - lam_init), scalar2=None, op0=ALU.mult), reads=[cB, pB], writes=[pB])

        qT = [sb("p2_q%d" % i, [128, HD * 2, TB], BF16) for i in range(2)]
        qB = [Buf() for _ in range(2)]; qds = [sc.dsem() for _ in range(2)]
        for i in range(2):
            sc.op("pool", lambda h, i=i: h.memset(qT[i][:], 0.0), writes=[qB[i]])
        psS = [pt("p2_psS%d" % i, [128, 2, 512]) for i in range(2)]; psSB = [PBuf() for _ in range(2)]
        psO = [pt("p2_psO%d" % m, [128, 512]) for m in range(2)]
        psOB = [PBuf() for m in range(2)]
        psE = [pt("p2_psE%d" % i, [128, 512]) for i in range(2)]; psEB = [PBuf() for _ in range(2)]
        pT = [sb("p2_pT%d" % i, [128, 2, 512], BF16) for i in range(3)]; pTB = [Buf() for _ in range(3)]
        X = [sb("p2_X%d" % m, [65, 512]) for m in range(2)]; XB = [Buf() for _ in range(2)]
        r = [sb("p2_r%d" % m, [64, 512]) for m in range(2)]; rB = [Buf() for _ in range(2)]
        o = sb("p2_o", [64, 512]); oB = Buf()
        sq = sb("p2_sq", [64, 512]); sqB = Buf()
        rs = sb("p2_rs", [64, 512]); rsB = Buf()
        ost = [sb("p2_ost%d" % i, [64, HD, 512], BF16) for i in range(2)]
        ostB = [Buf() for _ in range(2)]; ods = [sc.dsem() for _ in range(2)]
        scale = 32 ** -0.5
        cnt = 0
        pending = []

        def flush(n=None):
            c = len(pending) if n is None else min(n, len(pending))
            for _ in range(c):
                pending.pop(0)()

        for t in range(NB):
            tok0 = t * TB
            qb = t % 2
            sc.dma(qds[qb], [(qT[qb][32 * ((h2 % 2) * 2 + m2):32 * ((h2 % 2) * 2 + m2) + 32, h2 * 2 + m2, :],
                              scr["qTd"][h2, 32 * m2:32 * m2 + 32, tok0:tok0 + TB]) for h2 in range(HD) for m2 in range(2)],
                   writes=[qB[qb]])
            for hh in range(HD):
                nk = 4 * t + 4

                def c0_of(i):
                    return 128 * max(0, i - 4 * t)

                def qk(i):
                    sl = (cnt + i) % 2
                    c0 = c0_of(i)
                    for m in range(2):
                        sc.op("pe", lambda h: h.matmul(psS[sl][:, m, c0:], lhsT=kT[:, hh // 2, i * 128:(i + 1) * 128],
                                                       rhs=qT[qb][:, hh * 2 + m, c0:], start=True, stop=True),
                              reads=[cB, qB[qb]], writes=[psSB[sl]], inc=(m == 1))

                qk(0)
                for i in range(nk):
                    sl = (cnt + i) % 2
                    p3 = (cnt + i) % 3
                    if i + 1 < nk:
                        qk(i + 1)
                    c0 = c0_of(i)
                    sc.op("act", lambda h: h.activation(out=pT[p3][:, :, c0:], in_=psS[sl][:, :, c0:], func=AF.Exp, scale=scale),
                          reads=[psSB[sl]], writes=[pTB[p3]])
                    if i >= 4 * t:
                        di = i - 4 * t
                        sc.op("dve" if i % 2 == 0 else "pool", lambda h: h.tensor_tensor(
                            out=pT[p3][:, :, c0:], in0=pT[p3][:, :, c0:], in1=dmask[:, di:di + 1, c0:].broadcast_to([128, 2, 512 - c0]), op=ALU.mult),
                            reads=[cB, pTB[p3]], writes=[pTB[p3]])
                    for m in range(2):
                        sc.op("pe", lambda h: h.matmul(psO[m][:, c0:], lhsT=vfull[:, (i * HD + hh) * 65:(i * HD + hh) * 65 + 128], rhs=pT[p3][:, m, c0:],
                                                       start=(i == 0), stop=(i == nk - 1)),
                              reads=[cB, pTB[p3]], writes=[psOB[m]], inc=(i == nk - 1))
                    if i == 0:
                        npend0 = len(pending)
                    if i >= 1:
                        if i < nk - 1:
                            target_left = npend0 - (npend0 * i) // (nk - 1)
                            flush(max(0, len(pending) - target_left))
                        else:
                            flush()
                cnt += nk
                flush()
                sc.op("dve", lambda h: h.tensor_copy(out=X[0][:], in_=psO[0][0:65, :]), reads=[psOB[0]], writes=[XB[0]])
                sc.op("dve", lambda h: h.tensor_copy(out=X[1][:], in_=psO[1][0:65, :]), reads=[psOB[1]], writes=[XB[1]])

                def E(eng, fn, reads, writes):
                    pending.append(lambda: sc.op(eng, fn, reads=reads, writes=writes))

                for m in range(2):
                    E("pe", lambda h, m=m: h.matmul(psE[m][0:64, :], lhsT=sel[:], rhs=X[m][:], start=True, stop=True),
                      [cB, XB[m]], [psEB[m]])
                    E("dve", lambda h, m=m: h.reciprocal(out=r[m][:], in_=psE[m][0:64, :]), [psEB[m]], [rB[m]])
                    E("dve" if m == 0 else "pool", lambda h, m=m: h.tensor_tensor(
                        out=r[m][:], in0=X[m][0:64, :], in1=r[m][:], op=ALU.mult), [XB[m], rB[m]], [rB[m]])
                E("dve", lambda h: h.scalar_tensor_tensor(out=o[:], in0=r[1][:], scalar=neg_lam[:, 0:1], in1=r[0][:],
                                                          op0=ALU.mult, op1=ALU.add), [rB[0], rB[1], pB], [oB])
                E("pool", lambda h: h.tensor_tensor(out=sq[:], in0=o[:], in1=o[:], op=ALU.mult), [oB], [sqB])
                E("pe", lambda h: h.matmul(psE[0][0:64, :], lhsT=ones64[:], rhs=sq[:], start=True, stop=True), [cB, sqB], [psEB[0]])
                E("act", lambda h: h.activation(out=rs[:], in_=psE[0][0:64, :], func=AF.Ln, scale=1.0 / 64, bias=EPS), [psEB[0]], [rsB])
                E("act", lambda h: h.activation(out=rs[:], in_=rs[:], func=AF.Exp, scale=-0.5), [rsB], [rsB])
                E("dve", lambda h, qb=qb, hh=hh: h.scalar_tensor_tensor(out=ost[qb][:, hh, :], in0=o[:], scalar=gsc[:, 0:1], in1=rs[:],
                                                                       op0=ALU.mult, op1=ALU.mult), [oB, rsB, pB], [ostB[qb]])
            pending.append(lambda qb=qb, tok0=tok0: sc.dma(
                ods[qb], [(scr["mixT"][hh2 // 2, (hh2 % 2) * 64:(hh2 % 2) * 64 + 64, tok0:tok0 + TB], ost[qb][:, hh2, :])
                          for hh2 in range(HD)], reads=[ostB[qb]]))
        flush()


def phase3(k, l):
    nc, sc, cfg, W, C = k.nc, k.sc, k.cfg, k.W, k.C
    S, HD, HL = cfg.S, cfg.HD, cfg.HL
    scr = k.scr
    SBK = 2048
    NSB = S // SBK
    NC3 = HL // 2
    PATS = (1, 4, 16)
    scale = 64 ** -0.5
    with ExitStack() as ps:
        sb, pt = _mk(k, l, ps)
        qP = [sb("p3_q%d" % i, [128, NC3, SBK], BF16) for i in range(2)]; qB = Buf()
        for i in range(2):
            sc.op("pool", lambda h, i=i: h.memset(qP[i][:], 0.0), writes=[qB])
        kL = [sb("p3_k%d" % i, [128, NC3, SBK], BF16) for i in range(2)]; kB = [Buf() for _ in range(2)]
        vT = sb("p3_vT", [128, NC3, SBK], BF16); vTB = Buf()
        lds = [sc.dsem() for _ in range(3)]
        vt = [[sb("p3_vt%d_%d" % (pi, i), [128, 16, HL * 65], BF16) for i in range(2)] for pi in range(3)]
        vtB = [[Buf() for i in range(2)] for pi in range(3)]
        lmask = sb("p3_lmask", [128, 256], BF16); sel = sb("p3_sel", [65, 64])
        cB = Buf(); d0 = sc.dsem()
        sc.dma(d0, [(lmask[:], C["lmask"]), (sel[:], C["sel"])], writes=[cB])
        for pi in range(3):
            for i in range(2):
                sc.op("pool", lambda h: h.memset(vt[pi][i][:], 1.0), writes=[vtB[pi][i]])
        acc = [sb("p3_acc%d" % i, [65, SBK]) for i in range(2)]; accB = [Buf() for _ in range(2)]
        pT = [sb("p3_pT%d" % i, [128, 4, 256], BF16) for i in range(3)]; pTB = [Buf() for _ in range(3)]
        rr = sb("p3_rr", [64, 512]); rrB = Buf()
        ost = [sb("p3_ost%d" % i, [64, SBK], BF16) for i in range(2)]; ostB = [Buf() for _ in range(2)]
        ods = [sc.dsem() for _ in range(2)]
        psV = pt("p3_psV", [128, NC3, 128], BF16); psVB = PBuf()
        psS = [pt("p3_psS%d" % i, [128, 4, 256]) for i in range(2)]; psSB = [PBuf() for _ in range(2)]
        psO = [pt("p3_psO%d" % i, [128, 4, 128]) for i in range(2)]; psOB = [PBuf() for _ in range(2)]
        psE = pt("p3_psE", [128, 512]); psEB = PBuf()
        gi = 0
        hi = 0
        for u in range(NSB):
            ub = u % 2
            t0 = u * SBK
            sc.dma(lds[0], [(qP[par][64 * par:64 * par + 64, :, :],
                             scr["qTl"][:, 64 * par:64 * par + 64, t0:t0 + SBK].rearrange("c p s -> p c s")) for par in range(2)],
                   writes=[qB])
            sc.dma(lds[1], [(kL[ub][:], scr["kTl"][:, :, t0:t0 + SBK].rearrange("c p s -> p c s"))], writes=[kB[ub]])
            sc.dma(lds[2], [(vT[:], scr["vTl"][:, :, t0:t0 + SBK].rearrange("c p s -> p c s"))], writes=[vTB])
            vi = 0
            for pi, d in enumerate(PATS):
                for ti in range(16):
                    r_, nbl = ti % d, ti // d
                    off = 128 * d * nbl + r_
                    for c in range(NC3):
                        sc.op("pe", lambda h: h.transpose(out=psV[:, c, :], in_=vT[:, c, off:off + 127 * d + 1:d],
                                                          identity=k.ident_bf[:]),
                              reads=[vTB, k.cB], writes=[psVB], inc=(c == NC3 - 1))
                    dst = vt[pi][ub][:, ti, :].rearrange("p (h e) -> p h e", e=65)[:, :, 0:64]
                    src = psV[:].rearrange("p c (h e) -> p (c h) e", e=64)
                    if vi % 2 == 0:
                        sc.op("act", lambda h: h.copy(out=dst, in_=src), reads=[psVB], writes=[vtB[pi][ub]])
                    else:
                        sc.op("dve", lambda h: h.tensor_copy(out=dst, in_=src), reads=[psVB], writes=[vtB[pi][ub]])
                    vi += 1
            for hh in range(HL):
                c = hh // 2
                p0 = 64 * (hh % 2)
                ab = hi % 2
                hi += 1
                groups = [(pi, d, tg) for pi, d in enumerate(PATS) for tg in range(4)]
                gbase = gi
                gi += len(groups)

                def tiles_of(g):
                    pi, d, tg = groups[g]
                    tiles = []
                    for q4 in range(4):
                        ti = tg * 4 + q4
                        r_, nbl = ti % d, ti // d
                        off = 128 * d * nbl + r_
                        if nbl > 0:
                            prev = (ub, off - 128 * d, ti - d)
                        elif u > 0:
                            nbp = 16 // d - 1
                            prev = (1 - ub, 128 * d * nbp + r_, r_ + d * nbp)
                        else:
                            prev = None
                        tiles.append((ti, off, prev))
                    return tiles

                def g_qk(g):
                    pi, d, tg = groups[g]
                    sl = (gbase + g) % 2
                    for q4, (ti, off, prev) in enumerate(tiles_of(g)):
                        Q = qP[hh % 2][:, c, off:off + 127 * d + 1:d]
                        Kc = kL[ub][:, c, off:off + 127 * d + 1:d]
                        if prev is not None:
                            Kp = kL[prev[0]][:, c, prev[1]:prev[1] + 127 * d + 1:d]
                            rd = [qB, kB[ub], kB[prev[0]]]
                        else:
                            Kp = Kc
                            rd = [qB, kB[ub]]
                        sc.op("pe", lambda h: h.matmul(psS[sl][:, q4, 0:128], lhsT=Kp, rhs=Q, start=True, stop=True),
                              reads=rd, writes=[psSB[sl]], inc=False)
                        sc.op("pe", lambda h: h.matmul(psS[sl][:, q4, 128:256], lhsT=Kc, rhs=Q, start=True, stop=True),
                              reads=rd, writes=[psSB[sl]], inc=(q4 == 3))

                def g_exp(g):
                    sl = (gbase + g) % 2
                    p3 = (gbase + g) % 3
                    sc.op("act", lambda h: h.activation(out=pT[p3][:], in_=psS[sl][:], func=AF.Exp, scale=scale),
                          reads=[psSB[sl]], writes=[pTB[p3]])
                    sc.op("dve" if g % 2 == 0 else "pool", lambda h: h.tensor_tensor(
                        out=pT[p3][:], in0=pT[p3][:], in1=lmask[:, None, :].broadcast_to([128, 4, 256]), op=ALU.mult),
                        reads=[pTB[p3], cB], writes=[pTB[p3]])

                def g_pv(g):
                    pi, d, tg = groups[g]
                    sl = (gbase + g) % 2
                    p3 = (gbase + g) % 3
                    for q4, (ti, off, prev) in enumerate(tiles_of(g)):
                        if prev is not None:
                            sc.op("pe", lambda h: h.matmul(psO[sl][0:65, q4, :], lhsT=vt[pi][prev[0]][:, prev[2], hh * 65:(hh + 1) * 65],
                                                           rhs=pT[p3][:, q4, 0:128], start=True, stop=False),
                                  reads=[pTB[p3], vtB[pi][prev[0]]], writes=[psOB[sl]], inc=False)
                        sc.op("pe", lambda h: h.matmul(psO[sl][0:65, q4, :], lhsT=vt[pi][ub][:, ti, hh * 65:(hh + 1) * 65],
                                                       rhs=pT[p3][:, q4, 128:256], start=(prev is None), stop=True),
                              reads=[pTB[p3], vtB[pi][ub]], writes=[psOB[sl]], inc=(q4 == 3))

                def g_acc(g):
                    pi, d, tg = groups[g]
                    sl = (gbase + g) % 2
                    if d == 1:
                        dstv = acc[ab][:, tg * 512:(tg + 1) * 512].rearrange("p (a j) -> p a j", j=128)
                    elif d == 4:
                        dstv = acc[ab][:, tg * 512:(tg + 1) * 512].rearrange("p (j r) -> p r j", r=4)
                    else:
                        dstv = acc[ab][:].rearrange("p (j r) -> p r j", r=16)[:, tg * 4:tg * 4 + 4, :]
                    if pi == 0:
                        sc.op("dve", lambda h: h.tensor_copy(out=dstv, in_=psO[sl][0:65, :, :]), reads=[psOB[sl]], writes=[accB[ab]])
                    else:
                        sc.op("dve", lambda h: h.tensor_tensor(out=dstv, in0=dstv, in1=psO[sl][0:65, :, :], op=ALU.add),
                              reads=[psOB[sl], accB[ab]], writes=[accB[ab]])

                skewed(len(groups), [g_qk, g_exp, g_pv, g_acc])
                for sbk in range(4):
                    cs_ = slice(sbk * 512, (sbk + 1) * 512)
                    sc.op("pe", lambda h: h.matmul(psE[0:64, :], lhsT=sel[:], rhs=acc[ab][:, cs_], start=True, stop=True),
                          reads=[cB, accB[ab]], writes=[psEB])
                    sc.op("dve", lambda h: h.reciprocal(out=rr[:], in_=psE[0:64, :]), reads=[psEB], writes=[rrB])
                    sc.op("pool", lambda h: h.tensor_tensor(out=ost[ab][:, cs_], in0=acc[ab][0:64, cs_], in1=rr[:], op=ALU.mult),
                          reads=[rrB, accB[ab]], writes=[ostB[ab]])
                row = HD * 64 + hh * 64
                sc.dma(ods[ab], [(scr["mixT"][row // 128, (row % 128):(row % 128) + 64, t0:t0 + SBK], ost[ab][:])],
                       reads=[ostB[ab]])


def phase4(k, l):
    nc, sc, cfg, W, C = k.nc, k.sc, k.cfg, k.W, k.C
    S, HD, HL, G, HS, NB, NX = cfg.S, cfg.HD, cfg.HL, cfg.G, cfg.HS, cfg.NB, cfg.NXBC
    scr = k.scr
    XC = HS // 2
    HW = HS * 64
    with ExitStack() as ps:
        sb, pt = _mk(k, l, ps)
        triu = sb("p4_triu", [128, 128]); negm = sb("p4_negm", [128, 128])
        dtb = sb("p4_dtb", [128, HS]); alog = sb("p4_alog", [128, HS]); Dd = sb("p4_D", [128, HS]); gn = sb("p4_gn", [128, HW])
        negA = sb("p4_negA", [128, HS])
        cB = Buf(); d0 = sc.dsem()
        sc.dma(d0, [(triu[:], C["triu"]), (negm[:], C["negm"]),
                    (dtb[:], W["dt_bias"][l:l + 1, :].broadcast_to([128, HS])),
                    (alog[:], W["A_log"][l:l + 1, :].broadcast_to([128, HS])),
                    (Dd[:], W["ssd_D"][l:l + 1, :].broadcast_to([128, HS])),
                    (gn[:], W["ssd_norm"][l:l + 1, :].broadcast_to([128, HW]))], writes=[cB])
        pB = Buf()
        sc.op("act", lambda h: h.activation(out=negA[:], in_=alog[:], func=AF.Exp), reads=[cB], writes=[pB])
        sc.op("dve", lambda h: h.tensor_scalar(out=negA[:], in0=negA[:], scalar1=-1.0, scalar2=None, op0=ALU.mult), reads=[pB], writes=[pB])
        xb_ = [sb("p4_xb%d" % i, [128, NX, TB], BF16) for i in range(2)]
        z_ = [sb("p4_z%d" % i, [128, 4, HW], BF16) for i in range(2)]
        dt_ = [sb("p4_dt%d" % i, [128, 4, HS]) for i in range(2)]
        inB = [Buf() for _ in range(2)]; lds = [sc.dsem() for _ in range(2)]
        dtp = sb("p4_dtp", [128, 4, HS]); aa = sb("p4_a", [128, 4, HS]); dB = Buf()
        a_bc = sb("p4_abc", [128, HS, 128]); abB = Buf()
        cs_sb2 = [sb("p4_cs%d" % i, [128, HS]) for i in range(2)]; csl2 = [sb("p4_csl%d" % i, [128, HS]) for i in range(2)]; csB2 = [Buf() for _ in range(2)]
        arg = sb("p4_arg", [128, HS, 128]); argB = Buf()
        E = sb("p4_E", [128, HS, 128]); EB = Buf()
        MT2 = [sb("p4_MT%d" % i, [128, HS, 128], BF16) for i in range(2)]; MTB2 = [Buf() for _ in range(2)]
        x_sb2 = [sb("p4_x%d" % i, [128, HW]) for i in range(2)]; B_sb2 = [sb("p4_B%d" % i, [128, G * 128], BF16) for i in range(2)]; xB2 = [Buf() for _ in range(2)]
        xdt2 = [sb("p4_xdt%d" % i, [128, HS, 64], BF16) for i in range(2)]; xdtB2 = [Buf() for _ in range(2)]
        xdd2 = [sb("p4_xdd%d" % i, [128, HS, 64], BF16) for i in range(2)]; xddB2 = [Buf() for _ in range(2)]
        ecs2 = [sb("p4_ecs%d" % i, [128, HS]) for i in range(2)]; dst_ = sb("p4_dst", [128, HS]); edec2 = [sb("p4_edec%d" % i, [128, HS]) for i in range(2)]; eB2 = [Buf() for _ in range(2)]; dstB = Buf()
        t13 = [sb("p4_t1%d" % i, [128, HW]) for i in range(3)]; t1B3 = [Buf() for _ in range(3)]
        t33 = [sb("p4_t3%d" % i, [128, HW]) for i in range(3)]; t3B3 = [Buf() for _ in range(3)]
        yv = sb("p4_yv", [128, HW]); yvB = Buf()
        szb = [sb("p4_szb%d" % i, [128, 4, HW]) for i in range(2)]; szbB = [Buf() for _ in range(2)]
        junk = sb("p4_junk", [128, HW], BF16); junkB = Buf()
        ss2 = sb("p4_ss2", [128, G]); rs2 = sb("p4_rs2", [128, G]); ssB = Buf(); rsB = Buf()
        yn = sb("p4_yn", [128, HW], BF16); ynB = Buf()
        yst = [sb("p4_yst%d" % i, [128, XC, TB], BF16) for i in range(2)]; ystB = [Buf() for _ in range(2)]
        sds = [sc.dsem() for _ in range(2)]
        st = sb("p4_st", [128, HW]); st_bf = sb("p4_stbf", [128, HW], BF16); stB = Buf(); stbB = Buf()
        ps1 = pt("p4_ps1", [128, 512]); ps1B = PBuf()
        psR = pt("p4_psR", [128, 2, 512]); psRB = PBuf()
        psXT = pt("p4_psXT", [128, 1024], BF16); psXTB = PBuf(); psXTb = pt("p4_psXTb", [128, 1024], BF16); psXTbB = PBuf()
        psY = pt("p4_psY", [128, 512]); psYB = PBuf()
        psYO = pt("p4_psYO", [128, 512]); psYOB = PBuf()
        psS = pt("p4_psS", [128, 512]); psSB = PBuf()
        sc.op("pool", lambda h: h.memset(st[:], 0.0), writes=[stB])
        sc.op("pool", lambda h: h.memset(st_bf[:], 0.0), writes=[stbB])
        XTO = XC * 128 + G * 128
        def load4(tt):
            bb = tt % 2
            tk = tt * TB
            sc.dma(lds[bb], [(xb_[bb][:], scr["xbcT"][:, :, tk:tk + TB].rearrange("c p s -> p c s")),
                             (z_[bb][:], scr["zs"][tk:tk + TB, :].rearrange("(n p) f -> p n f", p=128)),
                             (dt_[bb][:], scr["dts"][tk:tk + TB, :].rearrange("(n p) f -> p n f", p=128))],
                   writes=[inB[bb]])

        load4(0)
        for t in range(NB):
            tok0 = t * TB
            b = t % 2
            if t + 1 < NB:
                load4(t + 1)
            sc.op("dve", lambda h: h.tensor_tensor(out=dtp[:], in0=dt_[b][:], in1=dtb[:, None, :].broadcast_to([128, 4, HS]), op=ALU.add),
                  reads=[inB[b], cB], writes=[dB])
            sc.op("act", lambda h: h.activation(out=dtp[:], in_=dtp[:], func=AF.Exp), reads=[dB], writes=[dB])
            sc.op("act", lambda h: h.activation(out=dtp[:], in_=dtp[:], func=AF.Ln, bias=1.0), reads=[dB], writes=[dB])
            sc.op("dve", lambda h: h.tensor_tensor(out=aa[:], in0=dtp[:], in1=negA[:, None, :].broadcast_to([128, 4, HS]), op=ALU.mult),
                  reads=[dB, pB], writes=[dB])
            sc.op("act", lambda h: h.activation(out=szb[b][:], in_=z_[b][:], func=AF.Silu), reads=[inB[b]], writes=[szbB[b]])
            def front(n):
                q = n % 2
                csl_ = slice(n * 128, (n + 1) * 128)
                sc.op("dve", lambda h: h.tensor_copy(out=a_bc[:], in_=aa[:, n, :, None].broadcast_to([128, HS, 128])),
                      reads=[dB], writes=[abB])
                sc.op("pe", lambda h: h.matmul(ps1[:, 256:256 + HS], lhsT=triu[:], rhs=aa[:, n, :], start=True, stop=True),
                      reads=[cB, dB], writes=[ps1B])
                for hh in range(HS):
                    sc.op("pe", lambda h: h.matmul(psR[:, hh // 4, (hh % 4) * 128:(hh % 4 + 1) * 128], lhsT=a_bc[:, hh, :], rhs=triu[:],
                                                   start=True, stop=True), reads=[cB, abB], writes=[psRB], inc=(hh == HS - 1))
                psRv = psR[:].rearrange("p a (b c) -> p (a b) c", c=128)[:, 0:HS, :]
                sc.op("act", lambda h: h.copy(out=cs_sb2[q][:], in_=ps1[:, 256:256 + HS]), reads=[ps1B], writes=[csB2[q]])
                sc.op("act", lambda h: h.copy(out=csl2[q][:], in_=psRv[:, :, 127]), reads=[psRB], writes=[csB2[q]])
                sc.op("dve", lambda h: h.tensor_tensor(out=arg[:], in0=psRv, in1=cs_sb2[q][:, :, None].broadcast_to([128, HS, 128]), op=ALU.subtract),
                      reads=[psRB, csB2[q]], writes=[argB])
                sc.op("dve", lambda h: h.tensor_tensor(out=arg[:], in0=arg[:], in1=negm[:, None, :].broadcast_to([128, HS, 128]), op=ALU.add),
                      reads=[argB, cB], writes=[argB])
                sc.op("act", lambda h: h.activation(out=E[:], in_=arg[:], func=AF.Exp), reads=[argB], writes=[EB])
                for g in range(G):
                    sc.op("pe", lambda h: h.matmul(ps1[:, g * 128:(g + 1) * 128], lhsT=xb_[b][:, XC + g, csl_], rhs=xb_[b][:, XC + G + g, csl_],
                                                   start=True, stop=True), reads=[inB[b]], writes=[ps1B], inc=(g == G - 1))
                for g in range(G):
                    sc.op("dve", lambda h: h.tensor_tensor(out=MT2[q][:, 3 * g:3 * g + 3, :], in0=E[:, 3 * g:3 * g + 3, :],
                                                           in1=ps1[:, None, g * 128:(g + 1) * 128].broadcast_to([128, 3, 128]), op=ALU.mult),
                          reads=[EB, ps1B], writes=[MTB2[q]])
                for j in range(XC + G):
                    sc.op("pe", lambda h: h.transpose(out=psXT[:, j * 128:(j + 1) * 128], in_=xb_[b][:, j, csl_], identity=k.ident_bf[:]),
                          reads=[inB[b], k.cB], writes=[psXTB], inc=(j == XC + G - 1))
                sc.op("act", lambda h: h.copy(out=x_sb2[q][:], in_=psXT[:, 0:HW]), reads=[psXTB], writes=[xB2[q]])
                sc.op("act", lambda h: h.copy(out=B_sb2[q][:], in_=psXT[:, HW:HW + G * 128]), reads=[psXTB], writes=[xB2[q]])
                sc.op("dve", lambda h: h.tensor_tensor(out=xdt2[q][:], in0=x_sb2[q][:].rearrange("p (h e) -> p h e", e=64),
                                                       in1=dtp[:, n, :, None].broadcast_to([128, HS, 64]), op=ALU.mult),
                      reads=[xB2[q], dB], writes=[xdtB2[q]])
                sc.op("act", lambda h: h.activation(out=ecs2[q][:], in_=cs_sb2[q][:], func=AF.Exp), reads=[csB2[q]], writes=[eB2[q]])
                sc.op("pool", lambda h: h.tensor_tensor(out=t33[n % 3][:].rearrange("p (h e) -> p h e", e=64), in0=x_sb2[q][:].rearrange("p (h e) -> p h e", e=64),
                                                        in1=Dd[:, :, None].broadcast_to([128, HS, 64]), op=ALU.mult),
                      reads=[xB2[q], cB], writes=[t3B3[n % 3]])
                sc.op("dve", lambda h: h.tensor_tensor(out=dst_[:], in0=csl2[q][:], in1=cs_sb2[q][:], op=ALU.subtract), reads=[csB2[q]], writes=[dstB])
                sc.op("act", lambda h: h.activation(out=dst_[:], in_=dst_[:], func=AF.Exp), reads=[dstB], writes=[dstB])
                sc.op("act", lambda h: h.activation(out=edec2[q][:], in_=csl2[q][:], func=AF.Exp), reads=[csB2[q]], writes=[eB2[q]])
                sc.op("dve", lambda h: h.tensor_tensor(out=xdd2[q][:], in0=xdt2[q][:], in1=dst_[:, :, None].broadcast_to([128, HS, 64]), op=ALU.mult),
                      reads=[xdtB2[q], dstB], writes=[xddB2[q]])

            def back(n):
                q = n % 2
                csl_ = slice(n * 128, (n + 1) * 128)
                for hh in range(HS):
                    sc.op("pe", lambda h: h.matmul(psY[:, hh * 64:(hh + 1) * 64], lhsT=MT2[q][:, hh, :], rhs=xdt2[q][:, hh, :], start=True, stop=True),
                          reads=[MTB2[q], xdtB2[q]], writes=[psYB], inc=(hh == HS - 1))
                for g in range(G):
                    sc.op("pe", lambda h: h.matmul(psYO[:, g * 192:(g + 1) * 192], lhsT=xb_[b][:, XC + G + g, csl_], rhs=st_bf[:, g * 192:(g + 1) * 192],
                                                   start=True, stop=True), reads=[inB[b], stbB], writes=[psYOB], inc=(g == G - 1))
                sc.op("dve", lambda h: h.tensor_tensor(out=t13[n % 3][:].rearrange("p (h e) -> p h e", e=64), in0=psYO[:, 0:HW].rearrange("p (h e) -> p h e", e=64),
                                                       in1=ecs2[q][:, :, None].broadcast_to([128, HS, 64]), op=ALU.mult),
                      reads=[psYOB, eB2[q]], writes=[t1B3[n % 3]])
                sc.op("dve", lambda h: h.tensor_tensor(out=t13[n % 3][:], in0=psY[:, 0:HW], in1=t13[n % 3][:], op=ALU.add), reads=[psYB, t1B3[n % 3]], writes=[t1B3[n % 3]])
                for g in range(G):
                    sc.op("pe", lambda h: h.matmul(psS[:, g * 192:(g + 1) * 192], lhsT=B_sb2[q][:, g * 128:(g + 1) * 128],
                                                   rhs=xdd2[q][:, 3 * g:3 * g + 3, :].rearrange("p h e -> p (h e)"), start=True, stop=True),
                          reads=[xB2[q], xddB2[q]], writes=[psSB], inc=(g == G - 1))
                sc.op("dve", lambda h: h.tensor_tensor(out=st[:].rearrange("p (h e) -> p h e", e=64), in0=st[:].rearrange("p (h e) -> p h e", e=64),
                                                       in1=edec2[q][:, :, None].broadcast_to([128, HS, 64]), op=ALU.mult),
                      reads=[stB, eB2[q]], writes=[stB])
                sc.op("dve", lambda h: h.tensor_tensor(out=st[:], in0=st[:], in1=psS[:, 0:HW], op=ALU.add), reads=[stB, psSB], writes=[stB])
                sc.op("pool", lambda h: h.tensor_copy(out=st_bf[:], in_=st[:]), reads=[stB], writes=[stbB])

            def post(n):
                q = n % 2
                csl_ = slice(n * 128, (n + 1) * 128)
                sc.op("pool", lambda h: h.tensor_tensor(out=yv[:], in0=t13[n % 3][:], in1=t33[n % 3][:], op=ALU.add), reads=[t1B3[n % 3], t3B3[n % 3]], writes=[yvB])
                sc.op("dve", lambda h: h.tensor_tensor(out=yv[:], in0=yv[:], in1=szb[b][:, n, :], op=ALU.mult), reads=[yvB, szbB[b]], writes=[yvB])
                for g in range(G):
                    sc.op("act", lambda h: h.activation(out=junk[:, 0:192], in_=yv[:, g * 192:(g + 1) * 192], func=AF.Square,
                                                        accum_out=ss2[:, g:g + 1]), reads=[yvB], writes=[junkB, ssB])
                rms_rstd(k, ps, ss2, rs2, G, 192, ssB, rsB)
                for g in range(G):
                    sc.op("dve", lambda h: h.scalar_tensor_tensor(out=yn[:, g * 192:(g + 1) * 192], in0=yv[:, g * 192:(g + 1) * 192],
                                                                  scalar=rs2[:, g:g + 1], in1=gn[:, g * 192:(g + 1) * 192],
                                                                  op0=ALU.mult, op1=ALU.mult), reads=[yvB, rsB, cB], writes=[ynB])
                for j in range(XC):
                    sc.op("pe", lambda h: h.transpose(out=psXTb[:, j * 128:(j + 1) * 128], in_=yn[:, j * 128:(j + 1) * 128],
                                                      identity=k.ident_bf[:]), reads=[ynB, k.cB], writes=[psXTbB], inc=(j == XC - 1))
                sc.op("act", lambda h: h.copy(out=yst[b][:, :, csl_], in_=psXTb[:, 0:XC * 128].rearrange("p (j s) -> p j s", s=128)),
                      reads=[psXTbB], writes=[ystB[b]])

            front(0)
            for n in range(4):
                if n + 1 < 4:
                    front(n + 1)
                back(n)
                if n > 0:
                    post(n - 1)
            post(3)
            c0 = (HD * 64 + HL * 64) // 128
            sc.dma(sds[b], [(scr["mixT"][c0:c0 + XC, :, tok0:tok0 + TB].rearrange("c p s -> p c s"), yst[b][:])], reads=[ystB[b]])


def post_norm_add(k, sc, psF, psFB, n, ss, ssB, rstd, rsB, gb, gB, yt, ytB, xres, xresB, xo, xoB, junk, junkB, add_eng="pool"):
    sc.op("act", lambda h: h.activation(out=junk[:], in_=psF[:], func=AF.Square, accum_out=ss[:, n:n + 1]),
          reads=[psFB], writes=[junkB, ssB])
    rms_rstd(k, None, ss[:, n:n + 1], rstd[:, n:n + 1], 1, D, ssB, rsB)
    sc.op("dve", lambda h: h.scalar_tensor_tensor(out=yt[:], in0=psF[:], scalar=rstd[:, n:n + 1], in1=gb[:],
                                                  op0=ALU.mult, op1=ALU.mult), reads=[psFB, rsB, gB], writes=[ytB])
    sc.op(add_eng, lambda h: h.tensor_tensor(out=xo[:, n, :], in0=xres[:, n, :], in1=yt[:], op=ALU.add),
          reads=[ytB, xresB], writes=[xoB])


def phase5a(k, l, x_src):
    nc, sc, cfg, W, C = k.nc, k.sc, k.cfg, k.W, k.C
    S, NB, MC = cfg.S, cfg.NB, cfg.MIXC
    scr = k.scr
    with ExitStack() as ps:
        sb, pt = _mk(k, l, ps)
        wout = sb("p5a_w", [128, MC, D], BF16)
        stg = [sb("p5a_stg%d" % i, [128, 512]) for i in range(3)]
        stgB = [Buf() for _ in range(3)]; ds = [sc.dsem() for _ in range(3)]
        k.gB, k.wB = Buf(), Buf()
        gb = sb("p5a_g", [128, D]); gB = Buf(); d0 = sc.dsem()
        sc.dma(d0, [(gb[:], W["g_postmix"][l:l + 1, :].broadcast_to([128, D]))], writes=[gB])
        load_weights_bf16(k, ps, wout, W["w_out"][l], D, None, stg, stgB, ds, MC)
        mT = [sb("p5a_m%d" % i, [128, MC, TB], BF16) for i in range(2)]; mB = [Buf() for _ in range(2)]
        xr = [sb("p5a_x%d" % i, [128, 4, D]) for i in range(2)]; xB = [Buf() for _ in range(2)]
        lds = [sc.dsem() for _ in range(2)]
        xo = [sb("p5a_xo%d" % i, [128, 4, D]) for i in range(2)]; xoB = [Buf() for _ in range(2)]
        sds = [sc.dsem() for _ in range(2)]
        psF = [pt("p5a_psF%d" % i, [128, D]) for i in range(2)]; psFB = [PBuf() for _ in range(2)]
        ss = sb("p5a_ss", [128, 4]); ssB = Buf(); rstd = sb("p5a_rstd", [128, 4]); rsB = Buf()
        yt = sb("p5a_yt", [128, D]); ytB = Buf()
        junk = sb("p5a_junk", [128, D], BF16); junkB = Buf()
        ci = 0
        def load5a(tt):
            bb = tt % 2
            tk = tt * TB
            sc.dma(lds[bb], [(mT[bb][:], scr["mixT"][:, :, tk:tk + TB].rearrange("c p s -> p c s")),
                             (xr[bb][:], x_src[tk:tk + TB, :].rearrange("(n p) d -> p n d", p=128))],
                   writes=[mB[bb], xB[bb]])

        load5a(0)
        for t in range(NB):
            tok0 = t * TB
            b = t % 2
            if t + 1 < NB:
                load5a(t + 1)
            for n in range(4):
                a = ci % 2
                ci += 1
                for half in range(2):
                    for kc in range(MC):
                        sc.op("pe", lambda h, kc=kc, half=half: h.matmul(
                            psF[a][:, half * 512:(half + 1) * 512], lhsT=mT[b][:, kc, n * 128:(n + 1) * 128],
                            rhs=wout[:, kc, half * 512:(half + 1) * 512], start=(kc == 0), stop=(kc == MC - 1)),
                            reads=[k.wB, mB[b]], writes=[psFB[a]], inc=(kc == MC - 1 and half == 1))
                post_norm_add(k, sc, psF[a], psFB[a], n, ss, ssB, rstd, rsB, gb, gB, yt, ytB, xr[b], xB[b], xo[b], xoB[b], junk, junkB,
                              add_eng=("dve" if n % 2 == 0 else "pool"))
            sc.dma(sds[b], [(scr["xa"][tok0:tok0 + TB, :].rearrange("(n p) d -> p n d", p=128), xo[b][:])], reads=[xoB[b]])


def phase5b(k, l, x_dst):
    nc, sc, cfg, W, C = k.nc, k.sc, k.cfg, k.W, k.C
    S, FC, TBF = cfg.S, cfg.FC, cfg.TBF
    NT = TBF // 128
    scr = k.scr
    with ExitStack() as ps:
        sb, pt = _mk(k, l, ps)
        wup = sb("p5b_wup", [128, 8, 2 * cfg.DFF], BF16)
        wdn = sb("p5b_wdn", [128, FC, D], BF16)
        stg = [sb("p5b_stg%d" % i, [128, 512]) for i in range(3)]
        stgB = [Buf() for _ in range(3)]; ds = [sc.dsem() for _ in range(3)]
        k.gB, k.wB = Buf(), Buf()
        gain = sb("p5b_gain", [128, 8]); gb = sb("p5b_g", [128, D]); gB = Buf(); d0 = sc.dsem()
        fcw = sb("p5b_fcw", [128, 2 * FC, 3]); fcb = sb("p5b_fcb", [128, 2 * FC])
        sc.dma(d0, [(gb[:], W["g_postffn"][l:l + 1, :].broadcast_to([128, D])), (gain[:], W["g_preffn"][l]),
                    (fcw[:], W["fcw"][l]), (fcb[:], W["fcb"][l])], writes=[gB, k.gB])
        load_weights_bf16(k, ps, wup, W["ffn_up"][l], 2 * cfg.DFF, gain, stg, stgB, ds, 8)
        load_weights_bf16(k, ps, wdn, W["ffn_down"][l], D, None, stg, stgB, ds, FC)
        xr2 = [sb("p5b_x%d" % i, [128, NT, D]) for i in range(2)]; xB2 = [[Buf() for _ in range(NT)] for i in range(2)]
        lds2 = [[sc.dsem() for _ in range(NT)] for i in range(2)]; sds2 = [sc.dsem() for _ in range(2)]
        xs2 = [sb("p5b_xs%d" % i, [128, NT, D], BF16) for i in range(2)]; xsB2 = [Buf() for _ in range(2)]
        hT2 = [sb("p5b_hT%d" % i, [128, 8, 2 + TBF], BF16) for i in range(2)]; hTB2 = [Buf() for _ in range(2)]
        aT = sb("p5b_aT", [128, FC, TBF], BF16); aTB = Buf()
        NU = 4
        u = [sb("p5b_u%d" % i, [128, 2 + TBF]) for i in range(NU)]; uB = [Buf() for _ in range(NU)]; uhB = [Buf() for _ in range(NU)]
        y = [sb("p5b_y%d" % i, [128, TBF]) for i in range(NU)]; yB = [Buf() for _ in range(NU)]
        sg = [sb("p5b_sg%d" % i, [128, TBF]) for i in range(2)]; sgB = [Buf() for _ in range(2)]
        ss = sb("p5b_ss", [128, 4]); ssB = Buf(); rstd = sb("p5b_rstd", [128, 4]); rsB = Buf()
        yt = sb("p5b_yt", [128, D]); ytB = Buf()
        junk = sb("p5b_junk", [128, D], BF16); junkB = Buf()
        psT = [pt("p5b_psT%d" % i, [128, 2, 512], BF16) for i in range(2)]; psTB = [PBuf() for _ in range(2)]
        psU = [pt("p5b_psU%d" % i, [128, 512]) for i in range(NU)]; psUB = [PBuf() for _ in range(NU)]
        psF = [pt("p5b_psF%d" % i, [128, D]) for i in range(1)]; psFB = [PBuf() for _ in range(1)]
        for i in range(2):
            sc.op("pool", lambda h, i=i: h.memset(hT2[i][:], 0.0), writes=[hTB2[i]])
        ci = 0
        NBLK = S // TBF

        def do_norm(t):
            b = t % 2
            tok0 = t * TBF
            for n in range(NT):
                sc.dma(lds2[b][n], [(xr2[b][:, n, :], scr["xa"][tok0 + n * 128:tok0 + (n + 1) * 128, :])], writes=[xB2[b][n]])
            for n in range(NT):
                sc.op("act", lambda h, n=n: h.activation(out=junk[:], in_=xr2[b][:, n, :], func=AF.Square, accum_out=ss[:, n:n + 1]),
                      reads=[xB2[b][n]], writes=[junkB, ssB])
            rms_rstd(k, ps, ss, rstd, NT, D, ssB, rsB)
            for n in range(NT):
                if n % 2 == 0:
                    sc.op("dve", lambda h, n=n: h.tensor_scalar(
                        out=xs2[b][:, n, :], in0=xr2[b][:, n, :], scalar1=rstd[:, n:n + 1], scalar2=None, op0=ALU.mult),
                        reads=[xB2[b][n], rsB], writes=[xsB2[b]])
                else:
                    sc.op("act", lambda h, n=n: h.activation(out=xs2[b][:, n, :], in_=xr2[b][:, n, :], func=AF.Copy, scale=rstd[:, n:n + 1]),
                          reads=[xB2[b][n], rsB], writes=[xsB2[b]])
            if t > 0:
                sc.op("pool", lambda h: h.tensor_copy(out=hT2[b][:, :, 0:2], in_=hT2[1 - b][:, :, TBF:TBF + 2]), reads=[hTB2[1 - b]], writes=[hTB2[b]])
            for cp in range(4):
                pi = cp % 2
                for c2 in range(2):
                    c = cp * 2 + c2
                    for n in range(NT):
                        sc.op("pe", lambda h, c=c, c2=c2, n=n: h.transpose(
                            out=psT[pi][:, c2, n * 128:(n + 1) * 128], in_=xs2[b][:, n, c * 128:(c + 1) * 128],
                            identity=k.ident_bf[:]), reads=[xsB2[b], k.cB], writes=[psTB[pi]], inc=(c2 == 1 and n == NT - 1))
                sc.op("act", lambda h, cp=cp: h.copy(out=hT2[b][:, 2 * cp:2 * cp + 2, 2:2 + TBF], in_=psT[pi][:, :, 0:TBF]),
                      reads=[psTB[pi]], writes=[hTB2[b]])

        def do_chunks(t):
            b = t % 2
            chunks = [(j, wi, c) for j in range(FC) for wi, c in enumerate((j, FC + j))]

            def st_pe(i):
                j, wi, c = chunks[i]
                a = i % NU
                for kc in range(8):
                    sc.op("pe", lambda h: h.matmul(psU[a][:, 0:TBF + 2], lhsT=wup[:, kc, c * 128:(c + 1) * 128], rhs=hT2[b][:, kc, :],
                                                   start=(kc == 0), stop=(kc == 7)), reads=[k.wB, hTB2[b]], writes=[psUB[a]], inc=(kc == 7))

            def st_copy(i):
                j, wi, c = chunks[i]
                a = i % NU
                sc.op("act", lambda h: h.copy(out=u[a][:, 0:2 + TBF], in_=psU[a][:, 0:2 + TBF]), reads=[psUB[a]], writes=[uB[a]])
                sc.op("act", lambda h: h.activation(out=y[a][:], in_=psU[a][:, 2:2 + TBF], func=AF.Identity, scale=fcw[:, c, 2:3], bias=fcb[:, c:c + 1]),
                      reads=[psUB[a], k.gB], writes=[yB[a]])

            def st_conv(i):
                j, wi, c = chunks[i]
                a = i % NU
                for kk in range(2):
                    sc.op("dve", lambda h: h.scalar_tensor_tensor(out=y[a][:], in0=u[a][:, kk:kk + TBF], scalar=fcw[:, c, kk:kk + 1], in1=y[a][:],
                                                                  op0=ALU.mult, op1=ALU.add), reads=[uB[a], yB[a], k.gB], writes=[yB[a]])

            def st_gate(i):
                j, wi, c = chunks[i]
                a = i % NU
                if wi == 0:
                    sc.op("act", lambda h: h.activation(out=sg[j % 2][:], in_=y[a][:], func=AF.Silu), reads=[yB[a]], writes=[sgB[j % 2]])
                else:
                    sc.op("pool", lambda h: h.tensor_tensor(out=aT[:, j, :], in0=sg[j % 2][:], in1=y[a][:], op=ALU.mult),
                          reads=[sgB[j % 2], yB[a]], writes=[aTB])

            skewed(len(chunks), [st_pe, st_copy, st_conv, st_gate])

        def do_down(t):
            b = t % 2
            tok0 = t * TBF
            for n in range(NT):
                a = 0
                for half in range(2):
                    for j in range(FC):
                        sc.op("pe", lambda h, j=j, half=half: h.matmul(
                            psF[a][:, half * 512:(half + 1) * 512], lhsT=aT[:, j, n * 128:(n + 1) * 128],
                            rhs=wdn[:, j, half * 512:(half + 1) * 512], start=(j == 0), stop=(j == FC - 1)),
                            reads=[k.wB, aTB], writes=[psFB[a]], inc=(j == FC - 1 and half == 1))
                post_norm_add(k, sc, psF[a], psFB[a], n, ss, ssB, rstd, rsB, gb, gB, yt, ytB, xr2[b], xB2[b][n], xr2[b], xB2[b][n], junk, junkB)
            sc.dma(sds2[b], [(x_dst[tok0:tok0 + TBF, :].rearrange("(n p) d -> p n d", p=128), xr2[b][:])], reads=xB2[b])

        do_norm(0)
        for t in range(NBLK):
            do_chunks(t)
            if t + 1 < NBLK:
                do_norm(t + 1)
            do_down(t)


def kernel(**inputs):
    cfg = Cfg()
    nc = build(cfg)
    hp = host_params(cfg, inputs)
    consts = host_consts(cfg)
    x = np.asarray(inputs["x"], np.float32)
    nb = x.shape[0]
    work = [0, 1, 4, 5][:nb]
    zeros = {"x": np.zeros_like(x[0])}
    for kk, v in hp.items():
        zeros[kk] = np.zeros_like(v)
    for kk, v in consts.items():
        zeros[kk] = np.zeros_like(v)
    in_maps = []
    for c in range(8):
        if c in work:
            m = {"x": np.ascontiguousarray(x[work.index(c)])}
            m.update(hp)
            m.update(consts)
        else:
            m = zeros
        in_maps.append(m)
    res = run_bass_kernel_spmd(nc, in_maps, core_ids=list(range(8)))
    return np.stack([np.asarray(res.results[c]["out"], np.float32) for c in work])
```

```python
import math
import os
from contextlib import ExitStack

import numpy as np
import ml_dtypes

import concourse.bass as bass
import concourse.mybir as mybir
from concourse.bass_utils import run_bass_kernel_spmd

F32 = mybir.dt.float32
BF16 = mybir.dt.bfloat16
AF = mybir.ActivationFunctionType
ALU = mybir.AluOpType
AX = mybir.AxisListType

D = 1024
EPS = 1e-6
TB = 512
SAME_ENGINE_SYNC = True


class Buf:
    __slots__ = ("name", "w", "r", "excl")

    def __init__(self, name="", excl=False):
        self.name = name
        self.w = None
        self.r = {}
        self.excl = excl


def PBuf(name=""):
    return Buf(name, True)


class _Eng:
    def __init__(self, name, h, sem):
        self.name, self.h, self.sem, self.cnt, self.seen = name, h, sem, 0, {}


class _DSem:
    def __init__(self, sem):
        self.sem, self.cnt = sem, 0


class Sched:
    def __init__(self, nc, es):
        self.nc = nc
        self.es = es
        self.E = {}
        for name, h in (("pe", nc.tensor), ("act", nc.scalar), ("dve", nc.vector),
                        ("pool", nc.gpsimd), ("sp", nc.sync)):
            self.E[name] = _Eng(name, h, es.enter_context(nc.semaphore("s_" + name)))
        self.bar = es.enter_context(nc.semaphore("s_bar"))
        self.barcnt = 0
        self.dsems = []
        self.nsem = 0

    def dsem(self):
        self.nsem += 1
        d = _DSem(self.es.enter_context(self.nc.semaphore("d%d" % self.nsem)))
        self.dsems.append(d)
        return d

    def _wait(self, e, reads, writes):
        deps = {}

        def add(t):
            if t is not None and deps.get(t[0], (None, 0))[1] < t[1]:
                deps[t[0]] = t

        for b in reads:
            add(b.w)
        for b in writes:
            add(b.w)
            for s, v in b.r.items():
                add((s, v))
        for s, (sem, val) in deps.items():
            if sem is e.sem and (e.name == "pe" or not SAME_ENGINE_SYNC):
                continue
            if e.seen.get(sem, 0) >= val:
                continue
            e.h.wait_ge(sem, val)
            e.seen[sem] = val

    @staticmethod
    def _mark(tok, reads, writes):
        for b in reads:
            if b.r.get(tok[0], 0) < tok[1]:
                b.r[tok[0]] = tok[1]
        for b in writes:
            b.w = tok
            b.r = {}

    def op(self, eng, fn, reads=(), writes=(), inc=True):
        e = self.E[eng]
        if any(b.excl for b in reads):
            writes = list(writes) + [b for b in reads if b.excl]
            reads = [b for b in reads if not b.excl]
        self._wait(e, reads, writes)
        ins = fn(e.h)
        if inc:
            e.cnt += 1
            ins.then_inc(e.sem, 1)
            tok = (e.sem, e.cnt)
        else:
            tok = (e.sem, e.cnt + 1)
        self._mark(tok, reads, writes)
        return tok

    def dma(self, ds, pairs, reads=(), writes=(), eng="sp"):
        e = self.E[eng]
        self._wait(e, reads, writes)
        for out, in_ in pairs:
            ds.cnt += 16
            e.h.dma_start(out=out, in_=in_).then_inc(ds.sem, 16)
        tok = (ds.sem, ds.cnt)
        self._mark(tok, reads, writes)
        return tok

    def barrier(self):
        sp = self.E["sp"]
        for o in self.E.values():
            if o is not sp and o.cnt > 0 and sp.seen.get(o.sem, 0) < o.cnt:
                sp.h.wait_ge(o.sem, o.cnt)
                sp.seen[o.sem] = o.cnt
        for d in self.dsems:
            if d.cnt > 0 and sp.seen.get(d.sem, 0) < d.cnt:
                sp.h.wait_ge(d.sem, d.cnt)
                sp.seen[d.sem] = d.cnt
        self.barcnt += 1
        sp.h.sem_inc(self.bar, 1)
        for o in self.E.values():
            if o is not sp:
                o.h.wait_ge(self.bar, self.barcnt)
            for o2 in self.E.values():
                o.seen[o2.sem] = o2.cnt
            for d in self.dsems:
                o.seen[d.sem] = d.cnt


class Cfg:
    def __init__(self, S=8192, L=2, HD=4, HL=6, G=2, DFF=2816, TBF=256):
        self.S, self.L, self.HD, self.HL, self.G, self.DFF, self.TBF = S, L, HD, HL, G, DFF, TBF
        self.HS = 3 * G
        self.NB = S // TB
        off = 0
        self.fm = []
        for nm, cnt, m in (("dq", HD, 64), ("dk", HD, 64), ("lq", HL // 2, 128), ("lk", HL // 2, 128),
                           ("lv", HL // 2, 128), ("xs", self.HS // 2, 128), ("B", G, 128), ("C", G, 128)):
            for i in range(cnt):
                self.fm.append((nm, i, off, m))
                off += m
        self.off_dv = off
        off += HD * 64
        self.off_z = off
        off += self.HS * 64 + self.HS
        self.NCOL = off
        self.NXBC = self.HS // 2 + 2 * G
        self.MIXC = (HD * 64 + HL * 64 + self.HS * 64) // 128
        self.FC = DFF // 128


def _rope_tables(S, head_dim, rows):
    half = head_dim // 2
    inv = np.exp(-math.log(10000.0) * np.arange(half, dtype=np.float32) / half).astype(np.float32)
    ang = np.arange(S, dtype=np.float32)[None, :] * inv[:, None]
    cos = np.cos(ang).astype(np.float32)
    sin = np.sin(ang).astype(np.float32)
    p = np.arange(rows)
    j = p % half
    first = (p % head_dim) < half
    cosT = cos[j]
    sinT = np.where(first[:, None], -sin[j], sin[j])
    perm = np.zeros((rows, rows), np.float32)
    partner = np.where(first, p + half, p - half)
    perm[p, partner] = 1.0
    return cosT.astype(np.float32), sinT.astype(np.float32), perm


def host_consts(cfg):
    S = cfg.S
    c = {}
    cd, sd, pd = _rope_tables(S, 32, 64)
    cl, sl, pl = _rope_tables(S, 64, 128)
    c["cosd"], c["sind"], c["cosl"], c["sinl"] = [a.astype(ml_dtypes.bfloat16) for a in (cd, sd, cl, sl)]
    c["permd"] = pd.astype(ml_dtypes.bfloat16)
    c["perml"] = pl.astype(ml_dtypes.bfloat16)
    c["ident_bf"] = np.eye(128, dtype=np.float32).astype(ml_dtypes.bfloat16)
    c["ident_f"] = np.eye(128, dtype=np.float32)
    k = np.arange(128)[:, None]
    q = np.arange(512)[None, :]
    c["dmask"] = np.stack([(q >= 128 * di + k) for di in range(4)]).astype(np.float32).astype(ml_dtypes.bfloat16)
    qq = np.arange(128)[None, :]
    c["lmask"] = np.concatenate([(qq <= k), (qq >= k)], axis=1).astype(np.float32).astype(ml_dtypes.bfloat16)
    c["triu"] = (k <= qq).astype(np.float32)
    c["negm"] = np.where(qq >= k, 0.0, -30000.0).astype(np.float32)
    sel = np.zeros((65, 64), np.float32)
    sel[64, :] = 1.0
    c["sel"] = sel
    c["ones64"] = np.ones((64, 64), np.float32)
    return c


CONST_SHAPES = None


def host_params(cfg, inp, b_heads=None):
    HD, HL, G, HS = cfg.HD, cfg.HL, cfg.G, cfg.HS
    w = np.asarray(inp["w_in"])
    L = w.shape[0]
    o = 0
    segs = {}
    for nm, n in (("dq", 256), ("dk", 256), ("dv", 256), ("lq", 384), ("lk", 384), ("lv", 384),
                  ("z", 384), ("xs", 384), ("B", 256), ("C", 256), ("dt", 6)):
        segs[nm] = (o, o + n)
        o += n
    cols = []
    for nm in ("dq", "dk", "lq", "lk", "lv", "xs", "B", "C", "dv", "z", "dt"):
        a, b = segs[nm]
        cols.append(np.arange(a, b))
    cols = np.concatenate(cols)
    p = {}
    p["w_in"] = np.ascontiguousarray(w[:, :, cols])
    p["g_premix"] = np.ascontiguousarray(np.asarray(inp["pre_mix_norm"]).reshape(L, 8, 128).transpose(0, 2, 1))
    p["g_preffn"] = np.ascontiguousarray(np.asarray(inp["pre_ffn_norm"]).reshape(L, 8, 128).transpose(0, 2, 1))
    p["g_postmix"] = np.ascontiguousarray(np.asarray(inp["post_mix_norm"]))
    p["g_postffn"] = np.ascontiguousarray(np.asarray(inp["post_ffn_norm"]))
    xo = segs["xs"][0]
    cw = np.asarray(inp["ssd_conv_w"])
    cb = np.asarray(inp["ssd_conv_b"])
    p["cw"] = np.ascontiguousarray(cw.reshape(L, 4, 7, 128).transpose(0, 3, 2, 1))
    p["cb"] = np.ascontiguousarray(cb.reshape(L, 7, 128).transpose(0, 2, 1))
    p["dt_bias"] = np.ascontiguousarray(np.asarray(inp["ssd_dt_bias"]))
    p["A_log"] = np.ascontiguousarray(np.asarray(inp["ssd_A_log"]))
    p["ssd_D"] = np.ascontiguousarray(np.asarray(inp["ssd_D"]))
    p["ssd_norm"] = np.ascontiguousarray(np.asarray(inp["ssd_norm"]))
    p["lam"] = np.ascontiguousarray(np.asarray(inp["diff_lambda"]).reshape(L, 128))
    p["hgain"] = np.ascontiguousarray(np.asarray(inp["diff_head_norm"]).reshape(L, 64, 1))
    p["w_out"] = np.ascontiguousarray(np.asarray(inp["w_out"]))
    p["ffn_up"] = np.ascontiguousarray(np.asarray(inp["ffn_up"]))
    fw = np.asarray(inp["ffn_conv_w"])
    fb = np.asarray(inp["ffn_conv_b"])
    p["fcw"] = np.ascontiguousarray(fw.reshape(L, 3, 44, 128).transpose(0, 3, 2, 1))
    p["fcb"] = np.ascontiguousarray(fb.reshape(L, 44, 128).transpose(0, 2, 1))
    p["ffn_down"] = np.ascontiguousarray(np.asarray(inp["ffn_down"]))
    return p


class K:
    pass


def build(cfg, debug=False):
    nc = bass.Bass("TRN2", target_bir_lowering=False)
    S, L, HD, HL, G, HS = cfg.S, cfg.L, cfg.HD, cfg.HL, cfg.G, cfg.HS
    NB = cfg.NB
    es = ExitStack()
    sc = Sched(nc, es)

    def din(name, shape, dt=F32):
        return nc.dram_tensor(name, list(shape), dt, kind="ExternalInput").ap()

    def dscr(name, shape, dt):
        return nc.dram_tensor(name, list(shape), dt, kind=("ExternalOutput" if debug else "Internal")).ap()

    x_in = din("x", [S, D])
    W = {}
    W["w_in"] = din("w_in", [L, D, cfg.NCOL])
    W["g_premix"] = din("g_premix", [L, 128, 8])
    W["g_preffn"] = din("g_preffn", [L, 128, 8])
    W["g_postmix"] = din("g_postmix", [L, D])
    W["g_postffn"] = din("g_postffn", [L, D])
    W["cw"] = din("cw", [L, 128, 7, 4])
    W["cb"] = din("cb", [L, 128, 7])
    W["dt_bias"] = din("dt_bias", [L, 6])
    W["A_log"] = din("A_log", [L, 6])
    W["ssd_D"] = din("ssd_D", [L, 6])
    W["ssd_norm"] = din("ssd_norm", [L, 384])
    W["lam"] = din("lam", [L, 128])
    W["hgain"] = din("hgain", [L, 64, 1])
    W["w_out"] = din("w_out", [L, D, D])
    W["ffn_up"] = din("ffn_up", [L, D, 2 * cfg.DFF])
    W["fcw"] = din("fcw", [L, 128, 44, 3])
    W["fcb"] = din("fcb", [L, 128, 44])
    W["ffn_down"] = din("ffn_down", [L, cfg.DFF, D])
    C = {}
    C["cosd"] = din("cosd", [64, S], BF16); C["sind"] = din("sind", [64, S], BF16)
    C["cosl"] = din("cosl", [128, S], BF16); C["sinl"] = din("sinl", [128, S], BF16)
    C["permd"] = din("permd", [64, 64], BF16); C["perml"] = din("perml", [128, 128], BF16)
    C["ident_bf"] = din("ident_bf", [128, 128], BF16); C["ident_f"] = din("ident_f", [128, 128])
    C["dmask"] = din("dmask", [4, 128, 512], BF16); C["lmask"] = din("lmask", [128, 256], BF16)
    C["triu"] = din("triu", [128, 128]); C["negm"] = din("negm", [128, 128])
    C["sel"] = din("sel", [65, 64]); C["ones64"] = din("ones64", [64, 64])
    out = nc.dram_tensor("out", [S, D], F32, kind="ExternalOutput").ap()

    qTd = dscr("qTd", [HD, 64, S], BF16); kTd = dscr("kTd", [HD, 64, S], BF16)
    vd = dscr("vd", [S, HD * 65], BF16)
    qTl = dscr("qTl", [HL // 2, 128, S], BF16); kTl = dscr("kTl", [HL // 2, 128, S], BF16)
    vTl = dscr("vTl", [HL // 2, 128, S], BF16)
    xbcT = dscr("xbcT", [cfg.NXBC, 128, S], BF16)
    zs = dscr("zs", [S, HS * 64], BF16)
    dts = dscr("dts", [S, HS], F32)
    mixT = dscr("mixT", [cfg.MIXC, 128, S], BF16)
    xa = dscr("xa", [S, D], F32)
    xb = dscr("xb", [S, D], F32)

    def sb(name, shape, dt=F32):
        return es.enter_context(nc.sbuf_tensor(name, list(shape), dt))

    ident_bf = sb("sb_ident_bf", [128, 128], BF16)
    ident_f = sb("sb_ident_f", [128, 128])
    cB = Buf("consts")
    cds = sc.dsem()
    sc.dma(cds, [(ident_bf[:], C["ident_bf"]), (ident_f[:], C["ident_f"])], writes=[cB])

    k = K()
    k.nc, k.sc, k.cfg, k.W, k.C, k.cB = nc, sc, cfg, W, C, cB
    k.ident_bf, k.ident_f = ident_bf, ident_f
    k.scr = dict(qTd=qTd, kTd=kTd, vd=vd, qTl=qTl, kTl=kTl, vTl=vTl, xbcT=xbcT, zs=zs, dts=dts,
                 mixT=mixT, xa=xa, xb=xb)

    stages = cfg.stages if hasattr(cfg, "stages") else "12345"
    for l in range(L):
        x_src = x_in if l == 0 else xb
        x_dst = out if l == L - 1 else xb
        if "1" in stages:
            phase1(k, l, x_src)
            sc.barrier()
        if "2" in stages:
            phase2(k, l)
            sc.barrier()
        if "3" in stages:
            phase3(k, l)
            sc.barrier()
        if "4" in stages:
            phase4(k, l)
            sc.barrier()
        if "5" in stages:
            phase5a(k, l, x_src)
            sc.barrier()
            phase5b(k, l, x_dst)
            sc.barrier()
    sc.barrier()
    es.close()
    return nc


def load_weights_bf16(k, ps, dst, src_rows, ncols, gain, stg, stgB, ds, kchunks, col_chunk=512):
    sc = k.sc
    i = 0
    for kc in range(kchunks):
        for c0 in range(0, ncols, col_chunk):
            cw = min(col_chunk, ncols - c0)
            sl = i % len(stg)
            sc.dma(ds[sl], [(stg[sl][:, :cw], src_rows[kc * 128:(kc + 1) * 128, c0:c0 + cw])], writes=[stgB[sl]])
            eng = "dve" if i % 2 == 0 else "pool"
            if gain is not None:
                if i % 2 == 0:
                    sc.op("dve", lambda h, sl=sl, cw=cw, kc=kc, c0=c0: h.tensor_scalar(
                        out=dst[:, kc, c0:c0 + cw], in0=stg[sl][:, :cw], scalar1=gain[:, kc:kc + 1], scalar2=None,
                        op0=ALU.mult), reads=[stgB[sl], k.gB], writes=[k.wB])
                else:
                    sc.op("act", lambda h, sl=sl, cw=cw, kc=kc, c0=c0: h.activation(
                        out=dst[:, kc, c0:c0 + cw], in_=stg[sl][:, :cw], func=AF.Copy, scale=gain[:, kc:kc + 1]),
                        reads=[stgB[sl], k.gB], writes=[k.wB])
            else:
                sc.op(eng, lambda h, sl=sl, cw=cw, kc=kc, c0=c0: h.tensor_copy(
                    out=dst[:, kc, c0:c0 + cw], in_=stg[sl][:, :cw]), reads=[stgB[sl]], writes=[k.wB])
            i += 1


def rms_rstd(k, ps, ss, rstd, n, width, ssB, rsB):
    sc = k.sc
    sc.op("act", lambda h: h.activation(out=rstd[:, :n], in_=ss[:, :n], func=AF.Ln, scale=1.0 / width, bias=EPS),
          reads=[ssB], writes=[rsB])
    sc.op("act", lambda h: h.activation(out=rstd[:, :n], in_=rstd[:, :n], func=AF.Exp, scale=-0.5), reads=[rsB], writes=[rsB])


def phase1(k, l, x_src):
    nc, sc, cfg, W, C = k.nc, k.sc, k.cfg, k.W, k.C
    S, HD, HL, G, HS, NB = cfg.S, cfg.HD, cfg.HL, cfg.G, cfg.HS, cfg.NB
    NX = cfg.NXBC
    with ExitStack() as ps:
        def sb(name, shape, dt=F32):
            return ps.enter_context(nc.sbuf_tensor("L%d_%s" % (l, name), list(shape), dt))

        def pt(name, shape, dt=F32):
            return ps.enter_context(nc.psum_tensor("L%d_%s" % (l, name), list(shape), dt))

        wbf = sb("p1_w", [128, 8, cfg.NCOL], BF16)
        gain = sb("p1_g", [128, 8])
        stg = [sb("p1_stg%d" % i, [128, 512]) for i in range(3)]
        stgB = [Buf() for _ in range(3)]
        ds = [sc.dsem() for _ in range(3)]
        k.gB, k.wB = Buf("gain"), Buf("w")
        dsm = sc.dsem()
        permd = sb("p1_permd", [64, 64], BF16)
        perml = sb("p1_perml", [128, 128], BF16)
        cw = sb("p1_cw", [128, 7, 4])
        cb = sb("p1_cb", [128, 7])
        sc.dma(dsm, [(gain[:], W["g_premix"][l]), (permd[:], C["permd"]), (perml[:], C["perml"]),
                     (cw[:], W["cw"][l]), (cb[:], W["cb"][l])], writes=[k.gB])
        CUT = int(os.environ.get("KCUT", "99"))
        if CUT >= 1:
            load_weights_bf16(k, ps, wbf, W["w_in"][l], cfg.NCOL, gain, stg, stgB, ds, 8)

        xt = sb("p1_x", [128, 4, D])
        xtB = [Buf() for _ in range(4)]
        xds = [sc.dsem() for _ in range(4)]
        junk = sb("p1_junk", [128, D], BF16)
        junkB = Buf()
        ss = sb("p1_ss", [128, 4]); ssB = Buf()
        rstd = sb("p1_rstd", [128, 4]); rsB = Buf()
        xs = sb("p1_xs", [128, 4, D], BF16); xsB = Buf()
        hT = sb("p1_hT", [128, 8, TB], BF16); hTB = Buf()
        psT = [pt("p1_psT%d" % i, [128, 2, TB], BF16) for i in range(2)]
        psTB = [PBuf() for _ in range(2)]
        psA = [pt("p1_psA%d" % i, [128, TB]) for i in range(3)]
        psAB = [PBuf() for _ in range(3)]
        psR = [pt("p1_psR%d" % i, [128, TB]) for i in range(2)]
        psRB = [PBuf() for _ in range(2)]
        psB = [pt("p1_psB%d" % i, [128, 512]) for i in range(1)]
        psBB = [PBuf() for _ in range(1)]
        tabs = {}
        for nm, rows in (("cosd", 64), ("sind", 64), ("cosl", 128), ("sinl", 128)):
            tabs[nm] = [sb("p1_%s%d" % (nm, i), [rows, TB], BF16) for i in range(2)]
        tabB = [Buf() for _ in range(2)]
        tds = [sc.dsem() for _ in range(2)]
        xbf = [sb("p1_xbf%d" % i, [128, TB], BF16) for i in range(4)]
        xbfB = [Buf() for _ in range(4)]
        t1 = [sb("p1_t1%d" % i, [128, TB]) for i in range(2)]
        t1B = [Buf() for _ in range(2)]
        t2 = [sb("p1_t2%d" % i, [128, TB]) for i in range(2)]
        t2B = [Buf() for _ in range(2)]
        qk_st = [sb("p1_qk%d" % i, [64, 2 * HD, TB], BF16) for i in range(2)]
        l_st = [sb("p1_l%d" % i, [128, 3 * (HL // 2), TB], BF16) for i in range(2)]
        xbc_st = [sb("p1_xbc%d" % i, [128, NX, TB], BF16) for i in range(2)]
        v_st = [sb("p1_v%d" % i, [128, 4, HD, 65], BF16) for i in range(2)]
        z_st = [sb("p1_z%d" % i, [128, 4, HS * 64], BF16) for i in range(2)]
        dt_st = [sb("p1_dt%d" % i, [128, 4, HS]) for i in range(2)]
        from collections import defaultdict
        stD = [defaultdict(Buf) for _ in range(2)]
        sds = [sc.dsem() for _ in range(2)]
        u = sb("p1_u", [128, NX, 3 + TB]); uB = [Buf() for _ in range(NX)]
        yc = [sb("p1_yc%d" % i, [128, TB]) for i in range(3)]
        ycB = [Buf() for _ in range(3)]
        sc.op("pool", lambda h: h.memset(u[:], 0.0), writes=uB)
        for i in range(2):
            sc.op("pool", lambda h, i=i: h.memset(v_st[i][:], 1.0), writes=[stD[i][("v", n)] for n in range(4)])

        ci = 0
        for t in range(NB):
            if CUT < 2:
                break
            pb = t % 2
            tok0 = t * TB

            def prefetch(tt):
                tk = tt * TB
                for n in range(4):
                    sc.dma(xds[n], [(xt[:, n, :], x_src[tk + n * 128:tk + (n + 1) * 128, :])], writes=[xtB[n]])
                sc.dma(tds[tt % 2], [(tabs[nm][tt % 2][:], C[nm][:, tk:tk + TB]) for nm in ("cosd", "sind", "cosl", "sinl")],
                       writes=[tabB[tt % 2]])

            if t == 0:
                prefetch(0)
            for n in range(4):
                sc.op("act", lambda h, n=n: h.activation(out=junk[:], in_=xt[:, n, :], func=AF.Square,
                                                         accum_out=ss[:, n:n + 1]),
                      reads=[xtB[n]], writes=[junkB, ssB])
            if CUT < 3:
                continue
            rms_rstd(k, ps, ss, rstd, 4, D, ssB, rsB)
            for n in range(4):
                if n % 2 == 0:
                    sc.op("dve", lambda h, n=n: h.tensor_scalar(
                        out=xs[:, n, :], in0=xt[:, n, :], scalar1=rstd[:, n:n + 1], scalar2=None, op0=ALU.mult),
                        reads=[xtB[n], rsB], writes=[xsB])
                else:
                    sc.op("act", lambda h, n=n: h.activation(out=xs[:, n, :], in_=xt[:, n, :], func=AF.Copy, scale=rstd[:, n:n + 1]),
                          reads=[xtB[n], rsB], writes=[xsB])
            if CUT < 4:
                continue
            for cp in range(4):
                pi = cp % 2
                for c2 in range(2):
                    c = cp * 2 + c2
                    for n in range(4):
                        last = (c2 == 1 and n == 3)
                        sc.op("pe", lambda h, c=c, c2=c2, n=n, pi=pi: h.transpose(
                            out=psT[pi][:, c2, n * 128:(n + 1) * 128], in_=xs[:, n, c * 128:(c + 1) * 128],
                            identity=k.ident_bf[:]), reads=[xsB, k.cB], writes=[psTB[pi]], inc=last)
                if cp % 2 == 0:
                    sc.op("act", lambda h, cp=cp, pi=pi: h.copy(out=hT[:, 2 * cp:2 * cp + 2, :], in_=psT[pi][:]),
                          reads=[psTB[pi]], writes=[hTB])
                else:
                    sc.op("dve", lambda h, cp=cp, pi=pi: h.tensor_copy(out=hT[:, 2 * cp:2 * cp + 2, :], in_=psT[pi][:]),
                          reads=[psTB[pi]], writes=[hTB])
            if CUT < 5:
                continue
            if t + 1 < NB:
                prefetch(t + 1)
            if t > 0:
                sc.op("pool", lambda h: h.tensor_copy(out=u[:, :, 0:3], in_=u[:, :, TB:TB + 3]), reads=uB, writes=uB)
            fm = cfg.fm

            def s_pe(i):
                nm, idx, off, M = fm[i]
                a = i % 3
                for kc in range(8):
                    sc.op("pe", lambda h: h.matmul(psA[a][0:M, :], lhsT=wbf[:, kc, off:off + M], rhs=hT[:, kc, :],
                                                   start=(kc == 0), stop=(kc == 7)), reads=[k.wB, hTB], writes=[psAB[a]], inc=(kc == 7))

            def s_copy(i):
                nm, idx, off, M = fm[i]
                a = i % 3
                if nm in ("dq", "dk", "lq", "lk"):
                    x4 = i % 4
                    sc.op("act", lambda h: h.copy(out=xbf[x4][0:M, :], in_=psA[a][0:M, :]), reads=[psAB[a]], writes=[xbfB[x4]])
                elif nm == "lv":
                    sc.op("act", lambda h: h.copy(out=l_st[pb][:, 2 * (HL // 2) + idx, :], in_=psA[a][:]),
                          reads=[psAB[a]], writes=[stD[pb][("lv", idx)]])
                else:
                    xi = {"xs": 0, "B": HS // 2, "C": HS // 2 + G}[nm] + idx
                    sc.op("act", lambda h: h.copy(out=u[:, xi, 3:3 + TB], in_=psA[a][:]), reads=[psAB[a]], writes=[uB[xi]])

            def s_mid(i):
                nm, idx, off, M = fm[i]
                if nm in ("dq", "dk", "lq", "lk"):
                    perm = permd if M == 64 else perml
                    x4 = i % 4
                    r2 = i % 2
                    sc.op("pe", lambda h: h.matmul(psR[r2][0:M, :], lhsT=perm[0:M, 0:M], rhs=xbf[x4][0:M, :], start=True, stop=True),
                          reads=[xbfB[x4], k.gB], writes=[psRB[r2]])
                elif nm != "lv":
                    xi = {"xs": 0, "B": HS // 2, "C": HS // 2 + G}[nm] + idx
                    gi = {"xs": 0, "B": 3, "C": 5}[nm] + idx
                    y3 = i % 3
                    sc.op("dve", lambda h: h.tensor_scalar(out=yc[y3][:], in0=u[:, xi, 3:3 + TB], scalar1=cw[:, gi, 3:4], scalar2=cb[:, gi:gi + 1],
                                                           op0=ALU.mult, op1=ALU.add), reads=[uB[xi], k.gB], writes=[ycB[y3]])

            def s_ew(i):
                nm, idx, off, M = fm[i]
                if nm in ("dq", "dk", "lq", "lk"):
                    cosn, sinn = ("cosd", "sind") if M == 64 else ("cosl", "sinl")
                    x4 = i % 4
                    r2 = i % 2
                    sc.op("dve", lambda h: h.tensor_tensor(out=t1[r2][0:M, :], in0=xbf[x4][0:M, :], in1=tabs[cosn][pb][0:M, :], op=ALU.mult),
                          reads=[xbfB[x4], tabB[pb]], writes=[t1B[r2]])
                    sc.op("dve", lambda h: h.tensor_tensor(out=t2[r2][0:M, :], in0=psR[r2][0:M, :], in1=tabs[sinn][pb][0:M, :], op=ALU.mult),
                          reads=[psRB[r2], tabB[pb]], writes=[t2B[r2]])
                elif nm != "lv":
                    xi = {"xs": 0, "B": HS // 2, "C": HS // 2 + G}[nm] + idx
                    gi = {"xs": 0, "B": 3, "C": 5}[nm] + idx
                    y3 = i % 3
                    for kk in range(3):
                        sc.op("dve", lambda h: h.scalar_tensor_tensor(out=yc[y3][:], in0=u[:, xi, kk:kk + TB], scalar=cw[:, gi, kk:kk + 1],
                                                                      in1=yc[y3][:], op0=ALU.mult, op1=ALU.add),
                              reads=[uB[xi], ycB[y3], k.gB], writes=[ycB[y3]])

            def s_fin(i):
                nm, idx, off, M = fm[i]
                if nm in ("dq", "dk", "lq", "lk"):
                    r2 = i % 2
                    if nm in ("dq", "dk"):
                        dst = qk_st[pb][:, (0 if nm == "dq" else HD) + idx, :]
                    else:
                        dst = l_st[pb][:, (0 if nm == "lq" else HL // 2) + idx, :]
                    sc.op("pool", lambda h: h.tensor_tensor(out=dst, in0=t1[r2][0:M, :], in1=t2[r2][0:M, :], op=ALU.add),
                          reads=[t1B[r2], t2B[r2]], writes=[stD[pb][(nm, idx)]])
                elif nm != "lv":
                    xi = {"xs": 0, "B": HS // 2, "C": HS // 2 + G}[nm] + idx
                    y3 = i % 3
                    sc.op("act", lambda h: h.activation(out=xbc_st[pb][:, xi, :], in_=yc[y3][:], func=AF.Silu),
                          reads=[ycB[y3]], writes=[stD[pb][("xbc", xi)]])

            skewed(len(fm), [s_pe, s_copy, s_mid, s_ew, s_fin])
            if CUT < 6:
                continue
            tmb = [psB[0], psR[0], psR[1]]
            tmB = [psBB[0], psRB[0], psRB[1]]
            for n in range(4):
                a = (2 * n) % 3
                for kc in range(8):
                    sc.op("pe", lambda h, kc=kc, n=n, a=a: h.matmul(
                        tmb[a][:, 0:HD * 64], lhsT=hT[:, kc, n * 128:(n + 1) * 128],
                        rhs=wbf[:, kc, cfg.off_dv:cfg.off_dv + HD * 64], start=(kc == 0), stop=(kc == 7)),
                        reads=[k.wB, hTB], writes=[tmB[a]], inc=(kc == 7))
                sc.op("act", lambda h, n=n, a=a: h.copy(
                    out=v_st[pb][:, n, :, 0:64], in_=tmb[a][:, 0:HD * 64].rearrange("p (h e) -> p h e", e=64)),
                    reads=[tmB[a]], writes=[stD[pb][("v", n)]])
                nz = HS * 64 + HS
                a = (2 * n + 1) % 3
                for kc in range(8):
                    sc.op("pe", lambda h, kc=kc, n=n, a=a: h.matmul(
                        tmb[a][:, 0:nz], lhsT=hT[:, kc, n * 128:(n + 1) * 128],
                        rhs=wbf[:, kc, cfg.off_z:cfg.off_z + nz], start=(kc == 0), stop=(kc == 7)),
                        reads=[k.wB, hTB], writes=[tmB[a]], inc=(kc == 7))
                sc.op("dve", lambda h, n=n, a=a: h.tensor_copy(out=z_st[pb][:, n, :], in_=tmb[a][:, 0:HS * 64]),
                      reads=[tmB[a]], writes=[stD[pb][("z", n)]])
                sc.op("dve", lambda h, n=n, a=a: h.tensor_copy(out=dt_st[pb][:, n, :], in_=tmb[a][:, HS * 64:nz]),
                      reads=[tmB[a]], writes=[stD[pb][("dt", n)]])
            if CUT < 7:
                continue
            scr = k.scr
            pairs = [
                (scr["qTd"][:, :, tok0:tok0 + TB].rearrange("h p s -> p h s"), qk_st[pb][:, 0:HD, :]),
                (scr["kTd"][:, :, tok0:tok0 + TB].rearrange("h p s -> p h s"), qk_st[pb][:, HD:2 * HD, :]),
                (scr["qTl"][:, :, tok0:tok0 + TB].rearrange("h p s -> p h s"), l_st[pb][:, 0:HL // 2, :]),
                (scr["kTl"][:, :, tok0:tok0 + TB].rearrange("h p s -> p h s"), l_st[pb][:, HL // 2:HL, :]),
                (scr["vTl"][:, :, tok0:tok0 + TB].rearrange("h p s -> p h s"), l_st[pb][:, HL:3 * (HL // 2), :]),
                (scr["xbcT"][:, :, tok0:tok0 + TB].rearrange("h p s -> p h s"), xbc_st[pb][:]),
                (scr["vd"][tok0:tok0 + TB, :].rearrange("(n p) f -> p n f", p=128),
                 v_st[pb][:].rearrange("p n h e -> p n (h e)")),
                (scr["zs"][tok0:tok0 + TB, :].rearrange("(n p) f -> p n f", p=128), z_st[pb][:]),
                (scr["dts"][tok0:tok0 + TB, :].rearrange("(n p) f -> p n f", p=128), dt_st[pb][:]),
            ]
            sc.dma(sds[pb], pairs, reads=list(stD[pb].values()))


def skewed(n, stages):
    ns = len(stages)
    for kk in range(n + ns - 1):
        for si, fn in enumerate(stages):
            i = kk - si
            if 0 <= i < n:
                fn(i)


def _mk(k, l, ps):
    nc = k.nc

    def sb(name, shape, dt=F32):
        return ps.enter_context(nc.sbuf_tensor("L%d_%s" % (l, name), list(shape), dt))

    def pt(name, shape, dt=F32):
        return ps.enter_context(nc.psum_tensor("L%d_%s" % (l, name), list(shape), dt))

    return sb, pt


def phase2(k, l):
    nc, sc, cfg, W, C = k.nc, k.sc, k.cfg, k.W, k.C
    S, HD, NB = cfg.S, cfg.HD, cfg.NB
    lam_init = 0.8 - 0.6 * math.exp(-0.3 * l)
    scr = k.scr
    with ExitStack() as ps:
        sb, pt = _mk(k, l, ps)
        kT = sb("p2_kT", [128, HD // 2, S], BF16)
        vfull = sb("p2_v", [128, (S // 128) * HD * 65 + 128], BF16)
        v = vfull[:, 0:(S // 128) * HD * 65].rearrange("p (n f) -> p n f", f=HD * 65)
        dmask = sb("p2_dmask", [128, 4, 512], BF16)
        sel = sb("p2_sel", [65, 64]); ones64 = sb("p2_ones", [64, 64])
        hg = sb("p2_hg", [64, 1]); lam = sb("p2_lam", [64, 128])
        lt = sb("p2_lt", [64, 64]); lsum = sb("p2_ls", [64, 2]); neg_lam = sb("p2_nl", [64, 1]); gsc = sb("p2_gsc", [64, 1])
        cB = Buf(); d0 = sc.dsem()
        sc.dma(d0, [(kT[:, a2, :], scr["kTd"][2 * a2:2 * a2 + 2].rearrange("b p s -> (b p) s")) for a2 in range(HD // 2)] +
               [(v, scr["vd"].rearrange("(n p) f -> p n f", p=128)),
                (dmask[:], C["dmask"].rearrange("d p q -> p d q")), (sel[:], C["sel"]), (ones64[:], C["ones64"]),
                (hg[:], W["hgain"][l]), (lam[:], W["lam"][l:l + 1, :].broadcast_to([64, 128]))], writes=[cB])
        pB = Buf()
        sc.op("pool", lambda h: h.memset(vfull[:, (S // 128) * HD * 65:], 0.0), writes=[cB])
        sc.op("dve", lambda h: h.tensor_tensor(out=lt[:, 0:32], in0=lam[:, 0:32], in1=lam[:, 32:64], op=ALU.mult), reads=[cB], writes=[pB])
        sc.op("dve", lambda h: h.tensor_tensor(out=lt[:, 32:64], in0=lam[:, 64:96], in1=lam[:, 96:128], op=ALU.mult), reads=[cB], writes=[pB])
        sc.op("dve", lambda h: h.tensor_reduce(out=lsum[:], in_=lt[:].rearrange("p (a b) -> p a b", b=32), axis=AX.X, op=ALU.add),
              reads=[pB], writes=[pB])
        sc.op("act", lambda h: h.activation(out=lsum[:], in_=lsum[:], func=AF.Exp), reads=[pB], writes=[pB])
        sc.op("dve", lambda h: h.tensor_tensor(out=neg_lam[:], in0=lsum[:, 1:2], in1=lsum[:, 0:1], op=ALU.subtract), reads=[pB], writes=[pB])
        sc.op("dve", lambda h: h.tensor_scalar(out=neg_lam[:], in0=neg_lam[:], scalar1=-lam_init, scalar2=None, op0=ALU.add), reads=[pB], writes=[pB])
        sc.op("dve", lambda h: h.tensor_scalar(out=gsc[:], in0=hg[:], scalar1=(1.0 - lam_init), scalar2=None, op0=ALU.mult), reads=[cB, pB], writes=[pB])

        qT = [sb("p2_q%d" % i, [128, HD * 2, TB], BF16) for i in range(2)]
        qB = [Buf() for _ in range(2)]; qds = [sc.dsem() for _ in range(2)]
        for i in range(2):
            sc.op("pool", lambda h, i=i: h.memset(qT[i][:], 0.0), writes=[qB[i]])
        psS = [pt("p2_psS%d" % i, [128, 2, 512]) for i in range(2)]; psSB = [PBuf() for _ in range(2)]
        psO = [pt("p2_psO%d" % m, [128, 512]) for m in range(2)]
        psOB = [PBuf() for m in range(2)]
        psE = [pt("p2_psE%d" % i, [128, 512]) for i in range(2)]; psEB = [PBuf() for _ in range(2)]
        pT = [sb("p2_pT%d" % i, [128, 2, 512], BF16) for i in range(3)]; pTB = [Buf() for _ in range(3)]
        X = [sb("p2_X%d" % m, [65, 512]) for m in range(2)]; XB = [Buf() for _ in range(2)]
        r = [sb("p2_r%d" % m, [64, 512]) for m in range(2)]; rB = [Buf() for _ in range(2)]
        o = sb("p2_o", [64, 512]); oB = Buf()
        sq = sb("p2_sq", [64, 512]); sqB = Buf()
        rs = sb("p2_rs", [64, 512]); rsB = Buf()
        ost = [sb("p2_ost%d" % i, [64, HD, 512], BF16) for i in range(2)]
        ostB = [Buf() for _ in range(2)]; ods = [sc.dsem() for _ in range(2)]
        scale = 32 ** -0.5
        cnt = 0
        pending = []

        def flush(n=None):
            c = len(pending) if n is None else min(n, len(pending))
            for _ in range(c):
                pending.pop(0)()

        for t in range(NB):
            tok0 = t * TB
            qb = t % 2
            sc.dma(qds[qb], [(qT[qb][32 * ((h2 % 2) * 2 + m2):32 * ((h2 % 2) * 2 + m2) + 32, h2 * 2 + m2, :],
                              scr["qTd"][h2, 32 * m2:32 * m2 + 32, tok0:tok0 + TB]) for h2 in range(HD) for m2 in range(2)],
                   writes=[qB[qb]])
            for hh in range(HD):
                nk = 4 * t + 4

                def c0_of(i):
                    return 128 * max(0, i - 4 * t)

                def qk(i):
                    sl = (cnt + i) % 2
                    c0 = c0_of(i)
                    for m in range(2):
                        sc.op("pe", lambda h: h.matmul(psS[sl][:, m, c0:], lhsT=kT[:, hh // 2, i * 128:(i + 1) * 128],
                                                       rhs=qT[qb][:, hh * 2 + m, c0:], start=True, stop=True),
                              reads=[cB, qB[qb]], writes=[psSB[sl]], inc=(m == 1))

                qk(0)
                for i in range(nk):
                    sl = (cnt + i) % 2
                    p3 = (cnt + i) % 3
                    if i + 1 < nk:
                        qk(i + 1)
                    c0 = c0_of(i)
                    sc.op("act", lambda h: h.activation(out=pT[p3][:, :, c0:], in_=psS[sl][:, :, c0:], func=AF.Exp, scale=scale),
                          reads=[psSB[sl]], writes=[pTB[p3]])
                    if i >= 4 * t:
                        di = i - 4 * t
                        sc.op("dve", lambda h: h.tensor_tensor(
                            out=pT[p3][:, :, c0:], in0=pT[p3][:, :, c0:], in1=dmask[:, di:di + 1, c0:].broadcast_to([128, 2, 512 - c0]), op=ALU.mult),
                            reads=[cB, pTB[p3]], writes=[pTB[p3]])
                    for m in range(2):
                        sc.op("pe", lambda h: h.matmul(psO[m][:, c0:], lhsT=vfull[:, (i * HD + hh) * 65:(i * HD + hh) * 65 + 128], rhs=pT[p3][:, m, c0:],
                                                       start=(i == 0), stop=(i == nk - 1)),
                              reads=[cB, pTB[p3]], writes=[psOB[m]], inc=(i == nk - 1))
                    if i == 0:
                        npend0 = len(pending)
                    if i >= 1:
                        if i < nk - 1:
                            target_left = npend0 - (npend0 * i) // (nk - 1)
                            flush(max(0, len(pending) - target_left))
                        else:
                            flush()
                cnt += nk
                flush()
                sc.op("dve", lambda h: h.tensor_copy(out=X[0][:], in_=psO[0][0:65, :]), reads=[psOB[0]], writes=[XB[0]])
                sc.op("dve", lambda h: h.tensor_copy(out=X[1][:], in_=psO[1][0:65, :]), reads=[psOB[1]], writes=[XB[1]])

                def E(eng, fn, reads, writes):
                    pending.append(lambda: sc.op(eng, fn, reads=reads, writes=writes))

                for m in range(2):
                    E("pe", lambda h, m=m: h.matmul(psE[m][0:64, :], lhsT=sel[:], rhs=X[m][:], start=True, stop=True),
                      [cB, XB[m]], [psEB[m]])
                    E("dve", lambda h, m=m: h.reciprocal(out=r[m][:], in_=psE[m][0:64, :]), [psEB[m]], [rB[m]])
                    E("dve" if m == 0 else "pool", lambda h, m=m: h.tensor_tensor(
                        out=r[m][:], in0=X[m][0:64, :], in1=r[m][:], op=ALU.mult), [XB[m], rB[m]], [rB[m]])
                E("dve", lambda h: h.scalar_tensor_tensor(out=o[:], in0=r[1][:], scalar=neg_lam[:, 0:1], in1=r[0][:],
                                                          op0=ALU.mult, op1=ALU.add), [rB[0], rB[1], pB], [oB])
                E("pool", lambda h: h.tensor_tensor(out=sq[:], in0=o[:], in1=o[:], op=ALU.mult), [oB], [sqB])
                E("pe", lambda h: h.matmul(psE[0][0:64, :], lhsT=ones64[:], rhs=sq[:], start=True, stop=True), [cB, sqB], [psEB[0]])
                E("act", lambda h: h.activation(out=rs[:], in_=psE[0][0:64, :], func=AF.Ln, scale=1.0 / 64, bias=EPS), [psEB[0]], [rsB])
                E("act", lambda h: h.activation(out=rs[:], in_=rs[:], func=AF.Exp, scale=-0.5), [rsB], [rsB])
                E("dve", lambda h, qb=qb, hh=hh: h.scalar_tensor_tensor(out=ost[qb][:, hh, :], in0=o[:], scalar=gsc[:, 0:1], in1=rs[:],
                                                                       op0=ALU.mult, op1=ALU.mult), [oB, rsB, pB], [ostB[qb]])
            pending.append(lambda qb=qb, tok0=tok0: sc.dma(
                ods[qb], [(scr["mixT"][hh2 // 2, (hh2 % 2) * 64:(hh2 % 2) * 64 + 64, tok0:tok0 + TB], ost[qb][:, hh2, :])
                          for hh2 in range(HD)], reads=[ostB[qb]]))
        flush()


def phase3(k, l):
    nc, sc, cfg, W, C = k.nc, k.sc, k.cfg, k.W, k.C
    S, HD, HL = cfg.S, cfg.HD, cfg.HL
    scr = k.scr
    SBK = 2048
    NSB = S // SBK
    NC3 = HL // 2
    PATS = (1, 4, 16)
    scale = 64 ** -0.5
    with ExitStack() as ps:
        sb, pt = _mk(k, l, ps)
        qP = [sb("p3_q%d" % i, [128, NC3, SBK], BF16) for i in range(2)]; qB = Buf()
        for i in range(2):
            sc.op("pool", lambda h, i=i: h.memset(qP[i][:], 0.0), writes=[qB])
        kL = [sb("p3_k%d" % i, [128, NC3, SBK], BF16) for i in range(2)]; kB = [Buf() for _ in range(2)]
        vT = sb("p3_vT", [128, NC3, SBK], BF16); vTB = Buf()
        lds = [sc.dsem() for _ in range(3)]
        vt = [[sb("p3_vt%d_%d" % (pi, i), [128, 16, HL * 65], BF16) for i in range(2)] for pi in range(3)]
        vtB = [[Buf() for i in range(2)] for pi in range(3)]
        lmask = sb("p3_lmask", [128, 256], BF16); sel = sb("p3_sel", [65, 64])
        cB = Buf(); d0 = sc.dsem()
        sc.dma(d0, [(lmask[:], C["lmask"]), (sel[:], C["sel"])], writes=[cB])
        for pi in range(3):
            for i in range(2):
                sc.op("pool", lambda h: h.memset(vt[pi][i][:], 1.0), writes=[vtB[pi][i]])
        acc = [sb("p3_acc%d" % i, [65, SBK]) for i in range(2)]; accB = [Buf() for _ in range(2)]
        pT = [sb("p3_pT%d" % i, [128, 4, 256], BF16) for i in range(3)]; pTB = [Buf() for _ in range(3)]
        rr = sb("p3_rr", [64, 512]); rrB = Buf()
        ost = [sb("p3_ost%d" % i, [64, SBK], BF16) for i in range(2)]; ostB = [Buf() for _ in range(2)]
        ods = [sc.dsem() for _ in range(2)]
        psV = pt("p3_psV", [128, NC3, 128], BF16); psVB = PBuf()
        psS = [pt("p3_psS%d" % i, [128, 4, 256]) for i in range(2)]; psSB = [PBuf() for _ in range(2)]
        psO = [pt("p3_psO%d" % i, [128, 4, 128]) for i in range(2)]; psOB = [PBuf() for _ in range(2)]
        psE = pt("p3_psE", [128, 512]); psEB = PBuf()
        gi = 0
        hi = 0
        for u in range(NSB):
            ub = u % 2
            t0 = u * SBK
            sc.dma(lds[0], [(qP[par][64 * par:64 * par + 64, :, :],
                             scr["qTl"][:, 64 * par:64 * par + 64, t0:t0 + SBK].rearrange("c p s -> p c s")) for par in range(2)],
                   writes=[qB])
            sc.dma(lds[1], [(kL[ub][:], scr["kTl"][:, :, t0:t0 + SBK].rearrange("c p s -> p c s"))], writes=[kB[ub]])
            sc.dma(lds[2], [(vT[:], scr["vTl"][:, :, t0:t0 + SBK].rearrange("c p s -> p c s"))], writes=[vTB])
            vi = 0
            for pi, d in enumerate(PATS):
                for ti in range(16):
                    r_, nbl = ti % d, ti // d
                    off = 128 * d * nbl + r_
                    for c in range(NC3):
                        sc.op("pe", lambda h: h.transpose(out=psV[:, c, :], in_=vT[:, c, off:off + 127 * d + 1:d],
                                                          identity=k.ident_bf[:]),
                              reads=[vTB, k.cB], writes=[psVB], inc=(c == NC3 - 1))
                    dst = vt[pi][ub][:, ti, :].rearrange("p (h e) -> p h e", e=65)[:, :, 0:64]
                    src = psV[:].rearrange("p c (h e) -> p (c h) e", e=64)
                    if vi % 2 == 0:
                        sc.op("act", lambda h: h.copy(out=dst, in_=src), reads=[psVB], writes=[vtB[pi][ub]])
                    else:
                        sc.op("dve", lambda h: h.tensor_copy(out=dst, in_=src), reads=[psVB], writes=[vtB[pi][ub]])
                    vi += 1
            for hh in range(HL):
                c = hh // 2
                p0 = 64 * (hh % 2)
                ab = hi % 2
                hi += 1
                groups = [(pi, d, tg) for pi, d in enumerate(PATS) for tg in range(4)]
                gbase = gi
                gi += len(groups)

                def tiles_of(g):
                    pi, d, tg = groups[g]
                    tiles = []
                    for q4 in range(4):
                        ti = tg * 4 + q4
                        r_, nbl = ti % d, ti // d
                        off = 128 * d * nbl + r_
                        if nbl > 0:
                            prev = (ub, off - 128 * d, ti - d)
                        elif u > 0:
                            nbp = 16 // d - 1
                            prev = (1 - ub, 128 * d * nbp + r_, r_ + d * nbp)
                        else:
                            prev = None
                        tiles.append((ti, off, prev))
                    return tiles

                def g_qk(g):
                    pi, d, tg = groups[g]
                    sl = (gbase + g) % 2
                    for q4, (ti, off, prev) in enumerate(tiles_of(g)):
                        Q = qP[hh % 2][:, c, off:off + 127 * d + 1:d]
                        Kc = kL[ub][:, c, off:off + 127 * d + 1:d]
                        if prev is not None:
                            Kp = kL[prev[0]][:, c, prev[1]:prev[1] + 127 * d + 1:d]
                            rd = [qB, kB[ub], kB[prev[0]]]
                        else:
                            Kp = Kc
                            rd = [qB, kB[ub]]
                        sc.op("pe", lambda h: h.matmul(psS[sl][:, q4, 0:128], lhsT=Kp, rhs=Q, start=True, stop=True),
                              reads=rd, writes=[psSB[sl]], inc=False)
                        sc.op("pe", lambda h: h.matmul(psS[sl][:, q4, 128:256], lhsT=Kc, rhs=Q, start=True, stop=True),
                              reads=rd, writes=[psSB[sl]], inc=(q4 == 3))

                def g_exp(g):
                    sl = (gbase + g) % 2
                    p3 = (gbase + g) % 3
                    sc.op("act", lambda h: h.activation(out=pT[p3][:], in_=psS[sl][:], func=AF.Exp, scale=scale),
                          reads=[psSB[sl]], writes=[pTB[p3]])
                    sc.op("dve", lambda h: h.tensor_tensor(
                        out=pT[p3][:], in0=pT[p3][:], in1=lmask[:, None, :].broadcast_to([128, 4, 256]), op=ALU.mult),
                        reads=[pTB[p3], cB], writes=[pTB[p3]])

                def g_pv(g):
                    pi, d, tg = groups[g]
                    sl = (gbase + g) % 2
                    p3 = (gbase + g) % 3
                    for q4, (ti, off, prev) in enumerate(tiles_of(g)):
                        if prev is not None:
                            sc.op("pe", lambda h: h.matmul(psO[sl][0:65, q4, :], lhsT=vt[pi][prev[0]][:, prev[2], hh * 65:(hh + 1) * 65],
                                                           rhs=pT[p3][:, q4, 0:128], start=True, stop=False),
                                  reads=[pTB[p3], vtB[pi][prev[0]]], writes=[psOB[sl]], inc=False)
                        sc.op("pe", lambda h: h.matmul(psO[sl][0:65, q4, :], lhsT=vt[pi][ub][:, ti, hh * 65:(hh + 1) * 65],
                                                       rhs=pT[p3][:, q4, 128:256], start=(prev is None), stop=True),
                              reads=[pTB[p3], vtB[pi][ub]], writes=[psOB[sl]], inc=(q4 == 3))

                def g_acc(g):
                    pi, d, tg = groups[g]
                    sl = (gbase + g) % 2
                    if d == 1:
                        dstv = acc[ab][:, tg * 512:(tg + 1) * 512].rearrange("p (a j) -> p a j", j=128)
                    elif d == 4:
                        dstv = acc[ab][:, tg * 512:(tg + 1) * 512].rearrange("p (j r) -> p r j", r=4)
                    else:
                        dstv = acc[ab][:].rearrange("p (j r) -> p r j", r=16)[:, tg * 4:tg * 4 + 4, :]
                    if pi == 0:
                        sc.op("dve", lambda h: h.tensor_copy(out=dstv, in_=psO[sl][0:65, :, :]), reads=[psOB[sl]], writes=[accB[ab]])
                    else:
                        sc.op("dve", lambda h: h.tensor_tensor(out=dstv, in0=dstv, in1=psO[sl][0:65, :, :], op=ALU.add),
                              reads=[psOB[sl], accB[ab]], writes=[accB[ab]])

                skewed(len(groups), [g_qk, g_exp, g_pv, g_acc])
                for sbk in range(4):
                    cs_ = slice(sbk * 512, (sbk + 1) * 512)
                    sc.op("pe", lambda h: h.matmul(psE[0:64, :], lhsT=sel[:], rhs=acc[ab][:, cs_], start=True, stop=True),
                          reads=[cB, accB[ab]], writes=[psEB])
                    sc.op("dve", lambda h: h.reciprocal(out=rr[:], in_=psE[0:64, :]), reads=[psEB], writes=[rrB])
                    sc.op("pool", lambda h: h.tensor_tensor(out=ost[ab][:, cs_], in0=acc[ab][0:64, cs_], in1=rr[:], op=ALU.mult),
                          reads=[rrB, accB[ab]], writes=[ostB[ab]])
                row = HD * 64 + hh * 64
                sc.dma(ods[ab], [(scr["mixT"][row // 128, (row % 128):(row % 128) + 64, t0:t0 + SBK], ost[ab][:])],
                       reads=[ostB[ab]])


def phase4(k, l):
    nc, sc, cfg, W, C = k.nc, k.sc, k.cfg, k.W, k.C
    S, HD, HL, G, HS, NB, NX = cfg.S, cfg.HD, cfg.HL, cfg.G, cfg.HS, cfg.NB, cfg.NXBC
    scr = k.scr
    XC = HS // 2
    HW = HS * 64
    with ExitStack() as ps:
        sb, pt = _mk(k, l, ps)
        triu = sb("p4_triu", [128, 128]); negm = sb("p4_negm", [128, 128])
        dtb = sb("p4_dtb", [128, HS]); alog = sb("p4_alog", [128, HS]); Dd = sb("p4_D", [128, HS]); gn = sb("p4_gn", [128, HW])
        negA = sb("p4_negA", [128, HS])
        cB = Buf(); d0 = sc.dsem()
        sc.dma(d0, [(triu[:], C["triu"]), (negm[:], C["negm"]),
                    (dtb[:], W["dt_bias"][l:l + 1, :].broadcast_to([128, HS])),
                    (alog[:], W["A_log"][l:l + 1, :].broadcast_to([128, HS])),
                    (Dd[:], W["ssd_D"][l:l + 1, :].broadcast_to([128, HS])),
                    (gn[:], W["ssd_norm"][l:l + 1, :].broadcast_to([128, HW]))], writes=[cB])
        pB = Buf()
        sc.op("act", lambda h: h.activation(out=negA[:], in_=alog[:], func=AF.Exp), reads=[cB], writes=[pB])
        sc.op("dve", lambda h: h.tensor_scalar(out=negA[:], in0=negA[:], scalar1=-1.0, scalar2=None, op0=ALU.mult), reads=[pB], writes=[pB])
        xb_ = [sb("p4_xb%d" % i, [128, NX, TB], BF16) for i in range(2)]
        z_ = [sb("p4_z%d" % i, [128, 4, HW], BF16) for i in range(2)]
        dt_ = [sb("p4_dt%d" % i, [128, 4, HS]) for i in range(2)]
        inB = [Buf() for _ in range(2)]; lds = [sc.dsem() for _ in range(2)]
        dtp = sb("p4_dtp", [128, 4, HS]); aa = sb("p4_a", [128, 4, HS]); dB = Buf()
        a_bc = sb("p4_abc", [128, HS, 128]); abB = Buf()
        cs_sb2 = [sb("p4_cs%d" % i, [128, HS]) for i in range(2)]; csl2 = [sb("p4_csl%d" % i, [128, HS]) for i in range(2)]; csB2 = [Buf() for _ in range(2)]
        arg = sb("p4_arg", [128, HS, 128]); argB = Buf()
        E = sb("p4_E", [128, HS, 128]); EB = Buf()
        MT2 = [sb("p4_MT%d" % i, [128, HS, 128], BF16) for i in range(2)]; MTB2 = [Buf() for _ in range(2)]
        x_sb2 = [sb("p4_x%d" % i, [128, HW]) for i in range(2)]; B_sb2 = [sb("p4_B%d" % i, [128, G * 128], BF16) for i in range(2)]; xB2 = [Buf() for _ in range(2)]
        xdt2 = [sb("p4_xdt%d" % i, [128, HS, 64], BF16) for i in range(2)]; xdtB2 = [Buf() for _ in range(2)]
        xdd2 = [sb("p4_xdd%d" % i, [128, HS, 64], BF16) for i in range(2)]; xddB2 = [Buf() for _ in range(2)]
        ecs2 = [sb("p4_ecs%d" % i, [128, HS]) for i in range(2)]; dst_ = sb("p4_dst", [128, HS]); edec2 = [sb("p4_edec%d" % i, [128, HS]) for i in range(2)]; eB2 = [Buf() for _ in range(2)]; dstB = Buf()
        t13 = [sb("p4_t1%d" % i, [128, HW]) for i in range(3)]; t1B3 = [Buf() for _ in range(3)]
        t33 = [sb("p4_t3%d" % i, [128, HW]) for i in range(3)]; t3B3 = [Buf() for _ in range(3)]
        yv = sb("p4_yv", [128, HW]); yvB = Buf()
        szb = [sb("p4_szb%d" % i, [128, 4, HW]) for i in range(2)]; szbB = [Buf() for _ in range(2)]
        junk = sb("p4_junk", [128, HW], BF16); junkB = Buf()
        ss2 = sb("p4_ss2", [128, G]); rs2 = sb("p4_rs2", [128, G]); ssB = Buf(); rsB = Buf()
        yn = sb("p4_yn", [128, HW], BF16); ynB = Buf()
        yst = [sb("p4_yst%d" % i, [128, XC, TB], BF16) for i in range(2)]; ystB = [Buf() for _ in range(2)]
        sds = [sc.dsem() for _ in range(2)]
        st = sb("p4_st", [128, HW]); st_bf = sb("p4_stbf", [128, HW], BF16); stB = Buf(); stbB = Buf()
        ps1 = pt("p4_ps1", [128, 512]); ps1B = PBuf()
        psR = pt("p4_psR", [128, 2, 512]); psRB = PBuf()
        psXT = pt("p4_psXT", [128, 1024], BF16); psXTB = PBuf(); psXTb = pt("p4_psXTb", [128, 1024], BF16); psXTbB = PBuf()
        psY = pt("p4_psY", [128, 512]); psYB = PBuf()
        psYO = pt("p4_psYO", [128, 512]); psYOB = PBuf()
        psS = pt("p4_psS", [128, 512]); psSB = PBuf()
        sc.op("pool", lambda h: h.memset(st[:], 0.0), writes=[stB])
        sc.op("pool", lambda h: h.memset(st_bf[:], 0.0), writes=[stbB])
        XTO = XC * 128 + G * 128
        def load4(tt):
            bb = tt % 2
            tk = tt * TB
            sc.dma(lds[bb], [(xb_[bb][:], scr["xbcT"][:, :, tk:tk + TB].rearrange("c p s -> p c s")),
                             (z_[bb][:], scr["zs"][tk:tk + TB, :].rearrange("(n p) f -> p n f", p=128)),
                             (dt_[bb][:], scr["dts"][tk:tk + TB, :].rearrange("(n p) f -> p n f", p=128))],
                   writes=[inB[bb]])

        load4(0)
        for t in range(NB):
            tok0 = t * TB
            b = t % 2
            if t + 1 < NB:
                load4(t + 1)
            sc.op("dve", lambda h: h.tensor_tensor(out=dtp[:], in0=dt_[b][:], in1=dtb[:, None, :].broadcast_to([128, 4, HS]), op=ALU.add),
                  reads=[inB[b], cB], writes=[dB])
            sc.op("act", lambda h: h.activation(out=dtp[:], in_=dtp[:], func=AF.Exp), reads=[dB], writes=[dB])
            sc.op("act", lambda h: h.activation(out=dtp[:], in_=dtp[:], func=AF.Ln, bias=1.0), reads=[dB], writes=[dB])
            sc.op("dve", lambda h: h.tensor_tensor(out=aa[:], in0=dtp[:], in1=negA[:, None, :].broadcast_to([128, 4, HS]), op=ALU.mult),
                  reads=[dB, pB], writes=[dB])
            sc.op("act", lambda h: h.activation(out=szb[b][:], in_=z_[b][:], func=AF.Silu), reads=[inB[b]], writes=[szbB[b]])
            def front(n):
                q = n % 2
                csl_ = slice(n * 128, (n + 1) * 128)
                sc.op("dve", lambda h: h.tensor_copy(out=a_bc[:], in_=aa[:, n, :, None].broadcast_to([128, HS, 128])),
                      reads=[dB], writes=[abB])
                sc.op("pe", lambda h: h.matmul(ps1[:, 256:256 + HS], lhsT=triu[:], rhs=aa[:, n, :], start=True, stop=True),
                      reads=[cB, dB], writes=[ps1B])
                for hh in range(HS):
                    sc.op("pe", lambda h: h.matmul(psR[:, hh // 4, (hh % 4) * 128:(hh % 4 + 1) * 128], lhsT=a_bc[:, hh, :], rhs=triu[:],
                                                   start=True, stop=True), reads=[cB, abB], writes=[psRB], inc=(hh == HS - 1))
                psRv = psR[:].rearrange("p a (b c) -> p (a b) c", c=128)[:, 0:HS, :]
                sc.op("act", lambda h: h.copy(out=cs_sb2[q][:], in_=ps1[:, 256:256 + HS]), reads=[ps1B], writes=[csB2[q]])
                sc.op("act", lambda h: h.copy(out=csl2[q][:], in_=psRv[:, :, 127]), reads=[psRB], writes=[csB2[q]])
                sc.op("dve", lambda h: h.tensor_tensor(out=arg[:], in0=psRv, in1=cs_sb2[q][:, :, None].broadcast_to([128, HS, 128]), op=ALU.subtract),
                      reads=[psRB, csB2[q]], writes=[argB])
                sc.op("dve", lambda h: h.tensor_tensor(out=arg[:], in0=arg[:], in1=negm[:, None, :].broadcast_to([128, HS, 128]), op=ALU.add),
                      reads=[argB, cB], writes=[argB])
                sc.op("act", lambda h: h.activation(out=E[:], in_=arg[:], func=AF.Exp), reads=[argB], writes=[EB])
                for g in range(G):
                    sc.op("pe", lambda h: h.matmul(ps1[:, g * 128:(g + 1) * 128], lhsT=xb_[b][:, XC + g, csl_], rhs=xb_[b][:, XC + G + g, csl_],
                                                   start=True, stop=True), reads=[inB[b]], writes=[ps1B], inc=(g == G - 1))
                for g in range(G):
                    sc.op("dve", lambda h: h.tensor_tensor(out=MT2[q][:, 3 * g:3 * g + 3, :], in0=E[:, 3 * g:3 * g + 3, :],
                                                           in1=ps1[:, None, g * 128:(g + 1) * 128].broadcast_to([128, 3, 128]), op=ALU.mult),
                          reads=[EB, ps1B], writes=[MTB2[q]])
                for j in range(XC + G):
                    sc.op("pe", lambda h: h.transpose(out=psXT[:, j * 128:(j + 1) * 128], in_=xb_[b][:, j, csl_], identity=k.ident_bf[:]),
                          reads=[inB[b], k.cB], writes=[psXTB], inc=(j == XC + G - 1))
                sc.op("act", lambda h: h.copy(out=x_sb2[q][:], in_=psXT[:, 0:HW]), reads=[psXTB], writes=[xB2[q]])
                sc.op("act", lambda h: h.copy(out=B_sb2[q][:], in_=psXT[:, HW:HW + G * 128]), reads=[psXTB], writes=[xB2[q]])
                sc.op("dve", lambda h: h.tensor_tensor(out=xdt2[q][:], in0=x_sb2[q][:].rearrange("p (h e) -> p h e", e=64),
                                                       in1=dtp[:, n, :, None].broadcast_to([128, HS, 64]), op=ALU.mult),
                      reads=[xB2[q], dB], writes=[xdtB2[q]])
                sc.op("act", lambda h: h.activation(out=ecs2[q][:], in_=cs_sb2[q][:], func=AF.Exp), reads=[csB2[q]], writes=[eB2[q]])
                sc.op("pool", lambda h: h.tensor_tensor(out=t33[n % 3][:].rearrange("p (h e) -> p h e", e=64), in0=x_sb2[q][:].rearrange("p (h e) -> p h e", e=64),
                                                        in1=Dd[:, :, None].broadcast_to([128, HS, 64]), op=ALU.mult),
                      reads=[xB2[q], cB], writes=[t3B3[n % 3]])
                sc.op("dve", lambda h: h.tensor_tensor(out=dst_[:], in0=csl2[q][:], in1=cs_sb2[q][:], op=ALU.subtract), reads=[csB2[q]], writes=[dstB])
                sc.op("act", lambda h: h.activation(out=dst_[:], in_=dst_[:], func=AF.Exp), reads=[dstB], writes=[dstB])
                sc.op("act", lambda h: h.activation(out=edec2[q][:], in_=csl2[q][:], func=AF.Exp), reads=[csB2[q]], writes=[eB2[q]])
                sc.op("dve", lambda h: h.tensor_tensor(out=xdd2[q][:], in0=xdt2[q][:], in1=dst_[:, :, None].broadcast_to([128, HS, 64]), op=ALU.mult),
                      reads=[xdtB2[q], dstB], writes=[xddB2[q]])

            def back(n):
                q = n % 2
                csl_ = slice(n * 128, (n + 1) * 128)
                for hh in range(HS):
                    sc.op("pe", lambda h: h.matmul(psY[:, hh * 64:(hh + 1) * 64], lhsT=MT2[q][:, hh, :], rhs=xdt2[q][:, hh, :], start=True, stop=True),
                          reads=[MTB2[q], xdtB2[q]], writes=[psYB], inc=(hh == HS - 1))
                for g in range(G):
                    sc.op("pe", lambda h: h.matmul(psYO[:, g * 192:(g + 1) * 192], lhsT=xb_[b][:, XC + G + g, csl_], rhs=st_bf[:, g * 192:(g + 1) * 192],
                                                   start=True, stop=True), reads=[inB[b], stbB], writes=[psYOB], inc=(g == G - 1))
                sc.op("dve", lambda h: h.tensor_tensor(out=t13[n % 3][:].rearrange("p (h e) -> p h e", e=64), in0=psYO[:, 0:HW].rearrange("p (h e) -> p h e", e=64),
                                                       in1=ecs2[q][:, :, None].broadcast_to([128, HS, 64]), op=ALU.mult),
                      reads=[psYOB, eB2[q]], writes=[t1B3[n % 3]])
                sc.op("dve", lambda h: h.tensor_tensor(out=t13[n % 3][:], in0=psY[:, 0:HW], in1=t13[n % 3][:], op=ALU.add), reads=[psYB, t1B3[n % 3]], writes=[t1B3[n % 3]])
                for g in range(G):
                    sc.op("pe", lambda h: h.matmul(psS[:, g * 192:(g + 1) * 192], lhsT=B_sb2[q][:, g * 128:(g + 1) * 128],
                                                   rhs=xdd2[q][:, 3 * g:3 * g + 3, :].rearrange("p h e -> p (h e)"), start=True, stop=True),
                          reads=[xB2[q], xddB2[q]], writes=[psSB], inc=(g == G - 1))
                sc.op("dve", lambda h: h.tensor_tensor(out=st[:].rearrange("p (h e) -> p h e", e=64), in0=st[:].rearrange("p (h e) -> p h e", e=64),
                                                       in1=edec2[q][:, :, None].broadcast_to([128, HS, 64]), op=ALU.mult),
                      reads=[stB, eB2[q]], writes=[stB])
                sc.op("dve", lambda h: h.tensor_tensor(out=st[:], in0=st[:], in1=psS[:, 0:HW], op=ALU.add), reads=[stB, psSB], writes=[stB])
                sc.op("pool", lambda h: h.tensor_copy(out=st_bf[:], in_=st[:]), reads=[stB], writes=[stbB])

            def post(n):
                q = n % 2
                csl_ = slice(n * 128, (n + 1) * 128)
                sc.op("pool", lambda h: h.tensor_tensor(out=yv[:], in0=t13[n % 3][:], in1=t33[n % 3][:], op=ALU.add), reads=[t1B3[n % 3], t3B3[n % 3]], writes=[yvB])
                sc.op("dve", lambda h: h.tensor_tensor(out=yv[:], in0=yv[:], in1=szb[b][:, n, :], op=ALU.mult), reads=[yvB, szbB[b]], writes=[yvB])
                for g in range(G):
                    sc.op("act", lambda h: h.activation(out=junk[:, 0:192], in_=yv[:, g * 192:(g + 1) * 192], func=AF.Square,
                                                        accum_out=ss2[:, g:g + 1]), reads=[yvB], writes=[junkB, ssB])
                rms_rstd(k, ps, ss2, rs2, G, 192, ssB, rsB)
                for g in range(G):
                    sc.op("dve", lambda h: h.scalar_tensor_tensor(out=yn[:, g * 192:(g + 1) * 192], in0=yv[:, g * 192:(g + 1) * 192],
                                                                  scalar=rs2[:, g:g + 1], in1=gn[:, g * 192:(g + 1) * 192],
                                                                  op0=ALU.mult, op1=ALU.mult), reads=[yvB, rsB, cB], writes=[ynB])
                for j in range(XC):
                    sc.op("pe", lambda h: h.transpose(out=psXTb[:, j * 128:(j + 1) * 128], in_=yn[:, j * 128:(j + 1) * 128],
                                                      identity=k.ident_bf[:]), reads=[ynB, k.cB], writes=[psXTbB], inc=(j == XC - 1))
                sc.op("act", lambda h: h.copy(out=yst[b][:, :, csl_], in_=psXTb[:, 0:XC * 128].rearrange("p (j s) -> p j s", s=128)),
                      reads=[psXTbB], writes=[ystB[b]])

            front(0)
            for n in range(4):
                if n + 1 < 4:
                    front(n + 1)
                back(n)
                if n > 0:
                    post(n - 1)
            post(3)
            c0 = (HD * 64 + HL * 64) // 128
            sc.dma(sds[b], [(scr["mixT"][c0:c0 + XC, :, tok0:tok0 + TB].rearrange("c p s -> p c s"), yst[b][:])], reads=[ystB[b]])


def post_norm_add(k, sc, psF, psFB, n, ss, ssB, rstd, rsB, gb, gB, yt, ytB, xres, xresB, xo, xoB, junk, junkB, add_eng="pool"):
    sc.op("act", lambda h: h.activation(out=junk[:], in_=psF[:], func=AF.Square, accum_out=ss[:, n:n + 1]),
          reads=[psFB], writes=[junkB, ssB])
    rms_rstd(k, None, ss[:, n:n + 1], rstd[:, n:n + 1], 1, D, ssB, rsB)
    sc.op("dve", lambda h: h.scalar_tensor_tensor(out=yt[:], in0=psF[:], scalar=rstd[:, n:n + 1], in1=gb[:],
                                                  op0=ALU.mult, op1=ALU.mult), reads=[psFB, rsB, gB], writes=[ytB])
    sc.op(add_eng, lambda h: h.tensor_tensor(out=xo[:, n, :], in0=xres[:, n, :], in1=yt[:], op=ALU.add),
          reads=[ytB, xresB], writes=[xoB])


def phase5a(k, l, x_src):
    nc, sc, cfg, W, C = k.nc, k.sc, k.cfg, k.W, k.C
    S, NB, MC = cfg.S, cfg.NB, cfg.MIXC
    scr = k.scr
    with ExitStack() as ps:
        sb, pt = _mk(k, l, ps)
        wout = sb("p5a_w", [128, MC, D], BF16)
        stg = [sb("p5a_stg%d" % i, [128, 512]) for i in range(3)]
        stgB = [Buf() for _ in range(3)]; ds = [sc.dsem() for _ in range(3)]
        k.gB, k.wB = Buf(), Buf()
        gb = sb("p5a_g", [128, D]); gB = Buf(); d0 = sc.dsem()
        sc.dma(d0, [(gb[:], W["g_postmix"][l:l + 1, :].broadcast_to([128, D]))], writes=[gB])
        load_weights_bf16(k, ps, wout, W["w_out"][l], D, None, stg, stgB, ds, MC)
        mT = [sb("p5a_m%d" % i, [128, MC, TB], BF16) for i in range(2)]; mB = [Buf() for _ in range(2)]
        xr = [sb("p5a_x%d" % i, [128, 4, D]) for i in range(2)]; xB = [Buf() for _ in range(2)]
        lds = [sc.dsem() for _ in range(2)]
        xo = [sb("p5a_xo%d" % i, [128, 4, D]) for i in range(2)]; xoB = [Buf() for _ in range(2)]
        sds = [sc.dsem() for _ in range(2)]
        psF = [pt("p5a_psF%d" % i, [128, D]) for i in range(2)]; psFB = [PBuf() for _ in range(2)]
        ss = sb("p5a_ss", [128, 4]); ssB = Buf(); rstd = sb("p5a_rstd", [128, 4]); rsB = Buf()
        yt = sb("p5a_yt", [128, D]); ytB = Buf()
        junk = sb("p5a_junk", [128, D], BF16); junkB = Buf()
        ci = 0
        def load5a(tt):
            bb = tt % 2
            tk = tt * TB
            sc.dma(lds[bb], [(mT[bb][:], scr["mixT"][:, :, tk:tk + TB].rearrange("c p s -> p c s")),
                             (xr[bb][:], x_src[tk:tk + TB, :].rearrange("(n p) d -> p n d", p=128))],
                   writes=[mB[bb], xB[bb]])

        load5a(0)
        for t in range(NB):
            tok0 = t * TB
            b = t % 2
            if t + 1 < NB:
                load5a(t + 1)
            for n in range(4):
                a = ci % 2
                ci += 1
                for half in range(2):
                    for kc in range(MC):
                        sc.op("pe", lambda h, kc=kc, half=half: h.matmul(
                            psF[a][:, half * 512:(half + 1) * 512], lhsT=mT[b][:, kc, n * 128:(n + 1) * 128],
                            rhs=wout[:, kc, half * 512:(half + 1) * 512], start=(kc == 0), stop=(kc == MC - 1)),
                            reads=[k.wB, mB[b]], writes=[psFB[a]], inc=(kc == MC - 1 and half == 1))
                post_norm_add(k, sc, psF[a], psFB[a], n, ss, ssB, rstd, rsB, gb, gB, yt, ytB, xr[b], xB[b], xo[b], xoB[b], junk, junkB,
                              add_eng=("dve" if n % 2 == 0 else "pool"))
            sc.dma(sds[b], [(scr["xa"][tok0:tok0 + TB, :].rearrange("(n p) d -> p n d", p=128), xo[b][:])], reads=[xoB[b]])


def phase5b(k, l, x_dst):
    nc, sc, cfg, W, C = k.nc, k.sc, k.cfg, k.W, k.C
    S, FC, TBF = cfg.S, cfg.FC, cfg.TBF
    NT = TBF // 128
    scr = k.scr
    with ExitStack() as ps:
        sb, pt = _mk(k, l, ps)
        wup = sb("p5b_wup", [128, 8, 2 * cfg.DFF], BF16)
        wdn = sb("p5b_wdn", [128, FC, D], BF16)
        stg = [sb("p5b_stg%d" % i, [128, 512]) for i in range(3)]
        stgB = [Buf() for _ in range(3)]; ds = [sc.dsem() for _ in range(3)]
        k.gB, k.wB = Buf(), Buf()
        gain = sb("p5b_gain", [128, 8]); gb = sb("p5b_g", [128, D]); gB = Buf(); d0 = sc.dsem()
        fcw = sb("p5b_fcw", [128, 2 * FC, 3]); fcb = sb("p5b_fcb", [128, 2 * FC])
        sc.dma(d0, [(gb[:], W["g_postffn"][l:l + 1, :].broadcast_to([128, D])), (gain[:], W["g_preffn"][l]),
                    (fcw[:], W["fcw"][l]), (fcb[:], W["fcb"][l])], writes=[gB, k.gB])
        load_weights_bf16(k, ps, wup, W["ffn_up"][l], 2 * cfg.DFF, gain, stg, stgB, ds, 8)
        load_weights_bf16(k, ps, wdn, W["ffn_down"][l], D, None, stg, stgB, ds, FC)
        xr2 = [sb("p5b_x%d" % i, [128, NT, D]) for i in range(2)]; xB2 = [[Buf() for _ in range(NT)] for i in range(2)]
        lds2 = [[sc.dsem() for _ in range(NT)] for i in range(2)]; sds2 = [sc.dsem() for _ in range(2)]
        xs2 = [sb("p5b_xs%d" % i, [128, NT, D], BF16) for i in range(2)]; xsB2 = [Buf() for _ in range(2)]
        hT2 = [sb("p5b_hT%d" % i, [128, 8, 2 + TBF], BF16) for i in range(2)]; hTB2 = [Buf() for _ in range(2)]
        aT = sb("p5b_aT", [128, FC, TBF], BF16); aTB = Buf()
        NU = 4
        u = [sb("p5b_u%d" % i, [128, 2 + TBF]) for i in range(NU)]; uB = [Buf() for _ in range(NU)]; uhB = [Buf() for _ in range(NU)]
        y = [sb("p5b_y%d" % i, [128, TBF]) for i in range(NU)]; yB = [Buf() for _ in range(NU)]
        sg = [sb("p5b_sg%d" % i, [128, TBF]) for i in range(2)]; sgB = [Buf() for _ in range(2)]
        ss = sb("p5b_ss", [128, 4]); ssB = Buf(); rstd = sb("p5b_rstd", [128, 4]); rsB = Buf()
        yt = sb("p5b_yt", [128, D]); ytB = Buf()
        junk = sb("p5b_junk", [128, D], BF16); junkB = Buf()
        psT = [pt("p5b_psT%d" % i, [128, 2, 512], BF16) for i in range(2)]; psTB = [PBuf() for _ in range(2)]
        psU = [pt("p5b_psU%d" % i, [128, 512]) for i in range(NU)]; psUB = [PBuf() for _ in range(NU)]
        psF = [pt("p5b_psF%d" % i, [128, D]) for i in range(1)]; psFB = [PBuf() for _ in range(1)]
        for i in range(2):
            sc.op("pool", lambda h, i=i: h.memset(hT2[i][:], 0.0), writes=[hTB2[i]])
        ci = 0
        NBLK = S // TBF

        def do_norm(t):
            b = t % 2
            tok0 = t * TBF
            for n in range(NT):
                sc.dma(lds2[b][n], [(xr2[b][:, n, :], scr["xa"][tok0 + n * 128:tok0 + (n + 1) * 128, :])], writes=[xB2[b][n]])
            for n in range(NT):
                sc.op("act", lambda h, n=n: h.activation(out=junk[:], in_=xr2[b][:, n, :], func=AF.Square, accum_out=ss[:, n:n + 1]),
                      reads=[xB2[b][n]], writes=[junkB, ssB])
            rms_rstd(k, ps, ss, rstd, NT, D, ssB, rsB)
            for n in range(NT):
                if n % 2 == 0:
                    sc.op("dve", lambda h, n=n: h.tensor_scalar(
                        out=xs2[b][:, n, :], in0=xr2[b][:, n, :], scalar1=rstd[:, n:n + 1], scalar2=None, op0=ALU.mult),
                        reads=[xB2[b][n], rsB], writes=[xsB2[b]])
                else:
                    sc.op("act", lambda h, n=n: h.activation(out=xs2[b][:, n, :], in_=xr2[b][:, n, :], func=AF.Copy, scale=rstd[:, n:n + 1]),
                          reads=[xB2[b][n], rsB], writes=[xsB2[b]])
            if t > 0:
                sc.op("pool", lambda h: h.tensor_copy(out=hT2[b][:, :, 0:2], in_=hT2[1 - b][:, :, TBF:TBF + 2]), reads=[hTB2[1 - b]], writes=[hTB2[b]])
            for cp in range(4):
                pi = cp % 2
                for c2 in range(2):
                    c = cp * 2 + c2
                    for n in range(NT):
                        sc.op("pe", lambda h, c=c, c2=c2, n=n: h.transpose(
                            out=psT[pi][:, c2, n * 128:(n + 1) * 128], in_=xs2[b][:, n, c * 128:(c + 1) * 128],
                            identity=k.ident_bf[:]), reads=[xsB2[b], k.cB], writes=[psTB[pi]], inc=(c2 == 1 and n == NT - 1))
                sc.op("act", lambda h, cp=cp: h.copy(out=hT2[b][:, 2 * cp:2 * cp + 2, 2:2 + TBF], in_=psT[pi][:, :, 0:TBF]),
                      reads=[psTB[pi]], writes=[hTB2[b]])

        def do_chunks(t):
            b = t % 2
            chunks = [(j, wi, c) for j in range(FC) for wi, c in enumerate((j, FC + j))]

            def st_pe(i):
                j, wi, c = chunks[i]
                a = i % NU
                for kc in range(8):
                    sc.op("pe", lambda h: h.matmul(psU[a][:, 0:TBF + 2], lhsT=wup[:, kc, c * 128:(c + 1) * 128], rhs=hT2[b][:, kc, :],
                                                   start=(kc == 0), stop=(kc == 7)), reads=[k.wB, hTB2[b]], writes=[psUB[a]], inc=(kc == 7))

            def st_copy(i):
                j, wi, c = chunks[i]
                a = i % NU
                sc.op("act", lambda h: h.copy(out=u[a][:, 0:2 + TBF], in_=psU[a][:, 0:2 + TBF]), reads=[psUB[a]], writes=[uB[a]])
                sc.op("act", lambda h: h.activation(out=y[a][:], in_=psU[a][:, 2:2 + TBF], func=AF.Identity, scale=fcw[:, c, 2:3], bias=fcb[:, c:c + 1]),
                      reads=[psUB[a], k.gB], writes=[yB[a]])

            def st_conv(i):
                j, wi, c = chunks[i]
                a = i % NU
                for kk in range(2):
                    sc.op("dve", lambda h: h.scalar_tensor_tensor(out=y[a][:], in0=u[a][:, kk:kk + TBF], scalar=fcw[:, c, kk:kk + 1], in1=y[a][:],
                                                                  op0=ALU.mult, op1=ALU.add), reads=[uB[a], yB[a], k.gB], writes=[yB[a]])

            def st_gate(i):
                j, wi, c = chunks[i]
                a = i % NU
                if wi == 0:
                    sc.op("act", lambda h: h.activation(out=sg[j % 2][:], in_=y[a][:], func=AF.Silu), reads=[yB[a]], writes=[sgB[j % 2]])
                else:
                    sc.op("pool", lambda h: h.tensor_tensor(out=aT[:, j, :], in0=sg[j % 2][:], in1=y[a][:], op=ALU.mult),
                          reads=[sgB[j % 2], yB[a]], writes=[aTB])

            skewed(len(chunks), [st_pe, st_copy, st_conv, st_gate])

        def do_down(t):
            b = t % 2
            tok0 = t * TBF
            for n in range(NT):
                a = 0
                for half in range(2):
                    for j in range(FC):
                        sc.op("pe", lambda h, j=j, half=half: h.matmul(
                            psF[a][:, half * 512:(half + 1) * 512], lhsT=aT[:, j, n * 128:(n + 1) * 128],
                            rhs=wdn[:, j, half * 512:(half + 1) * 512], start=(j == 0), stop=(j == FC - 1)),
                            reads=[k.wB, aTB], writes=[psFB[a]], inc=(j == FC - 1 and half == 1))
                post_norm_add(k, sc, psF[a], psFB[a], n, ss, ssB, rstd, rsB, gb, gB, yt, ytB, xr2[b], xB2[b][n], xr2[b], xB2[b][n], junk, junkB)
            sc.dma(sds2[b], [(x_dst[tok0:tok0 + TBF, :].rearrange("(n p) d -> p n d", p=128), xr2[b][:])], reads=xB2[b])

        do_norm(0)
        for t in range(NBLK):
            do_chunks(t)
            if t + 1 < NBLK:
                do_norm(t + 1)
            do_down(t)


def kernel(**inputs):
    cfg = Cfg()
    nc = build(cfg)
    hp = host_params(cfg, inputs)
    consts = host_consts(cfg)
    x = np.asarray(inputs["x"], np.float32)
    nb = x.shape[0]
    work = [0, 1, 4, 5][:nb]
    zeros = {"x": np.zeros_like(x[0])}
    for kk, v in hp.items():
        zeros[kk] = np.zeros_like(v)
    for kk, v in consts.items():
        zeros[kk] = np.zeros_like(v)
    in_maps = []
    for c in range(8):
        if c in work:
            m = {"x": np.ascontiguousarray(x[work.index(c)])}
            m.update(hp)
            m.update(consts)
        else:
            m = zeros
        in_maps.append(m)
    res = run_bass_kernel_spmd(nc, in_maps, core_ids=list(range(8)))
    return np.stack([np.asarray(res.results[c]["out"], np.float32) for c in work])
```

```python
import math
import os
from contextlib import ExitStack

import numpy as np
import ml_dtypes

import concourse.bass as bass
import concourse.mybir as mybir
from concourse.bass_utils import run_bass_kernel_spmd

F32 = mybir.dt.float32
BF16 = mybir.dt.bfloat16
AF = mybir.ActivationFunctionType
ALU = mybir.AluOpType
AX = mybir.AxisListType

D = 1024
EPS = 1e-6
TB = 512
SAME_ENGINE_SYNC = True


class Buf:
    __slots__ = ("name", "w", "r", "excl")

    def __init__(self, name="", excl=False):
        self.name = name
        self.w = None
        self.r = {}
        self.excl = excl


def PBuf(name=""):
    return Buf(name, True)


class _Eng:
    def __init__(self, name, h, sem):
        self.name, self.h, self.sem, self.cnt, self.seen = name, h, sem, 0, {}


class _DSem:
    def __init__(self, sem):
        self.sem, self.cnt = sem, 0


class Sched:
    def __init__(self, nc, es):
        self.nc = nc
        self.es = es
        self.E = {}
        for name, h in (("pe", nc.tensor), ("act", nc.scalar), ("dve", nc.vector),
                        ("pool", nc.gpsimd), ("sp", nc.sync)):
            self.E[name] = _Eng(name, h, es.enter_context(nc.semaphore("s_" + name)))
        self.bar = es.enter_context(nc.semaphore("s_bar"))
        self.barcnt = 0
        self.dsems = []
        self.nsem = 0

    def dsem(self):
        self.nsem += 1
        d = _DSem(self.es.enter_context(self.nc.semaphore("d%d" % self.nsem)))
        self.dsems.append(d)
        return d

    def _wait(self, e, reads, writes):
        deps = {}

        def add(t):
            if t is not None and deps.get(t[0], (None, 0))[1] < t[1]:
                deps[t[0]] = t

        for b in reads:
            add(b.w)
        for b in writes:
            add(b.w)
            for s, v in b.r.items():
                add((s, v))
        for s, (sem, val) in deps.items():
            if sem is e.sem and (e.name == "pe" or not SAME_ENGINE_SYNC):
                continue
            if e.seen.get(sem, 0) >= val:
                continue
            e.h.wait_ge(sem, val)
            e.seen[sem] = val

    @staticmethod
    def _mark(tok, reads, writes):
        for b in reads:
            if b.r.get(tok[0], 0) < tok[1]:
                b.r[tok[0]] = tok[1]
        for b in writes:
            b.w = tok
            b.r = {}

    def op(self, eng, fn, reads=(), writes=(), inc=True):
        e = self.E[eng]
        if any(b.excl for b in reads):
            writes = list(writes) + [b for b in reads if b.excl]
            reads = [b for b in reads if not b.excl]
        self._wait(e, reads, writes)
        ins = fn(e.h)
        if inc:
            e.cnt += 1
            ins.then_inc(e.sem, 1)
            tok = (e.sem, e.cnt)
        else:
            tok = (e.sem, e.cnt + 1)
        self._mark(tok, reads, writes)
        return tok

    def dma(self, ds, pairs, reads=(), writes=(), eng="sp"):
        e = self.E[eng]
        self._wait(e, reads, writes)
        for out, in_ in pairs:
            ds.cnt += 16
            e.h.dma_start(out=out, in_=in_).then_inc(ds.sem, 16)
        tok = (ds.sem, ds.cnt)
        self._mark(tok, reads, writes)
        return tok

    def barrier(self):
        sp = self.E["sp"]
        for o in self.E.values():
            if o is not sp and o.cnt > 0 and sp.seen.get(o.sem, 0) < o.cnt:
                sp.h.wait_ge(o.sem, o.cnt)
                sp.seen[o.sem] = o.cnt
        for d in self.dsems:
            if d.cnt > 0 and sp.seen.get(d.sem, 0) < d.cnt:
                sp.h.wait_ge(d.sem, d.cnt)
                sp.seen[d.sem] = d.cnt
        self.barcnt += 1
        sp.h.sem_inc(self.bar, 1)
        for o in self.E.values():
            if o is not sp:
                o.h.wait_ge(self.bar, self.barcnt)
            for o2 in self.E.values():
                o.seen[o2.sem] = o2.cnt
            for d in self.dsems:
                o.seen[d.sem] = d.cnt


class Cfg:
    def __init__(self, S=8192, L=2, HD=4, HL=6, G=2, DFF=2816, TBF=256):
        self.S, self.L, self.HD, self.HL, self.G, self.DFF, self.TBF = S, L, HD, HL, G, DFF, TBF
        self.HS = 3 * G
        self.NB = S // TB
        off = 0
        self.fm = []
        for nm, cnt, m in (("dq", HD, 64), ("dk", HD, 64), ("lq", HL // 2, 128), ("lk", HL // 2, 128),
                           ("lv", HL // 2, 128), ("xs", self.HS // 2, 128), ("B", G, 128), ("C", G, 128)):
            for i in range(cnt):
                self.fm.append((nm, i, off, m))
                off += m
        self.off_dv = off
        off += HD * 64
        self.off_z = off
        off += self.HS * 64 + self.HS
        self.NCOL = off
        self.NXBC = self.HS // 2 + 2 * G
        self.MIXC = (HD * 64 + HL * 64 + self.HS * 64) // 128
        self.FC = DFF // 128


def _rope_tables(S, head_dim, rows):
    half = head_dim // 2
    inv = np.exp(-math.log(10000.0) * np.arange(half, dtype=np.float32) / half).astype(np.float32)
    ang = np.arange(S, dtype=np.float32)[None, :] * inv[:, None]
    cos = np.cos(ang).astype(np.float32)
    sin = np.sin(ang).astype(np.float32)
    p = np.arange(rows)
    j = p % half
    first = (p % head_dim) < half
    cosT = cos[j]
    sinT = np.where(first[:, None], -sin[j], sin[j])
    perm = np.zeros((rows, rows), np.float32)
    partner = np.where(first, p + half, p - half)
    perm[p, partner] = 1.0
    return cosT.astype(np.float32), sinT.astype(np.float32), perm


def host_consts(cfg):
    S = cfg.S
    c = {}
    cd, sd, pd = _rope_tables(S, 32, 64)
    cl, sl, pl = _rope_tables(S, 64, 128)
    c["cosd"], c["sind"], c["cosl"], c["sinl"] = [a.astype(ml_dtypes.bfloat16) for a in (cd, sd, cl, sl)]
    c["permd"] = pd.astype(ml_dtypes.bfloat16)
    c["perml"] = pl.astype(ml_dtypes.bfloat16)
    c["ident_bf"] = np.eye(128, dtype=np.float32).astype(ml_dtypes.bfloat16)
    c["ident_f"] = np.eye(128, dtype=np.float32)
    k = np.arange(128)[:, None]
    q = np.arange(512)[None, :]
    c["dmask"] = np.stack([(q >= 128 * di + k) for di in range(4)]).astype(np.float32).astype(ml_dtypes.bfloat16)
    qq = np.arange(128)[None, :]
    c["lmask"] = np.concatenate([(qq <= k), (qq >= k)], axis=1).astype(np.float32).astype(ml_dtypes.bfloat16)
    c["triu"] = (k <= qq).astype(np.float32)
    c["negm"] = np.where(qq >= k, 0.0, -30000.0).astype(np.float32)
    sel = np.zeros((65, 64), np.float32)
    sel[64, :] = 1.0
    c["sel"] = sel
    c["ones64"] = np.ones((64, 64), np.float32)
    return c


CONST_SHAPES = None


def host_params(cfg, inp, b_heads=None):
    HD, HL, G, HS = cfg.HD, cfg.HL, cfg.G, cfg.HS
    w = np.asarray(inp["w_in"])
    L = w.shape[0]
    o = 0
    segs = {}
    for nm, n in (("dq", 256), ("dk", 256), ("dv", 256), ("lq", 384), ("lk", 384), ("lv", 384),
                  ("z", 384), ("xs", 384), ("B", 256), ("C", 256), ("dt", 6)):
        segs[nm] = (o, o + n)
        o += n
    cols = []
    for nm in ("dq", "dk", "lq", "lk", "lv", "xs", "B", "C", "dv", "z", "dt"):
        a, b = segs[nm]
        cols.append(np.arange(a, b))
    cols = np.concatenate(cols)
    p = {}
    p["w_in"] = np.ascontiguousarray(w[:, :, cols])
    p["g_premix"] = np.ascontiguousarray(np.asarray(inp["pre_mix_norm"]).reshape(L, 8, 128).transpose(0, 2, 1))
    p["g_preffn"] = np.ascontiguousarray(np.asarray(inp["pre_ffn_norm"]).reshape(L, 8, 128).transpose(0, 2, 1))
    p["g_postmix"] = np.ascontiguousarray(np.asarray(inp["post_mix_norm"]))
    p["g_postffn"] = np.ascontiguousarray(np.asarray(inp["post_ffn_norm"]))
    xo = segs["xs"][0]
    cw = np.asarray(inp["ssd_conv_w"])
    cb = np.asarray(inp["ssd_conv_b"])
    p["cw"] = np.ascontiguousarray(cw.reshape(L, 4, 7, 128).transpose(0, 3, 2, 1))
    p["cb"] = np.ascontiguousarray(cb.reshape(L, 7, 128).transpose(0, 2, 1))
    p["dt_bias"] = np.ascontiguousarray(np.asarray(inp["ssd_dt_bias"]))
    p["A_log"] = np.ascontiguousarray(np.asarray(inp["ssd_A_log"]))
    p["ssd_D"] = np.ascontiguousarray(np.asarray(inp["ssd_D"]))
    p["ssd_norm"] = np.ascontiguousarray(np.asarray(inp["ssd_norm"]))
    p["lam"] = np.ascontiguousarray(np.asarray(inp["diff_lambda"]).reshape(L, 128))
    p["hgain"] = np.ascontiguousarray(np.asarray(inp["diff_head_norm"]).reshape(L, 64, 1))
    p["w_out"] = np.ascontiguousarray(np.asarray(inp["w_out"]))
    p["ffn_up"] = np.ascontiguousarray(np.asarray(inp["ffn_up"]))
    fw = np.asarray(inp["ffn_conv_w"])
    fb = np.asarray(inp["ffn_conv_b"])
    p["fcw"] = np.ascontiguousarray(fw.reshape(L, 3, 44, 128).transpose(0, 3, 2, 1))
    p["fcb"] = np.ascontiguousarray(fb.reshape(L, 44, 128).transpose(0, 2, 1))
    p["ffn_down"] = np.ascontiguousarray(np.asarray(inp["ffn_down"]))
    return p


class K:
    pass


def build(cfg, debug=False):
    nc = bass.Bass("TRN2", target_bir_lowering=False)
    S, L, HD, HL, G, HS = cfg.S, cfg.L, cfg.HD, cfg.HL, cfg.G, cfg.HS
    NB = cfg.NB
    es = ExitStack()
    sc = Sched(nc, es)

    def din(name, shape, dt=F32):
        return nc.dram_tensor(name, list(shape), dt, kind="ExternalInput").ap()

    def dscr(name, shape, dt):
        return nc.dram_tensor(name, list(shape), dt, kind=("ExternalOutput" if debug else "Internal")).ap()

    x_in = din("x", [S, D])
    W = {}
    W["w_in"] = din("w_in", [L, D, cfg.NCOL])
    W["g_premix"] = din("g_premix", [L, 128, 8])
    W["g_preffn"] = din("g_preffn", [L, 128, 8])
    W["g_postmix"] = din("g_postmix", [L, D])
    W["g_postffn"] = din("g_postffn", [L, D])
    W["cw"] = din("cw", [L, 128, 7, 4])
    W["cb"] = din("cb", [L, 128, 7])
    W["dt_bias"] = din("dt_bias", [L, 6])
    W["A_log"] = din("A_log", [L, 6])
    W["ssd_D"] = din("ssd_D", [L, 6])
    W["ssd_norm"] = din("ssd_norm", [L, 384])
    W["lam"] = din("lam", [L, 128])
    W["hgain"] = din("hgain", [L, 64, 1])
    W["w_out"] = din("w_out", [L, D, D])
    W["ffn_up"] = din("ffn_up", [L, D, 2 * cfg.DFF])
    W["fcw"] = din("fcw", [L, 128, 44, 3])
    W["fcb"] = din("fcb", [L, 128, 44])
    W["ffn_down"] = din("ffn_down", [L, cfg.DFF, D])
    C = {}
    C["cosd"] = din("cosd", [64, S], BF16); C["sind"] = din("sind", [64, S], BF16)
    C["cosl"] = din("cosl", [128, S], BF16); C["sinl"] = din("sinl", [128, S], BF16)
    C["permd"] = din("permd", [64, 64], BF16); C["perml"] = din("perml", [128, 128], BF16)
    C["ident_bf"] = din("ident_bf", [128, 128], BF16); C["ident_f"] = din("ident_f", [128, 128])
    C["dmask"] = din("dmask", [4, 128, 512], BF16); C["lmask"] = din("lmask", [128, 256], BF16)
    C["triu"] = din("triu", [128, 128]); C["negm"] = din("negm", [128, 128])
    C["sel"] = din("sel", [65, 64]); C["ones64"] = din("ones64", [64, 64])
    out = nc.dram_tensor("out", [S, D], F32, kind="ExternalOutput").ap()

    qTd = dscr("qTd", [HD, 64, S], BF16); kTd = dscr("kTd", [HD, 64, S], BF16)
    vd = dscr("vd", [S, HD * 65], BF16)
    qTl = dscr("qTl", [HL // 2, 128, S], BF16); kTl = dscr("kTl", [HL // 2, 128, S], BF16)
    vTl = dscr("vTl", [HL // 2, 128, S], BF16)
    xbcT = dscr("xbcT", [cfg.NXBC, 128, S], BF16)
    zs = dscr("zs", [S, HS * 64], BF16)
    dts = dscr("dts", [S, HS], F32)
    mixT = dscr("mixT", [cfg.MIXC, 128, S], BF16)
    xa = dscr("xa", [S, D], F32)
    xb = dscr("xb", [S, D], F32)

    def sb(name, shape, dt=F32):
        return es.enter_context(nc.sbuf_tensor(name, list(shape), dt))

    ident_bf = sb("sb_ident_bf", [128, 128], BF16)
    ident_f = sb("sb_ident_f", [128, 128])
    cB = Buf("consts")
    cds = sc.dsem()
    sc.dma(cds, [(ident_bf[:], C["ident_bf"]), (ident_f[:], C["ident_f"])], writes=[cB])

    k = K()
    k.nc, k.sc, k.cfg, k.W, k.C, k.cB = nc, sc, cfg, W, C, cB
    k.ident_bf, k.ident_f = ident_bf, ident_f
    k.scr = dict(qTd=qTd, kTd=kTd, vd=vd, qTl=qTl, kTl=kTl, vTl=vTl, xbcT=xbcT, zs=zs, dts=dts,
                 mixT=mixT, xa=xa, xb=xb)

    stages = cfg.stages if hasattr(cfg, "stages") else "12345"
    for l in range(L):
        x_src = x_in if l == 0 else xb
        x_dst = out if l == L - 1 else xb
        if "1" in stages:
            phase1(k, l, x_src)
            sc.barrier()
        if "2" in stages:
            phase2(k, l)
            sc.barrier()
        if "3" in stages:
            phase3(k, l)
            sc.barrier()
        if "4" in stages:
            phase4(k, l)
            sc.barrier()
        if "5" in stages:
            phase5a(k, l, x_src)
            sc.barrier()
            phase5b(k, l, x_dst)
            sc.barrier()
    sc.barrier()
    es.close()
    return nc


def load_weights_bf16(k, ps, dst, src_rows, ncols, gain, stg, stgB, ds, kchunks, col_chunk=512):
    sc = k.sc
    i = 0
    for kc in range(kchunks):
        for c0 in range(0, ncols, col_chunk):
            cw = min(col_chunk, ncols - c0)
            sl = i % len(stg)
            sc.dma(ds[sl], [(stg[sl][:, :cw], src_rows[kc * 128:(kc + 1) * 128, c0:c0 + cw])], writes=[stgB[sl]])
            eng = "dve" if i % 2 == 0 else "pool"
            if gain is not None:
                if i % 2 == 0:
                    sc.op("dve", lambda h, sl=sl, cw=cw, kc=kc, c0=c0: h.tensor_scalar(
                        out=dst[:, kc, c0:c0 + cw], in0=stg[sl][:, :cw], scalar1=gain[:, kc:kc + 1], scalar2=None,
                        op0=ALU.mult), reads=[stgB[sl], k.gB], writes=[k.wB])
                else:
                    sc.op("act", lambda h, sl=sl, cw=cw, kc=kc, c0=c0: h.activation(
                        out=dst[:, kc, c0:c0 + cw], in_=stg[sl][:, :cw], func=AF.Copy, scale=gain[:, kc:kc + 1]),
                        reads=[stgB[sl], k.gB], writes=[k.wB])
            else:
                sc.op(eng, lambda h, sl=sl, cw=cw, kc=kc, c0=c0: h.tensor_copy(
                    out=dst[:, kc, c0:c0 + cw], in_=stg[sl][:, :cw]), reads=[stgB[sl]], writes=[k.wB])
            i += 1


def rms_rstd(k, ps, ss, rstd, n, width, ssB, rsB):
    sc = k.sc
    sc.op("act", lambda h: h.activation(out=rstd[:, :n], in_=ss[:, :n], func=AF.Ln, scale=1.0 / width, bias=EPS),
          reads=[ssB], writes=[rsB])
    sc.op("act", lambda h: h.activation(out=rstd[:, :n], in_=rstd[:, :n], func=AF.Exp, scale=-0.5), reads=[rsB], writes=[rsB])


def phase1(k, l, x_src):
    nc, sc, cfg, W, C = k.nc, k.sc, k.cfg, k.W, k.C
    S, HD, HL, G, HS, NB = cfg.S, cfg.HD, cfg.HL, cfg.G, cfg.HS, cfg.NB
    NX = cfg.NXBC
    with ExitStack() as ps:
        def sb(name, shape, dt=F32):
            return ps.enter_context(nc.sbuf_tensor("L%d_%s" % (l, name), list(shape), dt))

        def pt(name, shape, dt=F32):
            return ps.enter_context(nc.psum_tensor("L%d_%s" % (l, name), list(shape), dt))

        wbf = sb("p1_w", [128, 8, cfg.NCOL], BF16)
        gain = sb("p1_g", [128, 8])
        stg = [sb("p1_stg%d" % i, [128, 512]) for i in range(3)]
        stgB = [Buf() for _ in range(3)]
        ds = [sc.dsem() for _ in range(3)]
        k.gB, k.wB = Buf("gain"), Buf("w")
        dsm = sc.dsem()
        permd = sb("p1_permd", [64, 64], BF16)
        perml = sb("p1_perml", [128, 128], BF16)
        cw = sb("p1_cw", [128, 7, 4])
        cb = sb("p1_cb", [128, 7])
        sc.dma(dsm, [(gain[:], W["g_premix"][l]), (permd[:], C["permd"]), (perml[:], C["perml"]),
                     (cw[:], W["cw"][l]), (cb[:], W["cb"][l])], writes=[k.gB])
        CUT = int(os.environ.get("KCUT", "99"))
        if CUT >= 1:
            load_weights_bf16(k, ps, wbf, W["w_in"][l], cfg.NCOL, gain, stg, stgB, ds, 8)

        xt = sb("p1_x", [128, 4, D])
        xtB = [Buf() for _ in range(4)]
        xds = [sc.dsem() for _ in range(4)]
        junk = sb("p1_junk", [128, D], BF16)
        junkB = Buf()
        ss = sb("p1_ss", [128, 4]); ssB = Buf()
        rstd = sb("p1_rstd", [128, 4]); rsB = Buf()
        xs = sb("p1_xs", [128, 4, D], BF16); xsB = Buf()
        hT = sb("p1_hT", [128, 8, TB], BF16); hTB = Buf()
        psT = [pt("p1_psT%d" % i, [128, 2, TB], BF16) for i in range(2)]
        psTB = [PBuf() for _ in range(2)]
        psA = [pt("p1_psA%d" % i, [128, TB]) for i in range(3)]
        psAB = [PBuf() for _ in range(3)]
        psR = [pt("p1_psR%d" % i, [128, TB]) for i in range(2)]
        psRB = [PBuf() for _ in range(2)]
        psB = [pt("p1_psB%d" % i, [128, 512]) for i in range(1)]
        psBB = [PBuf() for _ in range(1)]
        tabs = {}
        for nm, rows in (("cosd", 64), ("sind", 64), ("cosl", 128), ("sinl", 128)):
            tabs[nm] = [sb("p1_%s%d" % (nm, i), [rows, TB], BF16) for i in range(2)]
        tabB = [Buf() for _ in range(2)]
        tds = [sc.dsem() for _ in range(2)]
        xbf = [sb("p1_xbf%d" % i, [128, TB], BF16) for i in range(4)]
        xbfB = [Buf() for _ in range(4)]
        t1 = [sb("p1_t1%d" % i, [128, TB]) for i in range(2)]
        t1B = [Buf() for _ in range(2)]
        t2 = [sb("p1_t2%d" % i, [128, TB]) for i in range(2)]
        t2B = [Buf() for _ in range(2)]
        qk_st = [sb("p1_qk%d" % i, [64, 2 * HD, TB], BF16) for i in range(2)]
        l_st = [sb("p1_l%d" % i, [128, 3 * (HL // 2), TB], BF16) for i in range(2)]
        xbc_st = [sb("p1_xbc%d" % i, [128, NX, TB], BF16) for i in range(2)]
        v_st = [sb("p1_v%d" % i, [128, 4, HD, 65], BF16) for i in range(2)]
        z_st = [sb("p1_z%d" % i, [128, 4, HS * 64], BF16) for i in range(2)]
        dt_st = [sb("p1_dt%d" % i, [128, 4, HS]) for i in range(2)]
        from collections import defaultdict
        stD = [defaultdict(Buf) for _ in range(2)]
        sds = [sc.dsem() for _ in range(2)]
        u = sb("p1_u", [128, NX, 3 + TB]); uB = [Buf() for _ in range(NX)]
        yc = [sb("p1_yc%d" % i, [128, TB]) for i in range(3)]
        ycB = [Buf() for _ in range(3)]
        sc.op("pool", lambda h: h.memset(u[:], 0.0), writes=uB)
        for i in range(2):
            sc.op("pool", lambda h, i=i: h.memset(v_st[i][:], 1.0), writes=[stD[i][("v", n)] for n in range(4)])

        ci = 0
        for t in range(NB):
            if CUT < 2:
                break
            pb = t % 2
            tok0 = t * TB

            def prefetch(tt):
                tk = tt * TB
                for n in range(4):
                    sc.dma(xds[n], [(xt[:, n, :], x_src[tk + n * 128:tk + (n + 1) * 128, :])], writes=[xtB[n]])
                sc.dma(tds[tt % 2], [(tabs[nm][tt % 2][:], C[nm][:, tk:tk + TB]) for nm in ("cosd", "sind", "cosl", "sinl")],
                       writes=[tabB[tt % 2]])

            if t == 0:
                prefetch(0)
            for n in range(4):
                sc.op("act", lambda h, n=n: h.activation(out=junk[:], in_=xt[:, n, :], func=AF.Square,
                                                         accum_out=ss[:, n:n + 1]),
                      reads=[xtB[n]], writes=[junkB, ssB])
            if CUT < 3:
                continue
            rms_rstd(k, ps, ss, rstd, 4, D, ssB, rsB)
            for n in range(4):
                if n % 2 == 0:
                    sc.op("dve", lambda h, n=n: h.tensor_scalar(
                        out=xs[:, n, :], in0=xt[:, n, :], scalar1=rstd[:, n:n + 1], scalar2=None, op0=ALU.mult),
                        reads=[xtB[n], rsB], writes=[xsB])
                else:
                    sc.op("act", lambda h, n=n: h.activation(out=xs[:, n, :], in_=xt[:, n, :], func=AF.Copy, scale=rstd[:, n:n + 1]),
                          reads=[xtB[n], rsB], writes=[xsB])
            if CUT < 4:
                continue
            for cp in range(4):
                pi = cp % 2
                for c2 in range(2):
                    c = cp * 2 + c2
                    for n in range(4):
                        last = (c2 == 1 and n == 3)
                        sc.op("pe", lambda h, c=c, c2=c2, n=n, pi=pi: h.transpose(
                            out=psT[pi][:, c2, n * 128:(n + 1) * 128], in_=xs[:, n, c * 128:(c + 1) * 128],
                            identity=k.ident_bf[:]), reads=[xsB, k.cB], writes=[psTB[pi]], inc=last)
                if cp % 2 == 0:
                    sc.op("act", lambda h, cp=cp, pi=pi: h.copy(out=hT[:, 2 * cp:2 * cp + 2, :], in_=psT[pi][:]),
                          reads=[psTB[pi]], writes=[hTB])
                else:
                    sc.op("dve", lambda h, cp=cp, pi=pi: h.tensor_copy(out=hT[:, 2 * cp:2 * cp + 2, :], in_=psT[pi][:]),
                          reads=[psTB[pi]], writes=[hTB])
            if CUT < 5:
                continue
            if t + 1 < NB:
                prefetch(t + 1)
            if t > 0:
                sc.op("pool", lambda h: h.tensor_copy(out=u[:, :, 0:3], in_=u[:, :, TB:TB + 3]), reads=uB, writes=uB)
            fm = cfg.fm

            def s_pe(i):
                nm, idx, off, M = fm[i]
                a = i % 3
                for kc in range(8):
                    sc.op("pe", lambda h: h.matmul(psA[a][0:M, :], lhsT=wbf[:, kc, off:off + M], rhs=hT[:, kc, :],
                                                   start=(kc == 0), stop=(kc == 7)), reads=[k.wB, hTB], writes=[psAB[a]], inc=(kc == 7))

            def s_copy(i):
                nm, idx, off, M = fm[i]
                a = i % 3
                if nm in ("dq", "dk", "lq", "lk"):
                    x4 = i % 4
                    sc.op("act", lambda h: h.copy(out=xbf[x4][0:M, :], in_=psA[a][0:M, :]), reads=[psAB[a]], writes=[xbfB[x4]])
                elif nm == "lv":
                    sc.op("act", lambda h: h.copy(out=l_st[pb][:, 2 * (HL // 2) + idx, :], in_=psA[a][:]),
                          reads=[psAB[a]], writes=[stD[pb][("lv", idx)]])
                else:
                    xi = {"xs": 0, "B": HS // 2, "C": HS // 2 + G}[nm] + idx
                    sc.op("act", lambda h: h.copy(out=u[:, xi, 3:3 + TB], in_=psA[a][:]), reads=[psAB[a]], writes=[uB[xi]])

            def s_mid(i):
                nm, idx, off, M = fm[i]
                if nm in ("dq", "dk", "lq", "lk"):
                    perm = permd if M == 64 else perml
                    x4 = i % 4
                    r2 = i % 2
                    sc.op("pe", lambda h: h.matmul(psR[r2][0:M, :], lhsT=perm[0:M, 0:M], rhs=xbf[x4][0:M, :], start=True, stop=True),
                          reads=[xbfB[x4], k.gB], writes=[psRB[r2]])
                elif nm != "lv":
                    xi = {"xs": 0, "B": HS // 2, "C": HS // 2 + G}[nm] + idx
                    gi = {"xs": 0, "B": 3, "C": 5}[nm] + idx
                    y3 = i % 3
                    sc.op("dve", lambda h: h.tensor_scalar(out=yc[y3][:], in0=u[:, xi, 3:3 + TB], scalar1=cw[:, gi, 3:4], scalar2=cb[:, gi:gi + 1],
                                                           op0=ALU.mult, op1=ALU.add), reads=[uB[xi], k.gB], writes=[ycB[y3]])

            def s_ew(i):
                nm, idx, off, M = fm[i]
                if nm in ("dq", "dk", "lq", "lk"):
                    cosn, sinn = ("cosd", "sind") if M == 64 else ("cosl", "sinl")
                    x4 = i % 4
                    r2 = i % 2
                    sc.op("dve", lambda h: h.tensor_tensor(out=t1[r2][0:M, :], in0=xbf[x4][0:M, :], in1=tabs[cosn][pb][0:M, :], op=ALU.mult),
                          reads=[xbfB[x4], tabB[pb]], writes=[t1B[r2]])
                    sc.op("dve", lambda h: h.tensor_tensor(out=t2[r2][0:M, :], in0=psR[r2][0:M, :], in1=tabs[sinn][pb][0:M, :], op=ALU.mult),
                          reads=[psRB[r2], tabB[pb]], writes=[t2B[r2]])
                elif nm != "lv":
                    xi = {"xs": 0, "B": HS // 2, "C": HS // 2 + G}[nm] + idx
                    gi = {"xs": 0, "B": 3, "C": 5}[nm] + idx
                    y3 = i % 3
                    for kk in range(3):
                        sc.op("dve", lambda h: h.scalar_tensor_tensor(out=yc[y3][:], in0=u[:, xi, kk:kk + TB], scalar=cw[:, gi, kk:kk + 1],
                                                                      in1=yc[y3][:], op0=ALU.mult, op1=ALU.add),
                              reads=[uB[xi], ycB[y3], k.gB], writes=[ycB[y3]])

            def s_fin(i):
                nm, idx, off, M = fm[i]
                if nm in ("dq", "dk", "lq", "lk"):
                    r2 = i % 2
                    if nm in ("dq", "dk"):
                        dst = qk_st[pb][:, (0 if nm == "dq" else HD) + idx, :]
                    else:
                        dst = l_st[pb][:, (0 if nm == "lq" else HL // 2) + idx, :]
                    sc.op("pool", lambda h: h.tensor_tensor(out=dst, in0=t1[r2][0:M, :], in1=t2[r2][0:M, :], op=ALU.add),
                          reads=[t1B[r2], t2B[r2]], writes=[stD[pb][(nm, idx)]])
                elif nm != "lv":
                    xi = {"xs": 0, "B": HS // 2, "C": HS // 2 + G}[nm] + idx
                    y3 = i % 3
                    sc.op("act", lambda h: h.activation(out=xbc_st[pb][:, xi, :], in_=yc[y3][:], func=AF.Silu),
                          reads=[ycB[y3]], writes=[stD[pb][("xbc", xi)]])

            skewed(len(fm), [s_pe, s_copy, s_mid, s_ew, s_fin])
            if CUT < 6:
                continue
            tmb = [psB[0], psR[0], psR[1]]
            tmB = [psBB[0], psRB[0], psRB[1]]
            for n in range(4):
                a = (2 * n) % 3
                for kc in range(8):
                    sc.op("pe", lambda h, kc=kc, n=n, a=a: h.matmul(
                        tmb[a][:, 0:HD * 64], lhsT=hT[:, kc, n * 128:(n + 1) * 128],
                        rhs=wbf[:, kc, cfg.off_dv:cfg.off_dv + HD * 64], start=(kc == 0), stop=(kc == 7)),
                        reads=[k.wB, hTB], writes=[tmB[a]], inc=(kc == 7))
                sc.op("act", lambda h, n=n, a=a: h.copy(
                    out=v_st[pb][:, n, :, 0:64], in_=tmb[a][:, 0:HD * 64].rearrange("p (h e) -> p h e", e=64)),
                    reads=[tmB[a]], writes=[stD[pb][("v", n)]])
                nz = HS * 64 + HS
                a = (2 * n + 1) % 3
                for kc in range(8):
                    sc.op("pe", lambda h, kc=kc, n=n, a=a: h.matmul(
                        tmb[a][:, 0:nz], lhsT=hT[:, kc, n * 128:(n + 1) * 128],
                        rhs=wbf[:, kc, cfg.off_z:cfg.off_z + nz], start=(kc == 0), stop=(kc == 7)),
                        reads=[k.wB, hTB], writes=[tmB[a]], inc=(kc == 7))
                sc.op("dve", lambda h, n=n, a=a: h.tensor_copy(out=z_st[pb][:, n, :], in_=tmb[a][:, 0:HS * 64]),
                      reads=[tmB[a]], writes=[stD[pb][("z", n)]])
                sc.op("dve", lambda h, n=n, a=a: h.tensor_copy(out=dt_st[pb][:, n, :], in_=tmb[a][:, HS * 64:nz]),
                      reads=[tmB[a]], writes=[stD[pb][("dt", n)]])
            if CUT < 7:
                continue
            scr = k.scr
            pairs = [
                (scr["qTd"][:, :, tok0:tok0 + TB].rearrange("h p s -> p h s"), qk_st[pb][:, 0:HD, :]),
                (scr["kTd"][:, :, tok0:tok0 + TB].rearrange("h p s -> p h s"), qk_st[pb][:, HD:2 * HD, :]),
                (scr["qTl"][:, :, tok0:tok0 + TB].rearrange("h p s -> p h s"), l_st[pb][:, 0:HL // 2, :]),
                (scr["kTl"][:, :, tok0:tok0 + TB].rearrange("h p s -> p h s"), l_st[pb][:, HL // 2:HL, :]),
                (scr["vTl"][:, :, tok0:tok0 + TB].rearrange("h p s -> p h s"), l_st[pb][:, HL:3 * (HL // 2), :]),
                (scr["xbcT"][:, :, tok0:tok0 + TB].rearrange("h p s -> p h s"), xbc_st[pb][:]),
                (scr["vd"][tok0:tok0 + TB, :].rearrange("(n p) f -> p n f", p=128),
                 v_st[pb][:].rearrange("p n h e -> p n (h e)")),
                (scr["zs"][tok0:tok0 + TB, :].rearrange("(n p) f -> p n f", p=128), z_st[pb][:]),
                (scr["dts"][tok0:tok0 + TB, :].rearrange("(n p) f -> p n f", p=128), dt_st[pb][:]),
            ]
            sc.dma(sds[pb], pairs, reads=list(stD[pb].values()))


def skewed(n, stages):
    ns = len(stages)
    for kk in range(n + ns - 1):
        for si, fn in enumerate(stages):
            i = kk - si
            if 0 <= i < n:
                fn(i)


def _mk(k, l, ps):
    nc = k.nc

    def sb(name, shape, dt=F32):
        return ps.enter_context(nc.sbuf_tensor("L%d_%s" % (l, name), list(shape), dt))

    def pt(name, shape, dt=F32):
        return ps.enter_context(nc.psum_tensor("L%d_%s" % (l, name), list(shape), dt))

    return sb, pt


def phase2(k, l):
    nc, sc, cfg, W, C = k.nc, k.sc, k.cfg, k.W, k.C
    S, HD, NB = cfg.S, cfg.HD, cfg.NB
    lam_init = 0.8 - 0.6 * math.exp(-0.3 * l)
    scr = k.scr
    with ExitStack() as ps:
        sb, pt = _mk(k, l, ps)
        kT = sb("p2_kT", [128, HD // 2, S], BF16)
        vfull = sb("p2_v", [128, (S // 128) * HD * 65 + 128], BF16)
        v = vfull[:, 0:(S // 128) * HD * 65].rearrange("p (n f) -> p n f", f=HD * 65)
        dmask = sb("p2_dmask", [128, 4, 512], BF16)
        sel = sb("p2_sel", [65, 64]); ones64 = sb("p2_ones", [64, 64])
        hg = sb("p2_hg", [64, 1]); lam = sb("p2_lam", [64, 128])
        lt = sb("p2_lt", [64, 64]); lsum = sb("p2_ls", [64, 2]); neg_lam = sb("p2_nl", [64, 1]); gsc = sb("p2_gsc", [64, 1])
        cB = Buf(); d0 = sc.dsem()
        sc.dma(d0, [(kT[:, a2, :], scr["kTd"][2 * a2:2 * a2 + 2].rearrange("b p s -> (b p) s")) for a2 in range(HD // 2)] +
               [(v, scr["vd"].rearrange("(n p) f -> p n f", p=128)),
                (dmask[:], C["dmask"].rearrange("d p q -> p d q")), (sel[:], C["sel"]), (ones64[:], C["ones64"]),
                (hg[:], W["hgain"][l]), (lam[:], W["lam"][l:l + 1, :].broadcast_to([64, 128]))], writes=[cB])
        pB = Buf()
        sc.op("pool", lambda h: h.memset(vfull[:, (S // 128) * HD * 65:], 0.0), writes=[cB])
        sc.op("dve", lambda h: h.tensor_tensor(out=lt[:, 0:32], in0=lam[:, 0:32], in1=lam[:, 32:64], op=ALU.mult), reads=[cB], writes=[pB])
        sc.op("dve", lambda h: h.tensor_tensor(out=lt[:, 32:64], in0=lam[:, 64:96], in1=lam[:, 96:128], op=ALU.mult), reads=[cB], writes=[pB])
        sc.op("dve", lambda h: h.tensor_reduce(out=lsum[:], in_=lt[:].rearrange("p (a b) -> p a b", b=32), axis=AX.X, op=ALU.add),
              reads=[pB], writes=[pB])
        sc.op("act", lambda h: h.activation(out=lsum[:], in_=lsum[:], func=AF.Exp), reads=[pB], writes=[pB])
        sc.op("dve", lambda h: h.tensor_tensor(out=neg_lam[:], in0=lsum[:, 1:2], in1=lsum[:, 0:1], op=ALU.subtract), reads=[pB], writes=[pB])
        sc.op("dve", lambda h: h.tensor_scalar(out=neg_lam[:], in0=neg_lam[:], scalar1=-lam_init, scalar2=None, op0=ALU.add), reads=[pB], writes=[pB])
        sc.op("dve", lambda h: h.tensor_scalar(out=gsc[:], in0=hg[:], scalar1=(1.0 - lam_init), scalar2=None, op0=ALU.mult), reads=[cB, pB], writes=[pB])

        qT = [sb("p2_q%d" % i, [128, HD * 2, TB], BF16) for i in range(2)]
        qB = [Buf() for _ in range(2)]; qds = [sc.dsem() for _ in range(2)]
        for i in range(2):
            sc.op("pool", lambda h, i=i: h.memset(qT[i][:], 0.0), writes=[qB[i]])
        psS = [pt("p2_psS%d" % i, [128, 2, 512]) for i in range(2)]; psSB = [PBuf() for _ in range(2)]
        psO = [pt("p2_psO%d" % m, [128, 512]) for m in range(2)]
        psOB = [PBuf() for m in range(2)]
        psE = [pt("p2_psE%d" % i, [128, 512]) for i in range(2)]; psEB = [PBuf() for _ in range(2)]
        pT = [sb("p2_pT%d" % i, [128, 2, 512], BF16) for i in range(3)]; pTB = [Buf() for _ in range(3)]
        X = [sb("p2_X%d" % m, [65, 512]) for m in range(2)]; XB = [Buf() for _ in range(2)]
        r = [sb("p2_r%d" % m, [64, 512]) for m in range(2)]; rB = [Buf() for _ in range(2)]
        o = sb("p2_o", [64, 512]); oB = Buf()
        sq = sb("p2_sq", [64, 512]); sqB = Buf()
        rs = sb("p2_rs", [64, 512]); rsB = Buf()
        ost = [sb("p2_ost%d" % i, [64, HD, 512], BF16) for i in range(2)]
        ostB = [Buf() for _ in range(2)]; ods = [sc.dsem() for _ in range(2)]
        scale = 32 ** -0.5
        cnt = 0
        pending = []

        def flush(n=None):
            c = len(pending) if n is None else min(n, len(pending))
            for _ in range(c):
                pending.pop(0)()

        for t in range(NB):
            tok0 = t * TB
            qb = t % 2
            sc.dma(qds[qb], [(qT[qb][32 * ((h2 % 2) * 2 + m2):32 * ((h2 % 2) * 2 + m2) + 32, h2 * 2 + m2, :],
                              scr["qTd"][h2, 32 * m2:32 * m2 + 32, tok0:tok0 + TB]) for h2 in range(HD) for m2 in range(2)],
                   writes=[qB[qb]])
            for hh in range(HD):
                nk = 4 * t + 4

                def c0_of(i):
                    return 128 * max(0, i - 4 * t)

                def qk(i):
                    sl = (cnt + i) % 2
                    c0 = c0_of(i)
                    for m in range(2):
                        sc.op("pe", lambda h: h.matmul(psS[sl][:, m, c0:], lhsT=kT[:, hh // 2, i * 128:(i + 1) * 128],
                                                       rhs=qT[qb][:, hh * 2 + m, c0:], start=True, stop=True),
                              reads=[cB, qB[qb]], writes=[psSB[sl]], inc=(m == 1))

                qk(0)
                for i in range(nk):
                    sl = (cnt + i) % 2
                    p3 = (cnt + i) % 3
                    if i + 1 < nk:
                        qk(i + 1)
                    c0 = c0_of(i)
                    sc.op("act", lambda h: h.activation(out=pT[p3][:, :, c0:], in_=psS[sl][:, :, c0:], func=AF.Exp, scale=scale),
                          reads=[psSB[sl]], writes=[pTB[p3]])
                    if i >= 4 * t:
                        di = i - 4 * t
                        sc.op("dve", lambda h: h.tensor_tensor(
                            out=pT[p3][:, :, c0:], in0=pT[p3][:, :, c0:], in1=dmask[:, di:di + 1, c0:].broadcast_to([128, 2, 512 - c0]), op=ALU.mult),
                            reads=[cB, pTB[p3]], writes=[pTB[p3]])
                    for m in range(2):
                        sc.op("pe", lambda h: h.matmul(psO[m][:, c0:], lhsT=vfull[:, (i * HD + hh) * 65:(i * HD + hh) * 65 + 128], rhs=pT[p3][:, m, c0:],
                                                       start=(i == 0), stop=(i == nk - 1)),
                              reads=[cB, pTB[p3]], writes=[psOB[m]], inc=(i == nk - 1))
                    if i == 0:
                        npend0 = len(pending)
                    if i >= 1:
                        if i < nk - 1:
                            target_left = npend0 - (npend0 * i) // (nk - 1)
                            flush(max(0, len(pending) - target_left))
                        else:
                            flush()
                cnt += nk
                flush()
                sc.op("dve", lambda h: h.tensor_copy(out=X[0][:], in_=psO[0][0:65, :]), reads=[psOB[0]], writes=[XB[0]])
                sc.op("dve", lambda h: h.tensor_copy(out=X[1][:], in_=psO[1][0:65, :]), reads=[psOB[1]], writes=[XB[1]])

                def E(eng, fn, reads, writes):
                    pending.append(lambda: sc.op(eng, fn, reads=reads, writes=writes))

                for m in range(2):
                    E("pe", lambda h, m=m: h.matmul(psE[m][0:64, :], lhsT=sel[:], rhs=X[m][:], start=True, stop=True),
                      [cB, XB[m]], [psEB[m]])
                    E("dve", lambda h, m=m: h.reciprocal(out=r[m][:], in_=psE[m][0:64, :]), [psEB[m]], [rB[m]])
                    E("dve" if m == 0 else "pool", lambda h, m=m: h.tensor_tensor(
                        out=r[m][:], in0=X[m][0:64, :], in1=r[m][:], op=ALU.mult), [XB[m], rB[m]], [rB[m]])
                E("dve", lambda h: h.scalar_tensor_tensor(out=o[:], in0=r[1][:], scalar=neg_lam[:, 0:1], in1=r[0][:],
                                                          op0=ALU.mult, op1=ALU.add), [rB[0], rB[1], pB], [oB])
                E("pool", lambda h: h.tensor_tensor(out=sq[:], in0=o[:], in1=o[:], op=ALU.mult), [oB], [sqB])
                E("pe", lambda h: h.matmul(psE[0][0:64, :], lhsT=ones64[:], rhs=sq[:], start=True, stop=True), [cB, sqB], [psEB[0]])
                E("act", lambda h: h.activation(out=rs[:], in_=psE[0][0:64, :], func=AF.Ln, scale=1.0 / 64, bias=EPS), [psEB[0]], [rsB])
                E("act", lambda h: h.activation(out=rs[:], in_=rs[:], func=AF.Exp, scale=-0.5), [rsB], [rsB])
                E("dve", lambda h, qb=qb, hh=hh: h.scalar_tensor_tensor(out=ost[qb][:, hh, :], in0=o[:], scalar=gsc[:, 0:1], in1=rs[:],
                                                                       op0=ALU.mult, op1=ALU.mult), [oB, rsB, pB], [ostB[qb]])
            pending.append(lambda qb=qb, tok0=tok0: sc.dma(
                ods[qb], [(scr["mixT"][hh2 // 2, (hh2 % 2) * 64:(hh2 % 2) * 64 + 64, tok0:tok0 + TB], ost[qb][:, hh2, :])
                          for hh2 in range(HD)], reads=[ostB[qb]]))
        flush()


def phase3(k, l):
    nc, sc, cfg, W, C = k.nc, k.sc, k.cfg, k.W, k.C
    S, HD, HL = cfg.S, cfg.HD, cfg.HL
    scr = k.scr
    SBK = 2048
    NSB = S // SBK
    NC3 = HL // 2
    PATS = (1, 4, 16)
    scale = 64 ** -0.5
    with ExitStack() as ps:
        sb, pt = _mk(k, l, ps)
        qP = [sb("p3_q%d" % i, [128, NC3, SBK], BF16) for i in range(2)]; qB = Buf()
        for i in range(2):
            sc.op("pool", lambda h, i=i: h.memset(qP[i][:], 0.0), writes=[qB])
        kL = [sb("p3_k%d" % i, [128, NC3, SBK], BF16) for i in range(2)]; kB = [Buf() for _ in range(2)]
        vT = sb("p3_vT", [128, NC3, SBK], BF16); vTB = Buf()
        lds = [sc.dsem() for _ in range(3)]
        vt = [[sb("p3_vt%d_%d" % (pi, i), [128, 16, HL * 65], BF16) for i in range(2)] for pi in range(3)]
        vtB = [[Buf() for i in range(2)] for pi in range(3)]
        lmask = sb("p3_lmask", [128, 256], BF16); sel = sb("p3_sel", [65, 64])
        cB = Buf(); d0 = sc.dsem()
        sc.dma(d0, [(lmask[:], C["lmask"]), (sel[:], C["sel"])], writes=[cB])
        for pi in range(3):
            for i in range(2):
                sc.op("pool", lambda h: h.memset(vt[pi][i][:], 1.0), writes=[vtB[pi][i]])
        acc = [sb("p3_acc%d" % i, [65, SBK]) for i in range(2)]; accB = [Buf() for _ in range(2)]
        pT = [sb("p3_pT%d" % i, [128, 4, 256], BF16) for i in range(3)]; pTB = [Buf() for _ in range(3)]
        rr = sb("p3_rr", [64, 512]); rrB = Buf()
        ost = [sb("p3_ost%d" % i, [64, SBK], BF16) for i in range(2)]; ostB = [Buf() for _ in range(2)]
        ods = [sc.dsem() for _ in range(2)]
        psV = pt("p3_psV", [128, NC3, 128], BF16); psVB = PBuf()
        psS = [pt("p3_psS%d" % i, [128, 4, 256]) for i in range(2)]; psSB = [PBuf() for _ in range(2)]
        psO = [pt("p3_psO%d" % i, [128, 4, 128]) for i in range(2)]; psOB = [PBuf() for _ in range(2)]
        psE = pt("p3_psE", [128, 512]); psEB = PBuf()
        gi = 0
        hi = 0
        for u in range(NSB):
            ub = u % 2
            t0 = u * SBK
            sc.dma(lds[0], [(qP[par][64 * par:64 * par + 64, :, :],
                             scr["qTl"][:, 64 * par:64 * par + 64, t0:t0 + SBK].rearrange("c p s -> p c s")) for par in range(2)],
                   writes=[qB])
            sc.dma(lds[1], [(kL[ub][:], scr["kTl"][:, :, t0:t0 + SBK].rearrange("c p s -> p c s"))], writes=[kB[ub]])
            sc.dma(lds[2], [(vT[:], scr["vTl"][:, :, t0:t0 + SBK].rearrange("c p s -> p c s"))], writes=[vTB])
            vi = 0
            for pi, d in enumerate(PATS):
                for ti in range(16):
                    r_, nbl = ti % d, ti // d
                    off = 128 * d * nbl + r_
                    for c in range(NC3):
                        sc.op("pe", lambda h: h.transpose(out=psV[:, c, :], in_=vT[:, c, off:off + 127 * d + 1:d],
                                                          identity=k.ident_bf[:]),
                              reads=[vTB, k.cB], writes=[psVB], inc=(c == NC3 - 1))
                    dst = vt[pi][ub][:, ti, :].rearrange("p (h e) -> p h e", e=65)[:, :, 0:64]
                    src = psV[:].rearrange("p c (h e) -> p (c h) e", e=64)
                    if vi % 2 == 0:
                        sc.op("act", lambda h: h.copy(out=dst, in_=src), reads=[psVB], writes=[vtB[pi][ub]])
                    else:
                        sc.op("dve", lambda h: h.tensor_copy(out=dst, in_=src), reads=[psVB], writes=[vtB[pi][ub]])
                    vi += 1
            for hh in range(HL):
                c = hh // 2
                p0 = 64 * (hh % 2)
                ab = hi % 2
                hi += 1
                groups = [(pi, d, tg) for pi, d in enumerate(PATS) for tg in range(4)]
                gbase = gi
                gi += len(groups)

                def tiles_of(g):
                    pi, d, tg = groups[g]
                    tiles = []
                    for q4 in range(4):
                        ti = tg * 4 + q4
                        r_, nbl = ti % d, ti // d
                        off = 128 * d * nbl + r_
                        if nbl > 0:
                            prev = (ub, off - 128 * d, ti - d)
                        elif u > 0:
                            nbp = 16 // d - 1
                            prev = (1 - ub, 128 * d * nbp + r_, r_ + d * nbp)
                        else:
                            prev = None
                        tiles.append((ti, off, prev))
                    return tiles

                def g_qk(g):
                    pi, d, tg = groups[g]
                    sl = (gbase + g) % 2
                    for q4, (ti, off, prev) in enumerate(tiles_of(g)):
                        Q = qP[hh % 2][:, c, off:off + 127 * d + 1:d]
                        Kc = kL[ub][:, c, off:off + 127 * d + 1:d]
                        if prev is not None:
                            Kp = kL[prev[0]][:, c, prev[1]:prev[1] + 127 * d + 1:d]
                            rd = [qB, kB[ub], kB[prev[0]]]
                        else:
                            Kp = Kc
                            rd = [qB, kB[ub]]
                        sc.op("pe", lambda h: h.matmul(psS[sl][:, q4, 0:128], lhsT=Kp, rhs=Q, start=True, stop=True),
                              reads=rd, writes=[psSB[sl]], inc=False)
                        sc.op("pe", lambda h: h.matmul(psS[sl][:, q4, 128:256], lhsT=Kc, rhs=Q, start=True, stop=True),
                              reads=rd, writes=[psSB[sl]], inc=(q4 == 3))

                def g_exp(g):
                    sl = (gbase + g) % 2
                    p3 = (gbase + g) % 3
                    sc.op("act", lambda h: h.activation(out=pT[p3][:], in_=psS[sl][:], func=AF.Exp, scale=scale),
                          reads=[psSB[sl]], writes=[pTB[p3]])
                    sc.op("dve", lambda h: h.tensor_tensor(
                        out=pT[p3][:], in0=pT[p3][:], in1=lmask[:, None, :].broadcast_to([128, 4, 256]), op=ALU.mult),
                        reads=[pTB[p3], cB], writes=[pTB[p3]])

                def g_pv(g):
                    pi, d, tg = groups[g]
                    sl = (gbase + g) % 2
                    p3 = (gbase + g) % 3
                    for q4, (ti, off, prev) in enumerate(tiles_of(g)):
                        if prev is not None:
                            sc.op("pe", lambda h: h.matmul(psO[sl][0:65, q4, :], lhsT=vt[pi][prev[0]][:, prev[2], hh * 65:(hh + 1) * 65],
                                                           rhs=pT[p3][:, q4, 0:128], start=True, stop=False),
                                  reads=[pTB[p3], vtB[pi][prev[0]]], writes=[psOB[sl]], inc=False)
                        sc.op("pe", lambda h: h.matmul(psO[sl][0:65, q4, :], lhsT=vt[pi][ub][:, ti, hh * 65:(hh + 1) * 65],
                                                       rhs=pT[p3][:, q4, 128:256], start=(prev is None), stop=True),
                              reads=[pTB[p3], vtB[pi][ub]], writes=[psOB[sl]], inc=(q4 == 3))

                def g_acc(g):
                    pi, d, tg = groups[g]
                    sl = (gbase + g) % 2
                    if d == 1:
                        dstv = acc[ab][:, tg * 512:(tg + 1) * 512].rearrange("p (a j) -> p a j", j=128)
                    elif d == 4:
                        dstv = acc[ab][:, tg * 512:(tg + 1) * 512].rearrange("p (j r) -> p r j", r=4)
                    else:
                        dstv = acc[ab][:].rearrange("p (j r) -> p r j", r=16)[:, tg * 4:tg * 4 + 4, :]
                    if pi == 0:
                        sc.op("dve", lambda h: h.tensor_copy(out=dstv, in_=psO[sl][0:65, :, :]), reads=[psOB[sl]], writes=[accB[ab]])
                    else:
                        sc.op("dve", lambda h: h.tensor_tensor(out=dstv, in0=dstv, in1=psO[sl][0:65, :, :], op=ALU.add),
                              reads=[psOB[sl], accB[ab]], writes=[accB[ab]])

                skewed(len(groups), [g_qk, g_exp, g_pv, g_acc])
                for sbk in range(4):
                    cs_ = slice(sbk * 512, (sbk + 1) * 512)
                    sc.op("pe", lambda h: h.matmul(psE[0:64, :], lhsT=sel[:], rhs=acc[ab][:, cs_], start=True, stop=True),
                          reads=[cB, accB[ab]], writes=[psEB])
                    sc.op("dve", lambda h: h.reciprocal(out=rr[:], in_=psE[0:64, :]), reads=[psEB], writes=[rrB])
                    sc.op("pool", lambda h: h.tensor_tensor(out=ost[ab][:, cs_], in0=acc[ab][0:64, cs_], in1=rr[:], op=ALU.mult),
                          reads=[rrB, accB[ab]], writes=[ostB[ab]])
                row = HD * 64 + hh * 64
                sc.dma(ods[ab], [(scr["mixT"][row // 128, (row % 128):(row % 128) + 64, t0:t0 + SBK], ost[ab][:])],
                       reads=[ostB[ab]])


def phase4(k, l):
    nc, sc, cfg, W, C = k.nc, k.sc, k.cfg, k.W, k.C
    S, HD, HL, G, HS, NB, NX = cfg.S, cfg.HD, cfg.HL, cfg.G, cfg.HS, cfg.NB, cfg.NXBC
    scr = k.scr
    XC = HS // 2
    HW = HS * 64
    with ExitStack() as ps:
        sb, pt = _mk(k, l, ps)
        triu = sb("p4_triu", [128, 128]); negm = sb("p4_negm", [128, 128])
        dtb = sb("p4_dtb", [128, HS]); alog = sb("p4_alog", [128, HS]); Dd = sb("p4_D", [128, HS]); gn = sb("p4_gn", [128, HW])
        negA = sb("p4_negA", [128, HS])
        cB = Buf(); d0 = sc.dsem()
        sc.dma(d0, [(triu[:], C["triu"]), (negm[:], C["negm"]),
                    (dtb[:], W["dt_bias"][l:l + 1, :].broadcast_to([128, HS])),
                    (alog[:], W["A_log"][l:l + 1, :].broadcast_to([128, HS])),
                    (Dd[:], W["ssd_D"][l:l + 1, :].broadcast_to([128, HS])),
                    (gn[:], W["ssd_norm"][l:l + 1, :].broadcast_to([128, HW]))], writes=[cB])
        pB = Buf()
        sc.op("act", lambda h: h.activation(out=negA[:], in_=alog[:], func=AF.Exp), reads=[cB], writes=[pB])
        sc.op("dve", lambda h: h.tensor_scalar(out=negA[:], in0=negA[:], scalar1=-1.0, scalar2=None, op0=ALU.mult), reads=[pB], writes=[pB])
        xb_ = [sb("p4_xb%d" % i, [128, NX, TB], BF16) for i in range(2)]
        z_ = [sb("p4_z%d" % i, [128, 4, HW], BF16) for i in range(2)]
        dt_ = [sb("p4_dt%d" % i, [128, 4, HS]) for i in range(2)]
        inB = [Buf() for _ in range(2)]; lds = [sc.dsem() for _ in range(2)]
        dtp = sb("p4_dtp", [128, 4, HS]); aa = sb("p4_a", [128, 4, HS]); dB = Buf()
        a_bc = sb("p4_abc", [128, HS, 128]); abB = Buf()
        cs_sb2 = [sb("p4_cs%d" % i, [128, HS]) for i in range(2)]; csl2 = [sb("p4_csl%d" % i, [128, HS]) for i in range(2)]; csB2 = [Buf() for _ in range(2)]
        arg = sb("p4_arg", [128, HS, 128]); argB = Buf()
        E = sb("p4_E", [128, HS, 128]); EB = Buf()
        MT2 = [sb("p4_MT%d" % i, [128, HS, 128], BF16) for i in range(2)]; MTB2 = [Buf() for _ in range(2)]
        x_sb2 = [sb("p4_x%d" % i, [128, HW]) for i in range(2)]; B_sb2 = [sb("p4_B%d" % i, [128, G * 128], BF16) for i in range(2)]; xB2 = [Buf() for _ in range(2)]
        xdt2 = [sb("p4_xdt%d" % i, [128, HS, 64], BF16) for i in range(2)]; xdtB2 = [Buf() for _ in range(2)]
        xdd2 = [sb("p4_xdd%d" % i, [128, HS, 64], BF16) for i in range(2)]; xddB2 = [Buf() for _ in range(2)]
        ecs2 = [sb("p4_ecs%d" % i, [128, HS]) for i in range(2)]; dst_ = sb("p4_dst", [128, HS]); edec2 = [sb("p4_edec%d" % i, [128, HS]) for i in range(2)]; eB2 = [Buf() for _ in range(2)]; dstB = Buf()
        t13 = [sb("p4_t1%d" % i, [128, HW]) for i in range(3)]; t1B3 = [Buf() for _ in range(3)]
        t33 = [sb("p4_t3%d" % i, [128, HW]) for i in range(3)]; t3B3 = [Buf() for _ in range(3)]
        yv = sb("p4_yv", [128, HW]); yvB = Buf()
        szb = [sb("p4_szb%d" % i, [128, 4, HW]) for i in range(2)]; szbB = [Buf() for _ in range(2)]
        junk = sb("p4_junk", [128, HW], BF16); junkB = Buf()
        ss2 = sb("p4_ss2", [128, G]); rs2 = sb("p4_rs2", [128, G]); ssB = Buf(); rsB = Buf()
        yn = sb("p4_yn", [128, HW], BF16); ynB = Buf()
        yst = [sb("p4_yst%d" % i, [128, XC, TB], BF16) for i in range(2)]; ystB = [Buf() for _ in range(2)]
        sds = [sc.dsem() for _ in range(2)]
        st = sb("p4_st", [128, HW]); st_bf = sb("p4_stbf", [128, HW], BF16); stB = Buf(); stbB = Buf()
        ps1 = pt("p4_ps1", [128, 512]); ps1B = PBuf()
        psR = pt("p4_psR", [128, 2, 512]); psRB = PBuf()
        psXT = pt("p4_psXT", [128, 1024], BF16); psXTB = PBuf(); psXTb = pt("p4_psXTb", [128, 1024], BF16); psXTbB = PBuf()
        psY = pt("p4_psY", [128, 512]); psYB = PBuf()
        psYO = pt("p4_psYO", [128, 512]); psYOB = PBuf()
        psS = pt("p4_psS", [128, 512]); psSB = PBuf()
        sc.op("pool", lambda h: h.memset(st[:], 0.0), writes=[stB])
        sc.op("pool", lambda h: h.memset(st_bf[:], 0.0), writes=[stbB])
        XTO = XC * 128 + G * 128
        def load4(tt):
            bb = tt % 2
            tk = tt * TB
            sc.dma(lds[bb], [(xb_[bb][:], scr["xbcT"][:, :, tk:tk + TB].rearrange("c p s -> p c s")),
                             (z_[bb][:], scr["zs"][tk:tk + TB, :].rearrange("(n p) f -> p n f", p=128)),
                             (dt_[bb][:], scr["dts"][tk:tk + TB, :].rearrange("(n p) f -> p n f", p=128))],
                   writes=[inB[bb]])

        load4(0)
        for t in range(NB):
            tok0 = t * TB
            b = t % 2
            if t + 1 < NB:
                load4(t + 1)
            sc.op("dve", lambda h: h.tensor_tensor(out=dtp[:], in0=dt_[b][:], in1=dtb[:, None, :].broadcast_to([128, 4, HS]), op=ALU.add),
                  reads=[inB[b], cB], writes=[dB])
            sc.op("act", lambda h: h.activation(out=dtp[:], in_=dtp[:], func=AF.Exp), reads=[dB], writes=[dB])
            sc.op("act", lambda h: h.activation(out=dtp[:], in_=dtp[:], func=AF.Ln, bias=1.0), reads=[dB], writes=[dB])
            sc.op("dve", lambda h: h.tensor_tensor(out=aa[:], in0=dtp[:], in1=negA[:, None, :].broadcast_to([128, 4, HS]), op=ALU.mult),
                  reads=[dB, pB], writes=[dB])
            sc.op("act", lambda h: h.activation(out=szb[b][:], in_=z_[b][:], func=AF.Silu), reads=[inB[b]], writes=[szbB[b]])
            def front(n):
                q = n % 2
                csl_ = slice(n * 128, (n + 1) * 128)
                sc.op("dve", lambda h: h.tensor_copy(out=a_bc[:], in_=aa[:, n, :, None].broadcast_to([128, HS, 128])),
                      reads=[dB], writes=[abB])
                sc.op("pe", lambda h: h.matmul(ps1[:, 256:256 + HS], lhsT=triu[:], rhs=aa[:, n, :], start=True, stop=True),
                      reads=[cB, dB], writes=[ps1B])
                for hh in range(HS):
                    sc.op("pe", lambda h: h.matmul(psR[:, hh // 4, (hh % 4) * 128:(hh % 4 + 1) * 128], lhsT=a_bc[:, hh, :], rhs=triu[:],
                                                   start=True, stop=True), reads=[cB, abB], writes=[psRB], inc=(hh == HS - 1))
                psRv = psR[:].rearrange("p a (b c) -> p (a b) c", c=128)[:, 0:HS, :]
                sc.op("act", lambda h: h.copy(out=cs_sb2[q][:], in_=ps1[:, 256:256 + HS]), reads=[ps1B], writes=[csB2[q]])
                sc.op("act", lambda h: h.copy(out=csl2[q][:], in_=psRv[:, :, 127]), reads=[psRB], writes=[csB2[q]])
                sc.op("dve", lambda h: h.tensor_tensor(out=arg[:], in0=psRv, in1=cs_sb2[q][:, :, None].broadcast_to([128, HS, 128]), op=ALU.subtract),
                      reads=[psRB, csB2[q]], writes=[argB])
                sc.op("dve", lambda h: h.tensor_tensor(out=arg[:], in0=arg[:], in1=negm[:, None, :].broadcast_to([128, HS, 128]), op=ALU.add),
                      reads=[argB, cB], writes=[argB])
                sc.op("act", lambda h: h.activation(out=E[:], in_=arg[:], func=AF.Exp), reads=[argB], writes=[EB])
                for g in range(G):
                    sc.op("pe", lambda h: h.matmul(ps1[:, g * 128:(g + 1) * 128], lhsT=xb_[b][:, XC + g, csl_], rhs=xb_[b][:, XC + G + g, csl_],
                                                   start=True, stop=True), reads=[inB[b]], writes=[ps1B], inc=(g == G - 1))
                for g in range(G):
                    sc.op("dve", lambda h: h.tensor_tensor(out=MT2[q][:, 3 * g:3 * g + 3, :], in0=E[:, 3 * g:3 * g + 3, :],
                                                           in1=ps1[:, None, g * 128:(g + 1) * 128].broadcast_to([128, 3, 128]), op=ALU.mult),
                          reads=[EB, ps1B], writes=[MTB2[q]])
                for j in range(XC + G):
                    sc.op("pe", lambda h: h.transpose(out=psXT[:, j * 128:(j + 1) * 128], in_=xb_[b][:, j, csl_], identity=k.ident_bf[:]),
                          reads=[inB[b], k.cB], writes=[psXTB], inc=(j == XC + G - 1))
                sc.op("act", lambda h: h.copy(out=x_sb2[q][:], in_=psXT[:, 0:HW]), reads=[psXTB], writes=[xB2[q]])
                sc.op("act", lambda h: h.copy(out=B_sb2[q][:], in_=psXT[:, HW:HW + G * 128]), reads=[psXTB], writes=[xB2[q]])
                sc.op("dve", lambda h: h.tensor_tensor(out=xdt2[q][:], in0=x_sb2[q][:].rearrange("p (h e) -> p h e", e=64),
                                                       in1=dtp[:, n, :, None].broadcast_to([128, HS, 64]), op=ALU.mult),
                      reads=[xB2[q], dB], writes=[xdtB2[q]])
                sc.op("act", lambda h: h.activation(out=ecs2[q][:], in_=cs_sb2[q][:], func=AF.Exp), reads=[csB2[q]], writes=[eB2[q]])
                sc.op("pool", lambda h: h.tensor_tensor(out=t33[n % 3][:].rearrange("p (h e) -> p h e", e=64), in0=x_sb2[q][:].rearrange("p (h e) -> p h e", e=64),
                                                        in1=Dd[:, :, None].broadcast_to([128, HS, 64]), op=ALU.mult),
                      reads=[xB2[q], cB], writes=[t3B3[n % 3]])
                sc.op("dve", lambda h: h.tensor_tensor(out=dst_[:], in0=csl2[q][:], in1=cs_sb2[q][:], op=ALU.subtract), reads=[csB2[q]], writes=[dstB])
                sc.op("act", lambda h: h.activation(out=dst_[:], in_=dst_[:], func=AF.Exp), reads=[dstB], writes=[dstB])
                sc.op("act", lambda h: h.activation(out=edec2[q][:], in_=csl2[q][:], func=AF.Exp), reads=[csB2[q]], writes=[eB2[q]])
                sc.op("dve", lambda h: h.tensor_tensor(out=xdd2[q][:], in0=xdt2[q][:], in1=dst_[:, :, None].broadcast_to([128, HS, 64]), op=ALU.mult),
                      reads=[xdtB2[q], dstB], writes=[xddB2[q]])

            def back(n):
                q = n % 2
                csl_ = slice(n * 128, (n + 1) * 128)
                for hh in range(HS):
                    sc.op("pe", lambda h: h.matmul(psY[:, hh * 64:(hh + 1) * 64], lhsT=MT2[q][:, hh, :], rhs=xdt2[q][:, hh, :], start=True, stop=True),
                          reads=[MTB2[q], xdtB2[q]], writes=[psYB], inc=(hh == HS - 1))
                for g in range(G):
                    sc.op("pe", lambda h: h.matmul(psYO[:, g * 192:(g + 1) * 192], lhsT=xb_[b][:, XC + G + g, csl_], rhs=st_bf[:, g * 192:(g + 1) * 192],
                                                   start=True, stop=True), reads=[inB[b], stbB], writes=[psYOB], inc=(g == G - 1))
                sc.op("dve", lambda h: h.tensor_tensor(out=t13[n % 3][:].rearrange("p (h e) -> p h e", e=64), in0=psYO[:, 0:HW].rearrange("p (h e) -> p h e", e=64),
                                                       in1=ecs2[q][:, :, None].broadcast_to([128, HS, 64]), op=ALU.mult),
                      reads=[psYOB, eB2[q]], writes=[t1B3[n % 3]])
                sc.op("dve", lambda h: h.tensor_tensor(out=t13[n % 3][:], in0=psY[:, 0:HW], in1=t13[n % 3][:], op=ALU.add), reads=[psYB, t1B3[n % 3]], writes=[t1B3[n % 3]])
                for g in range(G):
                    sc.op("pe", lambda h: h.matmul(psS[:, g * 192:(g + 1) * 192], lhsT=B_sb2[q][:, g * 128:(g + 1) * 128],
                                                   rhs=xdd2[q][:, 3 * g:3 * g + 3, :].rearrange("p h e -> p (h e)"), start=True, stop=True),
                          reads=[xB2[q], xddB2[q]], writes=[psSB], inc=(g == G - 1))
                sc.op("dve", lambda h: h.tensor_tensor(out=st[:].rearrange("p (h e) -> p h e", e=64), in0=st[:].rearrange("p (h e) -> p h e", e=64),
                                                       in1=edec2[q][:, :, None].broadcast_to([128, HS, 64]), op=ALU.mult),
                      reads=[stB, eB2[q]], writes=[stB])
                sc.op("dve", lambda h: h.tensor_tensor(out=st[:], in0=st[:], in1=psS[:, 0:HW], op=ALU.add), reads=[stB, psSB], writes=[stB])
                sc.op("dve", lambda h: h.tensor_copy(out=st_bf[:], in_=st[:]), reads=[stB], writes=[stbB])

            def post(n):
                q = n % 2
                csl_ = slice(n * 128, (n + 1) * 128)
                sc.op("pool", lambda h: h.tensor_tensor(out=yv[:], in0=t13[n % 3][:], in1=t33[n % 3][:], op=ALU.add), reads=[t1B3[n % 3], t3B3[n % 3]], writes=[yvB])
                sc.op("dve", lambda h: h.tensor_tensor(out=yv[:], in0=yv[:], in1=szb[b][:, n, :], op=ALU.mult), reads=[yvB, szbB[b]], writes=[yvB])
                for g in range(G):
                    sc.op("act", lambda h: h.activation(out=junk[:, 0:192], in_=yv[:, g * 192:(g + 1) * 192], func=AF.Square,
                                                        accum_out=ss2[:, g:g + 1]), reads=[yvB], writes=[junkB, ssB])
                rms_rstd(k, ps, ss2, rs2, G, 192, ssB, rsB)
                for g in range(G):
                    sc.op("dve", lambda h: h.scalar_tensor_tensor(out=yn[:, g * 192:(g + 1) * 192], in0=yv[:, g * 192:(g + 1) * 192],
                                                                  scalar=rs2[:, g:g + 1], in1=gn[:, g * 192:(g + 1) * 192],
                                                                  op0=ALU.mult, op1=ALU.mult), reads=[yvB, rsB, cB], writes=[ynB])
                for j in range(XC):
                    sc.op("pe", lambda h: h.transpose(out=psXTb[:, j * 128:(j + 1) * 128], in_=yn[:, j * 128:(j + 1) * 128],
                                                      identity=k.ident_bf[:]), reads=[ynB, k.cB], writes=[psXTbB], inc=(j == XC - 1))
                sc.op("act", lambda h: h.copy(out=yst[b][:, :, csl_], in_=psXTb[:, 0:XC * 128].rearrange("p (j s) -> p j s", s=128)),
                      reads=[psXTbB], writes=[ystB[b]])

            front(0)
            for n in range(4):
                if n + 1 < 4:
                    front(n + 1)
                back(n)
                if n > 0:
                    post(n - 1)
            post(3)
            c0 = (HD * 64 + HL * 64) // 128
            sc.dma(sds[b], [(scr["mixT"][c0:c0 + XC, :, tok0:tok0 + TB].rearrange("c p s -> p c s"), yst[b][:])], reads=[ystB[b]])


def post_norm_add(k, sc, psF, psFB, n, ss, ssB, rstd, rsB, gb, gB, yt, ytB, xres, xresB, xo, xoB, junk, junkB, add_eng="pool"):
    sc.op("act", lambda h: h.activation(out=junk[:], in_=psF[:], func=AF.Square, accum_out=ss[:, n:n + 1]),
          reads=[psFB], writes=[junkB, ssB])
    rms_rstd(k, None, ss[:, n:n + 1], rstd[:, n:n + 1], 1, D, ssB, rsB)
    sc.op("dve", lambda h: h.scalar_tensor_tensor(out=yt[:], in0=psF[:], scalar=rstd[:, n:n + 1], in1=gb[:],
                                                  op0=ALU.mult, op1=ALU.mult), reads=[psFB, rsB, gB], writes=[ytB])
    sc.op(add_eng, lambda h: h.tensor_tensor(out=xo[:, n, :], in0=xres[:, n, :], in1=yt[:], op=ALU.add),
          reads=[ytB, xresB], writes=[xoB])


def phase5a(k, l, x_src):
    nc, sc, cfg, W, C = k.nc, k.sc, k.cfg, k.W, k.C
    S, NB, MC = cfg.S, cfg.NB, cfg.MIXC
    scr = k.scr
    with ExitStack() as ps:
        sb, pt = _mk(k, l, ps)
        wout = sb("p5a_w", [128, MC, D], BF16)
        stg = [sb("p5a_stg%d" % i, [128, 512]) for i in range(3)]
        stgB = [Buf() for _ in range(3)]; ds = [sc.dsem() for _ in range(3)]
        k.gB, k.wB = Buf(), Buf()
        gb = sb("p5a_g", [128, D]); gB = Buf(); d0 = sc.dsem()
        sc.dma(d0, [(gb[:], W["g_postmix"][l:l + 1, :].broadcast_to([128, D]))], writes=[gB])
        load_weights_bf16(k, ps, wout, W["w_out"][l], D, None, stg, stgB, ds, MC)
        mT = [sb("p5a_m%d" % i, [128, MC, TB], BF16) for i in range(2)]; mB = [Buf() for _ in range(2)]
        xr = [sb("p5a_x%d" % i, [128, 4, D]) for i in range(2)]; xB = [Buf() for _ in range(2)]
        lds = [sc.dsem() for _ in range(2)]
        xo = [sb("p5a_xo%d" % i, [128, 4, D]) for i in range(2)]; xoB = [Buf() for _ in range(2)]
        sds = [sc.dsem() for _ in range(2)]
        psF = [pt("p5a_psF%d" % i, [128, D]) for i in range(2)]; psFB = [PBuf() for _ in range(2)]
        ss = sb("p5a_ss", [128, 4]); ssB = Buf(); rstd = sb("p5a_rstd", [128, 4]); rsB = Buf()
        yt = sb("p5a_yt", [128, D]); ytB = Buf()
        junk = sb("p5a_junk", [128, D], BF16); junkB = Buf()
        ci = 0
        def load5a(tt):
            bb = tt % 2
            tk = tt * TB
            sc.dma(lds[bb], [(mT[bb][:], scr["mixT"][:, :, tk:tk + TB].rearrange("c p s -> p c s")),
                             (xr[bb][:], x_src[tk:tk + TB, :].rearrange("(n p) d -> p n d", p=128))],
                   writes=[mB[bb], xB[bb]])

        load5a(0)
        for t in range(NB):
            tok0 = t * TB
            b = t % 2
            if t + 1 < NB:
                load5a(t + 1)
            for n in range(4):
                a = ci % 2
                ci += 1
                for half in range(2):
                    for kc in range(MC):
                        sc.op("pe", lambda h, kc=kc, half=half: h.matmul(
                            psF[a][:, half * 512:(half + 1) * 512], lhsT=mT[b][:, kc, n * 128:(n + 1) * 128],
                            rhs=wout[:, kc, half * 512:(half + 1) * 512], start=(kc == 0), stop=(kc == MC - 1)),
                            reads=[k.wB, mB[b]], writes=[psFB[a]], inc=(kc == MC - 1 and half == 1))
                post_norm_add(k, sc, psF[a], psFB[a], n, ss, ssB, rstd, rsB, gb, gB, yt, ytB, xr[b], xB[b], xo[b], xoB[b], junk, junkB,
                              add_eng=("dve" if n % 2 == 0 else "pool"))
            sc.dma(sds[b], [(scr["xa"][tok0:tok0 + TB, :].rearrange("(n p) d -> p n d", p=128), xo[b][:])], reads=[xoB[b]])


def phase5b(k, l, x_dst):
    nc, sc, cfg, W, C = k.nc, k.sc, k.cfg, k.W, k.C
    S, FC, TBF = cfg.S, cfg.FC, cfg.TBF
    NT = TBF // 128
    scr = k.scr
    with ExitStack() as ps:
        sb, pt = _mk(k, l, ps)
        wup = sb("p5b_wup", [128, 8, 2 * cfg.DFF], BF16)
        wdn = sb("p5b_wdn", [128, FC, D], BF16)
        stg = [sb("p5b_stg%d" % i, [128, 512]) for i in range(3)]
        stgB = [Buf() for _ in range(3)]; ds = [sc.dsem() for _ in range(3)]
        k.gB, k.wB = Buf(), Buf()
        gain = sb("p5b_gain", [128, 8]); gb = sb("p5b_g", [128, D]); gB = Buf(); d0 = sc.dsem()
        fcw = sb("p5b_fcw", [128, 2 * FC, 3]); fcb = sb("p5b_fcb", [128, 2 * FC])
        sc.dma(d0, [(gb[:], W["g_postffn"][l:l + 1, :].broadcast_to([128, D])), (gain[:], W["g_preffn"][l]),
                    (fcw[:], W["fcw"][l]), (fcb[:], W["fcb"][l])], writes=[gB, k.gB])
        load_weights_bf16(k, ps, wup, W["ffn_up"][l], 2 * cfg.DFF, gain, stg, stgB, ds, 8)
        load_weights_bf16(k, ps, wdn, W["ffn_down"][l], D, None, stg, stgB, ds, FC)
        xr2 = [sb("p5b_x%d" % i, [128, NT, D]) for i in range(2)]; xB2 = [[Buf() for _ in range(NT)] for i in range(2)]
        lds2 = [[sc.dsem() for _ in range(NT)] for i in range(2)]; sds2 = [sc.dsem() for _ in range(2)]
        xs2 = [sb("p5b_xs%d" % i, [128, NT, D], BF16) for i in range(2)]; xsB2 = [Buf() for _ in range(2)]
        hT2 = [sb("p5b_hT%d" % i, [128, 8, 2 + TBF], BF16) for i in range(2)]; hTB2 = [Buf() for _ in range(2)]
        aT = sb("p5b_aT", [128, FC, TBF], BF16); aTB = Buf()
        NU = 4
        u = [sb("p5b_u%d" % i, [128, 2 + TBF]) for i in range(NU)]; uB = [Buf() for _ in range(NU)]; uhB = [Buf() for _ in range(NU)]
        y = [sb("p5b_y%d" % i, [128, TBF]) for i in range(NU)]; yB = [Buf() for _ in range(NU)]
        sg = [sb("p5b_sg%d" % i, [128, TBF]) for i in range(2)]; sgB = [Buf() for _ in range(2)]
        ss = sb("p5b_ss", [128, 4]); ssB = Buf(); rstd = sb("p5b_rstd", [128, 4]); rsB = Buf()
        yt = sb("p5b_yt", [128, D]); ytB = Buf()
        junk = sb("p5b_junk", [128, D], BF16); junkB = Buf()
        psT = [pt("p5b_psT%d" % i, [128, 2, 512], BF16) for i in range(2)]; psTB = [PBuf() for _ in range(2)]
        psU = [pt("p5b_psU%d" % i, [128, 512]) for i in range(NU)]; psUB = [PBuf() for _ in range(NU)]
        psF = [pt("p5b_psF%d" % i, [128, D]) for i in range(1)]; psFB = [PBuf() for _ in range(1)]
        for i in range(2):
            sc.op("pool", lambda h, i=i: h.memset(hT2[i][:], 0.0), writes=[hTB2[i]])
        ci = 0
        NBLK = S // TBF

        def do_norm(t):
            b = t % 2
            tok0 = t * TBF
            for n in range(NT):
                sc.dma(lds2[b][n], [(xr2[b][:, n, :], scr["xa"][tok0 + n * 128:tok0 + (n + 1) * 128, :])], writes=[xB2[b][n]])
            for n in range(NT):
                sc.op("act", lambda h, n=n: h.activation(out=junk[:], in_=xr2[b][:, n, :], func=AF.Square, accum_out=ss[:, n:n + 1]),
                      reads=[xB2[b][n]], writes=[junkB, ssB])
            rms_rstd(k, ps, ss, rstd, NT, D, ssB, rsB)
            for n in range(NT):
                if n % 2 == 0:
                    sc.op("dve", lambda h, n=n: h.tensor_scalar(
                        out=xs2[b][:, n, :], in0=xr2[b][:, n, :], scalar1=rstd[:, n:n + 1], scalar2=None, op0=ALU.mult),
                        reads=[xB2[b][n], rsB], writes=[xsB2[b]])
                else:
                    sc.op("act", lambda h, n=n: h.activation(out=xs2[b][:, n, :], in_=xr2[b][:, n, :], func=AF.Copy, scale=rstd[:, n:n + 1]),
                          reads=[xB2[b][n], rsB], writes=[xsB2[b]])
            if t > 0:
                sc.op("pool", lambda h: h.tensor_copy(out=hT2[b][:, :, 0:2], in_=hT2[1 - b][:, :, TBF:TBF + 2]), reads=[hTB2[1 - b]], writes=[hTB2[b]])
            for cp in range(4):
                pi = cp % 2
                for c2 in range(2):
                    c = cp * 2 + c2
                    for n in range(NT):
                        sc.op("pe", lambda h, c=c, c2=c2, n=n: h.transpose(
                            out=psT[pi][:, c2, n * 128:(n + 1) * 128], in_=xs2[b][:, n, c * 128:(c + 1) * 128],
                            identity=k.ident_bf[:]), reads=[xsB2[b], k.cB], writes=[psTB[pi]], inc=(c2 == 1 and n == NT - 1))
                sc.op("act", lambda h, cp=cp: h.copy(out=hT2[b][:, 2 * cp:2 * cp + 2, 2:2 + TBF], in_=psT[pi][:, :, 0:TBF]),
                      reads=[psTB[pi]], writes=[hTB2[b]])

        def do_chunks(t):
            b = t % 2
            chunks = [(j, wi, c) for j in range(FC) for wi, c in enumerate((j, FC + j))]

            def st_pe(i):
                j, wi, c = chunks[i]
                a = i % NU
                for kc in range(8):
                    sc.op("pe", lambda h: h.matmul(psU[a][:, 0:TBF + 2], lhsT=wup[:, kc, c * 128:(c + 1) * 128], rhs=hT2[b][:, kc, :],
                                                   start=(kc == 0), stop=(kc == 7)), reads=[k.wB, hTB2[b]], writes=[psUB[a]], inc=(kc == 7))

            def st_copy(i):
                j, wi, c = chunks[i]
                a = i % NU
                sc.op("act", lambda h: h.copy(out=u[a][:, 0:2 + TBF], in_=psU[a][:, 0:2 + TBF]), reads=[psUB[a]], writes=[uB[a]])
                sc.op("act", lambda h: h.activation(out=y[a][:], in_=psU[a][:, 2:2 + TBF], func=AF.Identity, scale=fcw[:, c, 2:3], bias=fcb[:, c:c + 1]),
                      reads=[psUB[a], k.gB], writes=[yB[a]])

            def st_conv(i):
                j, wi, c = chunks[i]
                a = i % NU
                for kk in range(2):
                    sc.op("dve", lambda h: h.scalar_tensor_tensor(out=y[a][:], in0=u[a][:, kk:kk + TBF], scalar=fcw[:, c, kk:kk + 1], in1=y[a][:],
                                                                  op0=ALU.mult, op1=ALU.add), reads=[uB[a], yB[a], k.gB], writes=[yB[a]])

            def st_gate(i):
                j, wi, c = chunks[i]
                a = i % NU
                if wi == 0:
                    sc.op("act", lambda h: h.activation(out=sg[j % 2][:], in_=y[a][:], func=AF.Silu), reads=[yB[a]], writes=[sgB[j % 2]])
                else:
                    sc.op("pool", lambda h: h.tensor_tensor(out=aT[:, j, :], in0=sg[j % 2][:], in1=y[a][:], op=ALU.mult),
                          reads=[sgB[j % 2], yB[a]], writes=[aTB])

            skewed(len(chunks), [st_pe, st_copy, st_conv, st_gate])

        def do_down(t):
            b = t % 2
            tok0 = t * TBF
            for n in range(NT):
                a = 0
                for half in range(2):
                    for j in range(FC):
                        sc.op("pe", lambda h, j=j, half=half: h.matmul(
                            psF[a][:, half * 512:(half + 1) * 512], lhsT=aT[:, j, n * 128:(n + 1) * 128],
                            rhs=wdn[:, j, half * 512:(half + 1) * 512], start=(j == 0), stop=(j == FC - 1)),
                            reads=[k.wB, aTB], writes=[psFB[a]], inc=(j == FC - 1 and half == 1))
                post_norm_add(k, sc, psF[a], psFB[a], n, ss, ssB, rstd, rsB, gb, gB, yt, ytB, xr2[b], xB2[b][n], xr2[b], xB2[b][n], junk, junkB)
            sc.dma(sds2[b], [(x_dst[tok0:tok0 + TBF, :].rearrange("(n p) d -> p n d", p=128), xr2[b][:])], reads=xB2[b])

        do_norm(0)
        for t in range(NBLK):
            do_chunks(t)
            if t + 1 < NBLK:
                do_norm(t + 1)
            do_down(t)


def kernel(**inputs):
    cfg = Cfg()
    nc = build(cfg)
    hp = host_params(cfg, inputs)
    consts = host_consts(cfg)
    x = np.asarray(inputs["x"], np.float32)
    nb = x.shape[0]
    work = [0, 1, 4, 5][:nb]
    zeros = {"x": np.zeros_like(x[0])}
    for kk, v in hp.items():
        zeros[kk] = np.zeros_like(v)
    for kk, v in consts.items():
        zeros[kk] = np.zeros_like(v)
    in_maps = []
    for c in range(8):
        if c in work:
            m = {"x": np.ascontiguousarray(x[work.index(c)])}
            m.update(hp)
            m.update(consts)
        else:
            m = zeros
        in_maps.append(m)
    res = run_bass_kernel_spmd(nc, in_maps, core_ids=list(range(8)))
    return np.stack([np.asarray(res.results[c]["out"], np.float32) for c in work])
```
